# Optimizing a Trainium2 kernel written in Bass

```python
import math
import jax, jax.numpy as jnp
from jax import lax
import numpy as np

D_MODEL = 1024
BATCH = 16
SEQ = 2048
DEPTH = 2
DEC_BATCH = 8
DEC_SEQ = 2048
PAST_LEN = 128

HY_WIDTH = D_MODEL // 2
HY_ORDER = 2
FILT_EMB = 33
FILT_BANDS = (FILT_EMB - 1) // 2
FILT_HIDDEN = 64
DECAY_TARGET = 1e-2
FAST_DECAY_PCT = 0.3
SLOW_DECAY_PCT = 1.5
N_DIR = 2
HEAD_DIM = 64
N_Q_HEADS = (D_MODEL // 2) // HEAD_DIM
N_KV_HEADS = 2
GQA_GROUP = N_Q_HEADS // N_KV_HEADS
WINDOW = 128
BLOCK = 128
ROPE_THETA = 10000.0
ATTN_WIDTH = N_Q_HEADS * HEAD_DIM
KV_WIDTH = N_KV_HEADS * HEAD_DIM
HY_COLS = (HY_ORDER + 1) * HY_WIDTH
IN_COLS = HY_COLS + ATTN_WIDTH + 2 * KV_WIDTH + 2 * D_MODEL
SPLITS = (HY_COLS, HY_COLS + ATTN_WIDTH, HY_COLS + ATTN_WIDTH + KV_WIDTH,
          HY_COLS + ATTN_WIDTH + 2 * KV_WIDTH, HY_COLS + ATTN_WIDTH + 2 * KV_WIDTH + D_MODEL)
PEER_HEADS = 8
N_KEYS = 128
N_EXPERTS = N_KEYS * N_KEYS
PEER_TOPK = 16
PEER_QDIM = 256
PEER_HALF = PEER_QDIM // 2
PEER_CHUNK = 128
EPS = 1e-6

kernel_name = "hybrid_hyena_swa_peer_encoder"


def rmsnorm(x, gain):
    x32 = x.astype(jnp.float32)
    y = x32 * lax.rsqrt(jnp.mean(x32 * x32, axis=-1, keepdims=True) + EPS)
    return y.astype(x.dtype) * gain


def rope(x):
    L = x.shape[1]
    inv = ROPE_THETA ** (-jnp.arange(0, HEAD_DIM, 2, dtype=jnp.float32) / HEAD_DIM)
    ang = jnp.arange(L, dtype=jnp.float32)[:, None] * inv[None, :]
    cos = jnp.cos(ang)[None, :, None, :]
    sin = jnp.sin(ang)[None, :, None, :]
    x32 = x.astype(jnp.float32)
    x1, x2 = x32[..., :HEAD_DIM // 2], x32[..., HEAD_DIM // 2:]
    return jnp.concatenate([x1 * cos - x2 * sin, x2 * cos + x1 * sin], axis=-1).astype(x.dtype)


def implicit_filters(L, w1, b1, freq, w2, b2, w3):
    f32 = jnp.float32
    t = jnp.linspace(0.0, 1.0, L, dtype=f32)[:, None]
    w = 2.0 * math.pi * jnp.arange(L, dtype=f32)[:, None] / L
    bands = jnp.linspace(1e-4, FILT_BANDS - 1, FILT_BANDS, dtype=f32)[None, :]
    z = jnp.concatenate([t, jnp.cos(bands * w), -jnp.sin(bands * w)], axis=-1)
    fr = freq.astype(f32)
    a = jnp.sin(fr * (z @ w1.astype(f32) + b1.astype(f32)))
    a = jnp.sin(fr * (a @ w2.astype(f32) + b2.astype(f32)))
    hf = (a @ w3.astype(f32)).reshape(L, HY_ORDER, N_DIR, HY_WIDTH)
    max_decay = math.log(DECAY_TARGET) / FAST_DECAY_PCT
    min_decay = math.log(DECAY_TARGET) / SLOW_DECAY_PCT
    deltas = jnp.linspace(min_decay, max_decay, HY_WIDTH, dtype=f32)
    hf = hf * jnp.exp(-t * jnp.abs(deltas))[:, None, None, :]
    h_full = jnp.concatenate([hf[:, :, 0], jnp.zeros((1, HY_ORDER, HY_WIDTH), f32), hf[:0:-1, :, 1]], axis=0)
    return jnp.fft.rfft(h_full, axis=0)


def fftconv(u, Hf, bias):
    L = u.shape[1]
    u32 = u.astype(jnp.float32)
    U = jnp.fft.rfft(u32, n=2 * L, axis=1)
    y = jnp.fft.irfft(U * Hf[None], n=2 * L, axis=1)[:, :L]
    return (y + u32 * bias.astype(jnp.float32)).astype(u.dtype)


def window_attention(q, k, v, sink):
    B, L = q.shape[0], q.shape[1]
    nb = L // BLOCK
    qb = q.reshape(B, nb, BLOCK, N_KV_HEADS, GQA_GROUP, HEAD_DIM)
    pad = ((0, 0), (BLOCK, BLOCK), (0, 0), (0, 0))
    kp = jnp.pad(k, pad).reshape(B, nb + 2, BLOCK, N_KV_HEADS, HEAD_DIM)
    vp = jnp.pad(v, pad).reshape(B, nb + 2, BLOCK, N_KV_HEADS, HEAD_DIM)
    kb = jnp.concatenate([kp[:, :-2], kp[:, 1:-1], kp[:, 2:]], axis=2)
    vb = jnp.concatenate([vp[:, :-2], vp[:, 1:-1], vp[:, 2:]], axis=2)
    s = jnp.einsum('bnqkgd,bnjkd->bnkgqj', qb, kb).astype(jnp.float32) * (HEAD_DIM ** -0.5)
    blk = jnp.arange(nb)[:, None, None] * BLOCK
    qpos = blk + jnp.arange(BLOCK)[None, :, None]
    kpos = blk - BLOCK + jnp.arange(3 * BLOCK)[None, None, :]
    valid = (jnp.abs(kpos - qpos) <= WINDOW) & (kpos >= 0) & (kpos < L)
    s = jnp.where(valid[None, :, None, None], s, -jnp.inf)
    sk = sink.astype(jnp.float32).reshape(N_KV_HEADS, GQA_GROUP)[None, None, :, :, None, None]
    m = jnp.maximum(jnp.max(s, axis=-1, keepdims=True), sk)
    p = jnp.exp(s - m)
    p = p / (jnp.sum(p, axis=-1, keepdims=True) + jnp.exp(sk - m))
    o = jnp.einsum('bnkgqj,bnjkd->bnqkgd', p.astype(v.dtype), vb)
    return o.reshape(B, L, ATTN_WIDTH)


def token_mixer(h, w_in, conv_w, conv_b, f_w1, f_b1, f_freq, f_w2, f_b2, f_w3, f_bias,
                q_gain, k_gain, sink, w_pa, w_pb, w_out):
    B, L, _ = h.shape
    z = h @ w_in
    hy, q, k, v, ga, gb = jnp.split(z, SPLITS, axis=-1)
    hp = jnp.pad(hy, ((0, 0), (1, 1), (0, 0)))
    hy = hp[:, :-2] * conv_w[0] + hp[:, 1:-1] * conv_w[1] + hp[:, 2:] * conv_w[2] + conv_b
    v0, x1, x2 = jnp.split(hy, HY_ORDER + 1, axis=-1)
    Hf = implicit_filters(L, f_w1, f_b1, f_freq, f_w2, f_b2, f_w3)
    zz = x1 * fftconv(v0, Hf[:, 0], f_bias[0])
    ya = x2 * fftconv(zz, Hf[:, 1], f_bias[1])
    q = rope(rmsnorm(q.reshape(B, L, N_Q_HEADS, HEAD_DIM), q_gain))
    k = rope(rmsnorm(k.reshape(B, L, N_KV_HEADS, HEAD_DIM), k_gain))
    v = v.reshape(B, L, N_KV_HEADS, HEAD_DIM)
    yb = window_attention(q, k, v, sink)
    merged = jax.nn.sigmoid(ga) * (ya @ w_pa) + jax.nn.sigmoid(gb) * (yb @ w_pb)
    return merged @ w_out


def peer(h, wq, k1, k2, u_tab, v_tab):
    B, L, D = h.shape
    T = B * L
    ht = h.reshape(T, D)
    q = (ht @ wq).reshape(T, PEER_HEADS, PEER_QDIM)
    s1 = jnp.einsum('thd,nd->thn', q[..., :PEER_HALF], k1).astype(jnp.float32)
    s2 = jnp.einsum('thd,nd->thn', q[..., PEER_HALF:], k2).astype(jnp.float32)
    v1, i1 = lax.top_k(s1, PEER_TOPK)
    v2, i2 = lax.top_k(s2, PEER_TOPK)
    cand = (v1[..., :, None] + v2[..., None, :]).reshape(T, PEER_HEADS, PEER_TOPK * PEER_TOPK)
    vs, ic = lax.top_k(cand, PEER_TOPK)
    e1 = jnp.take_along_axis(i1, ic // PEER_TOPK, axis=-1)
    e2 = jnp.take_along_axis(i2, ic % PEER_TOPK, axis=-1)
    idx = e1 * N_KEYS + e2
    g = jax.nn.softmax(vs, axis=-1).astype(h.dtype)
    nc = T // PEER_CHUNK

    def chunk(args):
        xc, ec, gc = args
        a = jnp.einsum('td,thkd->thk', xc, u_tab[ec])
        w = gc * jax.nn.gelu(a, approximate=False)
        return jnp.einsum('thk,thkd->td', w, v_tab[ec])

    y = lax.map(chunk, (ht.reshape(nc, PEER_CHUNK, D),
                        idx.reshape(nc, PEER_CHUNK, PEER_HEADS, PEER_TOPK),
                        g.reshape(nc, PEER_CHUNK, PEER_HEADS, PEER_TOPK)))
    return y.reshape(B, L, D)


def trunk(x, c, w_mod, b_mod, g_norm1, g_norm2, w_in, conv_w, conv_b, f_w1, f_b1, f_freq, f_w2, f_b2, f_w3,
          f_bias, q_gain, k_gain, sink, w_pa, w_pb, w_out, peer_wq, peer_k1, peer_k2, peer_u, peer_v):
    for l in range(DEPTH):
        mod = jax.nn.silu(c) @ w_mod[l] + b_mod[l]
        sh1, sc1, gt1, sh2, sc2, gt2 = [m[:, None, :] for m in jnp.split(mod, 6, axis=-1)]
        h = rmsnorm(x, g_norm1[l]) * (1.0 + sc1) + sh1
        x = x + gt1 * token_mixer(h, w_in[l], conv_w[l], conv_b[l], f_w1[l], f_b1[l], f_freq[l], f_w2[l],
                                  f_b2[l], f_w3[l], f_bias[l], q_gain[l], k_gain[l], sink[l],
                                  w_pa[l], w_pb[l], w_out[l])
        h = rmsnorm(x, g_norm2[l]) * (1.0 + sc2) + sh2
        x = x + gt2 * peer(h, peer_wq[l], peer_k1[l], peer_k2[l], peer_u[l], peer_v[l])
    return x


def setup_inputs(seed: int = 0) -> dict:
    key = jax.random.key(seed)
    ks = iter(jax.random.split(key, 32))
    f32 = jnp.float32

    def nrm(shape, scale):
        return jax.random.normal(next(ks), shape, f32) * scale

    def gain(shape):
        return 1.0 + nrm(shape, 0.02)

    D = D_MODEL
    return {
        'x_prompt': nrm((BATCH, SEQ, D), 1.0),
        'x_sample': nrm((DEC_BATCH, DEC_SEQ, D), 1.0),
        'c_prompt': nrm((BATCH, D), 1.0),
        'c_sample': nrm((DEC_BATCH, D), 1.0),
        'w_mod': nrm((DEPTH, D, 6 * D), 0.5 * D ** -0.5),
        'b_mod': nrm((DEPTH, 6 * D), 0.02),
        'g_norm1': gain((DEPTH, D)),
        'g_norm2': gain((DEPTH, D)),
        'w_in': nrm((DEPTH, D, IN_COLS), D ** -0.5),
        'conv_w': nrm((DEPTH, 3, HY_COLS), 3 ** -0.5),
        'conv_b': nrm((DEPTH, HY_COLS), 0.02),
        'f_w1': nrm((DEPTH, FILT_EMB, FILT_HIDDEN), FILT_EMB ** -0.5),
        'f_b1': nrm((DEPTH, FILT_HIDDEN), 0.02),
        'f_freq': gain((DEPTH, FILT_HIDDEN)),
        'f_w2': nrm((DEPTH, FILT_HIDDEN, FILT_HIDDEN), FILT_HIDDEN ** -0.5),
        'f_b2': nrm((DEPTH, FILT_HIDDEN), 0.02),
        'f_w3': nrm((DEPTH, FILT_HIDDEN, HY_ORDER * N_DIR * HY_WIDTH), 0.02),
        'f_bias': nrm((DEPTH, HY_ORDER, HY_WIDTH), 0.5),
        'q_gain': gain((DEPTH, HEAD_DIM)),
        'k_gain': gain((DEPTH, HEAD_DIM)),
        'sink': nrm((DEPTH, N_Q_HEADS), 0.5),
        'w_pa': nrm((DEPTH, HY_WIDTH, D), HY_WIDTH ** -0.5),
        'w_pb': nrm((DEPTH, ATTN_WIDTH, D), ATTN_WIDTH ** -0.5),
        'w_out': nrm((DEPTH, D, D), D ** -0.5),
        'peer_wq': nrm((DEPTH, D, PEER_HEADS * PEER_QDIM), D ** -0.5),
        'peer_k1': nrm((DEPTH, N_KEYS, PEER_HALF), PEER_HALF ** -0.5),
        'peer_k2': nrm((DEPTH, N_KEYS, PEER_HALF), PEER_HALF ** -0.5),
        'peer_u': nrm((DEPTH, N_EXPERTS, D), D ** -0.5),
        'peer_v': nrm((DEPTH, N_EXPERTS, D), PEER_HEADS ** -0.5),
    }


def reference(x_prompt, x_sample, c_prompt, c_sample, w_mod, b_mod, g_norm1, g_norm2, w_in, conv_w, conv_b,
              f_w1, f_b1, f_freq, f_w2, f_b2, f_w3, f_bias, q_gain, k_gain, sink, w_pa, w_pb, w_out,
              peer_wq, peer_k1, peer_k2, peer_u, peer_v):
    y_prompt = trunk(x_prompt, c_prompt, w_mod, b_mod, g_norm1, g_norm2, w_in, conv_w, conv_b, f_w1, f_b1,
                     f_freq, f_w2, f_b2, f_w3, f_bias, q_gain, k_gain, sink, w_pa, w_pb, w_out,
                     peer_wq, peer_k1, peer_k2, peer_u, peer_v)
    y_sample = trunk(x_sample, c_sample, w_mod, b_mod, g_norm1, g_norm2, w_in, conv_w, conv_b, f_w1, f_b1,
                     f_freq, f_w2, f_b2, f_w3, f_bias, q_gain, k_gain, sink, w_pa, w_pb, w_out,
                     peer_wq, peer_k1, peer_k2, peer_u, peer_v)
    return (y_prompt, y_sample)
```

```python
import math
from contextlib import ExitStack

import numpy as np
import ml_dtypes
import concourse.bass as bass
import concourse.mybir as mybir
from concourse.bass_utils import run_bass_kernel_spmd

F32 = mybir.dt.float32
BF16 = mybir.dt.bfloat16
U32 = mybir.dt.uint32
I32 = mybir.dt.int32
AF = mybir.ActivationFunctionType
ALU = mybir.AluOpType
AX = mybir.AxisListType

L = 2048
D = 1024
NT = 16
EPS = 1e-6
NEG = -30000.0
MAGIC = 12582912.0
TWO_PI = 2.0 * math.pi


class Res:
    __slots__ = ("name", "w", "r")

    def __init__(self, name=""):
        self.name = name
        self.w = None
        self.r = {}


class Sched:
    ENGS = ("sync", "scalar", "vector", "gpsimd", "tensor")
    RING = 8

    def __init__(self, nc, es):
        self.nc = nc
        self.eng = {e: getattr(nc, e) for e in self.ENGS}
        self.sems = []
        self.semid = {}
        self.cnt = {}
        for e in self.ENGS:
            self.semid[e] = len(self.sems)
            self.sems.append(es.enter_context(nc.semaphore("c_" + e)))
            self.cnt[e] = 0
        self.ring = {}
        self.ring_cnt = {}
        for q in ("sync", "gpsimd", "tbl"):
            ids = []
            for i in range(self.RING):
                ids.append(len(self.sems))
                self.sems.append(es.enter_context(nc.semaphore(f"d_{q}{i}")))
            self.ring[q] = ids
            self.ring_cnt[q] = 0
        self.known = {e: {} for e in self.ENGS}
        self.nwaits = 0
        self.nops = 0

    def _waits(self, eng, reads, writes, extra=()):
        need = {}

        def add(ev):
            if ev is None:
                return
            s, v = ev
            if need.get(s, 0) < v:
                need[s] = v
        for r in reads:
            add(r.w)
        for w in writes:
            add(w.w)
            for s, v in w.r.items():
                add((s, v))
        for ev in extra:
            add(ev)
        kn = self.known[eng]
        e = self.eng[eng]
        own = self.semid[eng]
        for s, v in need.items():
            if eng == "tensor" and s == own:
                continue
            if kn.get(s, 0) >= v:
                continue
            e.wait_ge(self.sems[s], v)
            kn[s] = v
            self.nwaits += 1

    def _mark(self, ev, reads, writes):
        s, v = ev
        for w in writes:
            w.w = ev
            w.r = {}
        for r in reads:
            if r in writes:
                continue
            if r.r.get(s, 0) < v:
                r.r[s] = v

    def op(self, eng, fn, reads=(), writes=()):
        self._waits(eng, reads, writes)
        ins = fn(self.eng[eng])
        self.cnt[eng] += 1
        ev = (self.semid[eng], self.cnt[eng])
        ins.then_inc(self.sems[ev[0]], 1)
        self._mark(ev, reads, writes)
        self.nops += 1
        return ev

    def dma(self, q, fn, reads=(), writes=(), eng=None):
        eng = eng or q
        i = self.ring_cnt[q]
        self.ring_cnt[q] = i + 1
        slot = self.ring[q][i % self.RING]
        rnd = i // self.RING
        extra = []
        if rnd > 0:
            extra.append((slot, 16 * rnd))
        self._waits(eng, reads, writes, extra)
        ins = fn(self.eng[eng])
        ev = (slot, 16 * (rnd + 1))
        ins.then_inc(self.sems[slot], 16)
        self._mark(ev, reads, writes)
        self.nops += 1
        return ev

    def ring_events(self, q):
        evs = []
        n = self.ring_cnt[q]
        for k in range(self.RING):
            cntk = (n - k + self.RING - 1) // self.RING if n > k else 0
            if cntk > 0:
                evs.append((self.ring[q][k], 16 * cntk))
        return evs

    def wait_all(self, eng, q):
        self._waits(eng, (), (), self.ring_events(q))

    def barrier(self, include_tbl=False):
        evs = [(self.semid[e], self.cnt[e]) for e in self.ENGS if self.cnt[e] > 0]
        for q in self.ring:
            if q == "tbl" and not include_tbl:
                continue
            n = self.ring_cnt[q]
            for k in range(self.RING):
                cntk = (n - k + self.RING - 1) // self.RING if n > k else 0
                if cntk > 0:
                    evs.append((self.ring[q][k], 16 * cntk))
        for eng in self.ENGS:
            self._waits(eng, (), (), evs)


_CONST = {}


def _consts():
    if _CONST:
        return _CONST
    bf = ml_dtypes.bfloat16
    N2 = 2 * L
    n = np.arange(L, dtype=np.int64)
    fwd = np.zeros((16, 128, 16, 256), np.float32)
    nn = (np.arange(16)[None, :] * 128 + np.arange(128)[:, None])
    for fc in range(16):
        f = fc * 128 + np.arange(128)
        ang = 2.0 * np.pi * ((nn[:, :, None] * f[None, None, :]) % N2) / N2
        fwd[fc, :, :, 0:128] = np.cos(ang)
        fwd[fc, :, :, 128:256] = -np.sin(ang)
    fwd[0, :, :, 128] = np.where(nn % 2 == 0, 1.0, -1.0)
    inv = np.zeros((4, 128, 32, 512), np.float32)
    for tg in range(4):
        t = tg * 512 + np.arange(512)
        for fc in range(16):
            f = fc * 128 + np.arange(128)
            ang = 2.0 * np.pi * ((f[:, None] * t[None, :]) % N2) / N2
            gre = (2.0 / N2) * np.cos(ang)
            gim = -(2.0 / N2) * np.sin(ang)
            if fc == 0:
                gre[0, :] = 1.0 / N2
                gim[0, :] = np.where(t % 2 == 0, 1.0, -1.0) / N2
            inv[tg, :, fc, :] = gre
            inv[tg, :, 16 + fc, :] = gim
    tl = np.linspace(0.0, 1.0, L, dtype=np.float32)
    w = (np.float32(2.0 * math.pi) * np.arange(L, dtype=np.float32) / np.float32(L)).astype(np.float32)
    bands = np.linspace(1e-4, 15.0, 16, dtype=np.float32)
    bw = (bands[:, None] * w[None, :]).astype(np.float32).astype(np.float64)
    zfT = np.concatenate([tl[None, :].astype(np.float64), np.cos(bw), -np.sin(bw)], axis=0).astype(np.float32)
    negt = np.zeros((128, 16), np.float32)
    negt[:, :] = -tl[nn]
    max_decay = math.log(1e-2) / 0.3
    min_decay = math.log(1e-2) / 1.5
    absdelta = np.abs(np.linspace(min_decay, max_decay, 512, dtype=np.float32)).reshape(1, 512)
    invf = (10000.0 ** (-np.arange(0, 64, 2, dtype=np.float32) / np.float32(64))).astype(np.float32)
    ang = (np.arange(L, dtype=np.float32)[:, None] * invf[None, :]).astype(np.float32).astype(np.float64)
    cs = np.zeros((L, 128), np.float32)
    cs[:, 0:32] = np.cos(ang)
    cs[:, 32:64] = np.cos(ang)
    cs[:, 64:96] = np.sin(ang)
    cs[:, 96:128] = np.sin(ang)
    cs = np.ascontiguousarray(cs.reshape(16, 128, 128).transpose(1, 0, 2))
    j = np.arange(128)[:, None]
    q = np.arange(128)[None, :]
    mprev = np.where(j >= q, 0.0, NEG).astype(np.float32)
    mnext = np.where(j <= q, 0.0, NEG).astype(np.float32)
    masks = np.stack([np.tile(mprev, (1, 4)), np.tile(mnext, (1, 4))], axis=0)
    _CONST.update(
        fwdF=fwd.astype(bf), invG=inv.astype(bf), zfT=zfT, negt=negt, absdelta=absdelta,
        cs=cs, masks=masks.astype(bf), ident=np.eye(128, dtype=np.float32).astype(bf),
        iota16=np.tile(np.arange(16, dtype=np.float32)[None, :], (128, 1)),
        iota256=np.tile(np.arange(256, dtype=np.int32)[None, :], (128, 1)),
    )
    return _CONST


def build(NSEQ=3, DEPTH=2, do_mixer=True, do_peer=True, NTOK_PEER=NT):
    nc = bass.Bass("TRN2", target_bir_lowering=False)

    def din(name, shape, dt=F32):
        return nc.dram_tensor(name, list(shape), dt, kind="ExternalInput").ap()

    def dscr(name, shape, dt):
        return nc.dram_tensor(name, list(shape), dt, kind="Internal").ap()

    x_in = din("x", [NSEQ, L, D])
    cT_in = din("cT", [128, 8, NSEQ])
    w_mod = din("w_mod", [2, D, 6 * D])
    b_mod = din("b_mod", [2, 6 * D])
    g_norm1 = din("g_norm1", [2, D])
    g_norm2 = din("g_norm2", [2, D])
    w_in = din("w_in", [2, D, 4352])
    conv_w = din("conv_w", [2, 3, 1536])
    conv_b = din("conv_b", [2, 1536])
    f_w1 = din("f_w1", [2, 33, 64])
    f_b1 = din("f_b1", [2, 64])
    f_freq = din("f_freq", [2, 64])
    f_w2 = din("f_w2", [2, 64, 64])
    f_b2 = din("f_b2", [2, 64])
    f_w3 = din("f_w3", [2, 64, 2048])
    f_bias = din("f_bias", [2, 2, 512])
    q_gain = din("q_gain", [2, 64])
    k_gain = din("k_gain", [2, 64])
    sink = din("sink", [2, 8])
    w_pa = din("w_pa", [2, 512, D])
    w_pb = din("w_pb", [2, 512, D])
    w_out = din("w_out", [2, D, D])
    peer_wq = din("peer_wq", [2, D, 2048])
    peer_k1T = din("peer_k1T", [2, 128, 128])
    peer_k2T = din("peer_k2T", [2, 128, 128])
    peer_u = nc.dram_tensor("peer_u", [2 * 16384, D], F32, kind="ExternalInput")
    peer_v = nc.dram_tensor("peer_v", [2 * 16384, D], F32, kind="ExternalInput")
    c_fwdF = din("fwdF", [16, 128, 16, 256], BF16)
    c_invG = din("invG", [4, 128, 32, 512], BF16)
    c_zfT = din("zfT", [33, L])
    c_negt = din("negt", [128, 16])
    c_absd = din("absdelta", [1, 512])
    c_cs = din("cs", [128, 16, 128])
    c_masks = din("masks", [2, 128, 512], BF16)
    c_ident = din("ident", [128, 128], BF16)
    c_iota = din("iota16", [128, 16])
    c_iota256 = din("iota256", [128, 256], I32)
    y_out = nc.dram_tensor("y", [NSEQ, L, D], F32, kind="ExternalOutput").ap()

    modd = dscr("modd", [NSEQ, 6 * D], F32)
    whY = dscr("whY", [9, 128, 8, 512], BF16)
    wqa = dscr("wqa", [128, 8, 512], BF16)
    wkv = dscr("wkv", [128, 8, 256], BF16)
    wg = dscr("wg", [4, 128, 8, 512], BF16)
    wpa_s = dscr("wpa_s", [128, 4, 1024], BF16)
    wpb_s = dscr("wpb_s", [128, 4, 1024], BF16)
    wout_s = dscr("wout_s", [128, 8, 1024], BF16)
    pwq_s = dscr("pwq_s", [128, 8, 2048], BF16)
    x12 = dscr("x12", [2, L, 512], BF16)
    sigd = dscr("sigd", [2, L, 1024], BF16)
    hfd = dscr("hfd", [2, 16, 128, 2, 512], F32)
    uvds = [nc.dram_tensor(f"uvd{i}", [16384, 2048], BF16, kind="Internal") for i in range(2)]

    with ExitStack() as es:
        S = Sched(nc, es)
        es.enter_context(nc.allow_non_contiguous_dma(reason="small strided param loads"))

        def V(fn, r=(), w=()):
            return S.op("vector", fn, r, w)

        def A(fn, r=(), w=()):
            return S.op("scalar", fn, r, w)

        def G(fn, r=(), w=()):
            return S.op("gpsimd", fn, r, w)

        def PE(fn, r=(), w=()):
            return S.op("tensor", fn, r, w)

        def DMA(out, in_, r=(), w=(), q="sync"):
            return S.dma(q, lambda e: e.dma_start(out=out, in_=in_), r, w)

        uid = [0]

        def sb(stack, name, shape, dt):
            uid[0] += 1
            t = stack.enter_context(nc.sbuf_tensor(f"sb{uid[0]}_{name}", list(shape), dt))
            return t, Res(name)

        psbig = es.enter_context(nc.psum_tensor("psbig", [128, 4096], F32))
        ps = [psbig[:, i * 512:(i + 1) * 512] for i in range(8)]
        ps_r = [Res(f"ps{i}") for i in range(8)]

        def ps2(banks):
            return psbig[:, banks[0] * 512:(banks[1] + 1) * 512]
        ident, r_ident = sb(es, "ident", [128, 128], BF16)
        DMA(ident[:, :], c_ident, w=[r_ident])
        xres = [[Res(f"x{s}_{t}") for t in range(NT)] for s in range(NSEQ)]
        r_modd = Res("modd")
        r_hfd = Res("hfd")
        r_w = {k: Res(k) for k in ("whY", "wqa", "wkv", "wg", "wpa", "wpb", "wout", "pwq")}
        hnyq, r_hnyq = sb(es, "hnyq", [1, 2, 512], F32)

        def mod_phase(l):
            with ExitStack() as st:
                cT, r_cT = sb(st, "cT", [128, 8, NSEQ], F32)
                scT, r_scT = sb(st, "scT", [128, 8, NSEQ], F32)
                modr, r_modr = sb(st, "modr", [NSEQ, 6 * D], F32)
                bmb, r_bmb = sb(st, "bmb", [NSEQ, 6 * D], F32)
                g1b, r_g1b = sb(st, "g1b", [NSEQ, D], F32)
                g2b, r_g2b = sb(st, "g2b", [NSEQ, D], F32)
                wst = [sb(st, f"wst{i}", [128, 8, 512], F32) for i in range(2)]
                DMA(cT[:, :, :], cT_in, w=[r_cT])
                DMA(bmb[:, :], b_mod[l:l + 1, :].to_broadcast([NSEQ, 6 * D]), w=[r_bmb])
                DMA(g1b[:, :], g_norm1[l:l + 1, :].to_broadcast([NSEQ, D]), w=[r_g1b])
                DMA(g2b[:, :], g_norm2[l:l + 1, :].to_broadcast([NSEQ, D]), w=[r_g2b])
                A(lambda e: e.activation(out=scT[:, :, :], in_=cT[:, :, :], func=AF.Silu), [r_cT], [r_scT])
                wv = w_mod[l].rearrange("(k p) n -> p k n", p=128)
                for nb in range(12):
                    wt, r_wt = wst[nb % 2]
                    DMA(wt[:, :, :], wv[:, :, nb * 512:(nb + 1) * 512], w=[r_wt])
                    b = nb % 2
                    for k in range(8):
                        PE(lambda e: e.matmul(ps[b][0:NSEQ, :], lhsT=scT[:, k, :], rhs=wt[:, k, :],
                                              start=(k == 0), stop=(k == 7)),
                           [r_scT, r_wt], [ps_r[b]])
                    V(lambda e: e.tensor_tensor(out=modr[:, nb * 512:(nb + 1) * 512], in0=ps[b][0:NSEQ, :],
                                                in1=bmb[:, nb * 512:(nb + 1) * 512], op=ALU.add),
                      [ps_r[b], r_bmb], [r_modr])
                for (c0, gb_, rg) in ((1024, g1b, r_g1b), (4096, g2b, r_g2b)):
                    V(lambda e: e.scalar_tensor_tensor(out=modr[:, c0:c0 + 1024], in0=modr[:, c0:c0 + 1024],
                                                       scalar=1.0, in1=gb_[:, :], op0=ALU.add, op1=ALU.mult),
                      [r_modr, rg], [r_modr])
                DMA(modd, modr[:, :], r=[r_modr], w=[r_modd])
            S.barrier()

        def prep_weights(l):
            with ExitStack() as st:
                stg = [sb(st, f"pstg{i}", [128, 8, 512], F32) for i in range(2)]
                obf = [sb(st, f"pobf{i}", [128, 8, 512], BF16) for i in range(2)]
                cwb, r_cwb = sb(st, "cwb", [128, 3, 1536], F32)
                for j in range(3):
                    DMA(cwb[:, j, :], conv_w[l, j:j + 1, :].to_broadcast([128, 1536]), w=[r_cwb])
                cnt = [0]

                def one(src, dst, K, N, rdst, mul=None):
                    i = cnt[0] % 2
                    cnt[0] += 1
                    s_t, r_s = stg[i]
                    o_t, r_o = obf[i]
                    DMA(s_t[:, 0:K, 0:N], src, w=[r_s])
                    if mul is None:
                        if cnt[0] % 2 == 0:
                            V(lambda e: e.tensor_copy(out=o_t[:, 0:K, 0:N], in_=s_t[:, 0:K, 0:N]), [r_s], [r_o])
                        else:
                            A(lambda e: e.copy(out=o_t[:, 0:K, 0:N], in_=s_t[:, 0:K, 0:N]), [r_s], [r_o])
                    else:
                        for k in range(K):
                            V(lambda e: e.tensor_tensor(out=o_t[:, k, 0:N], in0=s_t[:, k, 0:N], in1=mul,
                                                        op=ALU.mult), [r_s, r_cwb], [r_o])
                    DMA(dst, o_t[:, 0:K, 0:N], r=[r_o], w=[rdst])

                wiv = w_in[l].rearrange("(k p) n -> p k n", p=128)
                for ob in range(3):
                    for j in range(3):
                        one(wiv[:, :, ob * 512:(ob + 1) * 512], whY[ob * 3 + j], 8, 512, r_w["whY"],
                            mul=cwb[:, j, ob * 512:(ob + 1) * 512])
                one(wiv[:, :, 1536:2048], wqa, 8, 512, r_w["wqa"])
                one(wiv[:, :, 2048:2304], wkv, 8, 256, r_w["wkv"])
                for gi in range(4):
                    one(wiv[:, :, 2304 + gi * 512:2304 + (gi + 1) * 512], wg[gi], 8, 512, r_w["wg"])
                pav = w_pa[l].rearrange("(k p) n -> p k n", p=128)
                pbv = w_pb[l].rearrange("(k p) n -> p k n", p=128)
                wov = w_out[l].rearrange("(k p) n -> p k n", p=128)
                pqv = peer_wq[l].rearrange("(k p) n -> p k n", p=128)
                for nb in range(2):
                    one(pav[:, :, nb * 512:(nb + 1) * 512], wpa_s[:, :, nb * 512:(nb + 1) * 512], 4, 512, r_w["wpa"])
                    one(pbv[:, :, nb * 512:(nb + 1) * 512], wpb_s[:, :, nb * 512:(nb + 1) * 512], 4, 512, r_w["wpb"])
                    one(wov[:, :, nb * 512:(nb + 1) * 512], wout_s[:, :, nb * 512:(nb + 1) * 512], 8, 512, r_w["wout"])
                for nb in range(4):
                    one(pqv[:, :, nb * 512:(nb + 1) * 512], pwq_s[:, :, nb * 512:(nb + 1) * 512], 8, 512, r_w["pwq"])
            S.barrier()

        def range_reduce_sin(st, dst, src_ps, bcol, fcol, r_cols, r_dst, psr, npart, tmp, r_tmp, tmp2, r_tmp2):
            V(lambda e: e.tensor_scalar(out=tmp[0:npart, :], in0=src_ps, scalar1=bcol, scalar2=fcol,
                                        op0=ALU.add, op1=ALU.mult), [psr, r_cols], [r_tmp])
            V(lambda e: e.tensor_scalar(out=tmp2[0:npart, :], in0=tmp[0:npart, :], scalar1=1.0 / TWO_PI,
                                        scalar2=MAGIC, op0=ALU.mult, op1=ALU.add), [r_tmp], [r_tmp2])
            V(lambda e: e.tensor_scalar(out=tmp2[0:npart, :], in0=tmp2[0:npart, :], scalar1=MAGIC,
                                        scalar2=-TWO_PI, op0=ALU.subtract, op1=ALU.mult), [r_tmp2], [r_tmp2])
            V(lambda e: e.tensor_tensor(out=tmp[0:npart, :], in0=tmp[0:npart, :], in1=tmp2[0:npart, :],
                                        op=ALU.add), [r_tmp, r_tmp2], [r_tmp])
            V(lambda e: e.tensor_scalar(out=tmp[0:npart, :], in0=tmp[0:npart, :], scalar1=3.1415925,
                                        scalar2=-3.1415925, op0=ALU.min, op1=ALU.max), [r_tmp], [r_tmp])
            A(lambda e: e.activation(out=dst, in_=tmp[0:npart, :], func=AF.Sin), [r_tmp], [r_dst])

        def filter_phase(l):
            with ExitStack() as st:
                zf, r_zf = sb(st, "zf", [33, L], F32)
                w1, r_w1 = sb(st, "fw1", [33, 64], F32)
                w2, r_w2 = sb(st, "fw2", [64, 64], F32)
                w3, r_w3 = sb(st, "fw3", [64, 2048], F32)
                cols, r_cols = sb(st, "fcols", [64, 4], F32)
                a1, r_a1 = sb(st, "fa1", [64, L], F32)
                a2, r_a2 = sb(st, "fa2", [64, L], F32)
                tmp, r_tmp = sb(st, "ftmp", [128, 512], F32)
                tmp2, r_tmp2 = sb(st, "ftmp2", [128, 512], F32)
                absd, r_absd = sb(st, "absd", [128, 512], F32)
                negt, r_negt = sb(st, "negt", [128, 16], F32)
                dec, r_dec = sb(st, "dec", [128, 512], F32)
                fa, r_fa = sb(st, "fa", [128, 16, 1024], BF16)
                fb, r_fb = sb(st, "fb", [128, 16, 1024], BF16)
                h0, r_h0 = sb(st, "h0", [128, 512], F32)
                h1, r_h1 = sb(st, "h1", [128, 512], F32)
                fbias, r_fbias = sb(st, "fbias", [128, 2, 512], F32)
                Fb = [sb(st, f"Fbf{i}", [128, 16, 256], BF16) for i in range(2)]
                ho = [sb(st, f"hout{i}", [128, 2, 512], F32) for i in range(2)]
                DMA(zf[:, :], c_zfT, w=[r_zf])
                DMA(w1[:, :], f_w1[l], w=[r_w1])
                DMA(w2[:, :], f_w2[l], w=[r_w2])
                DMA(w3[:, :], f_w3[l], w=[r_w3])
                DMA(cols[:, 0:1], f_b1[l:l + 1, :].rearrange("o n -> n o"), w=[r_cols])
                DMA(cols[:, 1:2], f_freq[l:l + 1, :].rearrange("o n -> n o"), w=[r_cols])
                DMA(cols[:, 2:3], f_b2[l:l + 1, :].rearrange("o n -> n o"), w=[r_cols])
                DMA(absd[:, :], c_absd.to_broadcast([128, 512]), w=[r_absd])
                DMA(negt[:, :], c_negt, w=[r_negt])
                for o in range(2):
                    DMA(fbias[:, o, :], f_bias[l, o:o + 1, :].to_broadcast([128, 512]), w=[r_fbias])
                for ch in range(4):
                    b = ch % 2
                    PE(lambda e: e.matmul(ps[b][0:64, :], lhsT=w1[:, :], rhs=zf[:, ch * 512:(ch + 1) * 512],
                                          start=True, stop=True), [r_w1, r_zf], [ps_r[b]])
                    range_reduce_sin(st, a1[:, ch * 512:(ch + 1) * 512], ps[b][0:64, :], cols[:, 0:1], cols[:, 1:2],
                                     r_cols, r_a1, ps_r[b], 64, tmp, r_tmp, tmp2, r_tmp2)
                for ch in range(4):
                    b = ch % 2
                    PE(lambda e: e.matmul(ps[b][0:64, :], lhsT=w2[:, :], rhs=a1[:, ch * 512:(ch + 1) * 512],
                                          start=True, stop=True), [r_w2, r_a1], [ps_r[b]])
                    range_reduce_sin(st, a2[:, ch * 512:(ch + 1) * 512], ps[b][0:64, :], cols[:, 2:3], cols[:, 1:2],
                                     r_cols, r_a2, ps_r[b], 64, tmp, r_tmp, tmp2, r_tmp2)
                for tc in range(16):
                    A(lambda e: e.activation(out=dec[:, :], in_=absd[:, :], func=AF.Exp, scale=negt[:, tc:tc + 1]),
                      [r_absd, r_negt], [r_dec])
                    for o in range(2):
                        for dr in range(2):
                            b = 2 + dr
                            cb = (o * 2 + dr) * 512
                            PE(lambda e: e.matmul(ps[b][:, :], lhsT=a2[:, tc * 128:(tc + 1) * 128],
                                                  rhs=w3[:, cb:cb + 512], start=True, stop=True),
                               [r_a2, r_w3], [ps_r[b]])
                            hh, r_hh = (h0, r_h0) if dr == 0 else (h1, r_h1)
                            V(lambda e: e.tensor_tensor(out=hh[:, :], in0=ps[b][:, :], in1=dec[:, :], op=ALU.mult),
                              [ps_r[b], r_dec], [r_hh])
                        if tc == 0:
                            V(lambda e: e.memset(h1[0:1, :], 0.0), [], [r_h1])
                        V(lambda e: e.tensor_tensor(out=fa[:, tc, o * 512:(o + 1) * 512], in0=h0[:, :], in1=h1[:, :],
                                                    op=ALU.add), [r_h0, r_h1], [r_fa])
                        V(lambda e: e.tensor_tensor(out=fb[:, tc, o * 512:(o + 1) * 512], in0=h0[:, :], in1=h1[:, :],
                                                    op=ALU.subtract), [r_h0, r_h1], [r_fb])
                for fc in range(16):
                    Ft, r_Ft = Fb[fc % 2]
                    DMA(Ft[:, :, :], c_fwdF[fc], w=[r_Ft])
                    hot, r_hot = ho[fc % 2]
                    for o in range(2):
                        bre, bim = 4 + 2 * (o % 2), 5 + 2 * (o % 2)
                        for tc in range(16):
                            PE(lambda e: e.matmul(ps[bre][:, :], lhsT=Ft[:, tc, 0:128],
                                                  rhs=fa[:, tc, o * 512:(o + 1) * 512], start=(tc == 0), stop=(tc == 15)),
                               [r_Ft, r_fa], [ps_r[bre]])
                        for tc in range(16):
                            PE(lambda e: e.matmul(ps[bim][:, :], lhsT=Ft[:, tc, 128:256],
                                                  rhs=fb[:, tc, o * 512:(o + 1) * 512], start=(tc == 0), stop=(tc == 15)),
                               [r_Ft, r_fb], [ps_r[bim]])
                        V(lambda e: e.tensor_tensor(out=hot[:, 0, :], in0=ps[bre][:, :], in1=fbias[:, o, :], op=ALU.add),
                          [ps_r[bre], r_fbias], [r_hot])
                        A(lambda e: e.copy(out=hot[:, 1, :], in_=ps[bim][:, :]), [ps_r[bim]], [r_hot])
                        DMA(hfd[o, fc], hot[:, :, :], r=[r_hot], w=[r_hfd])
                        if fc == 0:
                            for tc in range(16):
                                PE(lambda e: e.matmul(ps[0][0:1, :], lhsT=Ft[:, tc, 128:129],
                                                      rhs=fa[:, tc, o * 512:(o + 1) * 512], start=(tc == 0), stop=(tc == 15)),
                                   [r_Ft, r_fa], [ps_r[0]])
                            V(lambda e: e.tensor_tensor(out=hnyq[0:1, o, :], in0=ps[0][0:1, :], in1=fbias[0:1, o, :],
                                                        op=ALU.add), [ps_r[0], r_fbias], [r_hnyq])
            S.barrier()

        def qk_norm_rope(nh, src_ps, psr, gainb, r_gain, cst, r_cs, t, qf, r_qf, qsq, r_qsq, ss, r_ss, qr, r_qr):
            W = nh * 64
            A(lambda e: e.copy(out=qf[:, 0:W], in_=src_ps), [psr], [r_qf])
            V(lambda e: e.tensor_tensor(out=qsq[:, 0:W], in0=qf[:, 0:W], in1=qf[:, 0:W], op=ALU.mult), [r_qf], [r_qsq])
            V(lambda e: e.tensor_reduce(out=ss[:, 0:nh], in_=qsq[:, 0:W].rearrange("p (h d) -> p h d", h=nh),
                                        axis=AX.X, op=ALU.add), [r_qsq], [r_ss])
            A(lambda e: e.activation(out=ss[:, 8:8 + nh], in_=ss[:, 0:nh], func=AF.Sqrt, scale=1.0 / 64.0, bias=eps_t[:, 0:1]),
              [r_ss], [r_ss])
            V(lambda e: e.reciprocal(out=ss[:, 16:16 + nh], in_=ss[:, 8:8 + nh]), [r_ss], [r_ss])
            q3 = qf[:, 0:W].rearrange("p (h d) -> p h d", h=nh)
            V(lambda e: e.tensor_tensor(out=q3, in0=q3, in1=ss[:, 16:16 + nh].unsqueeze(2).to_broadcast([128, nh, 64]),
                                        op=ALU.mult), [r_qf, r_ss], [r_qf])
            V(lambda e: e.tensor_tensor(out=qf[:, 0:W], in0=qf[:, 0:W], in1=gainb[:, 0:W], op=ALU.mult),
              [r_qf, r_gain], [r_qf])
            cosb = cst[:, t, 0:64].unsqueeze(1).to_broadcast([128, nh, 64])
            sinb = cst[:, t, 64:128].unsqueeze(1).to_broadcast([128, nh, 64])
            s3 = qsq[:, 0:W].rearrange("p (h d) -> p h d", h=nh)
            V(lambda e: e.tensor_tensor(out=s3, in0=q3, in1=sinb, op=ALU.mult), [r_qf, r_cs], [r_qsq])
            V(lambda e: e.tensor_tensor(out=q3, in0=q3, in1=cosb, op=ALU.mult), [r_qf, r_cs], [r_qf])
            r3 = qr[:, 0:W].rearrange("p (h d) -> p h d", h=nh)
            V(lambda e: e.tensor_tensor(out=r3[:, :, 0:32], in0=q3[:, :, 0:32], in1=s3[:, :, 32:64], op=ALU.subtract),
              [r_qf, r_qsq], [r_qr])
            V(lambda e: e.tensor_tensor(out=r3[:, :, 32:64], in0=q3[:, :, 32:64], in1=s3[:, :, 0:32], op=ALU.add),
              [r_qf, r_qsq], [r_qr])

        eps_t, r_eps = sb(es, "eps_t", [128, 1], F32)
        V(lambda e: e.memset(eps_t[:, :], EPS), [], [r_eps])

        def psb(i):
            return ps[i][:, :].bitcast(BF16)

        def mixer_seq(l, s, hv, r_hv, ybt, r_ybt):
            src_x = x_in if l == 0 else y_out
            with ExitStack() as st:
                hT, r_hT = sb(st, "hT", [128, 8, L + 2], BF16)
                modT, r_modT = sb(st, "modT", [128, 2, 8], F32)
                DMA(modT[:, 0, :], modd[s, 0:1024].rearrange("(k p) -> p k", p=128), r=[r_modd], w=[r_modT])
                DMA(modT[:, 1, :], modd[s, 1024:2048].rearrange("(k p) -> p k", p=128), r=[r_modd], w=[r_modT])
                V(lambda e: e.memset(hT[:, :, 0:1], 0.0), [], [r_hT])
                V(lambda e: e.memset(hT[:, :, L + 1:L + 2], 0.0), [], [r_hT])
                xt = [sb(st, f"xt{i}", [128, D], F32) for i in range(2)]
                junk, r_junk = sb(st, "junk", [128, D], BF16)
                xn = [sb(st, f"xn{i}", [128, D], BF16) for i in range(2)]
                st8, r_st8 = sb(st, "st8", [128, 8], F32)
                evt = [sb(st, f"evt{i}", [128, 8, 128], F32) for i in range(2)]
                r_hTt = [Res(f"hT{t}") for t in range(NT)]
                for t in range(NT):
                    x_t, r_x = xt[t % 2]
                    xn_t, r_xn = xn[t % 2]
                    b = t % 2
                    DMA(x_t[:, :], src_x[s, t * 128:(t + 1) * 128, :], r=[xres[s][t]], w=[r_x])
                    A(lambda e: e.activation(out=junk[:, :], in_=x_t[:, :], func=AF.Square, accum_out=st8[:, 0:1]),
                      [r_x], [r_junk, r_st8])
                    A(lambda e: e.activation(out=st8[:, 1:2], in_=st8[:, 0:1], func=AF.Sqrt, scale=1.0 / D,
                                             bias=eps_t[:, 0:1]), [r_st8], [r_st8])
                    V(lambda e: e.reciprocal(out=st8[:, 2:3], in_=st8[:, 1:2]), [r_st8], [r_st8])
                    A(lambda e: e.activation(out=xn_t[:, :], in_=x_t[:, :], func=AF.Copy, scale=st8[:, 2:3]),
                      [r_x, r_st8], [r_xn])
                    for k in range(8):
                        PE(lambda e: e.transpose(out=psb(b)[:, k * 128:(k + 1) * 128], in_=xn_t[:, k * 128:(k + 1) * 128],
                                                 identity=ident[:, :]), [r_xn, r_ident], [ps_r[b]])
                    ev_t, r_ev = evt[t % 2]
                    V(lambda e: e.tensor_tensor(out=ev_t[:, :, :], in0=psb(b).rearrange("p (k q) -> p k q", k=8),
                                                in1=modT[:, 1, :].unsqueeze(2).to_broadcast([128, 8, 128]), op=ALU.mult),
                      [ps_r[b], r_modT], [r_ev])
                    G(lambda e: e.tensor_tensor(out=hT[:, :, 1 + t * 128:1 + (t + 1) * 128], in0=ev_t[:, :, :],
                                                in1=modT[:, 0, :].unsqueeze(2).to_broadcast([128, 8, 128]), op=ALU.add),
                      [r_ev, r_modT], [r_hTt[t], r_hT])
                wr = [sb(st, f"wr{i}", [128, 8, 512], BF16) for i in range(4)]
                cbb, r_cbb = sb(st, "cbb", [128, 1536], F32)
                DMA(cbb[:, :], conv_b[l:l + 1, :].to_broadcast([128, 1536]), w=[r_cbb])
                stg = [sb(st, f"stg{i}", [128, 512], BF16) for i in range(2)]
                wi = [0]

                def getw(src, K=8, N=512):
                    i = wi[0] % 4
                    wi[0] += 1
                    wt, r_wt = wr[i]
                    DMA(wt[:, 0:K, 0:N], src, r=[r_w["whY"], r_w["wqa"], r_w["wkv"], r_w["wg"]], w=[r_wt])
                    return wt, r_wt
                pbank = [2]

                def nextbank():
                    b = pbank[0]
                    pbank[0] = 2 + (pbank[0] - 2 + 1) % 4
                    return b
                si = [0]
                for ob in range(3):
                    wts = [getw(whY[ob * 3 + j]) for j in range(3)]
                    for t in range(NT):
                        b = nextbank()
                        for j in range(3):
                            wt, r_wt = wts[j]
                            for k in range(8):
                                PE(lambda e: e.matmul(ps[b][:, :], lhsT=hT[:, k, t * 128 + j:t * 128 + j + 128],
                                                      rhs=wt[:, k, :], start=(j == 0 and k == 0), stop=(j == 2 and k == 7)),
                                   [r_hT, r_wt], [ps_r[b]])
                        if ob == 0:
                            V(lambda e: e.tensor_tensor(out=hv[:, t, :], in0=ps[b][:, :], in1=cbb[:, 0:512], op=ALU.add),
                              [ps_r[b], r_cbb], [r_hv[t]])
                        else:
                            sg, r_sg = stg[si[0] % 2]
                            si[0] += 1
                            V(lambda e: e.tensor_tensor(out=sg[:, :], in0=ps[b][:, :], in1=cbb[:, ob * 512:(ob + 1) * 512],
                                                        op=ALU.add), [ps_r[b], r_cbb], [r_sg])
                            DMA(x12[ob - 1, t * 128:(t + 1) * 128, :], sg[:, :], r=[r_sg], w=[r_x12])
                for gi in range(4):
                    wt, r_wt = getw(wg[gi])
                    for t in range(NT):
                        b = nextbank()
                        for k in range(8):
                            PE(lambda e: e.matmul(ps[b][:, :], lhsT=hT[:, k, t * 128 + 1:t * 128 + 129], rhs=wt[:, k, :],
                                                  start=(k == 0), stop=(k == 7)), [r_hT, r_wt], [ps_r[b]])
                        sg, r_sg = stg[si[0] % 2]
                        si[0] += 1
                        A(lambda e: e.activation(out=sg[:, :], in_=ps[b][:, :], func=AF.Sigmoid), [ps_r[b]], [r_sg])
                        DMA(sigd[gi // 2, t * 128:(t + 1) * 128, (gi % 2) * 512:(gi % 2 + 1) * 512], sg[:, :],
                            r=[r_sg], w=[r_sigd])
                qT, r_qT = sb(st, "qT", [64, 8, L], BF16)
                kT, r_kT = sb(st, "kT", [64, 2, L], BF16)
                vt, r_vt = sb(st, "vt", [128, NT, 2, 65], BF16)
                cst, r_cs = sb(st, "cst", [128, NT, 128], F32)
                qgb, r_qgb = sb(st, "qgb", [128, 512], F32)
                kgb, r_kgb = sb(st, "kgb", [128, 128], F32)
                esk, r_esk = sb(st, "esk", [128, 8], F32)
                DMA(cst[:, :, :], c_cs, w=[r_cs])
                for h in range(8):
                    DMA(qgb[:, h * 64:(h + 1) * 64], q_gain[l:l + 1, :].to_broadcast([128, 64]), w=[r_qgb])
                for h in range(2):
                    DMA(kgb[:, h * 64:(h + 1) * 64], k_gain[l:l + 1, :].to_broadcast([128, 64]), w=[r_kgb])
                V(lambda e: e.tensor_scalar(out=qgb[:, :], in0=qgb[:, :], scalar1=0.125, scalar2=None, op0=ALU.mult),
                  [r_qgb], [r_qgb])
                DMA(esk[:, :], sink[l:l + 1, :].to_broadcast([128, 8]), w=[r_esk])
                A(lambda e: e.activation(out=esk[:, :], in_=esk[:, :], func=AF.Exp), [r_esk], [r_esk])
                V(lambda e: e.memset(vt[:, :, :, 64:65], 1.0), [], [r_vt])
                qbufs = [(sb(st, f"qf{i}", [128, 512], F32), sb(st, f"qsq{i}", [128, 512], F32),
                          sb(st, f"ss{i}", [128, 24], F32), sb(st, f"qr{i}", [128, 512], BF16)) for i in range(2)]
                wt_q, r_wt_q = getw(wqa)
                wt_k, r_wt_k = getw(wkv, 8, 256)

                def qk_front(kind, t):
                    b = nextbank()
                    if kind == 0:
                        for k in range(8):
                            PE(lambda e: e.matmul(ps[b][:, :], lhsT=hT[:, k, t * 128 + 1:t * 128 + 129], rhs=wt_q[:, k, :],
                                                  start=(k == 0), stop=(k == 7)), [r_hT, r_wt_q], [ps_r[b]])
                    else:
                        for k in range(8):
                            PE(lambda e: e.matmul(ps[b][:, 0:256], lhsT=hT[:, k, t * 128 + 1:t * 128 + 129], rhs=wt_k[:, k, 0:256],
                                                  start=(k == 0), stop=(k == 7)), [r_hT, r_wt_k], [ps_r[b]])
                    return b

                def qk_back(kind, t, b, ui):
                    (qf_, r_qf_), (qsq_, r_qsq_), (ss_, r_ss_), (qr_, r_qr_) = qbufs[ui % 2]
                    tb = ui % 2
                    if kind == 0:
                        qk_norm_rope(8, ps[b][:, :], ps_r[b], qgb, r_qgb, cst, r_cs, t, qf_, r_qf_, qsq_, r_qsq_, ss_, r_ss_, qr_, r_qr_)
                        for h in range(8):
                            PE(lambda e: e.transpose(out=psb(tb)[0:64, h * 128:(h + 1) * 128], in_=qr_[:, h * 64:(h + 1) * 64],
                                                     identity=ident[:, :]), [r_qr_, r_ident], [ps_r[tb]])
                        A(lambda e: e.copy(out=qT[:, :, t * 128:(t + 1) * 128],
                                           in_=psb(tb)[0:64, :].rearrange("p (h q) -> p h q", h=8)), [ps_r[tb]], [r_qT])
                    else:
                        A(lambda e: e.copy(out=vt[:, t, :, 0:64], in_=ps[b][:, 128:256].rearrange("p (h d) -> p h d", h=2)),
                          [ps_r[b]], [r_vt])
                        qk_norm_rope(2, ps[b][:, 0:128], ps_r[b], kgb, r_kgb, cst, r_cs, t, qf_, r_qf_, qsq_, r_qsq_, ss_, r_ss_, qr_, r_qr_)
                        for h in range(2):
                            PE(lambda e: e.transpose(out=psb(tb)[0:64, h * 128:(h + 1) * 128], in_=qr_[:, h * 64:(h + 1) * 64],
                                                     identity=ident[:, :]), [r_qr_, r_ident], [ps_r[tb]])
                        A(lambda e: e.copy(out=kT[:, :, t * 128:(t + 1) * 128],
                                           in_=psb(tb)[0:64, 0:256].rearrange("p (h q) -> p h q", h=2)), [ps_r[tb]], [r_kT])

                units = [(0, t) for t in range(NT)] + [(1, t) for t in range(NT)]
                pend = qk_front(*units[0])
                for ui, (kind, t) in enumerate(units):
                    nxt = qk_front(*units[ui + 1]) if ui + 1 < len(units) else None
                    qk_back(kind, t, pend, ui)
                    pend = nxt
                mk, r_mk = sb(st, "mk", [128, 2, 512], BF16)
                DMA(mk[:, 0, :], c_masks[0], w=[r_mk])
                DMA(mk[:, 1, :], c_masks[1], w=[r_mk])
                pT = [sb(st, f"pT{i}", [128, 3, 512], BF16) for i in range(2)]
                den, r_den = sb(st, "den", [128, 8], F32)
                dens = [(den, r_den), sb(st, "den2", [128, 8], F32)]

                def att_front(ai, n, kvh):
                    p_t, r_p = pT[ai % 2]
                    kbs = [kb for kb in (n - 1, n, n + 1) if 0 <= kb < NT]
                    for i, kb in enumerate(kbs):
                        b = nextbank()
                        PE(lambda e: e.matmul(ps[b][:, :], lhsT=kT[:, kvh, kb * 128:(kb + 1) * 128],
                                              rhs=qT[:, 4 * kvh:4 * kvh + 4, n * 128:(n + 1) * 128],
                                              start=True, stop=(kb == n)), [r_kT, r_qT], [ps_r[b]])
                        if kb != n:
                            mi = 0 if kb < n else 1
                            PE(lambda e: e.matmul(ps[b][:, :], lhsT=ident[:, :], rhs=mk[:, mi, :], start=False, stop=True),
                               [r_ident, r_mk], [ps_r[b]])
                        A(lambda e: e.activation(out=p_t[:, i, :], in_=ps[b][:, :], func=AF.Exp), [ps_r[b]], [r_p])

                def att_back(ai, n, kvh):
                    p_t, r_p = pT[ai % 2]
                    dn, r_dn = dens[ai % 2]
                    ob_ = 6 + (ai % 2)
                    kbs = [kb for kb in (n - 1, n, n + 1) if 0 <= kb < NT]
                    for h in range(4):
                        for i, kb in enumerate(kbs):
                            PE(lambda e: e.matmul(ps[ob_][:, h * 65:(h + 1) * 65], lhsT=p_t[:, i, h * 128:(h + 1) * 128],
                                                  rhs=vt[:, kb, kvh, :], start=(i == 0), stop=(i == len(kbs) - 1)),
                               [r_p, r_vt], [ps_r[ob_]])
                    o3 = ps[ob_][:, 0:260].rearrange("p (h d) -> p h d", h=4)
                    V(lambda e: e.tensor_tensor(out=dn[:, 0:4], in0=o3[:, :, 64], in1=esk[:, 4 * kvh:4 * kvh + 4],
                                                op=ALU.add), [ps_r[ob_], r_esk], [r_dn])
                    V(lambda e: e.reciprocal(out=dn[:, 4:8], in_=dn[:, 0:4]), [r_dn], [r_dn])
                    V(lambda e: e.tensor_tensor(
                        out=ybt[:, n, kvh * 256:(kvh + 1) * 256].rearrange("p (h d) -> p h d", h=4),
                        in0=o3[:, :, 0:64], in1=dn[:, 4:8].unsqueeze(2).to_broadcast([128, 4, 64]), op=ALU.mult),
                      [ps_r[ob_], r_dn], [r_ybt[n]])

                aunits = [(n, kvh) for n in range(NT) for kvh in range(2)]
                att_front(0, *aunits[0])
                for ai, (n, kvh) in enumerate(aunits):
                    if ai + 1 < len(aunits):
                        att_front(ai + 1, *aunits[ai + 1])
                    att_back(ai, n, kvh)
            S.barrier()
            with ExitStack() as st:
                Y, r_Y = sb(st, "Y", [128, 32, 512], BF16)
                Fb = [sb(st, f"Fb{i}", [128, 16, 256], BF16) for i in range(2)]
                Hb = [sb(st, f"Hb{i}", [128, 2, 512], F32) for i in range(2)]
                Gb = [sb(st, f"Gb{i}", [128, 16, 512], BF16) for i in range(4)]
                tm = [sb(st, f"tm{i}", [128, 512], F32) for i in range(4)]
                xg = [sb(st, f"xg{i}", [128, 512], BF16) for i in range(2)]
                r_Yc = [Res(f"Y{c}") for c in range(32)]
                gi_ = [0]
                for o in range(2):
                    for fc in range(16):
                        Ft, r_Ft = Fb[fc % 2]
                        Ht, r_Ht = Hb[fc % 2]
                        DMA(Ft[:, :, :], c_fwdF[fc], w=[r_Ft])
                        DMA(Ht[:, :, :], hfd[o, fc], r=[r_hfd], w=[r_Ht])
                        bre, bim = 2 * (fc % 2), 2 * (fc % 2) + 1
                        for tc in range(16):
                            PE(lambda e: e.matmul(ps[bre][:, :], lhsT=Ft[:, tc, 0:128], rhs=hv[:, tc, :],
                                                  start=(tc == 0), stop=(tc == 15)), [r_Ft, r_hv[tc]], [ps_r[bre]])
                        for tc in range(16):
                            PE(lambda e: e.matmul(ps[bim][:, :], lhsT=Ft[:, tc, 128:256], rhs=hv[:, tc, :],
                                                  start=(tc == 0), stop=(tc == 15)), [r_Ft, r_hv[tc]], [ps_r[bim]])
                        (t1, r1), (t2, r2), (t3, r3), (t4, r4) = tm
                        V(lambda e: e.tensor_tensor(out=t1[:, :], in0=ps[bre][:, :], in1=Ht[:, 0, :], op=ALU.mult),
                          [ps_r[bre], r_Ht], [r1])
                        V(lambda e: e.tensor_tensor(out=t2[:, :], in0=ps[bim][:, :], in1=Ht[:, 1, :], op=ALU.mult),
                          [ps_r[bim], r_Ht], [r2])
                        V(lambda e: e.tensor_tensor(out=t3[:, :], in0=ps[bre][:, :], in1=Ht[:, 1, :], op=ALU.mult),
                          [ps_r[bre], r_Ht], [r3])
                        V(lambda e: e.tensor_tensor(out=t4[:, :], in0=ps[bim][:, :], in1=Ht[:, 0, :], op=ALU.mult),
                          [ps_r[bim], r_Ht], [r4])
                        G(lambda e: e.tensor_tensor(out=Y[:, fc, :], in0=t1[:, :], in1=t2[:, :], op=ALU.subtract),
                          [r1, r2], [r_Yc[fc]])
                        G(lambda e: e.tensor_tensor(out=Y[:, 16 + fc, :], in0=t3[:, :], in1=t4[:, :], op=ALU.add),
                          [r3, r4], [r_Yc[16 + fc]])
                        if fc == 0:
                            V(lambda e: e.tensor_copy(out=Y[0:1, 0, :], in_=t1[0:1, :]), [r1], [r_Yc[0]])
                            V(lambda e: e.tensor_tensor(out=Y[0:1, 16, :], in0=ps[bim][0:1, :], in1=hnyq[0:1, o, :],
                                                        op=ALU.mult), [ps_r[bim], r_hnyq], [r_Yc[16]])
                    for tg in range(4):
                        gts = []
                        for half in range(2):
                            g_t, r_g = Gb[gi_[0] % 4]
                            gi_[0] += 1
                            DMA(g_t[:, :, :], c_invG[tg, :, half * 16:(half + 1) * 16, :], w=[r_g])
                            gts.append((g_t, r_g))
                        for ti in range(4):
                            t = tg * 4 + ti
                            b = 4 + (t % 4)
                            x_t, r_xg = xg[t % 2]
                            DMA(x_t[:, :], x12[o, t * 128:(t + 1) * 128, :], r=[r_x12], w=[r_xg])
                            for c in range(32):
                                g_t, r_g = gts[c // 16]
                                PE(lambda e: e.matmul(ps[b][:, :], lhsT=g_t[:, c % 16, ti * 128:(ti + 1) * 128], rhs=Y[:, c, :],
                                                      start=(c == 0), stop=(c == 31)), [r_g, r_Yc[c]], [ps_r[b]])
                            V(lambda e: e.tensor_tensor(out=hv[:, t, :], in0=ps[b][:, :], in1=x_t[:, :], op=ALU.mult),
                              [ps_r[b], r_xg], [r_hv[t]])
            S.barrier()
            with ExitStack() as st:
                wpa_t, r_wpa = sb(st, "wpa_t", [128, 4, 1024], BF16)
                wpb_t, r_wpb = sb(st, "wpb_t", [128, 4, 1024], BF16)
                wo_t, r_wo = sb(st, "wo_t", [128, 8, 1024], BF16)
                gtb, r_gtb = sb(st, "gtb", [128, D], F32)
                DMA(wpa_t[:, :, :], wpa_s, r=[r_w["wpa"]], w=[r_wpa])
                DMA(wpb_t[:, :, :], wpb_s, r=[r_w["wpb"]], w=[r_wpb])
                DMA(wo_t[:, :, :], wout_s, r=[r_w["wout"]], w=[r_wo])
                DMA(gtb[:, :], modd[s:s + 1, 2048:3072].to_broadcast([128, D]), r=[r_modd], w=[r_gtb])
                abT = [sb(st, f"abT{i}", [128, 8, 128], BF16) for i in range(2)]
                mT = [sb(st, f"mT{i}", [128, 8, 128], BF16) for i in range(2)]
                sg = [sb(st, f"sg{i}", [128, 2, D], BF16) for i in range(2)]
                m1, r_m1 = sb(st, "m1", [128, D], F32)
                m2, r_m2 = sb(st, "m2", [128, D], F32)
                mg, r_mg = sb(st, "mg", [128, D], BF16)
                xt = [sb(st, f"xt5{i}", [128, D], F32) for i in range(2)]
                mgs = [(mg, r_mg), sb(st, "mg2", [128, D], BF16)]
                o1, r_o1 = sb(st, "o1", [128, D], F32)

                def stX(t):
                    ab, r_ab = abT[t % 2]
                    s_t, r_s = sg[t % 2]
                    x_t, r_x = xt[t % 2]
                    mg_t, r_mgt = mgs[t % 2]
                    DMA(s_t[:, 0, :], sigd[0, t * 128:(t + 1) * 128, :], r=[r_sigd], w=[r_s])
                    DMA(s_t[:, 1, :], sigd[1, t * 128:(t + 1) * 128, :], r=[r_sigd], w=[r_s])
                    DMA(x_t[:, :], src_x[s, t * 128:(t + 1) * 128, :], r=[xres[s][t]], w=[r_x])
                    for k in range(4):
                        PE(lambda e: e.transpose(out=psb(0)[:, k * 128:(k + 1) * 128], in_=hv[:, t, k * 128:(k + 1) * 128],
                                                 identity=ident[:, :]), [r_hv[t], r_ident], [ps_r[0]])
                    for k in range(4):
                        PE(lambda e: e.transpose(out=psb(0)[:, (4 + k) * 128:(5 + k) * 128],
                                                 in_=ybt[:, t, k * 128:(k + 1) * 128], identity=ident[:, :]),
                           [r_ybt[t], r_ident], [ps_r[0]])
                    A(lambda e: e.copy(out=ab[:, :, :], in_=psb(0).rearrange("p (k q) -> p k q", k=8)), [ps_r[0]], [r_ab])
                    for nb in range(2):
                        for k in range(4):
                            PE(lambda e: e.matmul(ps[1 + nb][:, :], lhsT=ab[:, k, :], rhs=wpa_t[:, k, nb * 512:(nb + 1) * 512],
                                                  start=(k == 0), stop=(k == 3)), [r_ab, r_wpa], [ps_r[1 + nb]])
                        for k in range(4):
                            PE(lambda e: e.matmul(ps[3 + nb][:, :], lhsT=ab[:, 4 + k, :], rhs=wpb_t[:, k, nb * 512:(nb + 1) * 512],
                                                  start=(k == 0), stop=(k == 3)), [r_ab, r_wpb], [ps_r[3 + nb]])
                    for nb in range(2):
                        V(lambda e: e.tensor_tensor(out=m1[:, nb * 512:(nb + 1) * 512], in0=ps[1 + nb][:, :],
                                                    in1=s_t[:, 0, nb * 512:(nb + 1) * 512], op=ALU.mult),
                          [ps_r[1 + nb], r_s], [r_m1])
                        V(lambda e: e.tensor_tensor(out=m2[:, nb * 512:(nb + 1) * 512], in0=ps[3 + nb][:, :],
                                                    in1=s_t[:, 1, nb * 512:(nb + 1) * 512], op=ALU.mult),
                          [ps_r[3 + nb], r_s], [r_m2])
                    G(lambda e: e.tensor_tensor(out=mg_t[:, :], in0=m1[:, :], in1=m2[:, :], op=ALU.add), [r_m1, r_m2], [r_mgt])

                def stY(t):
                    mt, r_mt = mT[t % 2]
                    x_t, r_x = xt[t % 2]
                    mg_t, r_mgt = mgs[t % 2]
                    for k in range(8):
                        PE(lambda e: e.transpose(out=psb(5)[:, k * 128:(k + 1) * 128], in_=mg_t[:, k * 128:(k + 1) * 128],
                                                 identity=ident[:, :]), [r_mgt, r_ident], [ps_r[5]])
                    A(lambda e: e.copy(out=mt[:, :, :], in_=psb(5).rearrange("p (k q) -> p k q", k=8)), [ps_r[5]], [r_mt])
                    for nb in range(2):
                        for k in range(8):
                            PE(lambda e: e.matmul(ps[6 + nb][:, :], lhsT=mt[:, k, :], rhs=wo_t[:, k, nb * 512:(nb + 1) * 512],
                                                  start=(k == 0), stop=(k == 7)), [r_mt, r_wo], [ps_r[6 + nb]])
                        V(lambda e: e.tensor_tensor(out=o1[:, nb * 512:(nb + 1) * 512], in0=ps[6 + nb][:, :],
                                                    in1=gtb[:, nb * 512:(nb + 1) * 512], op=ALU.mult),
                          [ps_r[6 + nb], r_gtb], [r_o1])
                    G(lambda e: e.tensor_tensor(out=x_t[:, :], in0=x_t[:, :], in1=o1[:, :], op=ALU.add), [r_x, r_o1], [r_x])
                    DMA(y_out[s, t * 128:(t + 1) * 128, :], x_t[:, :], r=[r_x], w=[xres[s][t]])

                stX(0)
                for t in range(NT):
                    if t + 1 < NT:
                        stX(t + 1)
                    stY(t)
            S.barrier()

        r_x12 = Res("x12")
        r_sigd = Res("sigd")
        r_uvd = [Res("uvd0"), Res("uvd1")]

        tbl_pending = []

        def issue_tables(l, deferred=False):
            CH = 256 if deferred else 512
            for which, tab in ((0, peer_u), (1, peer_v)):
                tv = tab.ap()[l * 16384:(l + 1) * 16384, :]
                for ch in range(16384 // CH):
                    def go(which=which, tv=tv, ch=ch):
                        S.dma("tbl", lambda e: e.dma_start(out=uvds[l].ap()[ch * CH:(ch + 1) * CH, which * D:(which + 1) * D],
                                                           in_=tv[ch * CH:(ch + 1) * CH, :]), [], [Res()], eng="gpsimd")
                    if deferred:
                        tbl_pending.append(go)
                    else:
                        go()

        def tbl_trickle(n=1):
            for _ in range(n):
                if tbl_pending:
                    tbl_pending.pop(0)()

        def peer_seq(l, s, from_input=False):
            src_x = x_in if from_input else y_out
            if l > 0:
                tbl_trickle(len(tbl_pending))
            S.wait_all("gpsimd", "tbl")
            uvd = uvds[l]
            with ExitStack() as st:
                wq_t, r_wq = sb(st, "wq_t", [128, 8, 2048], BF16)
                DMA(wq_t[:, :, :], pwq_s, r=[r_w["pwq"]], w=[r_wq])
                kst, r_kst = sb(st, "kst", [128, 2, 128], F32)
                k12, r_k12 = sb(st, "k12", [128, 2, 128], BF16)
                DMA(kst[:, 0, :], peer_k1T[l], w=[r_kst])
                DMA(kst[:, 1, :], peer_k2T[l], w=[r_kst])
                V(lambda e: e.tensor_copy(out=k12[:, :, :], in_=kst[:, :, :]), [r_kst], [r_k12])
                a2b, r_a2b = sb(st, "a2b", [128, D], F32)
                sh2b, r_sh2b = sb(st, "sh2b", [128, D], F32)
                gt2b, r_gt2b = sb(st, "gt2b", [128, D], F32)
                DMA(sh2b[:, :], modd[s:s + 1, 3072:4096].to_broadcast([128, D]), r=[r_modd], w=[r_sh2b])
                DMA(a2b[:, :], modd[s:s + 1, 4096:5120].to_broadcast([128, D]), r=[r_modd], w=[r_a2b])
                DMA(gt2b[:, :], modd[s:s + 1, 5120:6144].to_broadcast([128, D]), r=[r_modd], w=[r_gt2b])
                iot, r_iot = sb(st, "iot", [128, 16], F32)
                DMA(iot[:, :], c_iota, w=[r_iot])
                ioti, r_ioti = sb(st, "ioti", [128, 256], I32)
                DMA(ioti[:, :], c_iota256, w=[r_ioti])
                xt = [sb(st, f"xp{i}", [128, D], F32) for i in range(3)]
                h2, r_h2 = sb(st, "h2", [128, D], F32)
                fin, r_fin = h2, r_h2
                h2b, r_h2b = sb(st, "h2b", [128, D], BF16)
                junk, r_junk = sb(st, "junkp", [128, D], BF16)
                junks = [(junk, r_junk), sb(st, "junk2", [128, D], BF16)]
                junkA, r_junkA = h2b, r_h2b
                st8, r_st8 = sb(st, "st8p", [128, 8], F32)
                h2T, r_h2T = sb(st, "h2T", [128, 8, 128], BF16)
                qT, r_qT = sb(st, "qTp", [128, 16, 128], BF16)
                sc, r_sc = sb(st, "sc", [128, 2, 8, 128], F32)
                r_sch = [[Res(f"sc{a}{b}") for b in range(8)] for a in range(2)]
                vv, r_vv = sb(st, "vv", [128, 2, 8, 16], F32)
                r_vvh = [[Res(f"vv{a}{b}") for b in range(8)] for a in range(2)]
                r_vvall = r_vvh[0] + r_vvh[1]
                r_vsh = [Res(f"vs{b}") for b in range(8)]
                ii, r_ii = sb(st, "ii", [128, 2, 8, 16], U32)
                iif, r_iif = sb(st, "iif", [128, 2, 8, 16], BF16)
                cand, r_cand = sb(st, "cand", [128, 8, 256], F32)
                r_cdh = [Res(f"cd{b}") for b in range(8)]
                vs, r_vs = sb(st, "vs", [128, 8, 16], F32)
                ic, r_ic = sb(st, "ic", [128, 8, 16], U32)
                icab, r_icab = sb(st, "icab", [128, 2, 8, 16], U32)
                icf, r_icf = sb(st, "icf", [128, 2, 8, 16], BF16)
                oh, r_oh = sb(st, "oh", [128, 8, 16, 16], BF16)
                iotb, r_iotb = sb(st, "iotb", [128, 16], BF16)
                V(lambda e: e.tensor_copy(out=iotb[:, :], in_=iot[:, :]), [r_iot], [r_iotb])
                ef, r_ef = sb(st, "ef", [128, 2, 8, 16], F32)
                idxf, r_idxf = sb(st, "idxf", [128, 128], F32)
                idxs = [sb(st, f"idx{i}", [128, 128], I32) for i in range(2)]
                ggs = [sb(st, f"gg{i}", [128, 8, 16], F32) for i in range(2)]
                sm, r_sm = sb(st, "sm", [128, 16], F32)
                aa, _ = sb(st, "aa", [128, 128], F32)
                ww, _ = sb(st, "ww", [128, 128], F32)
                NGB = 6
                gbuf = [sb(st, f"gbuf{i}", [128, 4, 2 * D], BF16) for i in range(NGB)]
                r_gs = [[Res(f"gs{b}_{i}") for i in range(4)] for b in range(NGB)]
                wds = [sb(st, f"wd{i}", [128, 4, 128], BF16) for i in range(2)]
                r_aaj = [Res(f"aa{j}") for j in range(128)]
                r_wwb = [Res(f"ww{j}") for j in range(32)]
                gi_ = [0]
                HP = ((4, 5), (6, 7))
                ACC = (2, 3)
                BT, BS = 0, 1

                def stageA(t):
                    slot = t % 2
                    x_t, r_x = xt[t % 3]
                    idx, r_idx = idxs[slot]
                    gg, r_gg = ggs[slot]
                    H2P = HP[slot]
                    DMA(x_t[:, :], src_x[s, t * 128:(t + 1) * 128, :], r=[xres[s][t]], w=[r_x])
                    A(lambda e: e.activation(out=junkA[:, :], in_=x_t[:, :], func=AF.Square, accum_out=st8[:, 0:1]),
                      [r_x], [r_junkA, r_st8])
                    A(lambda e: e.activation(out=st8[:, 1:2], in_=st8[:, 0:1], func=AF.Sqrt, scale=1.0 / D,
                                             bias=eps_t[:, 0:1]), [r_st8], [r_st8])
                    yield
                    V(lambda e: e.reciprocal(out=st8[:, 2:3], in_=st8[:, 1:2]), [r_st8], [r_st8])
                    V(lambda e: e.scalar_tensor_tensor(out=h2[:, :], in0=x_t[:, :], scalar=st8[:, 2:3], in1=a2b[:, :],
                                                       op0=ALU.mult, op1=ALU.mult), [r_x, r_st8, r_a2b], [r_h2])
                    V(lambda e: e.tensor_tensor(out=h2[:, :], in0=h2[:, :], in1=sh2b[:, :], op=ALU.add), [r_h2, r_sh2b], [r_h2])
                    A(lambda e: e.copy(out=h2b[:, :], in_=h2[:, :]), [r_h2], [r_h2b])
                    yield
                    for nb in range(2):
                        V(lambda e: e.tensor_copy(out=ps[H2P[nb]][:, :], in_=h2[:, nb * 512:(nb + 1) * 512]),
                          [r_h2], [ps_r[H2P[nb]]])
                    for k in range(8):
                        PE(lambda e: e.transpose(out=psb(BT)[:, k * 128:(k + 1) * 128], in_=h2b[:, k * 128:(k + 1) * 128],
                                                 identity=ident[:, :]), [r_h2b, r_ident], [ps_r[BT]])
                    yield
                    A(lambda e: e.copy(out=h2T[:, :, :], in_=psb(BT).rearrange("p (k q) -> p k q", k=8)), [ps_r[BT]], [r_h2T])
                    yield
                    for rnd in range(4):
                        for c4 in range(4):
                            cb = rnd * 4 + c4
                            for k in range(8):
                                PE(lambda e: e.matmul(ps[BT][:, c4 * 128:(c4 + 1) * 128], lhsT=wq_t[:, k, cb * 128:(cb + 1) * 128],
                                                      rhs=h2T[:, k, :], start=(k == 0), stop=(k == 7)),
                                   [r_wq, r_h2T], [ps_r[BT]])
                        yield
                        A(lambda e: e.copy(out=qT[:, rnd * 4:(rnd + 1) * 4, :],
                                           in_=ps[BT][:, :].rearrange("p (c q) -> p c q", c=4)), [ps_r[BT]], [r_qT])
                    yield
                    for hf in range(2):
                        for rnd in range(2):
                            for h4 in range(4):
                                h = rnd * 4 + h4
                                PE(lambda e: e.matmul(ps[BS][:, h4 * 128:(h4 + 1) * 128], lhsT=qT[:, 2 * h + hf, :],
                                                      rhs=k12[:, hf, :], start=True, stop=True), [r_qT, r_k12], [ps_r[BS]])
                            yield
                            A(lambda e: e.copy(out=sc[:, hf, rnd * 4:(rnd + 1) * 4, :],
                                               in_=ps[BS][:, :].rearrange("p (h n) -> p h n", h=4)), [ps_r[BS]],
                              [r_sch[hf][rnd * 4 + i] for i in range(4)])
                    yield
                    for hf in range(2):
                        sci = sc[:, hf, :, :].bitcast(U32)
                        V(lambda e: e.tensor_scalar(out=sci, in0=sci, scalar1=7, scalar2=7, op0=ALU.logical_shift_right,
                                                    op1=ALU.logical_shift_left), r_sch[hf], r_sch[hf])
                        V(lambda e: e.tensor_tensor(out=sci, in0=sci,
                                                    in1=ioti[:, 0:128].bitcast(U32).unsqueeze(1).to_broadcast([128, 8, 128]),
                                                    op=ALU.bitwise_or), r_sch[hf] + [r_ioti], r_sch[hf])
                        yield
                    for hf in range(2):
                        for h0 in range(0, 8, 2):
                            for h in (h0, h0 + 1):
                                V(lambda e: e.max(out=vv[:, hf, h, 0:8], in_=sc[:, hf, h, :]), [r_sch[hf][h]], [r_vvh[hf][h]])
                            for h in (h0, h0 + 1):
                                V(lambda e: e.match_replace(out=sc[:, hf, h, :], in_to_replace=vv[:, hf, h, 0:8],
                                                            in_values=sc[:, hf, h, :], imm_value=-1e30),
                                  [r_sch[hf][h], r_vvh[hf][h]], [r_sch[hf][h]])
                            for h in (h0, h0 + 1):
                                V(lambda e: e.max(out=vv[:, hf, h, 8:16], in_=sc[:, hf, h, :]), [r_sch[hf][h]], [r_vvh[hf][h]])
                            yield
                    V(lambda e: e.tensor_single_scalar(out=ii[:, :, :, :], in_=vv[:, :, :, :].bitcast(U32), scalar=127,
                                                       op=ALU.bitwise_and), r_vvall, [r_ii])
                    V(lambda e: e.tensor_copy(out=iif[:, :, :, :], in_=ii[:, :, :, :]), [r_ii], [r_iif])
                    c4v = cand[:, :, :].rearrange("p h (a b) -> p h a b", a=16)
                    V(lambda e: e.tensor_tensor(out=c4v, in0=vv[:, 0, :, :].unsqueeze(3).to_broadcast([128, 8, 16, 16]),
                                                in1=vv[:, 1, :, :].unsqueeze(2).to_broadcast([128, 8, 16, 16]), op=ALU.add),
                      r_vvall, r_cdh)
                    yield
                    cdi = cand[:, :, :].bitcast(U32)
                    V(lambda e: e.tensor_scalar(out=cdi, in0=cdi, scalar1=8, scalar2=8, op0=ALU.logical_shift_right,
                                                op1=ALU.logical_shift_left), r_cdh, r_cdh)
                    V(lambda e: e.tensor_tensor(out=cdi, in0=cdi,
                                                in1=ioti[:, :].bitcast(U32).unsqueeze(1).to_broadcast([128, 8, 256]),
                                                op=ALU.bitwise_or), r_cdh + [r_ioti], r_cdh)
                    yield
                    for h0 in range(0, 8, 2):
                        for h in (h0, h0 + 1):
                            V(lambda e: e.max(out=vs[:, h, 0:8], in_=cand[:, h, :]), [r_cdh[h]], [r_vsh[h]])
                        for h in (h0, h0 + 1):
                            V(lambda e: e.match_replace(out=cand[:, h, :], in_to_replace=vs[:, h, 0:8],
                                                        in_values=cand[:, h, :], imm_value=-1e30), [r_cdh[h], r_vsh[h]], [r_cdh[h]])
                        for h in (h0, h0 + 1):
                            V(lambda e: e.max(out=vs[:, h, 8:16], in_=cand[:, h, :]), [r_cdh[h]], [r_vsh[h]])
                        yield
                    V(lambda e: e.tensor_single_scalar(out=ic[:, :, :], in_=vs[:, :, :].bitcast(U32), scalar=255,
                                                       op=ALU.bitwise_and), r_vsh, [r_ic])
                    V(lambda e: e.tensor_single_scalar(out=icab[:, 0, :, :], in_=ic[:, :, :], scalar=4,
                                                       op=ALU.logical_shift_right), [r_ic], [r_icab])
                    V(lambda e: e.tensor_single_scalar(out=icab[:, 1, :, :], in_=ic[:, :, :], scalar=15,
                                                       op=ALU.bitwise_and), [r_ic], [r_icab])
                    V(lambda e: e.tensor_copy(out=icf[:, :, :, :], in_=icab[:, :, :, :]), [r_icab], [r_icf])
                    yield
                    for hf in range(2):
                        V(lambda e: e.tensor_tensor(out=oh[:, :, :, :],
                                                    in0=icf[:, hf, :, :].unsqueeze(3).to_broadcast([128, 8, 16, 16]),
                                                    in1=iotb[:, :].unsqueeze(1).unsqueeze(1).to_broadcast([128, 8, 16, 16]),
                                                    op=ALU.is_equal), [r_icf, r_iotb], [r_oh])
                        yield
                        V(lambda e: e.tensor_tensor(out=oh[:, :, :, :], in0=oh[:, :, :, :],
                                                    in1=iif[:, hf, :, :].unsqueeze(2).to_broadcast([128, 8, 16, 16]),
                                                    op=ALU.mult), [r_oh, r_iif], [r_oh])
                        yield
                        V(lambda e: e.tensor_reduce(out=ef[:, hf, :, :], in_=oh[:, :, :, :], axis=AX.X, op=ALU.add),
                          [r_oh], [r_ef])
                        yield
                    V(lambda e: e.scalar_tensor_tensor(out=idxf[:, :], in0=ef[:, 0, :, :].rearrange("p h k -> p (h k)"),
                                                       scalar=128.0, in1=ef[:, 1, :, :].rearrange("p h k -> p (h k)"),
                                                       op0=ALU.mult, op1=ALU.add), [r_ef], [r_idxf])
                    V(lambda e: e.tensor_copy(out=idx[:, :], in_=idxf[:, :]), [r_idxf], [r_idx])
                    V(lambda e: e.tensor_tensor(out=gg[:, :, :], in0=vs[:, :, :],
                                                in1=vs[:, :, 0:1].to_broadcast([128, 8, 16]), op=ALU.subtract), r_vsh, [r_gg])
                    A(lambda e: e.activation(out=gg[:, :, :], in_=gg[:, :, :], func=AF.Exp), [r_gg], [r_gg])
                    yield
                    V(lambda e: e.tensor_reduce(out=sm[:, 0:8], in_=gg[:, :, :], axis=AX.X, op=ALU.add), [r_gg], [r_sm])
                    V(lambda e: e.reciprocal(out=sm[:, 8:16], in_=sm[:, 0:8]), [r_sm], [r_sm])
                    V(lambda e: e.tensor_tensor(out=gg[:, :, :], in0=gg[:, :, :],
                                                in1=sm[:, 8:16].unsqueeze(2).to_broadcast([128, 8, 16]), op=ALU.mult),
                      [r_gg, r_sm], [r_gg])
                    yield

                def stageB(t, agen, prev_tail):
                    slot = t % 2
                    x_t, r_x = xt[t % 3]
                    idx, r_idx = idxs[slot]
                    gg, r_gg = ggs[slot]
                    H2P = HP[slot]
                    NBT = 32
                    pbuf = {}
                    for bt in range(NBT + 1):
                        if bt < NBT:
                            b = gi_[0] % NGB
                            gi_[0] += 1
                            pbuf[bt] = b
                            g_t = gbuf[b][0]
                            for i in range(4):
                                j = bt * 4 + i
                                S.dma("gpsimd", lambda e: e.indirect_dma_start(
                                    out=g_t[:, i, :], out_offset=None, in_=uvd[:, :],
                                    in_offset=bass.IndirectOffsetOnAxis(ap=idx[:, j:j + 1], axis=0)),
                                    [r_idx], [r_gs[b][i]])
                            for i in range(4):
                                j = bt * 4 + i
                                jk, r_jk = junks[j % 2]
                                V(lambda e: e.scalar_tensor_tensor(
                                    out=jk[:, :], in0=g_t[:, i, 0:D], scalar=1.0, in1=ps2(H2P), op0=ALU.mult, op1=ALU.mult,
                                    accum_out=aa[:, j:j + 1]),
                                  [r_gs[b][i], ps_r[H2P[0]], ps_r[H2P[1]]], [r_jk, r_aaj[j]])
                            A(lambda e: e.activation(out=ww[:, bt * 4:(bt + 1) * 4], in_=aa[:, bt * 4:(bt + 1) * 4], func=AF.Gelu),
                              [r_aaj[bt * 4 + i] for i in range(4)], [r_wwb[bt]])
                        if bt == 0 and prev_tail is not None:
                            prev_tail()
                        if bt in (18, 22, 26, 30):
                            tbl_trickle(1)
                        if agen is not None:
                            next(agen, None)
                            if bt >= 14:
                                next(agen, None)
                        if bt >= 1:
                            pb_ = bt - 1
                            b = pbuf[pb_]
                            g_t = gbuf[b][0]
                            wd_t, r_wd = wds[pb_ % 2]
                            V(lambda e: e.tensor_tensor(out=ww[:, pb_ * 4:(pb_ + 1) * 4], in0=ww[:, pb_ * 4:(pb_ + 1) * 4],
                                                        in1=gg[:, :, :].rearrange("p h k -> p (h k)")[:, pb_ * 4:(pb_ + 1) * 4],
                                                        op=ALU.mult), [r_wwb[pb_], r_gg], [r_wwb[pb_]])
                            for i in range(4):
                                j = pb_ * 4 + i
                                A(lambda e: e.activation(out=wd_t[:, i, :], in_=ident[:, :], func=AF.Copy, scale=ww[:, j:j + 1]),
                                  [r_ident, r_wwb[pb_]], [r_wd])
                            for i in range(4):
                                j = pb_ * 4 + i
                                for nb in range(2):
                                    PE(lambda e: e.matmul(ps[ACC[nb]][:, :], lhsT=wd_t[:, i, :],
                                                          rhs=g_t[:, i, D + nb * 512:D + (nb + 1) * 512],
                                                          start=(j == 0), stop=(j == 127)),
                                       [r_wd, r_gs[b][i]], [ps_r[ACC[nb]]])
                    if agen is not None:
                        for _ in agen:
                            pass
                    V(lambda e: e.tensor_tensor(out=fin[:, :], in0=ps2(ACC), in1=gt2b[:, :], op=ALU.mult),
                      [ps_r[ACC[0]], ps_r[ACC[1]], r_gt2b], [r_fin])

                    def tail():
                        V(lambda e: e.tensor_tensor(out=x_t[:, :], in0=x_t[:, :], in1=fin[:, :], op=ALU.add), [r_x, r_fin], [r_x])
                        DMA(y_out[s, t * 128:(t + 1) * 128, :], x_t[:, :], r=[r_x], w=[xres[s][t]])
                    return tail

                for _ in stageA(0):
                    pass
                tail = None
                for t in range(NTOK_PEER):
                    agen = stageA(t + 1) if t + 1 < NTOK_PEER else None
                    tail = stageB(t, agen, tail)
                tail()
            S.barrier()

        if do_peer:
            issue_tables(0)
            if DEPTH > 1:
                issue_tables(1, deferred=True)
        for l in range(DEPTH):
            mod_phase(l)
            prep_weights(l)
            if do_mixer:
                filter_phase(l)
            if do_peer and l == 0:
                pass
            for s in range(NSEQ):
                if do_mixer:
                    with ExitStack() as sq:
                        hv, _ = sb(sq, "hv", [128, NT, 512], BF16)
                        ybt, _ = sb(sq, "ybt", [128, NT, 512], BF16)
                        r_hv = [Res(f"hv{t}") for t in range(NT)]
                        r_ybt = [Res(f"yb{t}") for t in range(NT)]
                        mixer_seq(l, s, hv, r_hv, ybt, r_ybt)
                if do_peer:
                    peer_seq(l, s, from_input=(l == 0 and not do_mixer))
        S.barrier(include_tbl=True)
        print("bass program: ops", S.nops, "waits", S.nwaits, flush=True)
    return nc


N_CORES = 8
_NC_CACHE = {}


def kernel(**inputs):
    f32 = np.float32
    x = np.concatenate([np.asarray(inputs["x_prompt"], f32), np.asarray(inputs["x_sample"], f32)], axis=0)
    c = np.concatenate([np.asarray(inputs["c_prompt"], f32), np.asarray(inputs["c_sample"], f32)], axis=0)
    nseq = x.shape[0] // N_CORES
    cst = _consts()
    shared = {}
    for k in ("w_mod", "b_mod", "g_norm1", "g_norm2", "w_in", "conv_w", "conv_b", "f_w1", "f_b1", "f_freq", "f_w2",
              "f_b2", "f_w3", "f_bias", "q_gain", "k_gain", "sink", "w_pa", "w_pb", "w_out", "peer_wq"):
        shared[k] = np.ascontiguousarray(np.asarray(inputs[k], f32))
    shared["peer_k1T"] = np.ascontiguousarray(np.asarray(inputs["peer_k1"], f32).transpose(0, 2, 1))
    shared["peer_k2T"] = np.ascontiguousarray(np.asarray(inputs["peer_k2"], f32).transpose(0, 2, 1))
    shared["peer_u"] = np.ascontiguousarray(np.asarray(inputs["peer_u"], f32).reshape(2 * 16384, D))
    shared["peer_v"] = np.ascontiguousarray(np.asarray(inputs["peer_v"], f32).reshape(2 * 16384, D))
    for k in ("fwdF", "invG", "zfT", "negt", "absdelta", "cs", "masks", "ident", "iota16", "iota256"):
        shared[k] = cst[k]
    in_maps = []
    for i in range(N_CORES):
        m = dict(shared)
        m["x"] = np.ascontiguousarray(x[i * nseq:(i + 1) * nseq])
        ci = c[i * nseq:(i + 1) * nseq]
        m["cT"] = np.ascontiguousarray(ci.T.reshape(8, 128, nseq).transpose(1, 0, 2))
        in_maps.append(m)
    if "nc" not in _NC_CACHE:
        _NC_CACHE["nc"] = build(NSEQ=nseq, DEPTH=2)
    res = run_bass_kernel_spmd(_NC_CACHE["nc"], in_maps, core_ids=list(range(N_CORES)))
    y = np.concatenate([np.asarray(r["y"], f32) for r in res.results], axis=0)
    nb = inputs["x_prompt"].shape[0]
    return (np.ascontiguousarray(y[:nb]), np.ascontiguousarray(y[nb:]))
```

```python
import math
from contextlib import ExitStack

import numpy as np
import ml_dtypes
import concourse.bass as bass
import concourse.mybir as mybir
from concourse.bass_utils import run_bass_kernel_spmd

F32 = mybir.dt.float32
BF16 = mybir.dt.bfloat16
U32 = mybir.dt.uint32
I32 = mybir.dt.int32
AF = mybir.ActivationFunctionType
ALU = mybir.AluOpType
AX = mybir.AxisListType

L = 2048
D = 1024
NT = 16
EPS = 1e-6
NEG = -30000.0
MAGIC = 12582912.0
TWO_PI = 2.0 * math.pi


class Res:
    __slots__ = ("name", "w", "r")

    def __init__(self, name=""):
        self.name = name
        self.w = None
        self.r = {}


class Sched:
    ENGS = ("sync", "scalar", "vector", "gpsimd", "tensor")
    RING = 8

    def __init__(self, nc, es):
        self.nc = nc
        self.eng = {e: getattr(nc, e) for e in self.ENGS}
        self.sems = []
        self.semid = {}
        self.cnt = {}
        for e in self.ENGS:
            self.semid[e] = len(self.sems)
            self.sems.append(es.enter_context(nc.semaphore("c_" + e)))
            self.cnt[e] = 0
        self.ring = {}
        self.ring_cnt = {}
        for q in ("sync", "gpsimd", "tbl"):
            ids = []
            for i in range(self.RING):
                ids.append(len(self.sems))
                self.sems.append(es.enter_context(nc.semaphore(f"d_{q}{i}")))
            self.ring[q] = ids
            self.ring_cnt[q] = 0
        self.known = {e: {} for e in self.ENGS}
        self.nwaits = 0
        self.nops = 0

    def _waits(self, eng, reads, writes, extra=()):
        need = {}

        def add(ev):
            if ev is None:
                return
            s, v = ev
            if need.get(s, 0) < v:
                need[s] = v
        for r in reads:
            add(r.w)
        for w in writes:
            add(w.w)
            for s, v in w.r.items():
                add((s, v))
        for ev in extra:
            add(ev)
        kn = self.known[eng]
        e = self.eng[eng]
        own = self.semid[eng]
        for s, v in need.items():
            if eng == "tensor" and s == own:
                continue
            if kn.get(s, 0) >= v:
                continue
            e.wait_ge(self.sems[s], v)
            kn[s] = v
            self.nwaits += 1

    def _mark(self, ev, reads, writes):
        s, v = ev
        for w in writes:
            w.w = ev
            w.r = {}
        for r in reads:
            if r in writes:
                continue
            if r.r.get(s, 0) < v:
                r.r[s] = v

    def op(self, eng, fn, reads=(), writes=()):
        self._waits(eng, reads, writes)
        ins = fn(self.eng[eng])
        self.cnt[eng] += 1
        ev = (self.semid[eng], self.cnt[eng])
        ins.then_inc(self.sems[ev[0]], 1)
        self._mark(ev, reads, writes)
        self.nops += 1
        return ev

    def dma(self, q, fn, reads=(), writes=(), eng=None):
        eng = eng or q
        i = self.ring_cnt[q]
        self.ring_cnt[q] = i + 1
        slot = self.ring[q][i % self.RING]
        rnd = i // self.RING
        extra = []
        if rnd > 0:
            extra.append((slot, 16 * rnd))
        self._waits(eng, reads, writes, extra)
        ins = fn(self.eng[eng])
        ev = (slot, 16 * (rnd + 1))
        ins.then_inc(self.sems[slot], 16)
        self._mark(ev, reads, writes)
        self.nops += 1
        return ev

    def ring_events(self, q):
        evs = []
        n = self.ring_cnt[q]
        for k in range(self.RING):
            cntk = (n - k + self.RING - 1) // self.RING if n > k else 0
            if cntk > 0:
                evs.append((self.ring[q][k], 16 * cntk))
        return evs

    def wait_all(self, eng, q):
        self._waits(eng, (), (), self.ring_events(q))

    def barrier(self, include_tbl=False):
        evs = [(self.semid[e], self.cnt[e]) for e in self.ENGS if self.cnt[e] > 0]
        for q in self.ring:
            if q == "tbl" and not include_tbl:
                continue
            n = self.ring_cnt[q]
            for k in range(self.RING):
                cntk = (n - k + self.RING - 1) // self.RING if n > k else 0
                if cntk > 0:
                    evs.append((self.ring[q][k], 16 * cntk))
        for eng in self.ENGS:
            self._waits(eng, (), (), evs)


_CONST = {}


def _consts():
    if _CONST:
        return _CONST
    bf = ml_dtypes.bfloat16
    N2 = 2 * L
    n = np.arange(L, dtype=np.int64)
    fwd = np.zeros((16, 128, 16, 256), np.float32)
    nn = (np.arange(16)[None, :] * 128 + np.arange(128)[:, None])
    for fc in range(16):
        f = fc * 128 + np.arange(128)
        ang = 2.0 * np.pi * ((nn[:, :, None] * f[None, None, :]) % N2) / N2
        fwd[fc, :, :, 0:128] = np.cos(ang)
        fwd[fc, :, :, 128:256] = -np.sin(ang)
    fwd[0, :, :, 128] = np.where(nn % 2 == 0, 1.0, -1.0)
    inv = np.zeros((4, 128, 32, 512), np.float32)
    for tg in range(4):
        t = tg * 512 + np.arange(512)
        for fc in range(16):
            f = fc * 128 + np.arange(128)
            ang = 2.0 * np.pi * ((f[:, None] * t[None, :]) % N2) / N2
            gre = (2.0 / N2) * np.cos(ang)
            gim = -(2.0 / N2) * np.sin(ang)
            if fc == 0:
                gre[0, :] = 1.0 / N2
                gim[0, :] = np.where(t % 2 == 0, 1.0, -1.0) / N2
            inv[tg, :, fc, :] = gre
            inv[tg, :, 16 + fc, :] = gim
    tl = np.linspace(0.0, 1.0, L, dtype=np.float32)
    w = (np.float32(2.0 * math.pi) * np.arange(L, dtype=np.float32) / np.float32(L)).astype(np.float32)
    bands = np.linspace(1e-4, 15.0, 16, dtype=np.float32)
    bw = (bands[:, None] * w[None, :]).astype(np.float32).astype(np.float64)
    zfT = np.concatenate([tl[None, :].astype(np.float64), np.cos(bw), -np.sin(bw)], axis=0).astype(np.float32)
    negt = np.zeros((128, 16), np.float32)
    negt[:, :] = -tl[nn]
    max_decay = math.log(1e-2) / 0.3
    min_decay = math.log(1e-2) / 1.5
    absdelta = np.abs(np.linspace(min_decay, max_decay, 512, dtype=np.float32)).reshape(1, 512)
    invf = (10000.0 ** (-np.arange(0, 64, 2, dtype=np.float32) / np.float32(64))).astype(np.float32)
    ang = (np.arange(L, dtype=np.float32)[:, None] * invf[None, :]).astype(np.float32).astype(np.float64)
    cs = np.zeros((L, 128), np.float32)
    cs[:, 0:32] = np.cos(ang)
    cs[:, 32:64] = np.cos(ang)
    cs[:, 64:96] = np.sin(ang)
    cs[:, 96:128] = np.sin(ang)
    cs = np.ascontiguousarray(cs.reshape(16, 128, 128).transpose(1, 0, 2))
    j = np.arange(128)[:, None]
    q = np.arange(128)[None, :]
    mprev = np.where(j >= q, 0.0, NEG).astype(np.float32)
    mnext = np.where(j <= q, 0.0, NEG).astype(np.float32)
    masks = np.stack([np.tile(mprev, (1, 4)), np.tile(mnext, (1, 4))], axis=0)
    _CONST.update(
        fwdF=fwd.astype(bf), invG=inv.astype(bf), zfT=zfT, negt=negt, absdelta=absdelta,
        cs=cs, masks=masks.astype(bf), ident=np.eye(128, dtype=np.float32).astype(bf),
        iota16=np.tile(np.arange(16, dtype=np.float32)[None, :], (128, 1)),
        iota256=np.tile(np.arange(256, dtype=np.int32)[None, :], (128, 1)),
    )
    return _CONST


def build(NSEQ=3, DEPTH=2, do_mixer=True, do_peer=True, NTOK_PEER=NT):
    nc = bass.Bass("TRN2", target_bir_lowering=False)

    def din(name, shape, dt=F32):
        return nc.dram_tensor(name, list(shape), dt, kind="ExternalInput").ap()

    def dscr(name, shape, dt):
        return nc.dram_tensor(name, list(shape), dt, kind="Internal").ap()

    x_in = din("x", [NSEQ, L, D])
    cT_in = din("cT", [128, 8, NSEQ])
    w_mod = din("w_mod", [2, D, 6 * D])
    b_mod = din("b_mod", [2, 6 * D])
    g_norm1 = din("g_norm1", [2, D])
    g_norm2 = din("g_norm2", [2, D])
    w_in = din("w_in", [2, D, 4352])
    conv_w = din("conv_w", [2, 3, 1536])
    conv_b = din("conv_b", [2, 1536])
    f_w1 = din("f_w1", [2, 33, 64])
    f_b1 = din("f_b1", [2, 64])
    f_freq = din("f_freq", [2, 64])
    f_w2 = din("f_w2", [2, 64, 64])
    f_b2 = din("f_b2", [2, 64])
    f_w3 = din("f_w3", [2, 64, 2048])
    f_bias = din("f_bias", [2, 2, 512])
    q_gain = din("q_gain", [2, 64])
    k_gain = din("k_gain", [2, 64])
    sink = din("sink", [2, 8])
    w_pa = din("w_pa", [2, 512, D])
    w_pb = din("w_pb", [2, 512, D])
    w_out = din("w_out", [2, D, D])
    peer_wq = din("peer_wq", [2, D, 2048])
    peer_k1T = din("peer_k1T", [2, 128, 128])
    peer_k2T = din("peer_k2T", [2, 128, 128])
    peer_u = nc.dram_tensor("peer_u", [2 * 16384, D], F32, kind="ExternalInput")
    peer_v = nc.dram_tensor("peer_v", [2 * 16384, D], F32, kind="ExternalInput")
    c_fwdF = din("fwdF", [16, 128, 16, 256], BF16)
    c_invG = din("invG", [4, 128, 32, 512], BF16)
    c_zfT = din("zfT", [33, L])
    c_negt = din("negt", [128, 16])
    c_absd = din("absdelta", [1, 512])
    c_cs = din("cs", [128, 16, 128])
    c_masks = din("masks", [2, 128, 512], BF16)
    c_ident = din("ident", [128, 128], BF16)
    c_iota = din("iota16", [128, 16])
    c_iota256 = din("iota256", [128, 256], I32)
    y_out = nc.dram_tensor("y", [NSEQ, L, D], F32, kind="ExternalOutput").ap()

    modd = dscr("modd", [NSEQ, 6 * D], F32)
    whY = dscr("whY", [9, 128, 8, 512], BF16)
    wqa = dscr("wqa", [128, 8, 512], BF16)
    wkv = dscr("wkv", [128, 8, 256], BF16)
    wg = dscr("wg", [4, 128, 8, 512], BF16)
    wpa_s = dscr("wpa_s", [128, 4, 1024], BF16)
    wpb_s = dscr("wpb_s", [128, 4, 1024], BF16)
    wout_s = dscr("wout_s", [128, 8, 1024], BF16)
    pwq_s = dscr("pwq_s", [128, 8, 2048], BF16)
    x12 = dscr("x12", [2, L, 512], BF16)
    sigd = dscr("sigd", [2, L, 1024], BF16)
    hfd = dscr("hfd", [2, 16, 128, 2, 512], F32)
    uvds = [nc.dram_tensor(f"uvd{i}", [16384, 2048], BF16, kind="Internal") for i in range(2)]

    with ExitStack() as es:
        S = Sched(nc, es)
        es.enter_context(nc.allow_non_contiguous_dma(reason="small strided param loads"))

        def V(fn, r=(), w=()):
            return S.op("vector", fn, r, w)

        def A(fn, r=(), w=()):
            return S.op("scalar", fn, r, w)

        def G(fn, r=(), w=()):
            return S.op("gpsimd", fn, r, w)

        def PE(fn, r=(), w=()):
            return S.op("tensor", fn, r, w)

        def DMA(out, in_, r=(), w=(), q="sync"):
            return S.dma(q, lambda e: e.dma_start(out=out, in_=in_), r, w)

        uid = [0]

        def sb(stack, name, shape, dt):
            uid[0] += 1
            t = stack.enter_context(nc.sbuf_tensor(f"sb{uid[0]}_{name}", list(shape), dt))
            return t, Res(name)

        psbig = es.enter_context(nc.psum_tensor("psbig", [128, 4096], F32))
        ps = [psbig[:, i * 512:(i + 1) * 512] for i in range(8)]
        ps_r = [Res(f"ps{i}") for i in range(8)]

        def ps2(banks):
            return psbig[:, banks[0] * 512:(banks[1] + 1) * 512]
        ident, r_ident = sb(es, "ident", [128, 128], BF16)
        DMA(ident[:, :], c_ident, w=[r_ident])
        xres = [[Res(f"x{s}_{t}") for t in range(NT)] for s in range(NSEQ)]
        r_modd = Res("modd")
        r_hfd = Res("hfd")
        r_w = {k: Res(k) for k in ("whY", "wqa", "wkv", "wg", "wpa", "wpb", "wout", "pwq")}
        hnyq, r_hnyq = sb(es, "hnyq", [1, 2, 512], F32)

        def mod_phase(l):
            with ExitStack() as st:
                cT, r_cT = sb(st, "cT", [128, 8, NSEQ], F32)
                scT, r_scT = sb(st, "scT", [128, 8, NSEQ], F32)
                modr, r_modr = sb(st, "modr", [NSEQ, 6 * D], F32)
                bmb, r_bmb = sb(st, "bmb", [NSEQ, 6 * D], F32)
                g1b, r_g1b = sb(st, "g1b", [NSEQ, D], F32)
                g2b, r_g2b = sb(st, "g2b", [NSEQ, D], F32)
                wst = [sb(st, f"wst{i}", [128, 8, 512], F32) for i in range(2)]
                DMA(cT[:, :, :], cT_in, w=[r_cT])
                DMA(bmb[:, :], b_mod[l:l + 1, :].to_broadcast([NSEQ, 6 * D]), w=[r_bmb])
                DMA(g1b[:, :], g_norm1[l:l + 1, :].to_broadcast([NSEQ, D]), w=[r_g1b])
                DMA(g2b[:, :], g_norm2[l:l + 1, :].to_broadcast([NSEQ, D]), w=[r_g2b])
                A(lambda e: e.activation(out=scT[:, :, :], in_=cT[:, :, :], func=AF.Silu), [r_cT], [r_scT])
                wv = w_mod[l].rearrange("(k p) n -> p k n", p=128)
                for nb in range(12):
                    wt, r_wt = wst[nb % 2]
                    DMA(wt[:, :, :], wv[:, :, nb * 512:(nb + 1) * 512], w=[r_wt])
                    b = nb % 2
                    for k in range(8):
                        PE(lambda e: e.matmul(ps[b][0:NSEQ, :], lhsT=scT[:, k, :], rhs=wt[:, k, :],
                                              start=(k == 0), stop=(k == 7)),
                           [r_scT, r_wt], [ps_r[b]])
                    V(lambda e: e.tensor_tensor(out=modr[:, nb * 512:(nb + 1) * 512], in0=ps[b][0:NSEQ, :],
                                                in1=bmb[:, nb * 512:(nb + 1) * 512], op=ALU.add),
                      [ps_r[b], r_bmb], [r_modr])
                for (c0, gb_, rg) in ((1024, g1b, r_g1b), (4096, g2b, r_g2b)):
                    V(lambda e: e.scalar_tensor_tensor(out=modr[:, c0:c0 + 1024], in0=modr[:, c0:c0 + 1024],
                                                       scalar=1.0, in1=gb_[:, :], op0=ALU.add, op1=ALU.mult),
                      [r_modr, rg], [r_modr])
                DMA(modd, modr[:, :], r=[r_modr], w=[r_modd])
            S.barrier()

        def prep_weights(l):
            with ExitStack() as st:
                stg = [sb(st, f"pstg{i}", [128, 8, 512], F32) for i in range(2)]
                obf = [sb(st, f"pobf{i}", [128, 8, 512], BF16) for i in range(2)]
                cwb, r_cwb = sb(st, "cwb", [128, 3, 1536], F32)
                for j in range(3):
                    DMA(cwb[:, j, :], conv_w[l, j:j + 1, :].to_broadcast([128, 1536]), w=[r_cwb])
                cnt = [0]

                def one(src, dst, K, N, rdst, mul=None):
                    i = cnt[0] % 2
                    cnt[0] += 1
                    s_t, r_s = stg[i]
                    o_t, r_o = obf[i]
                    DMA(s_t[:, 0:K, 0:N], src, w=[r_s])
                    if mul is None:
                        if cnt[0] % 2 == 0:
                            V(lambda e: e.tensor_copy(out=o_t[:, 0:K, 0:N], in_=s_t[:, 0:K, 0:N]), [r_s], [r_o])
                        else:
                            A(lambda e: e.copy(out=o_t[:, 0:K, 0:N], in_=s_t[:, 0:K, 0:N]), [r_s], [r_o])
                    else:
                        for k in range(K):
                            V(lambda e: e.tensor_tensor(out=o_t[:, k, 0:N], in0=s_t[:, k, 0:N], in1=mul,
                                                        op=ALU.mult), [r_s, r_cwb], [r_o])
                    DMA(dst, o_t[:, 0:K, 0:N], r=[r_o], w=[rdst])

                wiv = w_in[l].rearrange("(k p) n -> p k n", p=128)
                for ob in range(3):
                    for j in range(3):
                        one(wiv[:, :, ob * 512:(ob + 1) * 512], whY[ob * 3 + j], 8, 512, r_w["whY"],
                            mul=cwb[:, j, ob * 512:(ob + 1) * 512])
                one(wiv[:, :, 1536:2048], wqa, 8, 512, r_w["wqa"])
                one(wiv[:, :, 2048:2304], wkv, 8, 256, r_w["wkv"])
                for gi in range(4):
                    one(wiv[:, :, 2304 + gi * 512:2304 + (gi + 1) * 512], wg[gi], 8, 512, r_w["wg"])
                pav = w_pa[l].rearrange("(k p) n -> p k n", p=128)
                pbv = w_pb[l].rearrange("(k p) n -> p k n", p=128)
                wov = w_out[l].rearrange("(k p) n -> p k n", p=128)
                pqv = peer_wq[l].rearrange("(k p) n -> p k n", p=128)
                for nb in range(2):
                    one(pav[:, :, nb * 512:(nb + 1) * 512], wpa_s[:, :, nb * 512:(nb + 1) * 512], 4, 512, r_w["wpa"])
                    one(pbv[:, :, nb * 512:(nb + 1) * 512], wpb_s[:, :, nb * 512:(nb + 1) * 512], 4, 512, r_w["wpb"])
                    one(wov[:, :, nb * 512:(nb + 1) * 512], wout_s[:, :, nb * 512:(nb + 1) * 512], 8, 512, r_w["wout"])
                for nb in range(4):
                    one(pqv[:, :, nb * 512:(nb + 1) * 512], pwq_s[:, :, nb * 512:(nb + 1) * 512], 8, 512, r_w["pwq"])
            S.barrier()

        def range_reduce_sin(st, dst, src_ps, bcol, fcol, r_cols, r_dst, psr, npart, tmp, r_tmp, tmp2, r_tmp2):
            V(lambda e: e.tensor_scalar(out=tmp[0:npart, :], in0=src_ps, scalar1=bcol, scalar2=fcol,
                                        op0=ALU.add, op1=ALU.mult), [psr, r_cols], [r_tmp])
            V(lambda e: e.tensor_scalar(out=tmp2[0:npart, :], in0=tmp[0:npart, :], scalar1=1.0 / TWO_PI,
                                        scalar2=MAGIC, op0=ALU.mult, op1=ALU.add), [r_tmp], [r_tmp2])
            V(lambda e: e.tensor_scalar(out=tmp2[0:npart, :], in0=tmp2[0:npart, :], scalar1=MAGIC,
                                        scalar2=-TWO_PI, op0=ALU.subtract, op1=ALU.mult), [r_tmp2], [r_tmp2])
            V(lambda e: e.tensor_tensor(out=tmp[0:npart, :], in0=tmp[0:npart, :], in1=tmp2[0:npart, :],
                                        op=ALU.add), [r_tmp, r_tmp2], [r_tmp])
            V(lambda e: e.tensor_scalar(out=tmp[0:npart, :], in0=tmp[0:npart, :], scalar1=3.1415925,
                                        scalar2=-3.1415925, op0=ALU.min, op1=ALU.max), [r_tmp], [r_tmp])
            A(lambda e: e.activation(out=dst, in_=tmp[0:npart, :], func=AF.Sin), [r_tmp], [r_dst])

        def filter_phase(l):
            with ExitStack() as st:
                zf, r_zf = sb(st, "zf", [33, L], F32)
                w1, r_w1 = sb(st, "fw1", [33, 64], F32)
                w2, r_w2 = sb(st, "fw2", [64, 64], F32)
                w3, r_w3 = sb(st, "fw3", [64, 2048], F32)
                cols, r_cols = sb(st, "fcols", [64, 4], F32)
                a1, r_a1 = sb(st, "fa1", [64, L], F32)
                a2, r_a2 = sb(st, "fa2", [64, L], F32)
                tmp, r_tmp = sb(st, "ftmp", [128, 512], F32)
                tmp2, r_tmp2 = sb(st, "ftmp2", [128, 512], F32)
                absd, r_absd = sb(st, "absd", [128, 512], F32)
                negt, r_negt = sb(st, "negt", [128, 16], F32)
                dec, r_dec = sb(st, "dec", [128, 512], F32)
                fa, r_fa = sb(st, "fa", [128, 16, 1024], BF16)
                fb, r_fb = sb(st, "fb", [128, 16, 1024], BF16)
                h0, r_h0 = sb(st, "h0", [128, 512], F32)
                h1, r_h1 = sb(st, "h1", [128, 512], F32)
                fbias, r_fbias = sb(st, "fbias", [128, 2, 512], F32)
                Fb = [sb(st, f"Fbf{i}", [128, 16, 256], BF16) for i in range(2)]
                ho = [sb(st, f"hout{i}", [128, 2, 512], F32) for i in range(2)]
                DMA(zf[:, :], c_zfT, w=[r_zf])
                DMA(w1[:, :], f_w1[l], w=[r_w1])
                DMA(w2[:, :], f_w2[l], w=[r_w2])
                DMA(w3[:, :], f_w3[l], w=[r_w3])
                DMA(cols[:, 0:1], f_b1[l:l + 1, :].rearrange("o n -> n o"), w=[r_cols])
                DMA(cols[:, 1:2], f_freq[l:l + 1, :].rearrange("o n -> n o"), w=[r_cols])
                DMA(cols[:, 2:3], f_b2[l:l + 1, :].rearrange("o n -> n o"), w=[r_cols])
                DMA(absd[:, :], c_absd.to_broadcast([128, 512]), w=[r_absd])
                DMA(negt[:, :], c_negt, w=[r_negt])
                for o in range(2):
                    DMA(fbias[:, o, :], f_bias[l, o:o + 1, :].to_broadcast([128, 512]), w=[r_fbias])
                for ch in range(4):
                    b = ch % 2
                    PE(lambda e: e.matmul(ps[b][0:64, :], lhsT=w1[:, :], rhs=zf[:, ch * 512:(ch + 1) * 512],
                                          start=True, stop=True), [r_w1, r_zf], [ps_r[b]])
                    range_reduce_sin(st, a1[:, ch * 512:(ch + 1) * 512], ps[b][0:64, :], cols[:, 0:1], cols[:, 1:2],
                                     r_cols, r_a1, ps_r[b], 64, tmp, r_tmp, tmp2, r_tmp2)
                for ch in range(4):
                    b = ch % 2
                    PE(lambda e: e.matmul(ps[b][0:64, :], lhsT=w2[:, :], rhs=a1[:, ch * 512:(ch + 1) * 512],
                                          start=True, stop=True), [r_w2, r_a1], [ps_r[b]])
                    range_reduce_sin(st, a2[:, ch * 512:(ch + 1) * 512], ps[b][0:64, :], cols[:, 2:3], cols[:, 1:2],
                                     r_cols, r_a2, ps_r[b], 64, tmp, r_tmp, tmp2, r_tmp2)
                for tc in range(16):
                    A(lambda e: e.activation(out=dec[:, :], in_=absd[:, :], func=AF.Exp, scale=negt[:, tc:tc + 1]),
                      [r_absd, r_negt], [r_dec])
                    for o in range(2):
                        for dr in range(2):
                            b = 2 + dr
                            cb = (o * 2 + dr) * 512
                            PE(lambda e: e.matmul(ps[b][:, :], lhsT=a2[:, tc * 128:(tc + 1) * 128],
                                                  rhs=w3[:, cb:cb + 512], start=True, stop=True),
                               [r_a2, r_w3], [ps_r[b]])
                            hh, r_hh = (h0, r_h0) if dr == 0 else (h1, r_h1)
                            V(lambda e: e.tensor_tensor(out=hh[:, :], in0=ps[b][:, :], in1=dec[:, :], op=ALU.mult),
                              [ps_r[b], r_dec], [r_hh])
                        if tc == 0:
                            V(lambda e: e.memset(h1[0:1, :], 0.0), [], [r_h1])
                        V(lambda e: e.tensor_tensor(out=fa[:, tc, o * 512:(o + 1) * 512], in0=h0[:, :], in1=h1[:, :],
                                                    op=ALU.add), [r_h0, r_h1], [r_fa])
                        V(lambda e: e.tensor_tensor(out=fb[:, tc, o * 512:(o + 1) * 512], in0=h0[:, :], in1=h1[:, :],
                                                    op=ALU.subtract), [r_h0, r_h1], [r_fb])
                for fc in range(16):
                    Ft, r_Ft = Fb[fc % 2]
                    DMA(Ft[:, :, :], c_fwdF[fc], w=[r_Ft])
                    hot, r_hot = ho[fc % 2]
                    for o in range(2):
                        bre, bim = 4 + 2 * (o % 2), 5 + 2 * (o % 2)
                        for tc in range(16):
                            PE(lambda e: e.matmul(ps[bre][:, :], lhsT=Ft[:, tc, 0:128],
                                                  rhs=fa[:, tc, o * 512:(o + 1) * 512], start=(tc == 0), stop=(tc == 15)),
                               [r_Ft, r_fa], [ps_r[bre]])
                        for tc in range(16):
                            PE(lambda e: e.matmul(ps[bim][:, :], lhsT=Ft[:, tc, 128:256],
                                                  rhs=fb[:, tc, o * 512:(o + 1) * 512], start=(tc == 0), stop=(tc == 15)),
                               [r_Ft, r_fb], [ps_r[bim]])
                        V(lambda e: e.tensor_tensor(out=hot[:, 0, :], in0=ps[bre][:, :], in1=fbias[:, o, :], op=ALU.add),
                          [ps_r[bre], r_fbias], [r_hot])
                        A(lambda e: e.copy(out=hot[:, 1, :], in_=ps[bim][:, :]), [ps_r[bim]], [r_hot])
                        DMA(hfd[o, fc], hot[:, :, :], r=[r_hot], w=[r_hfd])
                        if fc == 0:
                            for tc in range(16):
                                PE(lambda e: e.matmul(ps[0][0:1, :], lhsT=Ft[:, tc, 128:129],
                                                      rhs=fa[:, tc, o * 512:(o + 1) * 512], start=(tc == 0), stop=(tc == 15)),
                                   [r_Ft, r_fa], [ps_r[0]])
                            V(lambda e: e.tensor_tensor(out=hnyq[0:1, o, :], in0=ps[0][0:1, :], in1=fbias[0:1, o, :],
                                                        op=ALU.add), [ps_r[0], r_fbias], [r_hnyq])
            S.barrier()

        def qk_norm_rope(nh, src_ps, psr, gainb, r_gain, cst, r_cs, t, qf, r_qf, qsq, r_qsq, ss, r_ss, qr, r_qr):
            W = nh * 64
            A(lambda e: e.copy(out=qf[:, 0:W], in_=src_ps), [psr], [r_qf])
            V(lambda e: e.tensor_tensor(out=qsq[:, 0:W], in0=qf[:, 0:W], in1=qf[:, 0:W], op=ALU.mult), [r_qf], [r_qsq])
            V(lambda e: e.tensor_reduce(out=ss[:, 0:nh], in_=qsq[:, 0:W].rearrange("p (h d) -> p h d", h=nh),
                                        axis=AX.X, op=ALU.add), [r_qsq], [r_ss])
            A(lambda e: e.activation(out=ss[:, 8:8 + nh], in_=ss[:, 0:nh], func=AF.Sqrt, scale=1.0 / 64.0, bias=eps_t[:, 0:1]),
              [r_ss], [r_ss])
            V(lambda e: e.reciprocal(out=ss[:, 16:16 + nh], in_=ss[:, 8:8 + nh]), [r_ss], [r_ss])
            q3 = qf[:, 0:W].rearrange("p (h d) -> p h d", h=nh)
            V(lambda e: e.tensor_tensor(out=q3, in0=q3, in1=ss[:, 16:16 + nh].unsqueeze(2).to_broadcast([128, nh, 64]),
                                        op=ALU.mult), [r_qf, r_ss], [r_qf])
            V(lambda e: e.tensor_tensor(out=qf[:, 0:W], in0=qf[:, 0:W], in1=gainb[:, 0:W], op=ALU.mult),
              [r_qf, r_gain], [r_qf])
            cosb = cst[:, t, 0:64].unsqueeze(1).to_broadcast([128, nh, 64])
            sinb = cst[:, t, 64:128].unsqueeze(1).to_broadcast([128, nh, 64])
            s3 = qsq[:, 0:W].rearrange("p (h d) -> p h d", h=nh)
            V(lambda e: e.tensor_tensor(out=s3, in0=q3, in1=sinb, op=ALU.mult), [r_qf, r_cs], [r_qsq])
            V(lambda e: e.tensor_tensor(out=q3, in0=q3, in1=cosb, op=ALU.mult), [r_qf, r_cs], [r_qf])
            r3 = qr[:, 0:W].rearrange("p (h d) -> p h d", h=nh)
            V(lambda e: e.tensor_tensor(out=r3[:, :, 0:32], in0=q3[:, :, 0:32], in1=s3[:, :, 32:64], op=ALU.subtract),
              [r_qf, r_qsq], [r_qr])
            V(lambda e: e.tensor_tensor(out=r3[:, :, 32:64], in0=q3[:, :, 32:64], in1=s3[:, :, 0:32], op=ALU.add),
              [r_qf, r_qsq], [r_qr])

        eps_t, r_eps = sb(es, "eps_t", [128, 1], F32)
        V(lambda e: e.memset(eps_t[:, :], EPS), [], [r_eps])

        def psb(i):
            return ps[i][:, :].bitcast(BF16)

        def mixer_seq(l, s, hv, r_hv, ybt, r_ybt):
            src_x = x_in if l == 0 else y_out
            with ExitStack() as st:
                hT, r_hT = sb(st, "hT", [128, 8, L + 2], BF16)
                modT, r_modT = sb(st, "modT", [128, 2, 8], F32)
                DMA(modT[:, 0, :], modd[s, 0:1024].rearrange("(k p) -> p k", p=128), r=[r_modd], w=[r_modT])
                DMA(modT[:, 1, :], modd[s, 1024:2048].rearrange("(k p) -> p k", p=128), r=[r_modd], w=[r_modT])
                V(lambda e: e.memset(hT[:, :, 0:1], 0.0), [], [r_hT])
                V(lambda e: e.memset(hT[:, :, L + 1:L + 2], 0.0), [], [r_hT])
                xt = [sb(st, f"xt{i}", [128, D], F32) for i in range(2)]
                junk, r_junk = sb(st, "junk", [128, D], BF16)
                xn = [sb(st, f"xn{i}", [128, D], BF16) for i in range(2)]
                st8, r_st8 = sb(st, "st8", [128, 8], F32)
                evt = [sb(st, f"evt{i}", [128, 8, 128], F32) for i in range(2)]
                r_hTt = [Res(f"hT{t}") for t in range(NT)]
                for t in range(NT):
                    x_t, r_x = xt[t % 2]
                    xn_t, r_xn = xn[t % 2]
                    b = t % 2
                    DMA(x_t[:, :], src_x[s, t * 128:(t + 1) * 128, :], r=[xres[s][t]], w=[r_x])
                    A(lambda e: e.activation(out=junk[:, :], in_=x_t[:, :], func=AF.Square, accum_out=st8[:, 0:1]),
                      [r_x], [r_junk, r_st8])
                    A(lambda e: e.activation(out=st8[:, 1:2], in_=st8[:, 0:1], func=AF.Sqrt, scale=1.0 / D,
                                             bias=eps_t[:, 0:1]), [r_st8], [r_st8])
                    V(lambda e: e.reciprocal(out=st8[:, 2:3], in_=st8[:, 1:2]), [r_st8], [r_st8])
                    A(lambda e: e.activation(out=xn_t[:, :], in_=x_t[:, :], func=AF.Copy, scale=st8[:, 2:3]),
                      [r_x, r_st8], [r_xn])
                    for k in range(8):
                        PE(lambda e: e.transpose(out=psb(b)[:, k * 128:(k + 1) * 128], in_=xn_t[:, k * 128:(k + 1) * 128],
                                                 identity=ident[:, :]), [r_xn, r_ident], [ps_r[b]])
                    ev_t, r_ev = evt[t % 2]
                    V(lambda e: e.tensor_tensor(out=ev_t[:, :, :], in0=psb(b).rearrange("p (k q) -> p k q", k=8),
                                                in1=modT[:, 1, :].unsqueeze(2).to_broadcast([128, 8, 128]), op=ALU.mult),
                      [ps_r[b], r_modT], [r_ev])
                    G(lambda e: e.tensor_tensor(out=hT[:, :, 1 + t * 128:1 + (t + 1) * 128], in0=ev_t[:, :, :],
                                                in1=modT[:, 0, :].unsqueeze(2).to_broadcast([128, 8, 128]), op=ALU.add),
                      [r_ev, r_modT], [r_hTt[t], r_hT])
                wr = [sb(st, f"wr{i}", [128, 8, 512], BF16) for i in range(4)]
                cbb, r_cbb = sb(st, "cbb", [128, 1536], F32)
                DMA(cbb[:, :], conv_b[l:l + 1, :].to_broadcast([128, 1536]), w=[r_cbb])
                stg = [sb(st, f"stg{i}", [128, 512], BF16) for i in range(2)]
                wi = [0]

                def getw(src, K=8, N=512):
                    i = wi[0] % 4
                    wi[0] += 1
                    wt, r_wt = wr[i]
                    DMA(wt[:, 0:K, 0:N], src, r=[r_w["whY"], r_w["wqa"], r_w["wkv"], r_w["wg"]], w=[r_wt])
                    return wt, r_wt
                pbank = [2]

                def nextbank():
                    b = pbank[0]
                    pbank[0] = 2 + (pbank[0] - 2 + 1) % 4
                    return b
                si = [0]
                for ob in range(3):
                    wts = [getw(whY[ob * 3 + j]) for j in range(3)]
                    for t in range(NT):
                        b = nextbank()
                        for j in range(3):
                            wt, r_wt = wts[j]
                            for k in range(8):
                                PE(lambda e: e.matmul(ps[b][:, :], lhsT=hT[:, k, t * 128 + j:t * 128 + j + 128],
                                                      rhs=wt[:, k, :], start=(j == 0 and k == 0), stop=(j == 2 and k == 7)),
                                   [r_hT, r_wt], [ps_r[b]])
                        if ob == 0:
                            V(lambda e: e.tensor_tensor(out=hv[:, t, :], in0=ps[b][:, :], in1=cbb[:, 0:512], op=ALU.add),
                              [ps_r[b], r_cbb], [r_hv[t]])
                        else:
                            sg, r_sg = stg[si[0] % 2]
                            si[0] += 1
                            V(lambda e: e.tensor_tensor(out=sg[:, :], in0=ps[b][:, :], in1=cbb[:, ob * 512:(ob + 1) * 512],
                                                        op=ALU.add), [ps_r[b], r_cbb], [r_sg])
                            DMA(x12[ob - 1, t * 128:(t + 1) * 128, :], sg[:, :], r=[r_sg], w=[r_x12])
                for gi in range(4):
                    wt, r_wt = getw(wg[gi])
                    for t in range(NT):
                        b = nextbank()
                        for k in range(8):
                            PE(lambda e: e.matmul(ps[b][:, :], lhsT=hT[:, k, t * 128 + 1:t * 128 + 129], rhs=wt[:, k, :],
                                                  start=(k == 0), stop=(k == 7)), [r_hT, r_wt], [ps_r[b]])
                        sg, r_sg = stg[si[0] % 2]
                        si[0] += 1
                        A(lambda e: e.activation(out=sg[:, :], in_=ps[b][:, :], func=AF.Sigmoid), [ps_r[b]], [r_sg])
                        DMA(sigd[gi // 2, t * 128:(t + 1) * 128, (gi % 2) * 512:(gi % 2 + 1) * 512], sg[:, :],
                            r=[r_sg], w=[r_sigd])
                qT, r_qT = sb(st, "qT", [64, 8, L], BF16)
                kT, r_kT = sb(st, "kT", [64, 2, L], BF16)
                vt, r_vt = sb(st, "vt", [128, NT, 2, 65], BF16)
                cst, r_cs = sb(st, "cst", [128, NT, 128], F32)
                qgb, r_qgb = sb(st, "qgb", [128, 512], F32)
                kgb, r_kgb = sb(st, "kgb", [128, 128], F32)
                esk, r_esk = sb(st, "esk", [128, 8], F32)
                DMA(cst[:, :, :], c_cs, w=[r_cs])
                for h in range(8):
                    DMA(qgb[:, h * 64:(h + 1) * 64], q_gain[l:l + 1, :].to_broadcast([128, 64]), w=[r_qgb])
                for h in range(2):
                    DMA(kgb[:, h * 64:(h + 1) * 64], k_gain[l:l + 1, :].to_broadcast([128, 64]), w=[r_kgb])
                V(lambda e: e.tensor_scalar(out=qgb[:, :], in0=qgb[:, :], scalar1=0.125, scalar2=None, op0=ALU.mult),
                  [r_qgb], [r_qgb])
                DMA(esk[:, :], sink[l:l + 1, :].to_broadcast([128, 8]), w=[r_esk])
                A(lambda e: e.activation(out=esk[:, :], in_=esk[:, :], func=AF.Exp), [r_esk], [r_esk])
                V(lambda e: e.memset(vt[:, :, :, 64:65], 1.0), [], [r_vt])
                qbufs = [(sb(st, f"qf{i}", [128, 512], F32), sb(st, f"qsq{i}", [128, 512], F32),
                          sb(st, f"ss{i}", [128, 24], F32), sb(st, f"qr{i}", [128, 512], BF16)) for i in range(2)]
                wt_q, r_wt_q = getw(wqa)
                wt_k, r_wt_k = getw(wkv, 8, 256)

                def qk_front(kind, t):
                    b = nextbank()
                    if kind == 0:
                        for k in range(8):
                            PE(lambda e: e.matmul(ps[b][:, :], lhsT=hT[:, k, t * 128 + 1:t * 128 + 129], rhs=wt_q[:, k, :],
                                                  start=(k == 0), stop=(k == 7)), [r_hT, r_wt_q], [ps_r[b]])
                    else:
                        for k in range(8):
                            PE(lambda e: e.matmul(ps[b][:, 0:256], lhsT=hT[:, k, t * 128 + 1:t * 128 + 129], rhs=wt_k[:, k, 0:256],
                                                  start=(k == 0), stop=(k == 7)), [r_hT, r_wt_k], [ps_r[b]])
                    return b

                def qk_back(kind, t, b, ui):
                    (qf_, r_qf_), (qsq_, r_qsq_), (ss_, r_ss_), (qr_, r_qr_) = qbufs[ui % 2]
                    tb = ui % 2
                    if kind == 0:
                        qk_norm_rope(8, ps[b][:, :], ps_r[b], qgb, r_qgb, cst, r_cs, t, qf_, r_qf_, qsq_, r_qsq_, ss_, r_ss_, qr_, r_qr_)
                        for h in range(8):
                            PE(lambda e: e.transpose(out=psb(tb)[0:64, h * 128:(h + 1) * 128], in_=qr_[:, h * 64:(h + 1) * 64],
                                                     identity=ident[:, :]), [r_qr_, r_ident], [ps_r[tb]])
                        A(lambda e: e.copy(out=qT[:, :, t * 128:(t + 1) * 128],
                                           in_=psb(tb)[0:64, :].rearrange("p (h q) -> p h q", h=8)), [ps_r[tb]], [r_qT])
                    else:
                        A(lambda e: e.copy(out=vt[:, t, :, 0:64], in_=ps[b][:, 128:256].rearrange("p (h d) -> p h d", h=2)),
                          [ps_r[b]], [r_vt])
                        qk_norm_rope(2, ps[b][:, 0:128], ps_r[b], kgb, r_kgb, cst, r_cs, t, qf_, r_qf_, qsq_, r_qsq_, ss_, r_ss_, qr_, r_qr_)
                        for h in range(2):
                            PE(lambda e: e.transpose(out=psb(tb)[0:64, h * 128:(h + 1) * 128], in_=qr_[:, h * 64:(h + 1) * 64],
                                                     identity=ident[:, :]), [r_qr_, r_ident], [ps_r[tb]])
                        A(lambda e: e.copy(out=kT[:, :, t * 128:(t + 1) * 128],
                                           in_=psb(tb)[0:64, 0:256].rearrange("p (h q) -> p h q", h=2)), [ps_r[tb]], [r_kT])

                units = [(0, t) for t in range(NT)] + [(1, t) for t in range(NT)]
                pend = qk_front(*units[0])
                for ui, (kind, t) in enumerate(units):
                    nxt = qk_front(*units[ui + 1]) if ui + 1 < len(units) else None
                    qk_back(kind, t, pend, ui)
                    pend = nxt
                mk, r_mk = sb(st, "mk", [128, 2, 512], BF16)
                DMA(mk[:, 0, :], c_masks[0], w=[r_mk])
                DMA(mk[:, 1, :], c_masks[1], w=[r_mk])
                pT = [sb(st, f"pT{i}", [128, 3, 512], BF16) for i in range(2)]
                den, r_den = sb(st, "den", [128, 8], F32)
                dens = [(den, r_den), sb(st, "den2", [128, 8], F32)]

                def att_front(ai, n, kvh):
                    p_t, r_p = pT[ai % 2]
                    kbs = [kb for kb in (n - 1, n, n + 1) if 0 <= kb < NT]
                    for i, kb in enumerate(kbs):
                        b = nextbank()
                        PE(lambda e: e.matmul(ps[b][:, :], lhsT=kT[:, kvh, kb * 128:(kb + 1) * 128],
                                              rhs=qT[:, 4 * kvh:4 * kvh + 4, n * 128:(n + 1) * 128],
                                              start=True, stop=(kb == n)), [r_kT, r_qT], [ps_r[b]])
                        if kb != n:
                            mi = 0 if kb < n else 1
                            PE(lambda e: e.matmul(ps[b][:, :], lhsT=ident[:, :], rhs=mk[:, mi, :], start=False, stop=True),
                               [r_ident, r_mk], [ps_r[b]])
                        A(lambda e: e.activation(out=p_t[:, i, :], in_=ps[b][:, :], func=AF.Exp), [ps_r[b]], [r_p])

                def att_back(ai, n, kvh):
                    p_t, r_p = pT[ai % 2]
                    dn, r_dn = dens[ai % 2]
                    ob_ = 6 + (ai % 2)
                    kbs = [kb for kb in (n - 1, n, n + 1) if 0 <= kb < NT]
                    for h in range(4):
                        for i, kb in enumerate(kbs):
                            PE(lambda e: e.matmul(ps[ob_][:, h * 65:(h + 1) * 65], lhsT=p_t[:, i, h * 128:(h + 1) * 128],
                                                  rhs=vt[:, kb, kvh, :], start=(i == 0), stop=(i == len(kbs) - 1)),
                               [r_p, r_vt], [ps_r[ob_]])
                    o3 = ps[ob_][:, 0:260].rearrange("p (h d) -> p h d", h=4)
                    V(lambda e: e.tensor_tensor(out=dn[:, 0:4], in0=o3[:, :, 64], in1=esk[:, 4 * kvh:4 * kvh + 4],
                                                op=ALU.add), [ps_r[ob_], r_esk], [r_dn])
                    V(lambda e: e.reciprocal(out=dn[:, 4:8], in_=dn[:, 0:4]), [r_dn], [r_dn])
                    V(lambda e: e.tensor_tensor(
                        out=ybt[:, n, kvh * 256:(kvh + 1) * 256].rearrange("p (h d) -> p h d", h=4),
                        in0=o3[:, :, 0:64], in1=dn[:, 4:8].unsqueeze(2).to_broadcast([128, 4, 64]), op=ALU.mult),
                      [ps_r[ob_], r_dn], [r_ybt[n]])

                aunits = [(n, kvh) for n in range(NT) for kvh in range(2)]
                att_front(0, *aunits[0])
                for ai, (n, kvh) in enumerate(aunits):
                    if ai + 1 < len(aunits):
                        att_front(ai + 1, *aunits[ai + 1])
                    att_back(ai, n, kvh)
            S.barrier()
            with ExitStack() as st:
                Y, r_Y = sb(st, "Y", [128, 32, 512], BF16)
                Fb = [sb(st, f"Fb{i}", [128, 16, 256], BF16) for i in range(2)]
                Hb = [sb(st, f"Hb{i}", [128, 2, 512], F32) for i in range(2)]
                Gb = [sb(st, f"Gb{i}", [128, 16, 512], BF16) for i in range(4)]
                tm = [sb(st, f"tm{i}", [128, 512], F32) for i in range(4)]
                xg = [sb(st, f"xg{i}", [128, 512], BF16) for i in range(2)]
                r_Yc = [Res(f"Y{c}") for c in range(32)]
                gi_ = [0]
                for o in range(2):
                    for fc in range(16):
                        Ft, r_Ft = Fb[fc % 2]
                        Ht, r_Ht = Hb[fc % 2]
                        DMA(Ft[:, :, :], c_fwdF[fc], w=[r_Ft])
                        DMA(Ht[:, :, :], hfd[o, fc], r=[r_hfd], w=[r_Ht])
                        bre, bim = 2 * (fc % 2), 2 * (fc % 2) + 1
                        for tc in range(16):
                            PE(lambda e: e.matmul(ps[bre][:, :], lhsT=Ft[:, tc, 0:128], rhs=hv[:, tc, :],
                                                  start=(tc == 0), stop=(tc == 15)), [r_Ft, r_hv[tc]], [ps_r[bre]])
                        for tc in range(16):
                            PE(lambda e: e.matmul(ps[bim][:, :], lhsT=Ft[:, tc, 128:256], rhs=hv[:, tc, :],
                                                  start=(tc == 0), stop=(tc == 15)), [r_Ft, r_hv[tc]], [ps_r[bim]])
                        (t1, r1), (t2, r2), (t3, r3), (t4, r4) = tm
                        V(lambda e: e.tensor_tensor(out=t1[:, :], in0=ps[bre][:, :], in1=Ht[:, 0, :], op=ALU.mult),
                          [ps_r[bre], r_Ht], [r1])
                        V(lambda e: e.tensor_tensor(out=t2[:, :], in0=ps[bim][:, :], in1=Ht[:, 1, :], op=ALU.mult),
                          [ps_r[bim], r_Ht], [r2])
                        V(lambda e: e.tensor_tensor(out=t3[:, :], in0=ps[bre][:, :], in1=Ht[:, 1, :], op=ALU.mult),
                          [ps_r[bre], r_Ht], [r3])
                        V(lambda e: e.tensor_tensor(out=t4[:, :], in0=ps[bim][:, :], in1=Ht[:, 0, :], op=ALU.mult),
                          [ps_r[bim], r_Ht], [r4])
                        G(lambda e: e.tensor_tensor(out=Y[:, fc, :], in0=t1[:, :], in1=t2[:, :], op=ALU.subtract),
                          [r1, r2], [r_Yc[fc]])
                        G(lambda e: e.tensor_tensor(out=Y[:, 16 + fc, :], in0=t3[:, :], in1=t4[:, :], op=ALU.add),
                          [r3, r4], [r_Yc[16 + fc]])
                        if fc == 0:
                            V(lambda e: e.tensor_copy(out=Y[0:1, 0, :], in_=t1[0:1, :]), [r1], [r_Yc[0]])
                            V(lambda e: e.tensor_tensor(out=Y[0:1, 16, :], in0=ps[bim][0:1, :], in1=hnyq[0:1, o, :],
                                                        op=ALU.mult), [ps_r[bim], r_hnyq], [r_Yc[16]])
                    for tg in range(4):
                        gts = []
                        for half in range(2):
                            g_t, r_g = Gb[gi_[0] % 4]
                            gi_[0] += 1
                            DMA(g_t[:, :, :], c_invG[tg, :, half * 16:(half + 1) * 16, :], w=[r_g])
                            gts.append((g_t, r_g))
                        for ti in range(4):
                            t = tg * 4 + ti
                            b = 4 + (t % 4)
                            x_t, r_xg = xg[t % 2]
                            DMA(x_t[:, :], x12[o, t * 128:(t + 1) * 128, :], r=[r_x12], w=[r_xg])
                            for c in range(32):
                                g_t, r_g = gts[c // 16]
                                PE(lambda e: e.matmul(ps[b][:, :], lhsT=g_t[:, c % 16, ti * 128:(ti + 1) * 128], rhs=Y[:, c, :],
                                                      start=(c == 0), stop=(c == 31)), [r_g, r_Yc[c]], [ps_r[b]])
                            V(lambda e: e.tensor_tensor(out=hv[:, t, :], in0=ps[b][:, :], in1=x_t[:, :], op=ALU.mult),
                              [ps_r[b], r_xg], [r_hv[t]])
            S.barrier()
            with ExitStack() as st:
                wpa_t, r_wpa = sb(st, "wpa_t", [128, 4, 1024], BF16)
                wpb_t, r_wpb = sb(st, "wpb_t", [128, 4, 1024], BF16)
                wo_t, r_wo = sb(st, "wo_t", [128, 8, 1024], BF16)
                gtb, r_gtb = sb(st, "gtb", [128, D], F32)
                DMA(wpa_t[:, :, :], wpa_s, r=[r_w["wpa"]], w=[r_wpa])
                DMA(wpb_t[:, :, :], wpb_s, r=[r_w["wpb"]], w=[r_wpb])
                DMA(wo_t[:, :, :], wout_s, r=[r_w["wout"]], w=[r_wo])
                DMA(gtb[:, :], modd[s:s + 1, 2048:3072].to_broadcast([128, D]), r=[r_modd], w=[r_gtb])
                abT = [sb(st, f"abT{i}", [128, 8, 128], BF16) for i in range(2)]
                mT = [sb(st, f"mT{i}", [128, 8, 128], BF16) for i in range(2)]
                sg = [sb(st, f"sg{i}", [128, 2, D], BF16) for i in range(2)]
                m1, r_m1 = sb(st, "m1", [128, D], F32)
                m2, r_m2 = sb(st, "m2", [128, D], F32)
                mg, r_mg = sb(st, "mg", [128, D], BF16)
                xt = [sb(st, f"xt5{i}", [128, D], F32) for i in range(2)]
                mgs = [(mg, r_mg), sb(st, "mg2", [128, D], BF16)]
                o1, r_o1 = sb(st, "o1", [128, D], F32)

                def stX(t):
                    ab, r_ab = abT[t % 2]
                    s_t, r_s = sg[t % 2]
                    x_t, r_x = xt[t % 2]
                    mg_t, r_mgt = mgs[t % 2]
                    DMA(s_t[:, 0, :], sigd[0, t * 128:(t + 1) * 128, :], r=[r_sigd], w=[r_s])
                    DMA(s_t[:, 1, :], sigd[1, t * 128:(t + 1) * 128, :], r=[r_sigd], w=[r_s])
                    DMA(x_t[:, :], src_x[s, t * 128:(t + 1) * 128, :], r=[xres[s][t]], w=[r_x])
                    for k in range(4):
                        PE(lambda e: e.transpose(out=psb(0)[:, k * 128:(k + 1) * 128], in_=hv[:, t, k * 128:(k + 1) * 128],
                                                 identity=ident[:, :]), [r_hv[t], r_ident], [ps_r[0]])
                    for k in range(4):
                        PE(lambda e: e.transpose(out=psb(0)[:, (4 + k) * 128:(5 + k) * 128],
                                                 in_=ybt[:, t, k * 128:(k + 1) * 128], identity=ident[:, :]),
                           [r_ybt[t], r_ident], [ps_r[0]])
                    A(lambda e: e.copy(out=ab[:, :, :], in_=psb(0).rearrange("p (k q) -> p k q", k=8)), [ps_r[0]], [r_ab])
                    for nb in range(2):
                        for k in range(4):
                            PE(lambda e: e.matmul(ps[1 + nb][:, :], lhsT=ab[:, k, :], rhs=wpa_t[:, k, nb * 512:(nb + 1) * 512],
                                                  start=(k == 0), stop=(k == 3)), [r_ab, r_wpa], [ps_r[1 + nb]])
                        for k in range(4):
                            PE(lambda e: e.matmul(ps[3 + nb][:, :], lhsT=ab[:, 4 + k, :], rhs=wpb_t[:, k, nb * 512:(nb + 1) * 512],
                                                  start=(k == 0), stop=(k == 3)), [r_ab, r_wpb], [ps_r[3 + nb]])
                    for nb in range(2):
                        V(lambda e: e.tensor_tensor(out=m1[:, nb * 512:(nb + 1) * 512], in0=ps[1 + nb][:, :],
                                                    in1=s_t[:, 0, nb * 512:(nb + 1) * 512], op=ALU.mult),
                          [ps_r[1 + nb], r_s], [r_m1])
                        V(lambda e: e.tensor_tensor(out=m2[:, nb * 512:(nb + 1) * 512], in0=ps[3 + nb][:, :],
                                                    in1=s_t[:, 1, nb * 512:(nb + 1) * 512], op=ALU.mult),
                          [ps_r[3 + nb], r_s], [r_m2])
                    G(lambda e: e.tensor_tensor(out=mg_t[:, :], in0=m1[:, :], in1=m2[:, :], op=ALU.add), [r_m1, r_m2], [r_mgt])

                def stY(t):
                    mt, r_mt = mT[t % 2]
                    x_t, r_x = xt[t % 2]
                    mg_t, r_mgt = mgs[t % 2]
                    for k in range(8):
                        PE(lambda e: e.transpose(out=psb(5)[:, k * 128:(k + 1) * 128], in_=mg_t[:, k * 128:(k + 1) * 128],
                                                 identity=ident[:, :]), [r_mgt, r_ident], [ps_r[5]])
                    A(lambda e: e.copy(out=mt[:, :, :], in_=psb(5).rearrange("p (k q) -> p k q", k=8)), [ps_r[5]], [r_mt])
                    for nb in range(2):
                        for k in range(8):
                            PE(lambda e: e.matmul(ps[6 + nb][:, :], lhsT=mt[:, k, :], rhs=wo_t[:, k, nb * 512:(nb + 1) * 512],
                                                  start=(k == 0), stop=(k == 7)), [r_mt, r_wo], [ps_r[6 + nb]])
                        V(lambda e: e.tensor_tensor(out=o1[:, nb * 512:(nb + 1) * 512], in0=ps[6 + nb][:, :],
                                                    in1=gtb[:, nb * 512:(nb + 1) * 512], op=ALU.mult),
                          [ps_r[6 + nb], r_gtb], [r_o1])
                    G(lambda e: e.tensor_tensor(out=x_t[:, :], in0=x_t[:, :], in1=o1[:, :], op=ALU.add), [r_x, r_o1], [r_x])
                    DMA(y_out[s, t * 128:(t + 1) * 128, :], x_t[:, :], r=[r_x], w=[xres[s][t]])

                stX(0)
                for t in range(NT):
                    if t + 1 < NT:
                        stX(t + 1)
                    stY(t)
            S.barrier()

        r_x12 = Res("x12")
        r_sigd = Res("sigd")
        r_uvd = [Res("uvd0"), Res("uvd1")]

        tbl_pending = []

        def issue_tables(l, deferred=False):
            CH = 256 if deferred else 512
            for which, tab in ((0, peer_u), (1, peer_v)):
                tv = tab.ap()[l * 16384:(l + 1) * 16384, :]
                for ch in range(16384 // CH):
                    def go(which=which, tv=tv, ch=ch):
                        S.dma("tbl", lambda e: e.dma_start(out=uvds[l].ap()[ch * CH:(ch + 1) * CH, which * D:(which + 1) * D],
                                                           in_=tv[ch * CH:(ch + 1) * CH, :]), [], [Res()], eng="gpsimd")
                    if deferred:
                        tbl_pending.append(go)
                    else:
                        go()

        def tbl_trickle(n=1):
            for _ in range(n):
                if tbl_pending:
                    tbl_pending.pop(0)()

        def peer_seq(l, s, from_input=False):
            src_x = x_in if from_input else y_out
            if l > 0:
                tbl_trickle(len(tbl_pending))
            S.wait_all("gpsimd", "tbl")
            uvd = uvds[l]
            with ExitStack() as st:
                wq_t, r_wq = sb(st, "wq_t", [128, 8, 2048], BF16)
                DMA(wq_t[:, :, :], pwq_s, r=[r_w["pwq"]], w=[r_wq])
                kst, r_kst = sb(st, "kst", [128, 2, 128], F32)
                k12, r_k12 = sb(st, "k12", [128, 2, 128], BF16)
                DMA(kst[:, 0, :], peer_k1T[l], w=[r_kst])
                DMA(kst[:, 1, :], peer_k2T[l], w=[r_kst])
                V(lambda e: e.tensor_copy(out=k12[:, :, :], in_=kst[:, :, :]), [r_kst], [r_k12])
                a2b, r_a2b = sb(st, "a2b", [128, D], F32)
                sh2b, r_sh2b = sb(st, "sh2b", [128, D], F32)
                gt2b, r_gt2b = sb(st, "gt2b", [128, D], F32)
                DMA(sh2b[:, :], modd[s:s + 1, 3072:4096].to_broadcast([128, D]), r=[r_modd], w=[r_sh2b])
                DMA(a2b[:, :], modd[s:s + 1, 4096:5120].to_broadcast([128, D]), r=[r_modd], w=[r_a2b])
                DMA(gt2b[:, :], modd[s:s + 1, 5120:6144].to_broadcast([128, D]), r=[r_modd], w=[r_gt2b])
                iot, r_iot = sb(st, "iot", [128, 16], F32)
                DMA(iot[:, :], c_iota, w=[r_iot])
                ioti, r_ioti = sb(st, "ioti", [128, 256], I32)
                DMA(ioti[:, :], c_iota256, w=[r_ioti])
                xt = [sb(st, f"xp{i}", [128, D], F32) for i in range(3)]
                h2, r_h2 = sb(st, "h2", [128, D], F32)
                fin, r_fin = h2, r_h2
                h2b, r_h2b = sb(st, "h2b", [128, D], BF16)
                junk, r_junk = sb(st, "junkp", [128, D], BF16)
                junks = [(junk, r_junk), sb(st, "junk2", [128, D], BF16)]
                junkA, r_junkA = h2b, r_h2b
                st8, r_st8 = sb(st, "st8p", [128, 8], F32)
                h2T, r_h2T = sb(st, "h2T", [128, 8, 128], BF16)
                qT, r_qT = sb(st, "qTp", [128, 16, 128], BF16)
                sc, r_sc = sb(st, "sc", [128, 2, 8, 128], F32)
                r_sch = [[Res(f"sc{a}{b}") for b in range(8)] for a in range(2)]
                vv, r_vv = sb(st, "vv", [128, 2, 8, 16], F32)
                r_vvh = [[Res(f"vv{a}{b}") for b in range(8)] for a in range(2)]
                r_vvall = r_vvh[0] + r_vvh[1]
                r_vsh = [Res(f"vs{b}") for b in range(8)]
                ii, r_ii = sb(st, "ii", [128, 2, 8, 16], U32)
                iif, r_iif = sb(st, "iif", [128, 2, 8, 16], BF16)
                cand, r_cand = sb(st, "cand", [128, 8, 256], F32)
                r_cdh = [Res(f"cd{b}") for b in range(8)]
                vs, r_vs = sb(st, "vs", [128, 8, 16], F32)
                ic, r_ic = sb(st, "ic", [128, 8, 16], U32)
                icab, r_icab = sb(st, "icab", [128, 2, 8, 16], U32)
                icf, r_icf = sb(st, "icf", [128, 2, 8, 16], BF16)
                oh, r_oh = sb(st, "oh", [128, 8, 16, 16], BF16)
                iotb, r_iotb = sb(st, "iotb", [128, 16], BF16)
                V(lambda e: e.tensor_copy(out=iotb[:, :], in_=iot[:, :]), [r_iot], [r_iotb])
                ef, r_ef = sb(st, "ef", [128, 2, 8, 16], F32)
                idxf, r_idxf = sb(st, "idxf", [128, 128], F32)
                idxs = [sb(st, f"idx{i}", [128, 128], I32) for i in range(2)]
                ggs = [sb(st, f"gg{i}", [128, 8, 16], F32) for i in range(2)]
                sm, r_sm = sb(st, "sm", [128, 16], F32)
                aa, _ = sb(st, "aa", [128, 128], F32)
                ww, _ = sb(st, "ww", [128, 128], F32)
                NGB = 6
                gbuf = [sb(st, f"gbuf{i}", [128, 4, 2 * D], BF16) for i in range(NGB)]
                r_gs = [[Res(f"gs{b}_{i}") for i in range(4)] for b in range(NGB)]
                wds = [sb(st, f"wd{i}", [128, 4, 128], BF16) for i in range(2)]
                r_aaj = [Res(f"aa{j}") for j in range(128)]
                r_wwb = [Res(f"ww{j}") for j in range(32)]
                gi_ = [0]
                HP = ((4, 5), (6, 7))
                ACC = (2, 3)
                BT, BS = 0, 1

                def stageA(t):
                    slot = t % 2
                    x_t, r_x = xt[t % 3]
                    idx, r_idx = idxs[slot]
                    gg, r_gg = ggs[slot]
                    H2P = HP[slot]
                    DMA(x_t[:, :], src_x[s, t * 128:(t + 1) * 128, :], r=[xres[s][t]], w=[r_x])
                    yield
                    A(lambda e: e.activation(out=junkA[:, :], in_=x_t[:, :], func=AF.Square, accum_out=st8[:, 0:1]),
                      [r_x], [r_junkA, r_st8])
                    A(lambda e: e.activation(out=st8[:, 1:2], in_=st8[:, 0:1], func=AF.Sqrt, scale=1.0 / D,
                                             bias=eps_t[:, 0:1]), [r_st8], [r_st8])
                    yield
                    V(lambda e: e.reciprocal(out=st8[:, 2:3], in_=st8[:, 1:2]), [r_st8], [r_st8])
                    V(lambda e: e.scalar_tensor_tensor(out=h2[:, :], in0=x_t[:, :], scalar=st8[:, 2:3], in1=a2b[:, :],
                                                       op0=ALU.mult, op1=ALU.mult), [r_x, r_st8, r_a2b], [r_h2])
                    V(lambda e: e.tensor_tensor(out=h2[:, :], in0=h2[:, :], in1=sh2b[:, :], op=ALU.add), [r_h2, r_sh2b], [r_h2])
                    A(lambda e: e.copy(out=h2b[:, :], in_=h2[:, :]), [r_h2], [r_h2b])
                    yield
                    for nb in range(2):
                        V(lambda e: e.tensor_copy(out=ps[H2P[nb]][:, :], in_=h2[:, nb * 512:(nb + 1) * 512]),
                          [r_h2], [ps_r[H2P[nb]]])
                    for k in range(8):
                        PE(lambda e: e.transpose(out=psb(BT)[:, k * 128:(k + 1) * 128], in_=h2b[:, k * 128:(k + 1) * 128],
                                                 identity=ident[:, :]), [r_h2b, r_ident], [ps_r[BT]])
                    yield
                    A(lambda e: e.copy(out=h2T[:, :, :], in_=psb(BT).rearrange("p (k q) -> p k q", k=8)), [ps_r[BT]], [r_h2T])
                    yield
                    for rnd in range(4):
                        for c4 in range(4):
                            cb = rnd * 4 + c4
                            for k in range(8):
                                PE(lambda e: e.matmul(ps[BT][:, c4 * 128:(c4 + 1) * 128], lhsT=wq_t[:, k, cb * 128:(cb + 1) * 128],
                                                      rhs=h2T[:, k, :], start=(k == 0), stop=(k == 7)),
                                   [r_wq, r_h2T], [ps_r[BT]])
                        yield
                        A(lambda e: e.copy(out=qT[:, rnd * 4:(rnd + 1) * 4, :],
                                           in_=ps[BT][:, :].rearrange("p (c q) -> p c q", c=4)), [ps_r[BT]], [r_qT])
                    yield
                    for hf in range(2):
                        for rnd in range(2):
                            for h4 in range(4):
                                h = rnd * 4 + h4
                                PE(lambda e: e.matmul(ps[BS][:, h4 * 128:(h4 + 1) * 128], lhsT=qT[:, 2 * h + hf, :],
                                                      rhs=k12[:, hf, :], start=True, stop=True), [r_qT, r_k12], [ps_r[BS]])
                            yield
                            A(lambda e: e.copy(out=sc[:, hf, rnd * 4:(rnd + 1) * 4, :],
                                               in_=ps[BS][:, :].rearrange("p (h n) -> p h n", h=4)), [ps_r[BS]],
                              [r_sch[hf][rnd * 4 + i] for i in range(4)])
                    yield
                    for hf in range(2):
                        sci = sc[:, hf, :, :].bitcast(U32)
                        V(lambda e: e.tensor_scalar(out=sci, in0=sci, scalar1=7, scalar2=7, op0=ALU.logical_shift_right,
                                                    op1=ALU.logical_shift_left), r_sch[hf], r_sch[hf])
                        V(lambda e: e.tensor_tensor(out=sci, in0=sci,
                                                    in1=ioti[:, 0:128].bitcast(U32).unsqueeze(1).to_broadcast([128, 8, 128]),
                                                    op=ALU.bitwise_or), r_sch[hf] + [r_ioti], r_sch[hf])
                        yield
                    for hf in range(2):
                        for h0 in range(0, 8, 2):
                            for h in (h0, h0 + 1):
                                V(lambda e: e.max(out=vv[:, hf, h, 0:8], in_=sc[:, hf, h, :]), [r_sch[hf][h]], [r_vvh[hf][h]])
                            for h in (h0, h0 + 1):
                                V(lambda e: e.match_replace(out=sc[:, hf, h, :], in_to_replace=vv[:, hf, h, 0:8],
                                                            in_values=sc[:, hf, h, :], imm_value=-1e30),
                                  [r_sch[hf][h], r_vvh[hf][h]], [r_sch[hf][h]])
                            for h in (h0, h0 + 1):
                                V(lambda e: e.max(out=vv[:, hf, h, 8:16], in_=sc[:, hf, h, :]), [r_sch[hf][h]], [r_vvh[hf][h]])
                            yield
                    V(lambda e: e.tensor_single_scalar(out=ii[:, :, :, :], in_=vv[:, :, :, :].bitcast(U32), scalar=127,
                                                       op=ALU.bitwise_and), r_vvall, [r_ii])
                    V(lambda e: e.tensor_copy(out=iif[:, :, :, :], in_=ii[:, :, :, :]), [r_ii], [r_iif])
                    c4v = cand[:, :, :].rearrange("p h (a b) -> p h a b", a=16)
                    V(lambda e: e.tensor_tensor(out=c4v, in0=vv[:, 0, :, :].unsqueeze(3).to_broadcast([128, 8, 16, 16]),
                                                in1=vv[:, 1, :, :].unsqueeze(2).to_broadcast([128, 8, 16, 16]), op=ALU.add),
                      r_vvall, r_cdh)
                    yield
                    cdi = cand[:, :, :].bitcast(U32)
                    V(lambda e: e.tensor_scalar(out=cdi, in0=cdi, scalar1=8, scalar2=8, op0=ALU.logical_shift_right,
                                                op1=ALU.logical_shift_left), r_cdh, r_cdh)
                    V(lambda e: e.tensor_tensor(out=cdi, in0=cdi,
                                                in1=ioti[:, :].bitcast(U32).unsqueeze(1).to_broadcast([128, 8, 256]),
                                                op=ALU.bitwise_or), r_cdh + [r_ioti], r_cdh)
                    yield
                    for h0 in range(0, 8, 2):
                        for h in (h0, h0 + 1):
                            V(lambda e: e.max(out=vs[:, h, 0:8], in_=cand[:, h, :]), [r_cdh[h]], [r_vsh[h]])
                        for h in (h0, h0 + 1):
                            V(lambda e: e.match_replace(out=cand[:, h, :], in_to_replace=vs[:, h, 0:8],
                                                        in_values=cand[:, h, :], imm_value=-1e30), [r_cdh[h], r_vsh[h]], [r_cdh[h]])
                        for h in (h0, h0 + 1):
                            V(lambda e: e.max(out=vs[:, h, 8:16], in_=cand[:, h, :]), [r_cdh[h]], [r_vsh[h]])
                        yield
                    V(lambda e: e.tensor_single_scalar(out=ic[:, :, :], in_=vs[:, :, :].bitcast(U32), scalar=255,
                                                       op=ALU.bitwise_and), r_vsh, [r_ic])
                    V(lambda e: e.tensor_single_scalar(out=icab[:, 0, :, :], in_=ic[:, :, :], scalar=4,
                                                       op=ALU.logical_shift_right), [r_ic], [r_icab])
                    V(lambda e: e.tensor_single_scalar(out=icab[:, 1, :, :], in_=ic[:, :, :], scalar=15,
                                                       op=ALU.bitwise_and), [r_ic], [r_icab])
                    V(lambda e: e.tensor_copy(out=icf[:, :, :, :], in_=icab[:, :, :, :]), [r_icab], [r_icf])
                    yield
                    for hf in range(2):
                        V(lambda e: e.tensor_tensor(out=oh[:, :, :, :],
                                                    in0=icf[:, hf, :, :].unsqueeze(3).to_broadcast([128, 8, 16, 16]),
                                                    in1=iotb[:, :].unsqueeze(1).unsqueeze(1).to_broadcast([128, 8, 16, 16]),
                                                    op=ALU.is_equal), [r_icf, r_iotb], [r_oh])
                        yield
                        V(lambda e: e.tensor_tensor(out=oh[:, :, :, :], in0=oh[:, :, :, :],
                                                    in1=iif[:, hf, :, :].unsqueeze(2).to_broadcast([128, 8, 16, 16]),
                                                    op=ALU.mult), [r_oh, r_iif], [r_oh])
                        yield
                        V(lambda e: e.tensor_reduce(out=ef[:, hf, :, :], in_=oh[:, :, :, :], axis=AX.X, op=ALU.add),
                          [r_oh], [r_ef])
                        yield
                    V(lambda e: e.scalar_tensor_tensor(out=idxf[:, :], in0=ef[:, 0, :, :].rearrange("p h k -> p (h k)"),
                                                       scalar=128.0, in1=ef[:, 1, :, :].rearrange("p h k -> p (h k)"),
                                                       op0=ALU.mult, op1=ALU.add), [r_ef], [r_idxf])
                    V(lambda e: e.tensor_copy(out=idx[:, :], in_=idxf[:, :]), [r_idxf], [r_idx])
                    V(lambda e: e.tensor_tensor(out=gg[:, :, :], in0=vs[:, :, :],
                                                in1=vs[:, :, 0:1].to_broadcast([128, 8, 16]), op=ALU.subtract), r_vsh, [r_gg])
                    A(lambda e: e.activation(out=gg[:, :, :], in_=gg[:, :, :], func=AF.Exp), [r_gg], [r_gg])
                    yield
                    V(lambda e: e.tensor_reduce(out=sm[:, 0:8], in_=gg[:, :, :], axis=AX.X, op=ALU.add), [r_gg], [r_sm])
                    V(lambda e: e.reciprocal(out=sm[:, 8:16], in_=sm[:, 0:8]), [r_sm], [r_sm])
                    V(lambda e: e.tensor_tensor(out=gg[:, :, :], in0=gg[:, :, :],
                                                in1=sm[:, 8:16].unsqueeze(2).to_broadcast([128, 8, 16]), op=ALU.mult),
                      [r_gg, r_sm], [r_gg])
                    yield

                def stageB(t, agen, prev_tail):
                    slot = t % 2
                    x_t, r_x = xt[t % 3]
                    idx, r_idx = idxs[slot]
                    gg, r_gg = ggs[slot]
                    H2P = HP[slot]
                    NBT = 32
                    pbuf = {}
                    for bt in range(NBT + 1):
                        if bt < NBT:
                            b = gi_[0] % NGB
                            gi_[0] += 1
                            pbuf[bt] = b
                            g_t = gbuf[b][0]
                            for i in range(4):
                                j = bt * 4 + i
                                S.dma("gpsimd", lambda e: e.indirect_dma_start(
                                    out=g_t[:, i, :], out_offset=None, in_=uvd[:, :],
                                    in_offset=bass.IndirectOffsetOnAxis(ap=idx[:, j:j + 1], axis=0)),
                                    [r_idx], [r_gs[b][i]])
                            for i in range(4):
                                j = bt * 4 + i
                                jk, r_jk = junks[j % 2]
                                V(lambda e: e.scalar_tensor_tensor(
                                    out=jk[:, :], in0=g_t[:, i, 0:D], scalar=1.0, in1=ps2(H2P), op0=ALU.mult, op1=ALU.mult,
                                    accum_out=aa[:, j:j + 1]),
                                  [r_gs[b][i], ps_r[H2P[0]], ps_r[H2P[1]]], [r_jk, r_aaj[j]])
                            A(lambda e: e.activation(out=ww[:, bt * 4:(bt + 1) * 4], in_=aa[:, bt * 4:(bt + 1) * 4], func=AF.Gelu),
                              [r_aaj[bt * 4 + i] for i in range(4)], [r_wwb[bt]])
                        if bt == 0 and prev_tail is not None:
                            prev_tail()
                        if bt in (18, 22, 26, 30):
                            tbl_trickle(1)
                        if agen is not None:
                            npull = 1 if bt in (0, 2) else (0 if bt < 4 else (1 if bt < 16 else 2))
                            for _ in range(npull):
                                next(agen, None)
                        if bt >= 1:
                            pb_ = bt - 1
                            b = pbuf[pb_]
                            g_t = gbuf[b][0]
                            wd_t, r_wd = wds[pb_ % 2]
                            V(lambda e: e.tensor_tensor(out=ww[:, pb_ * 4:(pb_ + 1) * 4], in0=ww[:, pb_ * 4:(pb_ + 1) * 4],
                                                        in1=gg[:, :, :].rearrange("p h k -> p (h k)")[:, pb_ * 4:(pb_ + 1) * 4],
                                                        op=ALU.mult), [r_wwb[pb_], r_gg], [r_wwb[pb_]])
                            for i in range(4):
                                j = pb_ * 4 + i
                                A(lambda e: e.activation(out=wd_t[:, i, :], in_=ident[:, :], func=AF.Copy, scale=ww[:, j:j + 1]),
                                  [r_ident, r_wwb[pb_]], [r_wd])
                            for i in range(4):
                                j = pb_ * 4 + i
                                for nb in range(2):
                                    PE(lambda e: e.matmul(ps[ACC[nb]][:, :], lhsT=wd_t[:, i, :],
                                                          rhs=g_t[:, i, D + nb * 512:D + (nb + 1) * 512],
                                                          start=(j == 0), stop=(j == 127)),
                                       [r_wd, r_gs[b][i]], [ps_r[ACC[nb]]])
                    if agen is not None:
                        for _ in agen:
                            pass
                    def tail():
                        V(lambda e: e.tensor_tensor(out=fin[:, :], in0=ps2(ACC), in1=gt2b[:, :], op=ALU.mult),
                          [ps_r[ACC[0]], ps_r[ACC[1]], r_gt2b], [r_fin])
                        V(lambda e: e.tensor_tensor(out=x_t[:, :], in0=x_t[:, :], in1=fin[:, :], op=ALU.add), [r_x, r_fin], [r_x])
                        DMA(y_out[s, t * 128:(t + 1) * 128, :], x_t[:, :], r=[r_x], w=[xres[s][t]])
                    return tail

                for _ in stageA(0):
                    pass
                tail = None
                for t in range(NTOK_PEER):
                    agen = stageA(t + 1) if t + 1 < NTOK_PEER else None
                    tail = stageB(t, agen, tail)
                tail()
            S.barrier()

        if do_peer:
            issue_tables(0)
            if DEPTH > 1:
                issue_tables(1, deferred=True)
        for l in range(DEPTH):
            mod_phase(l)
            prep_weights(l)
            if do_mixer:
                filter_phase(l)
            if do_peer and l == 0:
                pass
            for s in range(NSEQ):
                if do_mixer:
                    with ExitStack() as sq:
                        hv, _ = sb(sq, "hv", [128, NT, 512], BF16)
                        ybt, _ = sb(sq, "ybt", [128, NT, 512], BF16)
                        r_hv = [Res(f"hv{t}") for t in range(NT)]
                        r_ybt = [Res(f"yb{t}") for t in range(NT)]
                        mixer_seq(l, s, hv, r_hv, ybt, r_ybt)
                if do_peer:
                    peer_seq(l, s, from_input=(l == 0 and not do_mixer))
        S.barrier(include_tbl=True)
        print("bass program: ops", S.nops, "waits", S.nwaits, flush=True)
    return nc


N_CORES = 8
_NC_CACHE = {}


def kernel(**inputs):
    f32 = np.float32
    x = np.concatenate([np.asarray(inputs["x_prompt"], f32), np.asarray(inputs["x_sample"], f32)], axis=0)
    c = np.concatenate([np.asarray(inputs["c_prompt"], f32), np.asarray(inputs["c_sample"], f32)], axis=0)
    nseq = x.shape[0] // N_CORES
    cst = _consts()
    shared = {}
    for k in ("w_mod", "b_mod", "g_norm1", "g_norm2", "w_in", "conv_w", "conv_b", "f_w1", "f_b1", "f_freq", "f_w2",
              "f_b2", "f_w3", "f_bias", "q_gain", "k_gain", "sink", "w_pa", "w_pb", "w_out", "peer_wq"):
        shared[k] = np.ascontiguousarray(np.asarray(inputs[k], f32))
    shared["peer_k1T"] = np.ascontiguousarray(np.asarray(inputs["peer_k1"], f32).transpose(0, 2, 1))
    shared["peer_k2T"] = np.ascontiguousarray(np.asarray(inputs["peer_k2"], f32).transpose(0, 2, 1))
    shared["peer_u"] = np.ascontiguousarray(np.asarray(inputs["peer_u"], f32).reshape(2 * 16384, D))
    shared["peer_v"] = np.ascontiguousarray(np.asarray(inputs["peer_v"], f32).reshape(2 * 16384, D))
    for k in ("fwdF", "invG", "zfT", "negt", "absdelta", "cs", "masks", "ident", "iota16", "iota256"):
        shared[k] = cst[k]
    in_maps = []
    for i in range(N_CORES):
        m = dict(shared)
        m["x"] = np.ascontiguousarray(x[i * nseq:(i + 1) * nseq])
        ci = c[i * nseq:(i + 1) * nseq]
        m["cT"] = np.ascontiguousarray(ci.T.reshape(8, 128, nseq).transpose(1, 0, 2))
        in_maps.append(m)
    if "nc" not in _NC_CACHE:
        _NC_CACHE["nc"] = build(NSEQ=nseq, DEPTH=2)
    res = run_bass_kernel_spmd(_NC_CACHE["nc"], in_maps, core_ids=list(range(N_CORES)))
    y = np.concatenate([np.asarray(r["y"], f32) for r in res.results], axis=0)
    nb = inputs["x_prompt"].shape[0]
    return (np.ascontiguousarray(y[:nb]), np.ascontiguousarray(y[nb:]))
```

```python
import math
from contextlib import ExitStack

import numpy as np
import ml_dtypes
import concourse.bass as bass
import concourse.mybir as mybir
from concourse.bass_utils import run_bass_kernel_spmd

F32 = mybir.dt.float32
BF16 = mybir.dt.bfloat16
U32 = mybir.dt.uint32
I32 = mybir.dt.int32
AF = mybir.ActivationFunctionType
ALU = mybir.AluOpType
AX = mybir.AxisListType

L = 2048
D = 1024
NT = 16
EPS = 1e-6
NEG = -30000.0
MAGIC = 12582912.0
TWO_PI = 2.0 * math.pi


class Res:
    __slots__ = ("name", "w", "r")

    def __init__(self, name=""):
        self.name = name
        self.w = None
        self.r = {}


class Sched:
    ENGS = ("sync", "scalar", "vector", "gpsimd", "tensor")
    RING = 8

    def __init__(self, nc, es):
        self.nc = nc
        self.eng = {e: getattr(nc, e) for e in self.ENGS}
        self.sems = []
        self.semid = {}
        self.cnt = {}
        for e in self.ENGS:
            self.semid[e] = len(self.sems)
            self.sems.append(es.enter_context(nc.semaphore("c_" + e)))
            self.cnt[e] = 0
        self.ring = {}
        self.ring_cnt = {}
        for q in ("sync", "gpsimd", "tbl"):
            ids = []
            for i in range(self.RING):
                ids.append(len(self.sems))
                self.sems.append(es.enter_context(nc.semaphore(f"d_{q}{i}")))
            self.ring[q] = ids
            self.ring_cnt[q] = 0
        self.known = {e: {} for e in self.ENGS}
        self.nwaits = 0
        self.nops = 0

    def _waits(self, eng, reads, writes, extra=()):
        need = {}

        def add(ev):
            if ev is None:
                return
            s, v = ev
            if need.get(s, 0) < v:
                need[s] = v
        for r in reads:
            add(r.w)
        for w in writes:
            add(w.w)
            for s, v in w.r.items():
                add((s, v))
        for ev in extra:
            add(ev)
        kn = self.known[eng]
        e = self.eng[eng]
        own = self.semid[eng]
        for s, v in need.items():
            if eng == "tensor" and s == own:
                continue
            if kn.get(s, 0) >= v:
                continue
            e.wait_ge(self.sems[s], v)
            kn[s] = v
            self.nwaits += 1

    def _mark(self, ev, reads, writes):
        s, v = ev
        for w in writes:
            w.w = ev
            w.r = {}
        for r in reads:
            if r in writes:
                continue
            if r.r.get(s, 0) < v:
                r.r[s] = v

    def op(self, eng, fn, reads=(), writes=()):
        self._waits(eng, reads, writes)
        ins = fn(self.eng[eng])
        self.cnt[eng] += 1
        ev = (self.semid[eng], self.cnt[eng])
        ins.then_inc(self.sems[ev[0]], 1)
        self._mark(ev, reads, writes)
        self.nops += 1
        return ev

    def dma(self, q, fn, reads=(), writes=(), eng=None):
        eng = eng or q
        i = self.ring_cnt[q]
        self.ring_cnt[q] = i + 1
        slot = self.ring[q][i % self.RING]
        rnd = i // self.RING
        extra = []
        if rnd > 0:
            extra.append((slot, 16 * rnd))
        self._waits(eng, reads, writes, extra)
        ins = fn(self.eng[eng])
        ev = (slot, 16 * (rnd + 1))
        ins.then_inc(self.sems[slot], 16)
        self._mark(ev, reads, writes)
        self.nops += 1
        return ev

    def ring_events(self, q):
        evs = []
        n = self.ring_cnt[q]
        for k in range(self.RING):
            cntk = (n - k + self.RING - 1) // self.RING if n > k else 0
            if cntk > 0:
                evs.append((self.ring[q][k], 16 * cntk))
        return evs

    def wait_all(self, eng, q):
        self._waits(eng, (), (), self.ring_events(q))

    def barrier(self, include_tbl=False):
        evs = [(self.semid[e], self.cnt[e]) for e in self.ENGS if self.cnt[e] > 0]
        for q in self.ring:
            if q == "tbl" and not include_tbl:
                continue
            n = self.ring_cnt[q]
            for k in range(self.RING):
                cntk = (n - k + self.RING - 1) // self.RING if n > k else 0
                if cntk > 0:
                    evs.append((self.ring[q][k], 16 * cntk))
        for eng in self.ENGS:
            self._waits(eng, (), (), evs)


_CONST = {}


def _consts():
    if _CONST:
        return _CONST
    bf = ml_dtypes.bfloat16
    N2 = 2 * L
    n = np.arange(L, dtype=np.int64)
    fwd = np.zeros((16, 128, 16, 256), np.float32)
    nn = (np.arange(16)[None, :] * 128 + np.arange(128)[:, None])
    for fc in range(16):
        f = fc * 128 + np.arange(128)
        ang = 2.0 * np.pi * ((nn[:, :, None] * f[None, None, :]) % N2) / N2
        fwd[fc, :, :, 0:128] = np.cos(ang)
        fwd[fc, :, :, 128:256] = -np.sin(ang)
    fwd[0, :, :, 128] = np.where(nn % 2 == 0, 1.0, -1.0)
    inv = np.zeros((4, 128, 32, 512), np.float32)
    for tg in range(4):
        t = tg * 512 + np.arange(512)
        for fc in range(16):
            f = fc * 128 + np.arange(128)
            ang = 2.0 * np.pi * ((f[:, None] * t[None, :]) % N2) / N2
            gre = (2.0 / N2) * np.cos(ang)
            gim = -(2.0 / N2) * np.sin(ang)
            if fc == 0:
                gre[0, :] = 1.0 / N2
                gim[0, :] = np.where(t % 2 == 0, 1.0, -1.0) / N2
            inv[tg, :, fc, :] = gre
            inv[tg, :, 16 + fc, :] = gim
    tl = np.linspace(0.0, 1.0, L, dtype=np.float32)
    w = (np.float32(2.0 * math.pi) * np.arange(L, dtype=np.float32) / np.float32(L)).astype(np.float32)
    bands = np.linspace(1e-4, 15.0, 16, dtype=np.float32)
    bw = (bands[:, None] * w[None, :]).astype(np.float32).astype(np.float64)
    zfT = np.concatenate([tl[None, :].astype(np.float64), np.cos(bw), -np.sin(bw)], axis=0).astype(np.float32)
    negt = np.zeros((128, 16), np.float32)
    negt[:, :] = -tl[nn]
    max_decay = math.log(1e-2) / 0.3
    min_decay = math.log(1e-2) / 1.5
    absdelta = np.abs(np.linspace(min_decay, max_decay, 512, dtype=np.float32)).reshape(1, 512)
    invf = (10000.0 ** (-np.arange(0, 64, 2, dtype=np.float32) / np.float32(64))).astype(np.float32)
    ang = (np.arange(L, dtype=np.float32)[:, None] * invf[None, :]).astype(np.float32).astype(np.float64)
    cs = np.zeros((L, 128), np.float32)
    cs[:, 0:32] = np.cos(ang)
    cs[:, 32:64] = np.cos(ang)
    cs[:, 64:96] = np.sin(ang)
    cs[:, 96:128] = np.sin(ang)
    cs = np.ascontiguousarray(cs.reshape(16, 128, 128).transpose(1, 0, 2))
    j = np.arange(128)[:, None]
    q = np.arange(128)[None, :]
    mprev = np.where(j >= q, 0.0, NEG).astype(np.float32)
    mnext = np.where(j <= q, 0.0, NEG).astype(np.float32)
    masks = np.stack([np.tile(mprev, (1, 4)), np.tile(mnext, (1, 4))], axis=0)
    _CONST.update(
        fwdF=fwd.astype(bf), invG=inv.astype(bf), zfT=zfT, negt=negt, absdelta=absdelta,
        cs=cs, masks=masks.astype(bf), ident=np.eye(128, dtype=np.float32).astype(bf),
        iota16=np.tile(np.arange(16, dtype=np.float32)[None, :], (128, 1)),
        iota256=np.tile(np.arange(256, dtype=np.int32)[None, :], (128, 1)),
    )
    return _CONST


def build(NSEQ=3, DEPTH=2, do_mixer=True, do_peer=True, NTOK_PEER=NT):
    nc = bass.Bass("TRN2", target_bir_lowering=False)

    def din(name, shape, dt=F32):
        return nc.dram_tensor(name, list(shape), dt, kind="ExternalInput").ap()

    def dscr(name, shape, dt):
        return nc.dram_tensor(name, list(shape), dt, kind="Internal").ap()

    x_in = din("x", [NSEQ, L, D])
    cT_in = din("cT", [128, 8, NSEQ])
    w_mod = din("w_mod", [2, D, 6 * D])
    b_mod = din("b_mod", [2, 6 * D])
    g_norm1 = din("g_norm1", [2, D])
    g_norm2 = din("g_norm2", [2, D])
    w_in = din("w_in", [2, D, 4352])
    conv_w = din("conv_w", [2, 3, 1536])
    conv_b = din("conv_b", [2, 1536])
    f_w1 = din("f_w1", [2, 33, 64])
    f_b1 = din("f_b1", [2, 64])
    f_freq = din("f_freq", [2, 64])
    f_w2 = din("f_w2", [2, 64, 64])
    f_b2 = din("f_b2", [2, 64])
    f_w3 = din("f_w3", [2, 64, 2048])
    f_bias = din("f_bias", [2, 2, 512])
    q_gain = din("q_gain", [2, 64])
    k_gain = din("k_gain", [2, 64])
    sink = din("sink", [2, 8])
    w_pa = din("w_pa", [2, 512, D])
    w_pb = din("w_pb", [2, 512, D])
    w_out = din("w_out", [2, D, D])
    peer_wq = din("peer_wq", [2, D, 2048])
    peer_k1T = din("peer_k1T", [2, 128, 128])
    peer_k2T = din("peer_k2T", [2, 128, 128])
    peer_u = nc.dram_tensor("peer_u", [2 * 16384, D], F32, kind="ExternalInput")
    peer_v = nc.dram_tensor("peer_v", [2 * 16384, D], F32, kind="ExternalInput")
    c_fwdF = din("fwdF", [16, 128, 16, 256], BF16)
    c_invG = din("invG", [4, 128, 32, 512], BF16)
    c_zfT = din("zfT", [33, L])
    c_negt = din("negt", [128, 16])
    c_absd = din("absdelta", [1, 512])
    c_cs = din("cs", [128, 16, 128])
    c_masks = din("masks", [2, 128, 512], BF16)
    c_ident = din("ident", [128, 128], BF16)
    c_iota = din("iota16", [128, 16])
    c_iota256 = din("iota256", [128, 256], I32)
    y_out = nc.dram_tensor("y", [NSEQ, L, D], F32, kind="ExternalOutput").ap()

    modd = dscr("modd", [NSEQ, 6 * D], F32)
    whY = dscr("whY", [9, 128, 8, 512], BF16)
    wqa = dscr("wqa", [128, 8, 512], BF16)
    wkv = dscr("wkv", [128, 8, 256], BF16)
    wg = dscr("wg", [4, 128, 8, 512], BF16)
    wpa_s = dscr("wpa_s", [128, 4, 1024], BF16)
    wpb_s = dscr("wpb_s", [128, 4, 1024], BF16)
    wout_s = dscr("wout_s", [128, 8, 1024], BF16)
    pwq_s = dscr("pwq_s", [128, 8, 2048], BF16)
    x12 = dscr("x12", [2, L, 512], BF16)
    sigd = dscr("sigd", [2, L, 1024], BF16)
    hfd = dscr("hfd", [2, 16, 128, 2, 512], F32)
    uvds = [nc.dram_tensor(f"uvd{i}", [16384, 2048], BF16, kind="Internal") for i in range(2)]

    with ExitStack() as es:
        S = Sched(nc, es)
        es.enter_context(nc.allow_non_contiguous_dma(reason="small strided param loads"))

        def V(fn, r=(), w=()):
            return S.op("vector", fn, r, w)

        def A(fn, r=(), w=()):
            return S.op("scalar", fn, r, w)

        def G(fn, r=(), w=()):
            return S.op("gpsimd", fn, r, w)

        def PE(fn, r=(), w=()):
            return S.op("tensor", fn, r, w)

        def DMA(out, in_, r=(), w=(), q="sync"):
            return S.dma(q, lambda e: e.dma_start(out=out, in_=in_), r, w)

        uid = [0]

        def sb(stack, name, shape, dt):
            uid[0] += 1
            t = stack.enter_context(nc.sbuf_tensor(f"sb{uid[0]}_{name}", list(shape), dt))
            return t, Res(name)

        psbig = es.enter_context(nc.psum_tensor("psbig", [128, 4096], F32))
        ps = [psbig[:, i * 512:(i + 1) * 512] for i in range(8)]
        ps_r = [Res(f"ps{i}") for i in range(8)]

        def ps2(banks):
            return psbig[:, banks[0] * 512:(banks[1] + 1) * 512]
        ident, r_ident = sb(es, "ident", [128, 128], BF16)
        DMA(ident[:, :], c_ident, w=[r_ident])
        xres = [[Res(f"x{s}_{t}") for t in range(NT)] for s in range(NSEQ)]
        r_modd = Res("modd")
        r_hfd = Res("hfd")
        r_w = {k: Res(k) for k in ("whY", "wqa", "wkv", "wg", "wpa", "wpb", "wout", "pwq")}
        hnyq, r_hnyq = sb(es, "hnyq", [1, 2, 512], F32)

        def mod_phase(l):
            with ExitStack() as st:
                cT, r_cT = sb(st, "cT", [128, 8, NSEQ], F32)
                scT, r_scT = sb(st, "scT", [128, 8, NSEQ], F32)
                modr, r_modr = sb(st, "modr", [NSEQ, 6 * D], F32)
                bmb, r_bmb = sb(st, "bmb", [NSEQ, 6 * D], F32)
                g1b, r_g1b = sb(st, "g1b", [NSEQ, D], F32)
                g2b, r_g2b = sb(st, "g2b", [NSEQ, D], F32)
                wst = [sb(st, f"wst{i}", [128, 8, 512], F32) for i in range(2)]
                DMA(cT[:, :, :], cT_in, w=[r_cT])
                DMA(bmb[:, :], b_mod[l:l + 1, :].to_broadcast([NSEQ, 6 * D]), w=[r_bmb])
                DMA(g1b[:, :], g_norm1[l:l + 1, :].to_broadcast([NSEQ, D]), w=[r_g1b])
                DMA(g2b[:, :], g_norm2[l:l + 1, :].to_broadcast([NSEQ, D]), w=[r_g2b])
                A(lambda e: e.activation(out=scT[:, :, :], in_=cT[:, :, :], func=AF.Silu), [r_cT], [r_scT])
                wv = w_mod[l].rearrange("(k p) n -> p k n", p=128)
                for nb in range(12):
                    wt, r_wt = wst[nb % 2]
                    DMA(wt[:, :, :], wv[:, :, nb * 512:(nb + 1) * 512], w=[r_wt])
                    b = nb % 2
                    for k in range(8):
                        PE(lambda e: e.matmul(ps[b][0:NSEQ, :], lhsT=scT[:, k, :], rhs=wt[:, k, :],
                                              start=(k == 0), stop=(k == 7)),
                           [r_scT, r_wt], [ps_r[b]])
                    V(lambda e: e.tensor_tensor(out=modr[:, nb * 512:(nb + 1) * 512], in0=ps[b][0:NSEQ, :],
                                                in1=bmb[:, nb * 512:(nb + 1) * 512], op=ALU.add),
                      [ps_r[b], r_bmb], [r_modr])
                for (c0, gb_, rg) in ((1024, g1b, r_g1b), (4096, g2b, r_g2b)):
                    V(lambda e: e.scalar_tensor_tensor(out=modr[:, c0:c0 + 1024], in0=modr[:, c0:c0 + 1024],
                                                       scalar=1.0, in1=gb_[:, :], op0=ALU.add, op1=ALU.mult),
                      [r_modr, rg], [r_modr])
                DMA(modd, modr[:, :], r=[r_modr], w=[r_modd])
            S.barrier()

        def prep_weights(l):
            with ExitStack() as st:
                stg = [sb(st, f"pstg{i}", [128, 8, 512], F32) for i in range(2)]
                obf = [sb(st, f"pobf{i}", [128, 8, 512], BF16) for i in range(2)]
                cwb, r_cwb = sb(st, "cwb", [128, 3, 1536], F32)
                for j in range(3):
                    DMA(cwb[:, j, :], conv_w[l, j:j + 1, :].to_broadcast([128, 1536]), w=[r_cwb])
                cnt = [0]

                def one(src, dst, K, N, rdst, mul=None):
                    i = cnt[0] % 2
                    cnt[0] += 1
                    s_t, r_s = stg[i]
                    o_t, r_o = obf[i]
                    DMA(s_t[:, 0:K, 0:N], src, w=[r_s])
                    if mul is None:
                        if cnt[0] % 2 == 0:
                            V(lambda e: e.tensor_copy(out=o_t[:, 0:K, 0:N], in_=s_t[:, 0:K, 0:N]), [r_s], [r_o])
                        else:
                            A(lambda e: e.copy(out=o_t[:, 0:K, 0:N], in_=s_t[:, 0:K, 0:N]), [r_s], [r_o])
                    else:
                        for k in range(K):
                            V(lambda e: e.tensor_tensor(out=o_t[:, k, 0:N], in0=s_t[:, k, 0:N], in1=mul,
                                                        op=ALU.mult), [r_s, r_cwb], [r_o])
                    DMA(dst, o_t[:, 0:K, 0:N], r=[r_o], w=[rdst])

                wiv = w_in[l].rearrange("(k p) n -> p k n", p=128)
                for ob in range(3):
                    for j in range(3):
                        one(wiv[:, :, ob * 512:(ob + 1) * 512], whY[ob * 3 + j], 8, 512, r_w["whY"],
                            mul=cwb[:, j, ob * 512:(ob + 1) * 512])
                one(wiv[:, :, 1536:2048], wqa, 8, 512, r_w["wqa"])
                one(wiv[:, :, 2048:2304], wkv, 8, 256, r_w["wkv"])
                for gi in range(4):
                    one(wiv[:, :, 2304 + gi * 512:2304 + (gi + 1) * 512], wg[gi], 8, 512, r_w["wg"])
                pav = w_pa[l].rearrange("(k p) n -> p k n", p=128)
                pbv = w_pb[l].rearrange("(k p) n -> p k n", p=128)
                wov = w_out[l].rearrange("(k p) n -> p k n", p=128)
                pqv = peer_wq[l].rearrange("(k p) n -> p k n", p=128)
                for nb in range(2):
                    one(pav[:, :, nb * 512:(nb + 1) * 512], wpa_s[:, :, nb * 512:(nb + 1) * 512], 4, 512, r_w["wpa"])
                    one(pbv[:, :, nb * 512:(nb + 1) * 512], wpb_s[:, :, nb * 512:(nb + 1) * 512], 4, 512, r_w["wpb"])
                    one(wov[:, :, nb * 512:(nb + 1) * 512], wout_s[:, :, nb * 512:(nb + 1) * 512], 8, 512, r_w["wout"])
                for nb in range(4):
                    one(pqv[:, :, nb * 512:(nb + 1) * 512], pwq_s[:, :, nb * 512:(nb + 1) * 512], 8, 512, r_w["pwq"])
            S.barrier()

        def range_reduce_sin(st, dst, src_ps, bcol, fcol, r_cols, r_dst, psr, npart, tmp, r_tmp, tmp2, r_tmp2):
            V(lambda e: e.tensor_scalar(out=tmp[0:npart, :], in0=src_ps, scalar1=bcol, scalar2=fcol,
                                        op0=ALU.add, op1=ALU.mult), [psr, r_cols], [r_tmp])
            V(lambda e: e.tensor_scalar(out=tmp2[0:npart, :], in0=tmp[0:npart, :], scalar1=1.0 / TWO_PI,
                                        scalar2=MAGIC, op0=ALU.mult, op1=ALU.add), [r_tmp], [r_tmp2])
            V(lambda e: e.tensor_scalar(out=tmp2[0:npart, :], in0=tmp2[0:npart, :], scalar1=MAGIC,
                                        scalar2=-TWO_PI, op0=ALU.subtract, op1=ALU.mult), [r_tmp2], [r_tmp2])
            V(lambda e: e.tensor_tensor(out=tmp[0:npart, :], in0=tmp[0:npart, :], in1=tmp2[0:npart, :],
                                        op=ALU.add), [r_tmp, r_tmp2], [r_tmp])
            V(lambda e: e.tensor_scalar(out=tmp[0:npart, :], in0=tmp[0:npart, :], scalar1=3.1415925,
                                        scalar2=-3.1415925, op0=ALU.min, op1=ALU.max), [r_tmp], [r_tmp])
            A(lambda e: e.activation(out=dst, in_=tmp[0:npart, :], func=AF.Sin), [r_tmp], [r_dst])

        def filter_phase(l):
            with ExitStack() as st:
                zf, r_zf = sb(st, "zf", [33, L], F32)
                w1, r_w1 = sb(st, "fw1", [33, 64], F32)
                w2, r_w2 = sb(st, "fw2", [64, 64], F32)
                w3, r_w3 = sb(st, "fw3", [64, 2048], F32)
                cols, r_cols = sb(st, "fcols", [64, 4], F32)
                a1, r_a1 = sb(st, "fa1", [64, L], F32)
                a2, r_a2 = sb(st, "fa2", [64, L], F32)
                tmp, r_tmp = sb(st, "ftmp", [128, 512], F32)
                tmp2, r_tmp2 = sb(st, "ftmp2", [128, 512], F32)
                absd, r_absd = sb(st, "absd", [128, 512], F32)
                negt, r_negt = sb(st, "negt", [128, 16], F32)
                dec, r_dec = sb(st, "dec", [128, 512], F32)
                fa, r_fa = sb(st, "fa", [128, 16, 1024], BF16)
                fb, r_fb = sb(st, "fb", [128, 16, 1024], BF16)
                h0, r_h0 = sb(st, "h0", [128, 512], F32)
                h1, r_h1 = sb(st, "h1", [128, 512], F32)
                fbias, r_fbias = sb(st, "fbias", [128, 2, 512], F32)
                Fb = [sb(st, f"Fbf{i}", [128, 16, 256], BF16) for i in range(2)]
                ho = [sb(st, f"hout{i}", [128, 2, 512], F32) for i in range(2)]
                DMA(zf[:, :], c_zfT, w=[r_zf])
                DMA(w1[:, :], f_w1[l], w=[r_w1])
                DMA(w2[:, :], f_w2[l], w=[r_w2])
                DMA(w3[:, :], f_w3[l], w=[r_w3])
                DMA(cols[:, 0:1], f_b1[l:l + 1, :].rearrange("o n -> n o"), w=[r_cols])
                DMA(cols[:, 1:2], f_freq[l:l + 1, :].rearrange("o n -> n o"), w=[r_cols])
                DMA(cols[:, 2:3], f_b2[l:l + 1, :].rearrange("o n -> n o"), w=[r_cols])
                DMA(absd[:, :], c_absd.to_broadcast([128, 512]), w=[r_absd])
                DMA(negt[:, :], c_negt, w=[r_negt])
                for o in range(2):
                    DMA(fbias[:, o, :], f_bias[l, o:o + 1, :].to_broadcast([128, 512]), w=[r_fbias])
                for ch in range(4):
                    b = ch % 2
                    PE(lambda e: e.matmul(ps[b][0:64, :], lhsT=w1[:, :], rhs=zf[:, ch * 512:(ch + 1) * 512],
                                          start=True, stop=True), [r_w1, r_zf], [ps_r[b]])
                    range_reduce_sin(st, a1[:, ch * 512:(ch + 1) * 512], ps[b][0:64, :], cols[:, 0:1], cols[:, 1:2],
                                     r_cols, r_a1, ps_r[b], 64, tmp, r_tmp, tmp2, r_tmp2)
                for ch in range(4):
                    b = ch % 2
                    PE(lambda e: e.matmul(ps[b][0:64, :], lhsT=w2[:, :], rhs=a1[:, ch * 512:(ch + 1) * 512],
                                          start=True, stop=True), [r_w2, r_a1], [ps_r[b]])
                    range_reduce_sin(st, a2[:, ch * 512:(ch + 1) * 512], ps[b][0:64, :], cols[:, 2:3], cols[:, 1:2],
                                     r_cols, r_a2, ps_r[b], 64, tmp, r_tmp, tmp2, r_tmp2)
                for tc in range(16):
                    A(lambda e: e.activation(out=dec[:, :], in_=absd[:, :], func=AF.Exp, scale=negt[:, tc:tc + 1]),
                      [r_absd, r_negt], [r_dec])
                    for o in range(2):
                        for dr in range(2):
                            b = 2 + dr
                            cb = (o * 2 + dr) * 512
                            PE(lambda e: e.matmul(ps[b][:, :], lhsT=a2[:, tc * 128:(tc + 1) * 128],
                                                  rhs=w3[:, cb:cb + 512], start=True, stop=True),
                               [r_a2, r_w3], [ps_r[b]])
                            hh, r_hh = (h0, r_h0) if dr == 0 else (h1, r_h1)
                            V(lambda e: e.tensor_tensor(out=hh[:, :], in0=ps[b][:, :], in1=dec[:, :], op=ALU.mult),
                              [ps_r[b], r_dec], [r_hh])
                        if tc == 0:
                            V(lambda e: e.memset(h1[0:1, :], 0.0), [], [r_h1])
                        V(lambda e: e.tensor_tensor(out=fa[:, tc, o * 512:(o + 1) * 512], in0=h0[:, :], in1=h1[:, :],
                                                    op=ALU.add), [r_h0, r_h1], [r_fa])
                        V(lambda e: e.tensor_tensor(out=fb[:, tc, o * 512:(o + 1) * 512], in0=h0[:, :], in1=h1[:, :],
                                                    op=ALU.subtract), [r_h0, r_h1], [r_fb])
                for fc in range(16):
                    Ft, r_Ft = Fb[fc % 2]
                    DMA(Ft[:, :, :], c_fwdF[fc], w=[r_Ft])
                    hot, r_hot = ho[fc % 2]
                    for o in range(2):
                        bre, bim = 4 + 2 * (o % 2), 5 + 2 * (o % 2)
                        for tc in range(16):
                            PE(lambda e: e.matmul(ps[bre][:, :], lhsT=Ft[:, tc, 0:128],
                                                  rhs=fa[:, tc, o * 512:(o + 1) * 512], start=(tc == 0), stop=(tc == 15)),
                               [r_Ft, r_fa], [ps_r[bre]])
                        for tc in range(16):
                            PE(lambda e: e.matmul(ps[bim][:, :], lhsT=Ft[:, tc, 128:256],
                                                  rhs=fb[:, tc, o * 512:(o + 1) * 512], start=(tc == 0), stop=(tc == 15)),
                               [r_Ft, r_fb], [ps_r[bim]])
                        V(lambda e: e.tensor_tensor(out=hot[:, 0, :], in0=ps[bre][:, :], in1=fbias[:, o, :], op=ALU.add),
                          [ps_r[bre], r_fbias], [r_hot])
                        A(lambda e: e.copy(out=hot[:, 1, :], in_=ps[bim][:, :]), [ps_r[bim]], [r_hot])
                        DMA(hfd[o, fc], hot[:, :, :], r=[r_hot], w=[r_hfd])
                        if fc == 0:
                            for tc in range(16):
                                PE(lambda e: e.matmul(ps[0][0:1, :], lhsT=Ft[:, tc, 128:129],
                                                      rhs=fa[:, tc, o * 512:(o + 1) * 512], start=(tc == 0), stop=(tc == 15)),
                                   [r_Ft, r_fa], [ps_r[0]])
                            V(lambda e: e.tensor_tensor(out=hnyq[0:1, o, :], in0=ps[0][0:1, :], in1=fbias[0:1, o, :],
                                                        op=ALU.add), [ps_r[0], r_fbias], [r_hnyq])
            S.barrier()

        def qk_norm_rope(nh, src_ps, psr, gainb, r_gain, cst, r_cs, t, qf, r_qf, qsq, r_qsq, ss, r_ss, qr, r_qr):
            W = nh * 64
            A(lambda e: e.copy(out=qf[:, 0:W], in_=src_ps), [psr], [r_qf])
            yield
            V(lambda e: e.tensor_tensor(out=qsq[:, 0:W], in0=qf[:, 0:W], in1=qf[:, 0:W], op=ALU.mult), [r_qf], [r_qsq])
            yield
            V(lambda e: e.tensor_reduce(out=ss[:, 0:nh], in_=qsq[:, 0:W].rearrange("p (h d) -> p h d", h=nh),
                                        axis=AX.X, op=ALU.add), [r_qsq], [r_ss])
            yield
            A(lambda e: e.activation(out=ss[:, 8:8 + nh], in_=ss[:, 0:nh], func=AF.Sqrt, scale=1.0 / 64.0, bias=eps_t[:, 0:1]),
              [r_ss], [r_ss])
            yield
            V(lambda e: e.reciprocal(out=ss[:, 16:16 + nh], in_=ss[:, 8:8 + nh]), [r_ss], [r_ss])
            yield
            q3 = qf[:, 0:W].rearrange("p (h d) -> p h d", h=nh)
            V(lambda e: e.tensor_tensor(out=q3, in0=q3, in1=ss[:, 16:16 + nh].unsqueeze(2).to_broadcast([128, nh, 64]),
                                        op=ALU.mult), [r_qf, r_ss], [r_qf])
            yield
            V(lambda e: e.tensor_tensor(out=qf[:, 0:W], in0=qf[:, 0:W], in1=gainb[:, 0:W], op=ALU.mult),
              [r_qf, r_gain], [r_qf])
            yield
            cosb = cst[:, t, 0:64].unsqueeze(1).to_broadcast([128, nh, 64])
            sinb = cst[:, t, 64:128].unsqueeze(1).to_broadcast([128, nh, 64])
            s3 = qsq[:, 0:W].rearrange("p (h d) -> p h d", h=nh)
            V(lambda e: e.tensor_tensor(out=s3, in0=q3, in1=sinb, op=ALU.mult), [r_qf, r_cs], [r_qsq])
            yield
            V(lambda e: e.tensor_tensor(out=q3, in0=q3, in1=cosb, op=ALU.mult), [r_qf, r_cs], [r_qf])
            yield
            r3 = qr[:, 0:W].rearrange("p (h d) -> p h d", h=nh)
            V(lambda e: e.tensor_tensor(out=r3[:, :, 0:32], in0=q3[:, :, 0:32], in1=s3[:, :, 32:64], op=ALU.subtract),
              [r_qf, r_qsq], [r_qr])
            yield
            V(lambda e: e.tensor_tensor(out=r3[:, :, 32:64], in0=q3[:, :, 32:64], in1=s3[:, :, 0:32], op=ALU.add),
              [r_qf, r_qsq], [r_qr])
            yield

        eps_t, r_eps = sb(es, "eps_t", [128, 1], F32)
        V(lambda e: e.memset(eps_t[:, :], EPS), [], [r_eps])

        def psb(i):
            return ps[i][:, :].bitcast(BF16)

        def mixer_seq(l, s, hv, r_hv, ybt, r_ybt):
            src_x = x_in if l == 0 else y_out
            with ExitStack() as st:
                hT, r_hT = sb(st, "hT", [128, 8, L + 2], BF16)
                modT, r_modT = sb(st, "modT", [128, 2, 8], F32)
                DMA(modT[:, 0, :], modd[s, 0:1024].rearrange("(k p) -> p k", p=128), r=[r_modd], w=[r_modT])
                DMA(modT[:, 1, :], modd[s, 1024:2048].rearrange("(k p) -> p k", p=128), r=[r_modd], w=[r_modT])
                V(lambda e: e.memset(hT[:, :, 0:1], 0.0), [], [r_hT])
                V(lambda e: e.memset(hT[:, :, L + 1:L + 2], 0.0), [], [r_hT])
                xt = [sb(st, f"xt{i}", [128, D], F32) for i in range(2)]
                junk, r_junk = sb(st, "junk", [128, D], BF16)
                xn = [sb(st, f"xn{i}", [128, D], BF16) for i in range(2)]
                st8, r_st8 = sb(st, "st8", [128, 8], F32)
                evt = [sb(st, f"evt{i}", [128, 8, 128], F32) for i in range(2)]
                r_hTt = [Res(f"hT{t}") for t in range(NT)]
                for t in range(NT):
                    x_t, r_x = xt[t % 2]
                    xn_t, r_xn = xn[t % 2]
                    b = t % 2
                    DMA(x_t[:, :], src_x[s, t * 128:(t + 1) * 128, :], r=[xres[s][t]], w=[r_x])
                    A(lambda e: e.activation(out=junk[:, :], in_=x_t[:, :], func=AF.Square, accum_out=st8[:, 0:1]),
                      [r_x], [r_junk, r_st8])
                    A(lambda e: e.activation(out=st8[:, 1:2], in_=st8[:, 0:1], func=AF.Sqrt, scale=1.0 / D,
                                             bias=eps_t[:, 0:1]), [r_st8], [r_st8])
                    V(lambda e: e.reciprocal(out=st8[:, 2:3], in_=st8[:, 1:2]), [r_st8], [r_st8])
                    A(lambda e: e.activation(out=xn_t[:, :], in_=x_t[:, :], func=AF.Copy, scale=st8[:, 2:3]),
                      [r_x, r_st8], [r_xn])
                    for k in range(8):
                        PE(lambda e: e.transpose(out=psb(b)[:, k * 128:(k + 1) * 128], in_=xn_t[:, k * 128:(k + 1) * 128],
                                                 identity=ident[:, :]), [r_xn, r_ident], [ps_r[b]])
                    ev_t, r_ev = evt[t % 2]
                    V(lambda e: e.tensor_tensor(out=ev_t[:, :, :], in0=psb(b).rearrange("p (k q) -> p k q", k=8),
                                                in1=modT[:, 1, :].unsqueeze(2).to_broadcast([128, 8, 128]), op=ALU.mult),
                      [ps_r[b], r_modT], [r_ev])
                    G(lambda e: e.tensor_tensor(out=hT[:, :, 1 + t * 128:1 + (t + 1) * 128], in0=ev_t[:, :, :],
                                                in1=modT[:, 0, :].unsqueeze(2).to_broadcast([128, 8, 128]), op=ALU.add),
                      [r_ev, r_modT], [r_hTt[t], r_hT])
                wr = [sb(st, f"wr{i}", [128, 8, 512], BF16) for i in range(4)]
                cbb, r_cbb = sb(st, "cbb", [128, 1536], F32)
                DMA(cbb[:, :], conv_b[l:l + 1, :].to_broadcast([128, 1536]), w=[r_cbb])
                stg = [sb(st, f"stg{i}", [128, 512], BF16) for i in range(2)]
                wi = [0]

                def getw(src, K=8, N=512):
                    i = wi[0] % 4
                    wi[0] += 1
                    wt, r_wt = wr[i]
                    DMA(wt[:, 0:K, 0:N], src, r=[r_w["whY"], r_w["wqa"], r_w["wkv"], r_w["wg"]], w=[r_wt])
                    return wt, r_wt
                pbank = [2]

                def nextbank():
                    b = pbank[0]
                    pbank[0] = 2 + (pbank[0] - 2 + 1) % 4
                    return b
                si = [0]
                for ob in range(3):
                    wts = [getw(whY[ob * 3 + j]) for j in range(3)]
                    for t in range(NT):
                        b = nextbank()
                        for j in range(3):
                            wt, r_wt = wts[j]
                            for k in range(8):
                                PE(lambda e: e.matmul(ps[b][:, :], lhsT=hT[:, k, t * 128 + j:t * 128 + j + 128],
                                                      rhs=wt[:, k, :], start=(j == 0 and k == 0), stop=(j == 2 and k == 7)),
                                   [r_hT, r_wt], [ps_r[b]])
                        if ob == 0:
                            V(lambda e: e.tensor_tensor(out=hv[:, t, :], in0=ps[b][:, :], in1=cbb[:, 0:512], op=ALU.add),
                              [ps_r[b], r_cbb], [r_hv[t]])
                        else:
                            sg, r_sg = stg[si[0] % 2]
                            si[0] += 1
                            V(lambda e: e.tensor_tensor(out=sg[:, :], in0=ps[b][:, :], in1=cbb[:, ob * 512:(ob + 1) * 512],
                                                        op=ALU.add), [ps_r[b], r_cbb], [r_sg])
                            DMA(x12[ob - 1, t * 128:(t + 1) * 128, :], sg[:, :], r=[r_sg], w=[r_x12])
                for gi in range(4):
                    wt, r_wt = getw(wg[gi])
                    for t in range(NT):
                        b = nextbank()
                        for k in range(8):
                            PE(lambda e: e.matmul(ps[b][:, :], lhsT=hT[:, k, t * 128 + 1:t * 128 + 129], rhs=wt[:, k, :],
                                                  start=(k == 0), stop=(k == 7)), [r_hT, r_wt], [ps_r[b]])
                        sg, r_sg = stg[si[0] % 2]
                        si[0] += 1
                        A(lambda e: e.activation(out=sg[:, :], in_=ps[b][:, :], func=AF.Sigmoid), [ps_r[b]], [r_sg])
                        DMA(sigd[gi // 2, t * 128:(t + 1) * 128, (gi % 2) * 512:(gi % 2 + 1) * 512], sg[:, :],
                            r=[r_sg], w=[r_sigd])
                qT, r_qT = sb(st, "qT", [64, 8, L], BF16)
                kT, r_kT = sb(st, "kT", [64, 2, L], BF16)
                vt, r_vt = sb(st, "vt", [128, NT, 2, 65], BF16)
                cst, r_cs = sb(st, "cst", [128, NT, 128], F32)
                qgb, r_qgb = sb(st, "qgb", [128, 512], F32)
                kgb, r_kgb = sb(st, "kgb", [128, 128], F32)
                esk, r_esk = sb(st, "esk", [128, 8], F32)
                DMA(cst[:, :, :], c_cs, w=[r_cs])
                for h in range(8):
                    DMA(qgb[:, h * 64:(h + 1) * 64], q_gain[l:l + 1, :].to_broadcast([128, 64]), w=[r_qgb])
                for h in range(2):
                    DMA(kgb[:, h * 64:(h + 1) * 64], k_gain[l:l + 1, :].to_broadcast([128, 64]), w=[r_kgb])
                V(lambda e: e.tensor_scalar(out=qgb[:, :], in0=qgb[:, :], scalar1=0.125, scalar2=None, op0=ALU.mult),
                  [r_qgb], [r_qgb])
                DMA(esk[:, :], sink[l:l + 1, :].to_broadcast([128, 8]), w=[r_esk])
                A(lambda e: e.activation(out=esk[:, :], in_=esk[:, :], func=AF.Exp), [r_esk], [r_esk])
                V(lambda e: e.memset(vt[:, :, :, 64:65], 1.0), [], [r_vt])
                qbufs = [(sb(st, f"qf{i}", [128, 512], F32), sb(st, f"qsq{i}", [128, 512], F32),
                          sb(st, f"ss{i}", [128, 24], F32), sb(st, f"qr{i}", [128, 512], BF16)) for i in range(2)]
                wt_q, r_wt_q = getw(wqa)
                wt_k, r_wt_k = getw(wkv, 8, 256)

                def qk_front(kind, t):
                    b = nextbank()
                    if kind == 0:
                        for k in range(8):
                            PE(lambda e: e.matmul(ps[b][:, :], lhsT=hT[:, k, t * 128 + 1:t * 128 + 129], rhs=wt_q[:, k, :],
                                                  start=(k == 0), stop=(k == 7)), [r_hT, r_wt_q], [ps_r[b]])
                    else:
                        for k in range(8):
                            PE(lambda e: e.matmul(ps[b][:, 0:256], lhsT=hT[:, k, t * 128 + 1:t * 128 + 129], rhs=wt_k[:, k, 0:256],
                                                  start=(k == 0), stop=(k == 7)), [r_hT, r_wt_k], [ps_r[b]])
                    return b

                def qk_back(kind, t, b, ui):
                    (qf_, r_qf_), (qsq_, r_qsq_), (ss_, r_ss_), (qr_, r_qr_) = qbufs[ui % 2]
                    tb = ui % 2
                    if kind == 0:
                        yield from qk_norm_rope(8, ps[b][:, :], ps_r[b], qgb, r_qgb, cst, r_cs, t, qf_, r_qf_, qsq_, r_qsq_, ss_, r_ss_, qr_, r_qr_)
                        for h in range(8):
                            PE(lambda e: e.transpose(out=psb(tb)[0:64, h * 128:(h + 1) * 128], in_=qr_[:, h * 64:(h + 1) * 64],
                                                     identity=ident[:, :]), [r_qr_, r_ident], [ps_r[tb]])
                        A(lambda e: e.copy(out=qT[:, :, t * 128:(t + 1) * 128],
                                           in_=psb(tb)[0:64, :].rearrange("p (h q) -> p h q", h=8)), [ps_r[tb]], [r_qT])
                    else:
                        A(lambda e: e.copy(out=vt[:, t, :, 0:64], in_=ps[b][:, 128:256].rearrange("p (h d) -> p h d", h=2)),
                          [ps_r[b]], [r_vt])
                        yield from qk_norm_rope(2, ps[b][:, 0:128], ps_r[b], kgb, r_kgb, cst, r_cs, t, qf_, r_qf_, qsq_, r_qsq_, ss_, r_ss_, qr_, r_qr_)
                        for h in range(2):
                            PE(lambda e: e.transpose(out=psb(tb)[0:64, h * 128:(h + 1) * 128], in_=qr_[:, h * 64:(h + 1) * 64],
                                                     identity=ident[:, :]), [r_qr_, r_ident], [ps_r[tb]])
                        A(lambda e: e.copy(out=kT[:, :, t * 128:(t + 1) * 128],
                                           in_=psb(tb)[0:64, 0:256].rearrange("p (h q) -> p h q", h=2)), [ps_r[tb]], [r_kT])

                units = [(0, t) for t in range(NT)] + [(1, t) for t in range(NT)]
                pend = [qk_front(*units[0]), qk_front(*units[1])]
                for pi in range(0, len(units), 2):
                    nxt = [qk_front(*units[pi + 2 + k]) for k in range(2) if pi + 2 + k < len(units)]
                    alive = [qk_back(units[pi + k][0], units[pi + k][1], pend[k], pi + k) for k in range(2)]
                    while alive:
                        for g in list(alive):
                            try:
                                next(g)
                            except StopIteration:
                                alive.remove(g)
                    pend = nxt
                mk, r_mk = sb(st, "mk", [128, 2, 512], BF16)
                DMA(mk[:, 0, :], c_masks[0], w=[r_mk])
                DMA(mk[:, 1, :], c_masks[1], w=[r_mk])
                pT = [sb(st, f"pT{i}", [128, 3, 512], BF16) for i in range(2)]
                den, r_den = sb(st, "den", [128, 8], F32)
                dens = [(den, r_den), sb(st, "den2", [128, 8], F32)]

                def att_front(ai, n, kvh):
                    p_t, r_p = pT[ai % 2]
                    kbs = [kb for kb in (n - 1, n, n + 1) if 0 <= kb < NT]
                    for i, kb in enumerate(kbs):
                        b = nextbank()
                        PE(lambda e: e.matmul(ps[b][:, :], lhsT=kT[:, kvh, kb * 128:(kb + 1) * 128],
                                              rhs=qT[:, 4 * kvh:4 * kvh + 4, n * 128:(n + 1) * 128],
                                              start=True, stop=(kb == n)), [r_kT, r_qT], [ps_r[b]])
                        if kb != n:
                            mi = 0 if kb < n else 1
                            PE(lambda e: e.matmul(ps[b][:, :], lhsT=ident[:, :], rhs=mk[:, mi, :], start=False, stop=True),
                               [r_ident, r_mk], [ps_r[b]])
                        A(lambda e: e.activation(out=p_t[:, i, :], in_=ps[b][:, :], func=AF.Exp), [ps_r[b]], [r_p])

                def att_back(ai, n, kvh):
                    p_t, r_p = pT[ai % 2]
                    dn, r_dn = dens[ai % 2]
                    ob_ = 6 + (ai % 2)
                    kbs = [kb for kb in (n - 1, n, n + 1) if 0 <= kb < NT]
                    for h in range(4):
                        for i, kb in enumerate(kbs):
                            PE(lambda e: e.matmul(ps[ob_][:, h * 65:(h + 1) * 65], lhsT=p_t[:, i, h * 128:(h + 1) * 128],
                                                  rhs=vt[:, kb, kvh, :], start=(i == 0), stop=(i == len(kbs) - 1)),
                               [r_p, r_vt], [ps_r[ob_]])
                    o3 = ps[ob_][:, 0:260].rearrange("p (h d) -> p h d", h=4)
                    V(lambda e: e.tensor_tensor(out=dn[:, 0:4], in0=o3[:, :, 64], in1=esk[:, 4 * kvh:4 * kvh + 4],
                                                op=ALU.add), [ps_r[ob_], r_esk], [r_dn])
                    V(lambda e: e.reciprocal(out=dn[:, 4:8], in_=dn[:, 0:4]), [r_dn], [r_dn])
                    V(lambda e: e.tensor_tensor(
                        out=ybt[:, n, kvh * 256:(kvh + 1) * 256].rearrange("p (h d) -> p h d", h=4),
                        in0=o3[:, :, 0:64], in1=dn[:, 4:8].unsqueeze(2).to_broadcast([128, 4, 64]), op=ALU.mult),
                      [ps_r[ob_], r_dn], [r_ybt[n]])

                aunits = [(n, kvh) for n in range(NT) for kvh in range(2)]
                att_front(0, *aunits[0])
                for ai, (n, kvh) in enumerate(aunits):
                    if ai + 1 < len(aunits):
                        att_front(ai + 1, *aunits[ai + 1])
                    att_back(ai, n, kvh)
            S.barrier()
            with ExitStack() as st:
                Y, r_Y = sb(st, "Y", [128, 32, 512], BF16)
                Fb = [sb(st, f"Fb{i}", [128, 16, 256], BF16) for i in range(2)]
                Hb = [sb(st, f"Hb{i}", [128, 2, 512], F32) for i in range(2)]
                Gb = [sb(st, f"Gb{i}", [128, 16, 512], BF16) for i in range(4)]
                tm = [sb(st, f"tm{i}", [128, 512], F32) for i in range(4)]
                xg = [sb(st, f"xg{i}", [128, 512], BF16) for i in range(2)]
                r_Yc = [Res(f"Y{c}") for c in range(32)]
                gi_ = [0]
                for o in range(2):
                    for fc in range(16):
                        Ft, r_Ft = Fb[fc % 2]
                        Ht, r_Ht = Hb[fc % 2]
                        DMA(Ft[:, :, :], c_fwdF[fc], w=[r_Ft])
                        DMA(Ht[:, :, :], hfd[o, fc], r=[r_hfd], w=[r_Ht])
                        bre, bim = 2 * (fc % 2), 2 * (fc % 2) + 1
                        for tc in range(16):
                            PE(lambda e: e.matmul(ps[bre][:, :], lhsT=Ft[:, tc, 0:128], rhs=hv[:, tc, :],
                                                  start=(tc == 0), stop=(tc == 15)), [r_Ft, r_hv[tc]], [ps_r[bre]])
                        for tc in range(16):
                            PE(lambda e: e.matmul(ps[bim][:, :], lhsT=Ft[:, tc, 128:256], rhs=hv[:, tc, :],
                                                  start=(tc == 0), stop=(tc == 15)), [r_Ft, r_hv[tc]], [ps_r[bim]])
                        (t1, r1), (t2, r2), (t3, r3), (t4, r4) = tm
                        V(lambda e: e.tensor_tensor(out=t1[:, :], in0=ps[bre][:, :], in1=Ht[:, 0, :], op=ALU.mult),
                          [ps_r[bre], r_Ht], [r1])
                        V(lambda e: e.tensor_tensor(out=t2[:, :], in0=ps[bim][:, :], in1=Ht[:, 1, :], op=ALU.mult),
                          [ps_r[bim], r_Ht], [r2])
                        V(lambda e: e.tensor_tensor(out=t3[:, :], in0=ps[bre][:, :], in1=Ht[:, 1, :], op=ALU.mult),
                          [ps_r[bre], r_Ht], [r3])
                        V(lambda e: e.tensor_tensor(out=t4[:, :], in0=ps[bim][:, :], in1=Ht[:, 0, :], op=ALU.mult),
                          [ps_r[bim], r_Ht], [r4])
                        G(lambda e: e.tensor_tensor(out=Y[:, fc, :], in0=t1[:, :], in1=t2[:, :], op=ALU.subtract),
                          [r1, r2], [r_Yc[fc]])
                        G(lambda e: e.tensor_tensor(out=Y[:, 16 + fc, :], in0=t3[:, :], in1=t4[:, :], op=ALU.add),
                          [r3, r4], [r_Yc[16 + fc]])
                        if fc == 0:
                            V(lambda e: e.tensor_copy(out=Y[0:1, 0, :], in_=t1[0:1, :]), [r1], [r_Yc[0]])
                            V(lambda e: e.tensor_tensor(out=Y[0:1, 16, :], in0=ps[bim][0:1, :], in1=hnyq[0:1, o, :],
                                                        op=ALU.mult), [ps_r[bim], r_hnyq], [r_Yc[16]])
                    for tg in range(4):
                        gts = []
                        for half in range(2):
                            g_t, r_g = Gb[gi_[0] % 4]
                            gi_[0] += 1
                            DMA(g_t[:, :, :], c_invG[tg, :, half * 16:(half + 1) * 16, :], w=[r_g])
                            gts.append((g_t, r_g))
                        for ti in range(4):
                            t = tg * 4 + ti
                            b = 4 + (t % 4)
                            x_t, r_xg = xg[t % 2]
                            DMA(x_t[:, :], x12[o, t * 128:(t + 1) * 128, :], r=[r_x12], w=[r_xg])
                            for c in range(32):
                                g_t, r_g = gts[c // 16]
                                PE(lambda e: e.matmul(ps[b][:, :], lhsT=g_t[:, c % 16, ti * 128:(ti + 1) * 128], rhs=Y[:, c, :],
                                                      start=(c == 0), stop=(c == 31)), [r_g, r_Yc[c]], [ps_r[b]])
                            V(lambda e: e.tensor_tensor(out=hv[:, t, :], in0=ps[b][:, :], in1=x_t[:, :], op=ALU.mult),
                              [ps_r[b], r_xg], [r_hv[t]])
            S.barrier()
            with ExitStack() as st:
                wpa_t, r_wpa = sb(st, "wpa_t", [128, 4, 1024], BF16)
                wpb_t, r_wpb = sb(st, "wpb_t", [128, 4, 1024], BF16)
                wo_t, r_wo = sb(st, "wo_t", [128, 8, 1024], BF16)
                gtb, r_gtb = sb(st, "gtb", [128, D], F32)
                DMA(wpa_t[:, :, :], wpa_s, r=[r_w["wpa"]], w=[r_wpa])
                DMA(wpb_t[:, :, :], wpb_s, r=[r_w["wpb"]], w=[r_wpb])
                DMA(wo_t[:, :, :], wout_s, r=[r_w["wout"]], w=[r_wo])
                DMA(gtb[:, :], modd[s:s + 1, 2048:3072].to_broadcast([128, D]), r=[r_modd], w=[r_gtb])
                abT = [sb(st, f"abT{i}", [128, 8, 128], BF16) for i in range(2)]
                mT = [sb(st, f"mT{i}", [128, 8, 128], BF16) for i in range(2)]
                sg = [sb(st, f"sg{i}", [128, 2, D], BF16) for i in range(2)]
                m1, r_m1 = sb(st, "m1", [128, D], F32)
                m2, r_m2 = sb(st, "m2", [128, D], F32)
                mg, r_mg = sb(st, "mg", [128, D], BF16)
                xt = [sb(st, f"xt5{i}", [128, D], F32) for i in range(2)]
                mgs = [(mg, r_mg), sb(st, "mg2", [128, D], BF16)]
                o1, r_o1 = sb(st, "o1", [128, D], F32)

                def stX(t):
                    ab, r_ab = abT[t % 2]
                    s_t, r_s = sg[t % 2]
                    x_t, r_x = xt[t % 2]
                    mg_t, r_mgt = mgs[t % 2]
                    DMA(s_t[:, 0, :], sigd[0, t * 128:(t + 1) * 128, :], r=[r_sigd], w=[r_s])
                    DMA(s_t[:, 1, :], sigd[1, t * 128:(t + 1) * 128, :], r=[r_sigd], w=[r_s])
                    DMA(x_t[:, :], src_x[s, t * 128:(t + 1) * 128, :], r=[xres[s][t]], w=[r_x])
                    for k in range(4):
                        PE(lambda e: e.transpose(out=psb(0)[:, k * 128:(k + 1) * 128], in_=hv[:, t, k * 128:(k + 1) * 128],
                                                 identity=ident[:, :]), [r_hv[t], r_ident], [ps_r[0]])
                    for k in range(4):
                        PE(lambda e: e.transpose(out=psb(0)[:, (4 + k) * 128:(5 + k) * 128],
                                                 in_=ybt[:, t, k * 128:(k + 1) * 128], identity=ident[:, :]),
                           [r_ybt[t], r_ident], [ps_r[0]])
                    A(lambda e: e.copy(out=ab[:, :, :], in_=psb(0).rearrange("p (k q) -> p k q", k=8)), [ps_r[0]], [r_ab])
                    for nb in range(2):
                        for k in range(4):
                            PE(lambda e: e.matmul(ps[1 + nb][:, :], lhsT=ab[:, k, :], rhs=wpa_t[:, k, nb * 512:(nb + 1) * 512],
                                                  start=(k == 0), stop=(k == 3)), [r_ab, r_wpa], [ps_r[1 + nb]])
                        for k in range(4):
                            PE(lambda e: e.matmul(ps[3 + nb][:, :], lhsT=ab[:, 4 + k, :], rhs=wpb_t[:, k, nb * 512:(nb + 1) * 512],
                                                  start=(k == 0), stop=(k == 3)), [r_ab, r_wpb], [ps_r[3 + nb]])
                    for nb in range(2):
                        V(lambda e: e.tensor_tensor(out=m1[:, nb * 512:(nb + 1) * 512], in0=ps[1 + nb][:, :],
                                                    in1=s_t[:, 0, nb * 512:(nb + 1) * 512], op=ALU.mult),
                          [ps_r[1 + nb], r_s], [r_m1])
                        V(lambda e: e.tensor_tensor(out=m2[:, nb * 512:(nb + 1) * 512], in0=ps[3 + nb][:, :],
                                                    in1=s_t[:, 1, nb * 512:(nb + 1) * 512], op=ALU.mult),
                          [ps_r[3 + nb], r_s], [r_m2])
                    G(lambda e: e.tensor_tensor(out=mg_t[:, :], in0=m1[:, :], in1=m2[:, :], op=ALU.add), [r_m1, r_m2], [r_mgt])

                def stY(t):
                    mt, r_mt = mT[t % 2]
                    x_t, r_x = xt[t % 2]
                    mg_t, r_mgt = mgs[t % 2]
                    for k in range(8):
                        PE(lambda e: e.transpose(out=psb(5)[:, k * 128:(k + 1) * 128], in_=mg_t[:, k * 128:(k + 1) * 128],
                                                 identity=ident[:, :]), [r_mgt, r_ident], [ps_r[5]])
                    A(lambda e: e.copy(out=mt[:, :, :], in_=psb(5).rearrange("p (k q) -> p k q", k=8)), [ps_r[5]], [r_mt])
                    for nb in range(2):
                        for k in range(8):
                            PE(lambda e: e.matmul(ps[6 + nb][:, :], lhsT=mt[:, k, :], rhs=wo_t[:, k, nb * 512:(nb + 1) * 512],
                                                  start=(k == 0), stop=(k == 7)), [r_mt, r_wo], [ps_r[6 + nb]])
                        V(lambda e: e.tensor_tensor(out=o1[:, nb * 512:(nb + 1) * 512], in0=ps[6 + nb][:, :],
                                                    in1=gtb[:, nb * 512:(nb + 1) * 512], op=ALU.mult),
                          [ps_r[6 + nb], r_gtb], [r_o1])
                    G(lambda e: e.tensor_tensor(out=x_t[:, :], in0=x_t[:, :], in1=o1[:, :], op=ALU.add), [r_x, r_o1], [r_x])
                    DMA(y_out[s, t * 128:(t + 1) * 128, :], x_t[:, :], r=[r_x], w=[xres[s][t]])

                stX(0)
                for t in range(NT):
                    if t + 1 < NT:
                        stX(t + 1)
                    stY(t)
            S.barrier()

        r_x12 = Res("x12")
        r_sigd = Res("sigd")
        r_uvd = [Res("uvd0"), Res("uvd1")]

        tbl_pending = []

        def issue_tables(l, deferred=False):
            CH = 256 if deferred else 512
            for which, tab in ((0, peer_u), (1, peer_v)):
                tv = tab.ap()[l * 16384:(l + 1) * 16384, :]
                for ch in range(16384 // CH):
                    def go(which=which, tv=tv, ch=ch):
                        S.dma("tbl", lambda e: e.dma_start(out=uvds[l].ap()[ch * CH:(ch + 1) * CH, which * D:(which + 1) * D],
                                                           in_=tv[ch * CH:(ch + 1) * CH, :]), [], [Res()], eng="gpsimd")
                    if deferred:
                        tbl_pending.append(go)
                    else:
                        go()

        def tbl_trickle(n=1):
            for _ in range(n):
                if tbl_pending:
                    tbl_pending.pop(0)()

        def peer_seq(l, s, from_input=False):
            src_x = x_in if from_input else y_out
            if l > 0:
                tbl_trickle(len(tbl_pending))
            S.wait_all("gpsimd", "tbl")
            uvd = uvds[l]
            with ExitStack() as st:
                wq_t, r_wq = sb(st, "wq_t", [128, 8, 2048], BF16)
                DMA(wq_t[:, :, :], pwq_s, r=[r_w["pwq"]], w=[r_wq])
                kst, r_kst = sb(st, "kst", [128, 2, 128], F32)
                k12, r_k12 = sb(st, "k12", [128, 2, 128], BF16)
                DMA(kst[:, 0, :], peer_k1T[l], w=[r_kst])
                DMA(kst[:, 1, :], peer_k2T[l], w=[r_kst])
                V(lambda e: e.tensor_copy(out=k12[:, :, :], in_=kst[:, :, :]), [r_kst], [r_k12])
                a2b, r_a2b = sb(st, "a2b", [128, D], F32)
                sh2b, r_sh2b = sb(st, "sh2b", [128, D], F32)
                gt2b, r_gt2b = sb(st, "gt2b", [128, D], F32)
                DMA(sh2b[:, :], modd[s:s + 1, 3072:4096].to_broadcast([128, D]), r=[r_modd], w=[r_sh2b])
                DMA(a2b[:, :], modd[s:s + 1, 4096:5120].to_broadcast([128, D]), r=[r_modd], w=[r_a2b])
                DMA(gt2b[:, :], modd[s:s + 1, 5120:6144].to_broadcast([128, D]), r=[r_modd], w=[r_gt2b])
                iot, r_iot = sb(st, "iot", [128, 16], F32)
                DMA(iot[:, :], c_iota, w=[r_iot])
                ioti, r_ioti = sb(st, "ioti", [128, 256], I32)
                DMA(ioti[:, :], c_iota256, w=[r_ioti])
                xt = [sb(st, f"xp{i}", [128, D], F32) for i in range(3)]
                h2, r_h2 = sb(st, "h2", [128, D], F32)
                fin, r_fin = h2, r_h2
                h2b, r_h2b = sb(st, "h2b", [128, D], BF16)
                junk, r_junk = sb(st, "junkp", [128, D], BF16)
                junks = [(junk, r_junk), sb(st, "junk2", [128, D], BF16)]
                junkA, r_junkA = h2b, r_h2b
                st8, r_st8 = sb(st, "st8p", [128, 8], F32)
                h2T, r_h2T = sb(st, "h2T", [128, 8, 128], BF16)
                qT, r_qT = sb(st, "qTp", [128, 16, 128], BF16)
                sc, r_sc = sb(st, "sc", [128, 2, 8, 128], F32)
                r_sch = [[Res(f"sc{a}{b}") for b in range(8)] for a in range(2)]
                vv, r_vv = sb(st, "vv", [128, 2, 8, 16], F32)
                r_vvh = [[Res(f"vv{a}{b}") for b in range(8)] for a in range(2)]
                r_vvall = r_vvh[0] + r_vvh[1]
                r_vsh = [Res(f"vs{b}") for b in range(8)]
                ii, r_ii = sb(st, "ii", [128, 2, 8, 16], U32)
                iif, r_iif = sb(st, "iif", [128, 2, 8, 16], BF16)
                cand, r_cand = sb(st, "cand", [128, 8, 256], F32)
                r_cdh = [Res(f"cd{b}") for b in range(8)]
                vs, r_vs = sb(st, "vs", [128, 8, 16], F32)
                ic, r_ic = sb(st, "ic", [128, 8, 16], U32)
                icab, r_icab = sb(st, "icab", [128, 2, 8, 16], U32)
                icf, r_icf = sb(st, "icf", [128, 2, 8, 16], BF16)
                oh, r_oh = sb(st, "oh", [128, 8, 16, 16], BF16)
                iotb, r_iotb = sb(st, "iotb", [128, 16], BF16)
                V(lambda e: e.tensor_copy(out=iotb[:, :], in_=iot[:, :]), [r_iot], [r_iotb])
                ef, r_ef = sb(st, "ef", [128, 2, 8, 16], F32)
                idxf, r_idxf = sb(st, "idxf", [128, 128], F32)
                idxs = [sb(st, f"idx{i}", [128, 128], I32) for i in range(2)]
                ggs = [sb(st, f"gg{i}", [128, 8, 16], F32) for i in range(2)]
                sm, r_sm = sb(st, "sm", [128, 16], F32)
                aa, _ = sb(st, "aa", [128, 128], F32)
                ww, _ = sb(st, "ww", [128, 128], F32)
                NGB = 6
                gbuf = [sb(st, f"gbuf{i}", [128, 4, 2 * D], BF16) for i in range(NGB)]
                r_gs = [[Res(f"gs{b}_{i}") for i in range(4)] for b in range(NGB)]
                wds = [sb(st, f"wd{i}", [128, 4, 128], BF16) for i in range(2)]
                r_aaj = [Res(f"aa{j}") for j in range(128)]
                r_wwb = [Res(f"ww{j}") for j in range(32)]
                gi_ = [0]
                HP = ((4, 5), (6, 7))
                ACC = (2, 3)
                BT, BS = 0, 1

                def stageA(t):
                    slot = t % 2
                    x_t, r_x = xt[t % 3]
                    idx, r_idx = idxs[slot]
                    gg, r_gg = ggs[slot]
                    H2P = HP[slot]
                    DMA(x_t[:, :], src_x[s, t * 128:(t + 1) * 128, :], r=[xres[s][t]], w=[r_x])
                    yield
                    A(lambda e: e.activation(out=junkA[:, :], in_=x_t[:, :], func=AF.Square, accum_out=st8[:, 0:1]),
                      [r_x], [r_junkA, r_st8])
                    A(lambda e: e.activation(out=st8[:, 1:2], in_=st8[:, 0:1], func=AF.Sqrt, scale=1.0 / D,
                                             bias=eps_t[:, 0:1]), [r_st8], [r_st8])
                    yield
                    V(lambda e: e.reciprocal(out=st8[:, 2:3], in_=st8[:, 1:2]), [r_st8], [r_st8])
                    V(lambda e: e.scalar_tensor_tensor(out=h2[:, :], in0=x_t[:, :], scalar=st8[:, 2:3], in1=a2b[:, :],
                                                       op0=ALU.mult, op1=ALU.mult), [r_x, r_st8, r_a2b], [r_h2])
                    V(lambda e: e.tensor_tensor(out=h2[:, :], in0=h2[:, :], in1=sh2b[:, :], op=ALU.add), [r_h2, r_sh2b], [r_h2])
                    A(lambda e: e.copy(out=h2b[:, :], in_=h2[:, :]), [r_h2], [r_h2b])
                    yield
                    for nb in range(2):
                        V(lambda e: e.tensor_copy(out=ps[H2P[nb]][:, :], in_=h2[:, nb * 512:(nb + 1) * 512]),
                          [r_h2], [ps_r[H2P[nb]]])
                    for k in range(8):
                        PE(lambda e: e.transpose(out=psb(BT)[:, k * 128:(k + 1) * 128], in_=h2b[:, k * 128:(k + 1) * 128],
                                                 identity=ident[:, :]), [r_h2b, r_ident], [ps_r[BT]])
                    yield
                    A(lambda e: e.copy(out=h2T[:, :, :], in_=psb(BT).rearrange("p (k q) -> p k q", k=8)), [ps_r[BT]], [r_h2T])
                    yield
                    for rnd in range(4):
                        for c4 in range(4):
                            cb = rnd * 4 + c4
                            for k in range(8):
                                PE(lambda e: e.matmul(ps[BT][:, c4 * 128:(c4 + 1) * 128], lhsT=wq_t[:, k, cb * 128:(cb + 1) * 128],
                                                      rhs=h2T[:, k, :], start=(k == 0), stop=(k == 7)),
                                   [r_wq, r_h2T], [ps_r[BT]])
                        yield
                        A(lambda e: e.copy(out=qT[:, rnd * 4:(rnd + 1) * 4, :],
                                           in_=ps[BT][:, :].rearrange("p (c q) -> p c q", c=4)), [ps_r[BT]], [r_qT])
                    yield
                    for hf in range(2):
                        for rnd in range(2):
                            for h4 in range(4):
                                h = rnd * 4 + h4
                                PE(lambda e: e.matmul(ps[BS][:, h4 * 128:(h4 + 1) * 128], lhsT=qT[:, 2 * h + hf, :],
                                                      rhs=k12[:, hf, :], start=True, stop=True), [r_qT, r_k12], [ps_r[BS]])
                            yield
                            A(lambda e: e.copy(out=sc[:, hf, rnd * 4:(rnd + 1) * 4, :],
                                               in_=ps[BS][:, :].rearrange("p (h n) -> p h n", h=4)), [ps_r[BS]],
                              [r_sch[hf][rnd * 4 + i] for i in range(4)])
                    yield
                    for hf in range(2):
                        sci = sc[:, hf, :, :].bitcast(U32)
                        V(lambda e: e.tensor_scalar(out=sci, in0=sci, scalar1=7, scalar2=7, op0=ALU.logical_shift_right,
                                                    op1=ALU.logical_shift_left), r_sch[hf], r_sch[hf])
                        V(lambda e: e.tensor_tensor(out=sci, in0=sci,
                                                    in1=ioti[:, 0:128].bitcast(U32).unsqueeze(1).to_broadcast([128, 8, 128]),
                                                    op=ALU.bitwise_or), r_sch[hf] + [r_ioti], r_sch[hf])
                        yield
                    for hf in range(2):
                        for h0 in range(0, 8, 2):
                            for h in (h0, h0 + 1):
                                V(lambda e: e.max(out=vv[:, hf, h, 0:8], in_=sc[:, hf, h, :]), [r_sch[hf][h]], [r_vvh[hf][h]])
                            for h in (h0, h0 + 1):
                                V(lambda e: e.match_replace(out=sc[:, hf, h, :], in_to_replace=vv[:, hf, h, 0:8],
                                                            in_values=sc[:, hf, h, :], imm_value=-1e30),
                                  [r_sch[hf][h], r_vvh[hf][h]], [r_sch[hf][h]])
                            for h in (h0, h0 + 1):
                                V(lambda e: e.max(out=vv[:, hf, h, 8:16], in_=sc[:, hf, h, :]), [r_sch[hf][h]], [r_vvh[hf][h]])
                            yield
                    V(lambda e: e.tensor_single_scalar(out=ii[:, :, :, :], in_=vv[:, :, :, :].bitcast(U32), scalar=127,
                                                       op=ALU.bitwise_and), r_vvall, [r_ii])
                    V(lambda e: e.tensor_copy(out=iif[:, :, :, :], in_=ii[:, :, :, :]), [r_ii], [r_iif])
                    c4v = cand[:, :, :].rearrange("p h (a b) -> p h a b", a=16)
                    V(lambda e: e.tensor_tensor(out=c4v, in0=vv[:, 0, :, :].unsqueeze(3).to_broadcast([128, 8, 16, 16]),
                                                in1=vv[:, 1, :, :].unsqueeze(2).to_broadcast([128, 8, 16, 16]), op=ALU.add),
                      r_vvall, r_cdh)
                    yield
                    cdi = cand[:, :, :].bitcast(U32)
                    V(lambda e: e.tensor_scalar(out=cdi, in0=cdi, scalar1=8, scalar2=8, op0=ALU.logical_shift_right,
                                                op1=ALU.logical_shift_left), r_cdh, r_cdh)
                    V(lambda e: e.tensor_tensor(out=cdi, in0=cdi,
                                                in1=ioti[:, :].bitcast(U32).unsqueeze(1).to_broadcast([128, 8, 256]),
                                                op=ALU.bitwise_or), r_cdh + [r_ioti], r_cdh)
                    yield
                    for h0 in range(0, 8, 2):
                        for h in (h0, h0 + 1):
                            V(lambda e: e.max(out=vs[:, h, 0:8], in_=cand[:, h, :]), [r_cdh[h]], [r_vsh[h]])
                        for h in (h0, h0 + 1):
                            V(lambda e: e.match_replace(out=cand[:, h, :], in_to_replace=vs[:, h, 0:8],
                                                        in_values=cand[:, h, :], imm_value=-1e30), [r_cdh[h], r_vsh[h]], [r_cdh[h]])
                        for h in (h0, h0 + 1):
                            V(lambda e: e.max(out=vs[:, h, 8:16], in_=cand[:, h, :]), [r_cdh[h]], [r_vsh[h]])
                        yield
                    V(lambda e: e.tensor_single_scalar(out=ic[:, :, :], in_=vs[:, :, :].bitcast(U32), scalar=255,
                                                       op=ALU.bitwise_and), r_vsh, [r_ic])
                    V(lambda e: e.tensor_single_scalar(out=icab[:, 0, :, :], in_=ic[:, :, :], scalar=4,
                                                       op=ALU.logical_shift_right), [r_ic], [r_icab])
                    V(lambda e: e.tensor_single_scalar(out=icab[:, 1, :, :], in_=ic[:, :, :], scalar=15,
                                                       op=ALU.bitwise_and), [r_ic], [r_icab])
                    V(lambda e: e.tensor_copy(out=icf[:, :, :, :], in_=icab[:, :, :, :]), [r_icab], [r_icf])
                    yield
                    for hf in range(2):
                        V(lambda e: e.tensor_tensor(out=oh[:, :, :, :],
                                                    in0=icf[:, hf, :, :].unsqueeze(3).to_broadcast([128, 8, 16, 16]),
                                                    in1=iotb[:, :].unsqueeze(1).unsqueeze(1).to_broadcast([128, 8, 16, 16]),
                                                    op=ALU.is_equal), [r_icf, r_iotb], [r_oh])
                        yield
                        V(lambda e: e.tensor_tensor(out=oh[:, :, :, :], in0=oh[:, :, :, :],
                                                    in1=iif[:, hf, :, :].unsqueeze(2).to_broadcast([128, 8, 16, 16]),
                                                    op=ALU.mult), [r_oh, r_iif], [r_oh])
                        yield
                        V(lambda e: e.tensor_reduce(out=ef[:, hf, :, :], in_=oh[:, :, :, :], axis=AX.X, op=ALU.add),
                          [r_oh], [r_ef])
                        yield
                    V(lambda e: e.scalar_tensor_tensor(out=idxf[:, :], in0=ef[:, 0, :, :].rearrange("p h k -> p (h k)"),
                                                       scalar=128.0, in1=ef[:, 1, :, :].rearrange("p h k -> p (h k)"),
                                                       op0=ALU.mult, op1=ALU.add), [r_ef], [r_idxf])
                    V(lambda e: e.tensor_copy(out=idx[:, :], in_=idxf[:, :]), [r_idxf], [r_idx])
                    V(lambda e: e.tensor_tensor(out=gg[:, :, :], in0=vs[:, :, :],
                                                in1=vs[:, :, 0:1].to_broadcast([128, 8, 16]), op=ALU.subtract), r_vsh, [r_gg])
                    A(lambda e: e.activation(out=gg[:, :, :], in_=gg[:, :, :], func=AF.Exp), [r_gg], [r_gg])
                    yield
                    V(lambda e: e.tensor_reduce(out=sm[:, 0:8], in_=gg[:, :, :], axis=AX.X, op=ALU.add), [r_gg], [r_sm])
                    V(lambda e: e.reciprocal(out=sm[:, 8:16], in_=sm[:, 0:8]), [r_sm], [r_sm])
                    V(lambda e: e.tensor_tensor(out=gg[:, :, :], in0=gg[:, :, :],
                                                in1=sm[:, 8:16].unsqueeze(2).to_broadcast([128, 8, 16]), op=ALU.mult),
                      [r_gg, r_sm], [r_gg])
                    yield

                def stageB(t, agen, prev_tail):
                    slot = t % 2
                    x_t, r_x = xt[t % 3]
                    idx, r_idx = idxs[slot]
                    gg, r_gg = ggs[slot]
                    H2P = HP[slot]
                    NBT = 32
                    pbuf = {}
                    for bt in range(NBT + 1):
                        if bt < NBT:
                            b = gi_[0] % NGB
                            gi_[0] += 1
                            pbuf[bt] = b
                            g_t = gbuf[b][0]
                            for i in range(4):
                                j = bt * 4 + i
                                S.dma("gpsimd", lambda e: e.indirect_dma_start(
                                    out=g_t[:, i, :], out_offset=None, in_=uvd[:, :],
                                    in_offset=bass.IndirectOffsetOnAxis(ap=idx[:, j:j + 1], axis=0)),
                                    [r_idx], [r_gs[b][i]])
                            for i in range(4):
                                j = bt * 4 + i
                                jk, r_jk = junks[j % 2]
                                V(lambda e: e.scalar_tensor_tensor(
                                    out=jk[:, :], in0=g_t[:, i, 0:D], scalar=1.0, in1=ps2(H2P), op0=ALU.mult, op1=ALU.mult,
                                    accum_out=aa[:, j:j + 1]),
                                  [r_gs[b][i], ps_r[H2P[0]], ps_r[H2P[1]]], [r_jk, r_aaj[j]])
                            A(lambda e: e.activation(out=ww[:, bt * 4:(bt + 1) * 4], in_=aa[:, bt * 4:(bt + 1) * 4], func=AF.Gelu),
                              [r_aaj[bt * 4 + i] for i in range(4)], [r_wwb[bt]])
                        if bt == 0 and prev_tail is not None:
                            prev_tail()
                        if bt in (18, 22, 26, 30):
                            tbl_trickle(1)
                        if agen is not None:
                            npull = 1 if bt in (0, 2) else (0 if bt < 4 else (1 if bt < 16 else 2))
                            for _ in range(npull):
                                next(agen, None)
                        if bt >= 1:
                            pb_ = bt - 1
                            b = pbuf[pb_]
                            g_t = gbuf[b][0]
                            wd_t, r_wd = wds[pb_ % 2]
                            V(lambda e: e.tensor_tensor(out=ww[:, pb_ * 4:(pb_ + 1) * 4], in0=ww[:, pb_ * 4:(pb_ + 1) * 4],
                                                        in1=gg[:, :, :].rearrange("p h k -> p (h k)")[:, pb_ * 4:(pb_ + 1) * 4],
                                                        op=ALU.mult), [r_wwb[pb_], r_gg], [r_wwb[pb_]])
                            for i in range(4):
                                j = pb_ * 4 + i
                                A(lambda e: e.activation(out=wd_t[:, i, :], in_=ident[:, :], func=AF.Copy, scale=ww[:, j:j + 1]),
                                  [r_ident, r_wwb[pb_]], [r_wd])
                            for i in range(4):
                                j = pb_ * 4 + i
                                for nb in range(2):
                                    PE(lambda e: e.matmul(ps[ACC[nb]][:, :], lhsT=wd_t[:, i, :],
                                                          rhs=g_t[:, i, D + nb * 512:D + (nb + 1) * 512],
                                                          start=(j == 0), stop=(j == 127)),
                                       [r_wd, r_gs[b][i]], [ps_r[ACC[nb]]])
                    if agen is not None:
                        for _ in agen:
                            pass
                    def tail():
                        V(lambda e: e.tensor_tensor(out=fin[:, :], in0=ps2(ACC), in1=gt2b[:, :], op=ALU.mult),
                          [ps_r[ACC[0]], ps_r[ACC[1]], r_gt2b], [r_fin])
                        V(lambda e: e.tensor_tensor(out=x_t[:, :], in0=x_t[:, :], in1=fin[:, :], op=ALU.add), [r_x, r_fin], [r_x])
                        DMA(y_out[s, t * 128:(t + 1) * 128, :], x_t[:, :], r=[r_x], w=[xres[s][t]])
                    return tail

                for _ in stageA(0):
                    pass
                tail = None
                for t in range(NTOK_PEER):
                    agen = stageA(t + 1) if t + 1 < NTOK_PEER else None
                    tail = stageB(t, agen, tail)
                tail()
            S.barrier()

        if do_peer:
            issue_tables(0)
            if DEPTH > 1:
                issue_tables(1, deferred=True)
        for l in range(DEPTH):
            mod_phase(l)
            prep_weights(l)
            if do_mixer:
                filter_phase(l)
            if do_peer and l == 0:
                pass
            for s in range(NSEQ):
                if do_mixer:
                    with ExitStack() as sq:
                        hv, _ = sb(sq, "hv", [128, NT, 512], BF16)
                        ybt, _ = sb(sq, "ybt", [128, NT, 512], BF16)
                        r_hv = [Res(f"hv{t}") for t in range(NT)]
                        r_ybt = [Res(f"yb{t}") for t in range(NT)]
                        mixer_seq(l, s, hv, r_hv, ybt, r_ybt)
                if do_peer:
                    peer_seq(l, s, from_input=(l == 0 and not do_mixer))
        S.barrier(include_tbl=True)
        print("bass program: ops", S.nops, "waits", S.nwaits, flush=True)
    return nc


N_CORES = 8
_NC_CACHE = {}


def kernel(**inputs):
    f32 = np.float32
    x = np.concatenate([np.asarray(inputs["x_prompt"], f32), np.asarray(inputs["x_sample"], f32)], axis=0)
    c = np.concatenate([np.asarray(inputs["c_prompt"], f32), np.asarray(inputs["c_sample"], f32)], axis=0)
    nseq = x.shape[0] // N_CORES
    cst = _consts()
    shared = {}
    for k in ("w_mod", "b_mod", "g_norm1", "g_norm2", "w_in", "conv_w", "conv_b", "f_w1", "f_b1", "f_freq", "f_w2",
              "f_b2", "f_w3", "f_bias", "q_gain", "k_gain", "sink", "w_pa", "w_pb", "w_out", "peer_wq"):
        shared[k] = np.ascontiguousarray(np.asarray(inputs[k], f32))
    shared["peer_k1T"] = np.ascontiguousarray(np.asarray(inputs["peer_k1"], f32).transpose(0, 2, 1))
    shared["peer_k2T"] = np.ascontiguousarray(np.asarray(inputs["peer_k2"], f32).transpose(0, 2, 1))
    shared["peer_u"] = np.ascontiguousarray(np.asarray(inputs["peer_u"], f32).reshape(2 * 16384, D))
    shared["peer_v"] = np.ascontiguousarray(np.asarray(inputs["peer_v"], f32).reshape(2 * 16384, D))
    for k in ("fwdF", "invG", "zfT", "negt", "absdelta", "cs", "masks", "ident", "iota16", "iota256"):
        shared[k] = cst[k]
    in_maps = []
    for i in range(N_CORES):
        m = dict(shared)
        m["x"] = np.ascontiguousarray(x[i * nseq:(i + 1) * nseq])
        ci = c[i * nseq:(i + 1) * nseq]
        m["cT"] = np.ascontiguousarray(ci.T.reshape(8, 128, nseq).transpose(1, 0, 2))
        in_maps.append(m)
    if "nc" not in _NC_CACHE:
        _NC_CACHE["nc"] = build(NSEQ=nseq, DEPTH=2)
    res = run_bass_kernel_spmd(_NC_CACHE["nc"], in_maps, core_ids=list(range(N_CORES)))
    y = np.concatenate([np.asarray(r["y"], f32) for r in res.results], axis=0)
    nb = inputs["x_prompt"].shape[0]
    return (np.ascontiguousarray(y[:nb]), np.ascontiguousarray(y[nb:]))
```

```python
import math
from contextlib import ExitStack

import numpy as np
import ml_dtypes
import concourse.bass as bass
import concourse.mybir as mybir
from concourse.bass_utils import run_bass_kernel_spmd

F32 = mybir.dt.float32
BF16 = mybir.dt.bfloat16
U32 = mybir.dt.uint32
I32 = mybir.dt.int32
AF = mybir.ActivationFunctionType
ALU = mybir.AluOpType
AX = mybir.AxisListType

L = 2048
D = 1024
NT = 16
EPS = 1e-6
NEG = -30000.0
MAGIC = 12582912.0
TWO_PI = 2.0 * math.pi


class Res:
    __slots__ = ("name", "w", "r")

    def __init__(self, name=""):
        self.name = name
        self.w = None
        self.r = {}


class Sched:
    ENGS = ("sync", "scalar", "vector", "gpsimd", "tensor")
    RING = 8

    def __init__(self, nc, es):
        self.nc = nc
        self.eng = {e: getattr(nc, e) for e in self.ENGS}
        self.sems = []
        self.semid = {}
        self.cnt = {}
        for e in self.ENGS:
            self.semid[e] = len(self.sems)
            self.sems.append(es.enter_context(nc.semaphore("c_" + e)))
            self.cnt[e] = 0
        self.ring = {}
        self.ring_cnt = {}
        for q in ("sync", "gpsimd", "tbl"):
            ids = []
            for i in range(self.RING):
                ids.append(len(self.sems))
                self.sems.append(es.enter_context(nc.semaphore(f"d_{q}{i}")))
            self.ring[q] = ids
            self.ring_cnt[q] = 0
        self.known = {e: {} for e in self.ENGS}
        self.nwaits = 0
        self.nops = 0

    def _waits(self, eng, reads, writes, extra=()):
        need = {}

        def add(ev):
            if ev is None:
                return
            s, v = ev
            if need.get(s, 0) < v:
                need[s] = v
        for r in reads:
            add(r.w)
        for w in writes:
            add(w.w)
            for s, v in w.r.items():
                add((s, v))
        for ev in extra:
            add(ev)
        kn = self.known[eng]
        e = self.eng[eng]
        own = self.semid[eng]
        for s, v in need.items():
            if eng == "tensor" and s == own:
                continue
            if kn.get(s, 0) >= v:
                continue
            e.wait_ge(self.sems[s], v)
            kn[s] = v
            self.nwaits += 1

    def _mark(self, ev, reads, writes):
        s, v = ev
        for w in writes:
            w.w = ev
            w.r = {}
        for r in reads:
            if r in writes:
                continue
            if r.r.get(s, 0) < v:
                r.r[s] = v

    def op(self, eng, fn, reads=(), writes=()):
        self._waits(eng, reads, writes)
        ins = fn(self.eng[eng])
        self.cnt[eng] += 1
        ev = (self.semid[eng], self.cnt[eng])
        ins.then_inc(self.sems[ev[0]], 1)
        self._mark(ev, reads, writes)
        self.nops += 1
        return ev

    def dma(self, q, fn, reads=(), writes=(), eng=None):
        eng = eng or q
        i = self.ring_cnt[q]
        self.ring_cnt[q] = i + 1
        slot = self.ring[q][i % self.RING]
        rnd = i // self.RING
        extra = []
        if rnd > 0:
            extra.append((slot, 16 * rnd))
        self._waits(eng, reads, writes, extra)
        ins = fn(self.eng[eng])
        ev = (slot, 16 * (rnd + 1))
        ins.then_inc(self.sems[slot], 16)
        self._mark(ev, reads, writes)
        self.nops += 1
        return ev

    def ring_events(self, q):
        evs = []
        n = self.ring_cnt[q]
        for k in range(self.RING):
            cntk = (n - k + self.RING - 1) // self.RING if n > k else 0
            if cntk > 0:
                evs.append((self.ring[q][k], 16 * cntk))
        return evs

    def wait_all(self, eng, q):
        self._waits(eng, (), (), self.ring_events(q))

    def barrier(self, include_tbl=False):
        evs = [(self.semid[e], self.cnt[e]) for e in self.ENGS if self.cnt[e] > 0]
        for q in self.ring:
            if q == "tbl" and not include_tbl:
                continue
            n = self.ring_cnt[q]
            for k in range(self.RING):
                cntk = (n - k + self.RING - 1) // self.RING if n > k else 0
                if cntk > 0:
                    evs.append((self.ring[q][k], 16 * cntk))
        for eng in self.ENGS:
            self._waits(eng, (), (), evs)


_CONST = {}


def _consts():
    if _CONST:
        return _CONST
    bf = ml_dtypes.bfloat16
    N2 = 2 * L
    n = np.arange(L, dtype=np.int64)
    fwd = np.zeros((16, 128, 16, 256), np.float32)
    nn = (np.arange(16)[None, :] * 128 + np.arange(128)[:, None])
    for fc in range(16):
        f = fc * 128 + np.arange(128)
        ang = 2.0 * np.pi * ((nn[:, :, None] * f[None, None, :]) % N2) / N2
        fwd[fc, :, :, 0:128] = np.cos(ang)
        fwd[fc, :, :, 128:256] = -np.sin(ang)
    fwd[0, :, :, 128] = np.where(nn % 2 == 0, 1.0, -1.0)
    inv = np.zeros((4, 128, 32, 512), np.float32)
    for tg in range(4):
        t = tg * 512 + np.arange(512)
        for fc in range(16):
            f = fc * 128 + np.arange(128)
            ang = 2.0 * np.pi * ((f[:, None] * t[None, :]) % N2) / N2
            gre = (2.0 / N2) * np.cos(ang)
            gim = -(2.0 / N2) * np.sin(ang)
            if fc == 0:
                gre[0, :] = 1.0 / N2
                gim[0, :] = np.where(t % 2 == 0, 1.0, -1.0) / N2
            inv[tg, :, fc, :] = gre
            inv[tg, :, 16 + fc, :] = gim
    tl = np.linspace(0.0, 1.0, L, dtype=np.float32)
    w = (np.float32(2.0 * math.pi) * np.arange(L, dtype=np.float32) / np.float32(L)).astype(np.float32)
    bands = np.linspace(1e-4, 15.0, 16, dtype=np.float32)
    bw = (bands[:, None] * w[None, :]).astype(np.float32).astype(np.float64)
    zfT = np.concatenate([tl[None, :].astype(np.float64), np.cos(bw), -np.sin(bw)], axis=0).astype(np.float32)
    negt = np.zeros((128, 16), np.float32)
    negt[:, :] = -tl[nn]
    max_decay = math.log(1e-2) / 0.3
    min_decay = math.log(1e-2) / 1.5
    absdelta = np.abs(np.linspace(min_decay, max_decay, 512, dtype=np.float32)).reshape(1, 512)
    invf = (10000.0 ** (-np.arange(0, 64, 2, dtype=np.float32) / np.float32(64))).astype(np.float32)
    ang = (np.arange(L, dtype=np.float32)[:, None] * invf[None, :]).astype(np.float32).astype(np.float64)
    cs = np.zeros((L, 128), np.float32)
    cs[:, 0:32] = np.cos(ang)
    cs[:, 32:64] = np.cos(ang)
    cs[:, 64:96] = np.sin(ang)
    cs[:, 96:128] = np.sin(ang)
    cs = np.ascontiguousarray(cs.reshape(16, 128, 128).transpose(1, 0, 2))
    j = np.arange(128)[:, None]
    q = np.arange(128)[None, :]
    mprev = np.where(j >= q, 0.0, NEG).astype(np.float32)
    mnext = np.where(j <= q, 0.0, NEG).astype(np.float32)
    masks = np.stack([np.tile(mprev, (1, 4)), np.tile(mnext, (1, 4))], axis=0)
    _CONST.update(
        fwdF=fwd.astype(bf), invG=inv.astype(bf), zfT=zfT, negt=negt, absdelta=absdelta,
        cs=cs, masks=masks.astype(bf), ident=np.eye(128, dtype=np.float32).astype(bf),
        iota16=np.tile(np.arange(16, dtype=np.float32)[None, :], (128, 1)),
        iota256=np.tile(np.arange(256, dtype=np.int32)[None, :], (128, 1)),
    )
    return _CONST


def build(NSEQ=3, DEPTH=2, do_mixer=True, do_peer=True, NTOK_PEER=NT):
    nc = bass.Bass("TRN2", target_bir_lowering=False)

    def din(name, shape, dt=F32):
        return nc.dram_tensor(name, list(shape), dt, kind="ExternalInput").ap()

    def dscr(name, shape, dt):
        return nc.dram_tensor(name, list(shape), dt, kind="Internal").ap()

    x_in = din("x", [NSEQ, L, D])
    cT_in = din("cT", [128, 8, NSEQ])
    w_mod = din("w_mod", [2, D, 6 * D])
    b_mod = din("b_mod", [2, 6 * D])
    g_norm1 = din("g_norm1", [2, D])
    g_norm2 = din("g_norm2", [2, D])
    w_in = din("w_in", [2, D, 4352])
    conv_w = din("conv_w", [2, 3, 1536])
    conv_b = din("conv_b", [2, 1536])
    f_w1 = din("f_w1", [2, 33, 64])
    f_b1 = din("f_b1", [2, 64])
    f_freq = din("f_freq", [2, 64])
    f_w2 = din("f_w2", [2, 64, 64])
    f_b2 = din("f_b2", [2, 64])
    f_w3 = din("f_w3", [2, 64, 2048])
    f_bias = din("f_bias", [2, 2, 512])
    q_gain = din("q_gain", [2, 64])
    k_gain = din("k_gain", [2, 64])
    sink = din("sink", [2, 8])
    w_pa = din("w_pa", [2, 512, D])
    w_pb = din("w_pb", [2, 512, D])
    w_out = din("w_out", [2, D, D])
    peer_wq = din("peer_wq", [2, D, 2048])
    peer_k1T = din("peer_k1T", [2, 128, 128])
    peer_k2T = din("peer_k2T", [2, 128, 128])
    peer_u = nc.dram_tensor("peer_u", [2 * 16384, D], F32, kind="ExternalInput")
    peer_v = nc.dram_tensor("peer_v", [2 * 16384, D], F32, kind="ExternalInput")
    c_fwdF = din("fwdF", [16, 128, 16, 256], BF16)
    c_invG = din("invG", [4, 128, 32, 512], BF16)
    c_zfT = din("zfT", [33, L])
    c_negt = din("negt", [128, 16])
    c_absd = din("absdelta", [1, 512])
    c_cs = din("cs", [128, 16, 128])
    c_masks = din("masks", [2, 128, 512], BF16)
    c_ident = din("ident", [128, 128], BF16)
    c_iota = din("iota16", [128, 16])
    c_iota256 = din("iota256", [128, 256], I32)
    y_out = nc.dram_tensor("y", [NSEQ, L, D], F32, kind="ExternalOutput").ap()

    modd = dscr("modd", [NSEQ, 6 * D], F32)
    whY = dscr("whY", [9, 128, 8, 512], BF16)
    wqa = dscr("wqa", [128, 8, 512], BF16)
    wkv = dscr("wkv", [128, 8, 256], BF16)
    wg = dscr("wg", [4, 128, 8, 512], BF16)
    wpa_s = dscr("wpa_s", [128, 4, 1024], BF16)
    wpb_s = dscr("wpb_s", [128, 4, 1024], BF16)
    wout_s = dscr("wout_s", [128, 8, 1024], BF16)
    pwq_s = dscr("pwq_s", [128, 8, 2048], BF16)
    x12 = dscr("x12", [2, L, 512], BF16)
    sigd = dscr("sigd", [2, L, 1024], BF16)
    hfd = dscr("hfd", [2, 16, 128, 2, 512], F32)
    uvds = [nc.dram_tensor(f"uvd{i}", [16384, 2048], BF16, kind="Internal") for i in range(2)]

    with ExitStack() as es:
        S = Sched(nc, es)
        es.enter_context(nc.allow_non_contiguous_dma(reason="small strided param loads"))

        def V(fn, r=(), w=()):
            return S.op("vector", fn, r, w)

        def A(fn, r=(), w=()):
            return S.op("scalar", fn, r, w)

        def G(fn, r=(), w=()):
            return S.op("gpsimd", fn, r, w)

        def PE(fn, r=(), w=()):
            return S.op("tensor", fn, r, w)

        def DMA(out, in_, r=(), w=(), q="sync"):
            return S.dma(q, lambda e: e.dma_start(out=out, in_=in_), r, w)

        uid = [0]

        def sb(stack, name, shape, dt):
            uid[0] += 1
            t = stack.enter_context(nc.sbuf_tensor(f"sb{uid[0]}_{name}", list(shape), dt))
            return t, Res(name)

        psbig = es.enter_context(nc.psum_tensor("psbig", [128, 4096], F32))
        ps = [psbig[:, i * 512:(i + 1) * 512] for i in range(8)]
        ps_r = [Res(f"ps{i}") for i in range(8)]

        def ps2(banks):
            return psbig[:, banks[0] * 512:(banks[1] + 1) * 512]
        ident, r_ident = sb(es, "ident", [128, 128], BF16)
        DMA(ident[:, :], c_ident, w=[r_ident])
        xres = [[Res(f"x{s}_{t}") for t in range(NT)] for s in range(NSEQ)]
        r_modd = Res("modd")
        r_hfd = Res("hfd")
        r_w = {k: Res(k) for k in ("whY", "wqa", "wkv", "wg", "wpa", "wpb", "wout", "pwq")}
        hnyq, r_hnyq = sb(es, "hnyq", [1, 2, 512], F32)

        def mod_phase(l):
            with ExitStack() as st:
                cT, r_cT = sb(st, "cT", [128, 8, NSEQ], F32)
                scT, r_scT = sb(st, "scT", [128, 8, NSEQ], F32)
                modr, r_modr = sb(st, "modr", [NSEQ, 6 * D], F32)
                bmb, r_bmb = sb(st, "bmb", [NSEQ, 6 * D], F32)
                g1b, r_g1b = sb(st, "g1b", [NSEQ, D], F32)
                g2b, r_g2b = sb(st, "g2b", [NSEQ, D], F32)
                wst = [sb(st, f"wst{i}", [128, 8, 512], F32) for i in range(2)]
                DMA(cT[:, :, :], cT_in, w=[r_cT])
                DMA(bmb[:, :], b_mod[l:l + 1, :].to_broadcast([NSEQ, 6 * D]), w=[r_bmb])
                DMA(g1b[:, :], g_norm1[l:l + 1, :].to_broadcast([NSEQ, D]), w=[r_g1b])
                DMA(g2b[:, :], g_norm2[l:l + 1, :].to_broadcast([NSEQ, D]), w=[r_g2b])
                A(lambda e: e.activation(out=scT[:, :, :], in_=cT[:, :, :], func=AF.Silu), [r_cT], [r_scT])
                wv = w_mod[l].rearrange("(k p) n -> p k n", p=128)
                for nb in range(12):
                    wt, r_wt = wst[nb % 2]
                    DMA(wt[:, :, :], wv[:, :, nb * 512:(nb + 1) * 512], w=[r_wt])
                    b = nb % 2
                    for k in range(8):
                        PE(lambda e: e.matmul(ps[b][0:NSEQ, :], lhsT=scT[:, k, :], rhs=wt[:, k, :],
                                              start=(k == 0), stop=(k == 7)),
                           [r_scT, r_wt], [ps_r[b]])
                    V(lambda e: e.tensor_tensor(out=modr[:, nb * 512:(nb + 1) * 512], in0=ps[b][0:NSEQ, :],
                                                in1=bmb[:, nb * 512:(nb + 1) * 512], op=ALU.add),
                      [ps_r[b], r_bmb], [r_modr])
                for (c0, gb_, rg) in ((1024, g1b, r_g1b), (4096, g2b, r_g2b)):
                    V(lambda e: e.scalar_tensor_tensor(out=modr[:, c0:c0 + 1024], in0=modr[:, c0:c0 + 1024],
                                                       scalar=1.0, in1=gb_[:, :], op0=ALU.add, op1=ALU.mult),
                      [r_modr, rg], [r_modr])
                DMA(modd, modr[:, :], r=[r_modr], w=[r_modd])
            S.barrier()

        def prep_weights(l):
            with ExitStack() as st:
                stg = [sb(st, f"pstg{i}", [128, 8, 512], F32) for i in range(2)]
                obf = [sb(st, f"pobf{i}", [128, 8, 512], BF16) for i in range(2)]
                cwb, r_cwb = sb(st, "cwb", [128, 3, 1536], F32)
                for j in range(3):
                    DMA(cwb[:, j, :], conv_w[l, j:j + 1, :].to_broadcast([128, 1536]), w=[r_cwb])
                cnt = [0]

                def one(src, dst, K, N, rdst, mul=None):
                    i = cnt[0] % 2
                    cnt[0] += 1
                    s_t, r_s = stg[i]
                    o_t, r_o = obf[i]
                    DMA(s_t[:, 0:K, 0:N], src, w=[r_s])
                    if mul is None:
                        if cnt[0] % 2 == 0:
                            V(lambda e: e.tensor_copy(out=o_t[:, 0:K, 0:N], in_=s_t[:, 0:K, 0:N]), [r_s], [r_o])
                        else:
                            A(lambda e: e.copy(out=o_t[:, 0:K, 0:N], in_=s_t[:, 0:K, 0:N]), [r_s], [r_o])
                    else:
                        for k in range(K):
                            V(lambda e: e.tensor_tensor(out=o_t[:, k, 0:N], in0=s_t[:, k, 0:N], in1=mul,
                                                        op=ALU.mult), [r_s, r_cwb], [r_o])
                    DMA(dst, o_t[:, 0:K, 0:N], r=[r_o], w=[rdst])

                wiv = w_in[l].rearrange("(k p) n -> p k n", p=128)
                for ob in range(3):
                    for j in range(3):
                        one(wiv[:, :, ob * 512:(ob + 1) * 512], whY[ob * 3 + j], 8, 512, r_w["whY"],
                            mul=cwb[:, j, ob * 512:(ob + 1) * 512])
                one(wiv[:, :, 1536:2048], wqa, 8, 512, r_w["wqa"])
                one(wiv[:, :, 2048:2304], wkv, 8, 256, r_w["wkv"])
                for gi in range(4):
                    one(wiv[:, :, 2304 + gi * 512:2304 + (gi + 1) * 512], wg[gi], 8, 512, r_w["wg"])
                pav = w_pa[l].rearrange("(k p) n -> p k n", p=128)
                pbv = w_pb[l].rearrange("(k p) n -> p k n", p=128)
                wov = w_out[l].rearrange("(k p) n -> p k n", p=128)
                pqv = peer_wq[l].rearrange("(k p) n -> p k n", p=128)
                for nb in range(2):
                    one(pav[:, :, nb * 512:(nb + 1) * 512], wpa_s[:, :, nb * 512:(nb + 1) * 512], 4, 512, r_w["wpa"])
                    one(pbv[:, :, nb * 512:(nb + 1) * 512], wpb_s[:, :, nb * 512:(nb + 1) * 512], 4, 512, r_w["wpb"])
                    one(wov[:, :, nb * 512:(nb + 1) * 512], wout_s[:, :, nb * 512:(nb + 1) * 512], 8, 512, r_w["wout"])
                for nb in range(4):
                    one(pqv[:, :, nb * 512:(nb + 1) * 512], pwq_s[:, :, nb * 512:(nb + 1) * 512], 8, 512, r_w["pwq"])
            S.barrier()

        def range_reduce_sin(st, dst, src_ps, bcol, fcol, r_cols, r_dst, psr, npart, tmp, r_tmp, tmp2, r_tmp2):
            V(lambda e: e.tensor_scalar(out=tmp[0:npart, :], in0=src_ps, scalar1=bcol, scalar2=fcol,
                                        op0=ALU.add, op1=ALU.mult), [psr, r_cols], [r_tmp])
            V(lambda e: e.tensor_scalar(out=tmp2[0:npart, :], in0=tmp[0:npart, :], scalar1=1.0 / TWO_PI,
                                        scalar2=MAGIC, op0=ALU.mult, op1=ALU.add), [r_tmp], [r_tmp2])
            V(lambda e: e.tensor_scalar(out=tmp2[0:npart, :], in0=tmp2[0:npart, :], scalar1=MAGIC,
                                        scalar2=-TWO_PI, op0=ALU.subtract, op1=ALU.mult), [r_tmp2], [r_tmp2])
            V(lambda e: e.tensor_tensor(out=tmp[0:npart, :], in0=tmp[0:npart, :], in1=tmp2[0:npart, :],
                                        op=ALU.add), [r_tmp, r_tmp2], [r_tmp])
            V(lambda e: e.tensor_scalar(out=tmp[0:npart, :], in0=tmp[0:npart, :], scalar1=3.1415925,
                                        scalar2=-3.1415925, op0=ALU.min, op1=ALU.max), [r_tmp], [r_tmp])
            A(lambda e: e.activation(out=dst, in_=tmp[0:npart, :], func=AF.Sin), [r_tmp], [r_dst])

        def filter_phase(l):
            with ExitStack() as st:
                zf, r_zf = sb(st, "zf", [33, L], F32)
                w1, r_w1 = sb(st, "fw1", [33, 64], F32)
                w2, r_w2 = sb(st, "fw2", [64, 64], F32)
                w3, r_w3 = sb(st, "fw3", [64, 2048], F32)
                cols, r_cols = sb(st, "fcols", [64, 4], F32)
                a1, r_a1 = sb(st, "fa1", [64, L], F32)
                a2, r_a2 = sb(st, "fa2", [64, L], F32)
                tmp, r_tmp = sb(st, "ftmp", [128, 512], F32)
                tmp2, r_tmp2 = sb(st, "ftmp2", [128, 512], F32)
                absd, r_absd = sb(st, "absd", [128, 512], F32)
                negt, r_negt = sb(st, "negt", [128, 16], F32)
                dec, r_dec = sb(st, "dec", [128, 512], F32)
                fa, r_fa = sb(st, "fa", [128, 16, 1024], BF16)
                fb, r_fb = sb(st, "fb", [128, 16, 1024], BF16)
                h0, r_h0 = sb(st, "h0", [128, 512], F32)
                h1, r_h1 = sb(st, "h1", [128, 512], F32)
                fbias, r_fbias = sb(st, "fbias", [128, 2, 512], F32)
                Fb = [sb(st, f"Fbf{i}", [128, 16, 256], BF16) for i in range(2)]
                ho = [sb(st, f"hout{i}", [128, 2, 512], F32) for i in range(2)]
                DMA(zf[:, :], c_zfT, w=[r_zf])
                DMA(w1[:, :], f_w1[l], w=[r_w1])
                DMA(w2[:, :], f_w2[l], w=[r_w2])
                DMA(w3[:, :], f_w3[l], w=[r_w3])
                DMA(cols[:, 0:1], f_b1[l:l + 1, :].rearrange("o n -> n o"), w=[r_cols])
                DMA(cols[:, 1:2], f_freq[l:l + 1, :].rearrange("o n -> n o"), w=[r_cols])
                DMA(cols[:, 2:3], f_b2[l:l + 1, :].rearrange("o n -> n o"), w=[r_cols])
                DMA(absd[:, :], c_absd.to_broadcast([128, 512]), w=[r_absd])
                DMA(negt[:, :], c_negt, w=[r_negt])
                for o in range(2):
                    DMA(fbias[:, o, :], f_bias[l, o:o + 1, :].to_broadcast([128, 512]), w=[r_fbias])
                for ch in range(4):
                    b = ch % 2
                    PE(lambda e: e.matmul(ps[b][0:64, :], lhsT=w1[:, :], rhs=zf[:, ch * 512:(ch + 1) * 512],
                                          start=True, stop=True), [r_w1, r_zf], [ps_r[b]])
                    range_reduce_sin(st, a1[:, ch * 512:(ch + 1) * 512], ps[b][0:64, :], cols[:, 0:1], cols[:, 1:2],
                                     r_cols, r_a1, ps_r[b], 64, tmp, r_tmp, tmp2, r_tmp2)
                for ch in range(4):
                    b = ch % 2
                    PE(lambda e: e.matmul(ps[b][0:64, :], lhsT=w2[:, :], rhs=a1[:, ch * 512:(ch + 1) * 512],
                                          start=True, stop=True), [r_w2, r_a1], [ps_r[b]])
                    range_reduce_sin(st, a2[:, ch * 512:(ch + 1) * 512], ps[b][0:64, :], cols[:, 2:3], cols[:, 1:2],
                                     r_cols, r_a2, ps_r[b], 64, tmp, r_tmp, tmp2, r_tmp2)
                for tc in range(16):
                    A(lambda e: e.activation(out=dec[:, :], in_=absd[:, :], func=AF.Exp, scale=negt[:, tc:tc + 1]),
                      [r_absd, r_negt], [r_dec])
                    for o in range(2):
                        for dr in range(2):
                            b = 2 + dr
                            cb = (o * 2 + dr) * 512
                            PE(lambda e: e.matmul(ps[b][:, :], lhsT=a2[:, tc * 128:(tc + 1) * 128],
                                                  rhs=w3[:, cb:cb + 512], start=True, stop=True),
                               [r_a2, r_w3], [ps_r[b]])
                            hh, r_hh = (h0, r_h0) if dr == 0 else (h1, r_h1)
                            V(lambda e: e.tensor_tensor(out=hh[:, :], in0=ps[b][:, :], in1=dec[:, :], op=ALU.mult),
                              [ps_r[b], r_dec], [r_hh])
                        if tc == 0:
                            V(lambda e: e.memset(h1[0:1, :], 0.0), [], [r_h1])
                        V(lambda e: e.tensor_tensor(out=fa[:, tc, o * 512:(o + 1) * 512], in0=h0[:, :], in1=h1[:, :],
                                                    op=ALU.add), [r_h0, r_h1], [r_fa])
                        V(lambda e: e.tensor_tensor(out=fb[:, tc, o * 512:(o + 1) * 512], in0=h0[:, :], in1=h1[:, :],
                                                    op=ALU.subtract), [r_h0, r_h1], [r_fb])
                for fc in range(16):
                    Ft, r_Ft = Fb[fc % 2]
                    DMA(Ft[:, :, :], c_fwdF[fc], w=[r_Ft])
                    hot, r_hot = ho[fc % 2]
                    for o in range(2):
                        bre, bim = 4 + 2 * (o % 2), 5 + 2 * (o % 2)
                        for tc in range(16):
                            PE(lambda e: e.matmul(ps[bre][:, :], lhsT=Ft[:, tc, 0:128],
                                                  rhs=fa[:, tc, o * 512:(o + 1) * 512], start=(tc == 0), stop=(tc == 15)),
                               [r_Ft, r_fa], [ps_r[bre]])
                        for tc in range(16):
                            PE(lambda e: e.matmul(ps[bim][:, :], lhsT=Ft[:, tc, 128:256],
                                                  rhs=fb[:, tc, o * 512:(o + 1) * 512], start=(tc == 0), stop=(tc == 15)),
                               [r_Ft, r_fb], [ps_r[bim]])
                        V(lambda e: e.tensor_tensor(out=hot[:, 0, :], in0=ps[bre][:, :], in1=fbias[:, o, :], op=ALU.add),
                          [ps_r[bre], r_fbias], [r_hot])
                        A(lambda e: e.copy(out=hot[:, 1, :], in_=ps[bim][:, :]), [ps_r[bim]], [r_hot])
                        DMA(hfd[o, fc], hot[:, :, :], r=[r_hot], w=[r_hfd])
                        if fc == 0:
                            for tc in range(16):
                                PE(lambda e: e.matmul(ps[0][0:1, :], lhsT=Ft[:, tc, 128:129],
                                                      rhs=fa[:, tc, o * 512:(o + 1) * 512], start=(tc == 0), stop=(tc == 15)),
                                   [r_Ft, r_fa], [ps_r[0]])
                            V(lambda e: e.tensor_tensor(out=hnyq[0:1, o, :], in0=ps[0][0:1, :], in1=fbias[0:1, o, :],
                                                        op=ALU.add), [ps_r[0], r_fbias], [r_hnyq])
            S.barrier()

        def qk_norm_rope(nh, src_ps, psr, gainb, r_gain, cst, r_cs, t, qf, r_qf, qsq, r_qsq, ss, r_ss, qr, r_qr):
            W = nh * 64
            A(lambda e: e.copy(out=qf[:, 0:W], in_=src_ps), [psr], [r_qf])
            yield
            V(lambda e: e.tensor_tensor(out=qsq[:, 0:W], in0=qf[:, 0:W], in1=qf[:, 0:W], op=ALU.mult), [r_qf], [r_qsq])
            yield
            V(lambda e: e.tensor_reduce(out=ss[:, 0:nh], in_=qsq[:, 0:W].rearrange("p (h d) -> p h d", h=nh),
                                        axis=AX.X, op=ALU.add), [r_qsq], [r_ss])
            yield
            A(lambda e: e.activation(out=ss[:, 8:8 + nh], in_=ss[:, 0:nh], func=AF.Sqrt, scale=1.0 / 64.0, bias=eps_t[:, 0:1]),
              [r_ss], [r_ss])
            yield
            V(lambda e: e.reciprocal(out=ss[:, 16:16 + nh], in_=ss[:, 8:8 + nh]), [r_ss], [r_ss])
            yield
            q3 = qf[:, 0:W].rearrange("p (h d) -> p h d", h=nh)
            V(lambda e: e.tensor_tensor(out=q3, in0=q3, in1=ss[:, 16:16 + nh].unsqueeze(2).to_broadcast([128, nh, 64]),
                                        op=ALU.mult), [r_qf, r_ss], [r_qf])
            yield
            V(lambda e: e.tensor_tensor(out=qf[:, 0:W], in0=qf[:, 0:W], in1=gainb[:, 0:W], op=ALU.mult),
              [r_qf, r_gain], [r_qf])
            yield
            cosb = cst[:, t, 0:64].unsqueeze(1).to_broadcast([128, nh, 64])
            sinb = cst[:, t, 64:128].unsqueeze(1).to_broadcast([128, nh, 64])
            s3 = qsq[:, 0:W].rearrange("p (h d) -> p h d", h=nh)
            V(lambda e: e.tensor_tensor(out=s3, in0=q3, in1=sinb, op=ALU.mult), [r_qf, r_cs], [r_qsq])
            yield
            V(lambda e: e.tensor_tensor(out=q3, in0=q3, in1=cosb, op=ALU.mult), [r_qf, r_cs], [r_qf])
            yield
            r3 = qr[:, 0:W].rearrange("p (h d) -> p h d", h=nh)
            V(lambda e: e.tensor_tensor(out=r3[:, :, 0:32], in0=q3[:, :, 0:32], in1=s3[:, :, 32:64], op=ALU.subtract),
              [r_qf, r_qsq], [r_qr])
            yield
            V(lambda e: e.tensor_tensor(out=r3[:, :, 32:64], in0=q3[:, :, 32:64], in1=s3[:, :, 0:32], op=ALU.add),
              [r_qf, r_qsq], [r_qr])
            yield

        eps_t, r_eps = sb(es, "eps_t", [128, 1], F32)
        V(lambda e: e.memset(eps_t[:, :], EPS), [], [r_eps])

        def psb(i):
            return ps[i][:, :].bitcast(BF16)

        def mixer_seq(l, s, hv, r_hv, ybt, r_ybt):
            src_x = x_in if l == 0 else y_out
            with ExitStack() as st:
                hT, r_hT = sb(st, "hT", [128, 8, L + 2], BF16)
                modT, r_modT = sb(st, "modT", [128, 2, 8], F32)
                DMA(modT[:, 0, :], modd[s, 0:1024].rearrange("(k p) -> p k", p=128), r=[r_modd], w=[r_modT])
                DMA(modT[:, 1, :], modd[s, 1024:2048].rearrange("(k p) -> p k", p=128), r=[r_modd], w=[r_modT])
                V(lambda e: e.memset(hT[:, :, 0:1], 0.0), [], [r_hT])
                V(lambda e: e.memset(hT[:, :, L + 1:L + 2], 0.0), [], [r_hT])
                xt = [sb(st, f"xt{i}", [128, D], F32) for i in range(2)]
                junk, r_junk = sb(st, "junk", [128, D], BF16)
                xn = [sb(st, f"xn{i}", [128, D], BF16) for i in range(2)]
                st8, r_st8 = sb(st, "st8", [128, 8], F32)
                evt = [sb(st, f"evt{i}", [128, 8, 128], F32) for i in range(2)]
                r_hTt = [Res(f"hT{t}") for t in range(NT)]
                for t in range(NT):
                    x_t, r_x = xt[t % 2]
                    xn_t, r_xn = xn[t % 2]
                    b = t % 2
                    DMA(x_t[:, :], src_x[s, t * 128:(t + 1) * 128, :], r=[xres[s][t]], w=[r_x])
                    A(lambda e: e.activation(out=junk[:, :], in_=x_t[:, :], func=AF.Square, accum_out=st8[:, 0:1]),
                      [r_x], [r_junk, r_st8])
                    A(lambda e: e.activation(out=st8[:, 1:2], in_=st8[:, 0:1], func=AF.Sqrt, scale=1.0 / D,
                                             bias=eps_t[:, 0:1]), [r_st8], [r_st8])
                    V(lambda e: e.reciprocal(out=st8[:, 2:3], in_=st8[:, 1:2]), [r_st8], [r_st8])
                    A(lambda e: e.activation(out=xn_t[:, :], in_=x_t[:, :], func=AF.Copy, scale=st8[:, 2:3]),
                      [r_x, r_st8], [r_xn])
                    for k in range(8):
                        PE(lambda e: e.transpose(out=psb(b)[:, k * 128:(k + 1) * 128], in_=xn_t[:, k * 128:(k + 1) * 128],
                                                 identity=ident[:, :]), [r_xn, r_ident], [ps_r[b]])
                    ev_t, r_ev = evt[t % 2]
                    V(lambda e: e.tensor_tensor(out=ev_t[:, :, :], in0=psb(b).rearrange("p (k q) -> p k q", k=8),
                                                in1=modT[:, 1, :].unsqueeze(2).to_broadcast([128, 8, 128]), op=ALU.mult),
                      [ps_r[b], r_modT], [r_ev])
                    G(lambda e: e.tensor_tensor(out=hT[:, :, 1 + t * 128:1 + (t + 1) * 128], in0=ev_t[:, :, :],
                                                in1=modT[:, 0, :].unsqueeze(2).to_broadcast([128, 8, 128]), op=ALU.add),
                      [r_ev, r_modT], [r_hTt[t], r_hT])
                wr = [sb(st, f"wr{i}", [128, 8, 512], BF16) for i in range(4)]
                cbb, r_cbb = sb(st, "cbb", [128, 1536], F32)
                DMA(cbb[:, :], conv_b[l:l + 1, :].to_broadcast([128, 1536]), w=[r_cbb])
                stg = [sb(st, f"stg{i}", [128, 512], BF16) for i in range(4)]
                wi = [0]

                def getw(src, K=8, N=512):
                    i = wi[0] % 4
                    wi[0] += 1
                    wt, r_wt = wr[i]
                    DMA(wt[:, 0:K, 0:N], src, r=[r_w["whY"], r_w["wqa"], r_w["wkv"], r_w["wg"]], w=[r_wt])
                    return wt, r_wt
                pbank = [2]

                def nextbank():
                    b = pbank[0]
                    pbank[0] = 2 + (pbank[0] - 2 + 1) % 4
                    return b
                si = [0]
                for ob in range(3):
                    wts = [getw(whY[ob * 3 + j]) for j in range(3)]
                    for t in range(NT):
                        b = nextbank()
                        for j in range(3):
                            wt, r_wt = wts[j]
                            for k in range(8):
                                PE(lambda e: e.matmul(ps[b][:, :], lhsT=hT[:, k, t * 128 + j:t * 128 + j + 128],
                                                      rhs=wt[:, k, :], start=(j == 0 and k == 0), stop=(j == 2 and k == 7)),
                                   [r_hT, r_wt], [ps_r[b]])
                        if ob == 0:
                            V(lambda e: e.tensor_tensor(out=hv[:, t, :], in0=ps[b][:, :], in1=cbb[:, 0:512], op=ALU.add),
                              [ps_r[b], r_cbb], [r_hv[t]])
                        else:
                            sg, r_sg = stg[si[0] % 4]
                            si[0] += 1
                            V(lambda e: e.tensor_tensor(out=sg[:, :], in0=ps[b][:, :], in1=cbb[:, ob * 512:(ob + 1) * 512],
                                                        op=ALU.add), [ps_r[b], r_cbb], [r_sg])
                            DMA(x12[ob - 1, t * 128:(t + 1) * 128, :], sg[:, :], r=[r_sg], w=[r_x12])
                for gi in range(4):
                    wt, r_wt = getw(wg[gi])
                    for t in range(NT):
                        b = nextbank()
                        for k in range(8):
                            PE(lambda e: e.matmul(ps[b][:, :], lhsT=hT[:, k, t * 128 + 1:t * 128 + 129], rhs=wt[:, k, :],
                                                  start=(k == 0), stop=(k == 7)), [r_hT, r_wt], [ps_r[b]])
                        sg, r_sg = stg[si[0] % 4]
                        si[0] += 1
                        A(lambda e: e.activation(out=sg[:, :], in_=ps[b][:, :], func=AF.Sigmoid), [ps_r[b]], [r_sg])
                        DMA(sigd[gi // 2, t * 128:(t + 1) * 128, (gi % 2) * 512:(gi % 2 + 1) * 512], sg[:, :],
                            r=[r_sg], w=[r_sigd])
                qT, r_qT = sb(st, "qT", [64, 8, L], BF16)
                kT, r_kT = sb(st, "kT", [64, 2, L], BF16)
                vt, r_vt = sb(st, "vt", [128, NT, 2, 65], BF16)
                cst, r_cs = sb(st, "cst", [128, NT, 128], F32)
                qgb, r_qgb = sb(st, "qgb", [128, 512], F32)
                kgb, r_kgb = sb(st, "kgb", [128, 128], F32)
                esk, r_esk = sb(st, "esk", [128, 8], F32)
                DMA(cst[:, :, :], c_cs, w=[r_cs])
                for h in range(8):
                    DMA(qgb[:, h * 64:(h + 1) * 64], q_gain[l:l + 1, :].to_broadcast([128, 64]), w=[r_qgb])
                for h in range(2):
                    DMA(kgb[:, h * 64:(h + 1) * 64], k_gain[l:l + 1, :].to_broadcast([128, 64]), w=[r_kgb])
                V(lambda e: e.tensor_scalar(out=qgb[:, :], in0=qgb[:, :], scalar1=0.125, scalar2=None, op0=ALU.mult),
                  [r_qgb], [r_qgb])
                DMA(esk[:, :], sink[l:l + 1, :].to_broadcast([128, 8]), w=[r_esk])
                A(lambda e: e.activation(out=esk[:, :], in_=esk[:, :], func=AF.Exp), [r_esk], [r_esk])
                V(lambda e: e.memset(vt[:, :, :, 64:65], 1.0), [], [r_vt])
                qbufs = [(sb(st, f"qf{i}", [128, 512], F32), sb(st, f"qsq{i}", [128, 512], F32),
                          sb(st, f"ss{i}", [128, 24], F32), sb(st, f"qr{i}", [128, 512], BF16)) for i in range(2)]
                wt_q, r_wt_q = getw(wqa)
                wt_k, r_wt_k = getw(wkv, 8, 256)

                def qk_front(kind, t):
                    b = nextbank()
                    if kind == 0:
                        for k in range(8):
                            PE(lambda e: e.matmul(ps[b][:, :], lhsT=hT[:, k, t * 128 + 1:t * 128 + 129], rhs=wt_q[:, k, :],
                                                  start=(k == 0), stop=(k == 7)), [r_hT, r_wt_q], [ps_r[b]])
                    else:
                        for k in range(8):
                            PE(lambda e: e.matmul(ps[b][:, 0:256], lhsT=hT[:, k, t * 128 + 1:t * 128 + 129], rhs=wt_k[:, k, 0:256],
                                                  start=(k == 0), stop=(k == 7)), [r_hT, r_wt_k], [ps_r[b]])
                    return b

                def qk_back(kind, t, b, ui):
                    (qf_, r_qf_), (qsq_, r_qsq_), (ss_, r_ss_), (qr_, r_qr_) = qbufs[ui % 2]
                    tb = ui % 2
                    if kind == 0:
                        yield from qk_norm_rope(8, ps[b][:, :], ps_r[b], qgb, r_qgb, cst, r_cs, t, qf_, r_qf_, qsq_, r_qsq_, ss_, r_ss_, qr_, r_qr_)
                        for h in range(8):
                            PE(lambda e: e.transpose(out=psb(tb)[0:64, h * 128:(h + 1) * 128], in_=qr_[:, h * 64:(h + 1) * 64],
                                                     identity=ident[:, :]), [r_qr_, r_ident], [ps_r[tb]])
                        A(lambda e: e.copy(out=qT[:, :, t * 128:(t + 1) * 128],
                                           in_=psb(tb)[0:64, :].rearrange("p (h q) -> p h q", h=8)), [ps_r[tb]], [r_qT])
                    else:
                        A(lambda e: e.copy(out=vt[:, t, :, 0:64], in_=ps[b][:, 128:256].rearrange("p (h d) -> p h d", h=2)),
                          [ps_r[b]], [r_vt])
                        yield from qk_norm_rope(2, ps[b][:, 0:128], ps_r[b], kgb, r_kgb, cst, r_cs, t, qf_, r_qf_, qsq_, r_qsq_, ss_, r_ss_, qr_, r_qr_)
                        for h in range(2):
                            PE(lambda e: e.transpose(out=psb(tb)[0:64, h * 128:(h + 1) * 128], in_=qr_[:, h * 64:(h + 1) * 64],
                                                     identity=ident[:, :]), [r_qr_, r_ident], [ps_r[tb]])
                        A(lambda e: e.copy(out=kT[:, :, t * 128:(t + 1) * 128],
                                           in_=psb(tb)[0:64, 0:256].rearrange("p (h q) -> p h q", h=2)), [ps_r[tb]], [r_kT])

                units = [(0, t) for t in range(NT)] + [(1, t) for t in range(NT)]
                pend = [qk_front(*units[0]), qk_front(*units[1])]
                for pi in range(0, len(units), 2):
                    nxt = [qk_front(*units[pi + 2 + k]) for k in range(2) if pi + 2 + k < len(units)]
                    alive = [qk_back(units[pi + k][0], units[pi + k][1], pend[k], pi + k) for k in range(2)]
                    while alive:
                        for g in list(alive):
                            try:
                                next(g)
                            except StopIteration:
                                alive.remove(g)
                    pend = nxt
                mk, r_mk = sb(st, "mk", [128, 2, 512], BF16)
                DMA(mk[:, 0, :], c_masks[0], w=[r_mk])
                DMA(mk[:, 1, :], c_masks[1], w=[r_mk])
                pT = [sb(st, f"pT{i}", [128, 3, 512], BF16) for i in range(2)]
                den, r_den = sb(st, "den", [128, 8], F32)
                dens = [(den, r_den), sb(st, "den2", [128, 8], F32)]

                def att_front(ai, n, kvh):
                    p_t, r_p = pT[ai % 2]
                    kbs = [kb for kb in (n - 1, n, n + 1) if 0 <= kb < NT]
                    for i, kb in enumerate(kbs):
                        b = nextbank()
                        PE(lambda e: e.matmul(ps[b][:, :], lhsT=kT[:, kvh, kb * 128:(kb + 1) * 128],
                                              rhs=qT[:, 4 * kvh:4 * kvh + 4, n * 128:(n + 1) * 128],
                                              start=True, stop=(kb == n)), [r_kT, r_qT], [ps_r[b]])
                        if kb != n:
                            mi = 0 if kb < n else 1
                            PE(lambda e: e.matmul(ps[b][:, :], lhsT=ident[:, :], rhs=mk[:, mi, :], start=False, stop=True),
                               [r_ident, r_mk], [ps_r[b]])
                        A(lambda e: e.activation(out=p_t[:, i, :], in_=ps[b][:, :], func=AF.Exp), [ps_r[b]], [r_p])

                def att_back(ai, n, kvh):
                    p_t, r_p = pT[ai % 2]
                    dn, r_dn = dens[ai % 2]
                    ob_ = 6 + (ai % 2)
                    kbs = [kb for kb in (n - 1, n, n + 1) if 0 <= kb < NT]
                    for h in range(4):
                        for i, kb in enumerate(kbs):
                            PE(lambda e: e.matmul(ps[ob_][:, h * 65:(h + 1) * 65], lhsT=p_t[:, i, h * 128:(h + 1) * 128],
                                                  rhs=vt[:, kb, kvh, :], start=(i == 0), stop=(i == len(kbs) - 1)),
                               [r_p, r_vt], [ps_r[ob_]])
                    o3 = ps[ob_][:, 0:260].rearrange("p (h d) -> p h d", h=4)
                    V(lambda e: e.tensor_tensor(out=dn[:, 0:4], in0=o3[:, :, 64], in1=esk[:, 4 * kvh:4 * kvh + 4],
                                                op=ALU.add), [ps_r[ob_], r_esk], [r_dn])
                    V(lambda e: e.reciprocal(out=dn[:, 4:8], in_=dn[:, 0:4]), [r_dn], [r_dn])
                    V(lambda e: e.tensor_tensor(
                        out=ybt[:, n, kvh * 256:(kvh + 1) * 256].rearrange("p (h d) -> p h d", h=4),
                        in0=o3[:, :, 0:64], in1=dn[:, 4:8].unsqueeze(2).to_broadcast([128, 4, 64]), op=ALU.mult),
                      [ps_r[ob_], r_dn], [r_ybt[n]])

                aunits = [(n, kvh) for n in range(NT) for kvh in range(2)]
                att_front(0, *aunits[0])
                for ai, (n, kvh) in enumerate(aunits):
                    if ai + 1 < len(aunits):
                        att_front(ai + 1, *aunits[ai + 1])
                    att_back(ai, n, kvh)
            S.barrier()
            with ExitStack() as st:
                Y, r_Y = sb(st, "Y", [128, 32, 512], BF16)
                Fb = [sb(st, f"Fb{i}", [128, 16, 256], BF16) for i in range(2)]
                Hb = [sb(st, f"Hb{i}", [128, 2, 512], F32) for i in range(2)]
                Gb = [sb(st, f"Gb{i}", [128, 16, 512], BF16) for i in range(4)]
                tm = [sb(st, f"tm{i}", [128, 512], F32) for i in range(4)]
                xg = [sb(st, f"xg{i}", [128, 512], BF16) for i in range(2)]
                r_Yc = [Res(f"Y{c}") for c in range(32)]
                gi_ = [0]
                for o in range(2):
                    for fc in range(16):
                        Ft, r_Ft = Fb[fc % 2]
                        Ht, r_Ht = Hb[fc % 2]
                        DMA(Ft[:, :, :], c_fwdF[fc], w=[r_Ft])
                        DMA(Ht[:, :, :], hfd[o, fc], r=[r_hfd], w=[r_Ht])
                        bre, bim = 2 * (fc % 2), 2 * (fc % 2) + 1
                        for tc in range(16):
                            PE(lambda e: e.matmul(ps[bre][:, :], lhsT=Ft[:, tc, 0:128], rhs=hv[:, tc, :],
                                                  start=(tc == 0), stop=(tc == 15)), [r_Ft, r_hv[tc]], [ps_r[bre]])
                        for tc in range(16):
                            PE(lambda e: e.matmul(ps[bim][:, :], lhsT=Ft[:, tc, 128:256], rhs=hv[:, tc, :],
                                                  start=(tc == 0), stop=(tc == 15)), [r_Ft, r_hv[tc]], [ps_r[bim]])
                        (t1, r1), (t2, r2), (t3, r3), (t4, r4) = tm
                        V(lambda e: e.tensor_tensor(out=t1[:, :], in0=ps[bre][:, :], in1=Ht[:, 0, :], op=ALU.mult),
                          [ps_r[bre], r_Ht], [r1])
                        V(lambda e: e.tensor_tensor(out=t2[:, :], in0=ps[bim][:, :], in1=Ht[:, 1, :], op=ALU.mult),
                          [ps_r[bim], r_Ht], [r2])
                        V(lambda e: e.tensor_tensor(out=t3[:, :], in0=ps[bre][:, :], in1=Ht[:, 1, :], op=ALU.mult),
                          [ps_r[bre], r_Ht], [r3])
                        V(lambda e: e.tensor_tensor(out=t4[:, :], in0=ps[bim][:, :], in1=Ht[:, 0, :], op=ALU.mult),
                          [ps_r[bim], r_Ht], [r4])
                        G(lambda e: e.tensor_tensor(out=Y[:, fc, :], in0=t1[:, :], in1=t2[:, :], op=ALU.subtract),
                          [r1, r2], [r_Yc[fc]])
                        G(lambda e: e.tensor_tensor(out=Y[:, 16 + fc, :], in0=t3[:, :], in1=t4[:, :], op=ALU.add),
                          [r3, r4], [r_Yc[16 + fc]])
                        if fc == 0:
                            V(lambda e: e.tensor_copy(out=Y[0:1, 0, :], in_=t1[0:1, :]), [r1], [r_Yc[0]])
                            V(lambda e: e.tensor_tensor(out=Y[0:1, 16, :], in0=ps[bim][0:1, :], in1=hnyq[0:1, o, :],
                                                        op=ALU.mult), [ps_r[bim], r_hnyq], [r_Yc[16]])
                    for tg in range(4):
                        gts = []
                        for half in range(2):
                            g_t, r_g = Gb[gi_[0] % 4]
                            gi_[0] += 1
                            DMA(g_t[:, :, :], c_invG[tg, :, half * 16:(half + 1) * 16, :], w=[r_g])
                            gts.append((g_t, r_g))
                        for ti in range(4):
                            t = tg * 4 + ti
                            b = 4 + (t % 4)
                            x_t, r_xg = xg[t % 2]
                            DMA(x_t[:, :], x12[o, t * 128:(t + 1) * 128, :], r=[r_x12], w=[r_xg])
                            for c in range(32):
                                g_t, r_g = gts[c // 16]
                                PE(lambda e: e.matmul(ps[b][:, :], lhsT=g_t[:, c % 16, ti * 128:(ti + 1) * 128], rhs=Y[:, c, :],
                                                      start=(c == 0), stop=(c == 31)), [r_g, r_Yc[c]], [ps_r[b]])
                            V(lambda e: e.tensor_tensor(out=hv[:, t, :], in0=ps[b][:, :], in1=x_t[:, :], op=ALU.mult),
                              [ps_r[b], r_xg], [r_hv[t]])
            S.barrier()
            with ExitStack() as st:
                wpa_t, r_wpa = sb(st, "wpa_t", [128, 4, 1024], BF16)
                wpb_t, r_wpb = sb(st, "wpb_t", [128, 4, 1024], BF16)
                wo_t, r_wo = sb(st, "wo_t", [128, 8, 1024], BF16)
                gtb, r_gtb = sb(st, "gtb", [128, D], F32)
                DMA(wpa_t[:, :, :], wpa_s, r=[r_w["wpa"]], w=[r_wpa])
                DMA(wpb_t[:, :, :], wpb_s, r=[r_w["wpb"]], w=[r_wpb])
                DMA(wo_t[:, :, :], wout_s, r=[r_w["wout"]], w=[r_wo])
                DMA(gtb[:, :], modd[s:s + 1, 2048:3072].to_broadcast([128, D]), r=[r_modd], w=[r_gtb])
                abT = [sb(st, f"abT{i}", [128, 8, 128], BF16) for i in range(2)]
                mT = [sb(st, f"mT{i}", [128, 8, 128], BF16) for i in range(2)]
                sg = [sb(st, f"sg{i}", [128, 2, D], BF16) for i in range(2)]
                m1, r_m1 = sb(st, "m1", [128, D], F32)
                m2, r_m2 = sb(st, "m2", [128, D], F32)
                mg, r_mg = sb(st, "mg", [128, D], BF16)
                xt = [sb(st, f"xt5{i}", [128, D], F32) for i in range(2)]
                mgs = [(mg, r_mg), sb(st, "mg2", [128, D], BF16)]
                o1, r_o1 = sb(st, "o1", [128, D], F32)

                def stX(t):
                    ab, r_ab = abT[t % 2]
                    s_t, r_s = sg[t % 2]
                    x_t, r_x = xt[t % 2]
                    mg_t, r_mgt = mgs[t % 2]
                    DMA(s_t[:, 0, :], sigd[0, t * 128:(t + 1) * 128, :], r=[r_sigd], w=[r_s])
                    DMA(s_t[:, 1, :], sigd[1, t * 128:(t + 1) * 128, :], r=[r_sigd], w=[r_s])
                    DMA(x_t[:, :], src_x[s, t * 128:(t + 1) * 128, :], r=[xres[s][t]], w=[r_x])
                    for k in range(4):
                        PE(lambda e: e.transpose(out=psb(0)[:, k * 128:(k + 1) * 128], in_=hv[:, t, k * 128:(k + 1) * 128],
                                                 identity=ident[:, :]), [r_hv[t], r_ident], [ps_r[0]])
                    for k in range(4):
                        PE(lambda e: e.transpose(out=psb(0)[:, (4 + k) * 128:(5 + k) * 128],
                                                 in_=ybt[:, t, k * 128:(k + 1) * 128], identity=ident[:, :]),
                           [r_ybt[t], r_ident], [ps_r[0]])
                    A(lambda e: e.copy(out=ab[:, :, :], in_=psb(0).rearrange("p (k q) -> p k q", k=8)), [ps_r[0]], [r_ab])
                    for nb in range(2):
                        for k in range(4):
                            PE(lambda e: e.matmul(ps[1 + nb][:, :], lhsT=ab[:, k, :], rhs=wpa_t[:, k, nb * 512:(nb + 1) * 512],
                                                  start=(k == 0), stop=(k == 3)), [r_ab, r_wpa], [ps_r[1 + nb]])
                        for k in range(4):
                            PE(lambda e: e.matmul(ps[3 + nb][:, :], lhsT=ab[:, 4 + k, :], rhs=wpb_t[:, k, nb * 512:(nb + 1) * 512],
                                                  start=(k == 0), stop=(k == 3)), [r_ab, r_wpb], [ps_r[3 + nb]])
                    for nb in range(2):
                        V(lambda e: e.tensor_tensor(out=m1[:, nb * 512:(nb + 1) * 512], in0=ps[1 + nb][:, :],
                                                    in1=s_t[:, 0, nb * 512:(nb + 1) * 512], op=ALU.mult),
                          [ps_r[1 + nb], r_s], [r_m1])
                        V(lambda e: e.tensor_tensor(out=m2[:, nb * 512:(nb + 1) * 512], in0=ps[3 + nb][:, :],
                                                    in1=s_t[:, 1, nb * 512:(nb + 1) * 512], op=ALU.mult),
                          [ps_r[3 + nb], r_s], [r_m2])
                    G(lambda e: e.tensor_tensor(out=mg_t[:, :], in0=m1[:, :], in1=m2[:, :], op=ALU.add), [r_m1, r_m2], [r_mgt])

                def stY(t):
                    mt, r_mt = mT[t % 2]
                    x_t, r_x = xt[t % 2]
                    mg_t, r_mgt = mgs[t % 2]
                    for k in range(8):
                        PE(lambda e: e.transpose(out=psb(5)[:, k * 128:(k + 1) * 128], in_=mg_t[:, k * 128:(k + 1) * 128],
                                                 identity=ident[:, :]), [r_mgt, r_ident], [ps_r[5]])
                    A(lambda e: e.copy(out=mt[:, :, :], in_=psb(5).rearrange("p (k q) -> p k q", k=8)), [ps_r[5]], [r_mt])
                    for nb in range(2):
                        for k in range(8):
                            PE(lambda e: e.matmul(ps[6 + nb][:, :], lhsT=mt[:, k, :], rhs=wo_t[:, k, nb * 512:(nb + 1) * 512],
                                                  start=(k == 0), stop=(k == 7)), [r_mt, r_wo], [ps_r[6 + nb]])
                        V(lambda e: e.tensor_tensor(out=o1[:, nb * 512:(nb + 1) * 512], in0=ps[6 + nb][:, :],
                                                    in1=gtb[:, nb * 512:(nb + 1) * 512], op=ALU.mult),
                          [ps_r[6 + nb], r_gtb], [r_o1])
                    G(lambda e: e.tensor_tensor(out=x_t[:, :], in0=x_t[:, :], in1=o1[:, :], op=ALU.add), [r_x, r_o1], [r_x])
                    DMA(y_out[s, t * 128:(t + 1) * 128, :], x_t[:, :], r=[r_x], w=[xres[s][t]])

                stX(0)
                for t in range(NT):
                    if t + 1 < NT:
                        stX(t + 1)
                    stY(t)
            S.barrier()

        r_x12 = Res("x12")
        r_sigd = Res("sigd")
        r_uvd = [Res("uvd0"), Res("uvd1")]

        tbl_pending = []

        def issue_tables(l, deferred=False):
            CH = 256 if deferred else 512
            for which, tab in ((0, peer_u), (1, peer_v)):
                tv = tab.ap()[l * 16384:(l + 1) * 16384, :]
                for ch in range(16384 // CH):
                    def go(which=which, tv=tv, ch=ch):
                        S.dma("tbl", lambda e: e.dma_start(out=uvds[l].ap()[ch * CH:(ch + 1) * CH, which * D:(which + 1) * D],
                                                           in_=tv[ch * CH:(ch + 1) * CH, :]), [], [Res()], eng="gpsimd")
                    if deferred:
                        tbl_pending.append(go)
                    else:
                        go()

        def tbl_trickle(n=1):
            for _ in range(n):
                if tbl_pending:
                    tbl_pending.pop(0)()

        def peer_seq(l, s, from_input=False):
            src_x = x_in if from_input else y_out
            if l > 0:
                tbl_trickle(len(tbl_pending))
            S.wait_all("gpsimd", "tbl")
            uvd = uvds[l]
            with ExitStack() as st:
                wq_t, r_wq = sb(st, "wq_t", [128, 8, 2048], BF16)
                DMA(wq_t[:, :, :], pwq_s, r=[r_w["pwq"]], w=[r_wq])
                kst, r_kst = sb(st, "kst", [128, 2, 128], F32)
                k12, r_k12 = sb(st, "k12", [128, 2, 128], BF16)
                DMA(kst[:, 0, :], peer_k1T[l], w=[r_kst])
                DMA(kst[:, 1, :], peer_k2T[l], w=[r_kst])
                V(lambda e: e.tensor_copy(out=k12[:, :, :], in_=kst[:, :, :]), [r_kst], [r_k12])
                a2b, r_a2b = sb(st, "a2b", [128, D], F32)
                sh2b, r_sh2b = sb(st, "sh2b", [128, D], F32)
                gt2b, r_gt2b = sb(st, "gt2b", [128, D], F32)
                DMA(sh2b[:, :], modd[s:s + 1, 3072:4096].to_broadcast([128, D]), r=[r_modd], w=[r_sh2b])
                DMA(a2b[:, :], modd[s:s + 1, 4096:5120].to_broadcast([128, D]), r=[r_modd], w=[r_a2b])
                DMA(gt2b[:, :], modd[s:s + 1, 5120:6144].to_broadcast([128, D]), r=[r_modd], w=[r_gt2b])
                iot, r_iot = sb(st, "iot", [128, 16], F32)
                DMA(iot[:, :], c_iota, w=[r_iot])
                ioti, r_ioti = sb(st, "ioti", [128, 256], I32)
                DMA(ioti[:, :], c_iota256, w=[r_ioti])
                xt = [sb(st, f"xp{i}", [128, D], F32) for i in range(3)]
                h2, r_h2 = sb(st, "h2", [128, D], F32)
                fin, r_fin = h2, r_h2
                h2b, r_h2b = sb(st, "h2b", [128, D], BF16)
                junk, r_junk = sb(st, "junkp", [128, D], BF16)
                junks = [(junk, r_junk), sb(st, "junk2", [128, D], BF16)]
                junkA, r_junkA = h2b, r_h2b
                st8, r_st8 = sb(st, "st8p", [128, 8], F32)
                h2T, r_h2T = sb(st, "h2T", [128, 8, 128], BF16)
                qT, r_qT = sb(st, "qTp", [128, 16, 128], BF16)
                sc, r_sc = sb(st, "sc", [128, 2, 8, 128], F32)
                r_sch = [[Res(f"sc{a}{b}") for b in range(8)] for a in range(2)]
                vv, r_vv = sb(st, "vv", [128, 2, 8, 16], F32)
                r_vvh = [[Res(f"vv{a}{b}") for b in range(8)] for a in range(2)]
                r_vvall = r_vvh[0] + r_vvh[1]
                r_vsh = [Res(f"vs{b}") for b in range(8)]
                ii, r_ii = sb(st, "ii", [128, 2, 8, 16], U32)
                iif, r_iif = sb(st, "iif", [128, 2, 8, 16], BF16)
                cand, r_cand = sb(st, "cand", [128, 8, 256], F32)
                r_cdh = [Res(f"cd{b}") for b in range(8)]
                vs, r_vs = sb(st, "vs", [128, 8, 16], F32)
                ic, r_ic = sb(st, "ic", [128, 8, 16], U32)
                icab, r_icab = sb(st, "icab", [128, 2, 8, 16], U32)
                icf, r_icf = sb(st, "icf", [128, 2, 8, 16], BF16)
                oh, r_oh = sb(st, "oh", [128, 8, 16, 16], BF16)
                iotb, r_iotb = sb(st, "iotb", [128, 16], BF16)
                V(lambda e: e.tensor_copy(out=iotb[:, :], in_=iot[:, :]), [r_iot], [r_iotb])
                ef, r_ef = sb(st, "ef", [128, 2, 8, 16], F32)
                idxf, r_idxf = sb(st, "idxf", [128, 128], F32)
                idxs = [sb(st, f"idx{i}", [128, 128], I32) for i in range(2)]
                ggs = [sb(st, f"gg{i}", [128, 8, 16], F32) for i in range(2)]
                sm, r_sm = sb(st, "sm", [128, 16], F32)
                aa, _ = sb(st, "aa", [128, 128], F32)
                ww, _ = sb(st, "ww", [128, 128], F32)
                NGB = 6
                gbuf = [sb(st, f"gbuf{i}", [128, 4, 2 * D], BF16) for i in range(NGB)]
                r_gs = [[Res(f"gs{b}_{i}") for i in range(4)] for b in range(NGB)]
                wds = [sb(st, f"wd{i}", [128, 4, 128], BF16) for i in range(2)]
                r_aaj = [Res(f"aa{j}") for j in range(128)]
                r_wwb = [Res(f"ww{j}") for j in range(32)]
                gi_ = [0]
                HP = ((4, 5), (6, 7))
                ACC = (2, 3)
                BT, BS = 0, 1

                def stageA(t):
                    slot = t % 2
                    x_t, r_x = xt[t % 3]
                    idx, r_idx = idxs[slot]
                    gg, r_gg = ggs[slot]
                    H2P = HP[slot]
                    DMA(x_t[:, :], src_x[s, t * 128:(t + 1) * 128, :], r=[xres[s][t]], w=[r_x])
                    yield
                    A(lambda e: e.activation(out=junkA[:, :], in_=x_t[:, :], func=AF.Square, accum_out=st8[:, 0:1]),
                      [r_x], [r_junkA, r_st8])
                    A(lambda e: e.activation(out=st8[:, 1:2], in_=st8[:, 0:1], func=AF.Sqrt, scale=1.0 / D,
                                             bias=eps_t[:, 0:1]), [r_st8], [r_st8])
                    yield
                    V(lambda e: e.reciprocal(out=st8[:, 2:3], in_=st8[:, 1:2]), [r_st8], [r_st8])
                    V(lambda e: e.scalar_tensor_tensor(out=h2[:, :], in0=x_t[:, :], scalar=st8[:, 2:3], in1=a2b[:, :],
                                                       op0=ALU.mult, op1=ALU.mult), [r_x, r_st8, r_a2b], [r_h2])
                    V(lambda e: e.tensor_tensor(out=h2[:, :], in0=h2[:, :], in1=sh2b[:, :], op=ALU.add), [r_h2, r_sh2b], [r_h2])
                    A(lambda e: e.copy(out=h2b[:, :], in_=h2[:, :]), [r_h2], [r_h2b])
                    yield
                    for nb in range(2):
                        V(lambda e: e.tensor_copy(out=ps[H2P[nb]][:, :], in_=h2[:, nb * 512:(nb + 1) * 512]),
                          [r_h2], [ps_r[H2P[nb]]])
                    for k in range(8):
                        PE(lambda e: e.transpose(out=psb(BT)[:, k * 128:(k + 1) * 128], in_=h2b[:, k * 128:(k + 1) * 128],
                                                 identity=ident[:, :]), [r_h2b, r_ident], [ps_r[BT]])
                    yield
                    A(lambda e: e.copy(out=h2T[:, :, :], in_=psb(BT).rearrange("p (k q) -> p k q", k=8)), [ps_r[BT]], [r_h2T])
                    yield
                    for rnd in range(4):
                        for c4 in range(4):
                            cb = rnd * 4 + c4
                            for k in range(8):
                                PE(lambda e: e.matmul(ps[BT][:, c4 * 128:(c4 + 1) * 128], lhsT=wq_t[:, k, cb * 128:(cb + 1) * 128],
                                                      rhs=h2T[:, k, :], start=(k == 0), stop=(k == 7)),
                                   [r_wq, r_h2T], [ps_r[BT]])
                        yield
                        A(lambda e: e.copy(out=qT[:, rnd * 4:(rnd + 1) * 4, :],
                                           in_=ps[BT][:, :].rearrange("p (c q) -> p c q", c=4)), [ps_r[BT]], [r_qT])
                    yield
                    for hf in range(2):
                        for rnd in range(2):
                            for h4 in range(4):
                                h = rnd * 4 + h4
                                PE(lambda e: e.matmul(ps[BS][:, h4 * 128:(h4 + 1) * 128], lhsT=qT[:, 2 * h + hf, :],
                                                      rhs=k12[:, hf, :], start=True, stop=True), [r_qT, r_k12], [ps_r[BS]])
                            yield
                            A(lambda e: e.copy(out=sc[:, hf, rnd * 4:(rnd + 1) * 4, :],
                                               in_=ps[BS][:, :].rearrange("p (h n) -> p h n", h=4)), [ps_r[BS]],
                              [r_sch[hf][rnd * 4 + i] for i in range(4)])
                    yield
                    for hf in range(2):
                        sci = sc[:, hf, :, :].bitcast(U32)
                        V(lambda e: e.tensor_scalar(out=sci, in0=sci, scalar1=7, scalar2=7, op0=ALU.logical_shift_right,
                                                    op1=ALU.logical_shift_left), r_sch[hf], r_sch[hf])
                        V(lambda e: e.tensor_tensor(out=sci, in0=sci,
                                                    in1=ioti[:, 0:128].bitcast(U32).unsqueeze(1).to_broadcast([128, 8, 128]),
                                                    op=ALU.bitwise_or), r_sch[hf] + [r_ioti], r_sch[hf])
                        yield
                    for hf in range(2):
                        for h0 in range(0, 8, 2):
                            for h in (h0, h0 + 1):
                                V(lambda e: e.max(out=vv[:, hf, h, 0:8], in_=sc[:, hf, h, :]), [r_sch[hf][h]], [r_vvh[hf][h]])
                            for h in (h0, h0 + 1):
                                V(lambda e: e.match_replace(out=sc[:, hf, h, :], in_to_replace=vv[:, hf, h, 0:8],
                                                            in_values=sc[:, hf, h, :], imm_value=-1e30),
                                  [r_sch[hf][h], r_vvh[hf][h]], [r_sch[hf][h]])
                            for h in (h0, h0 + 1):
                                V(lambda e: e.max(out=vv[:, hf, h, 8:16], in_=sc[:, hf, h, :]), [r_sch[hf][h]], [r_vvh[hf][h]])
                            yield
                    V(lambda e: e.tensor_single_scalar(out=ii[:, :, :, :], in_=vv[:, :, :, :].bitcast(U32), scalar=127,
                                                       op=ALU.bitwise_and), r_vvall, [r_ii])
                    V(lambda e: e.tensor_copy(out=iif[:, :, :, :], in_=ii[:, :, :, :]), [r_ii], [r_iif])
                    c4v = cand[:, :, :].rearrange("p h (a b) -> p h a b", a=16)
                    V(lambda e: e.tensor_tensor(out=c4v, in0=vv[:, 0, :, :].unsqueeze(3).to_broadcast([128, 8, 16, 16]),
                                                in1=vv[:, 1, :, :].unsqueeze(2).to_broadcast([128, 8, 16, 16]), op=ALU.add),
                      r_vvall, r_cdh)
                    yield
                    cdi = cand[:, :, :].bitcast(U32)
                    V(lambda e: e.tensor_scalar(out=cdi, in0=cdi, scalar1=8, scalar2=8, op0=ALU.logical_shift_right,
                                                op1=ALU.logical_shift_left), r_cdh, r_cdh)
                    V(lambda e: e.tensor_tensor(out=cdi, in0=cdi,
                                                in1=ioti[:, :].bitcast(U32).unsqueeze(1).to_broadcast([128, 8, 256]),
                                                op=ALU.bitwise_or), r_cdh + [r_ioti], r_cdh)
                    yield
                    for h0 in range(0, 8, 2):
                        for h in (h0, h0 + 1):
                            V(lambda e: e.max(out=vs[:, h, 0:8], in_=cand[:, h, :]), [r_cdh[h]], [r_vsh[h]])
                        for h in (h0, h0 + 1):
                            V(lambda e: e.match_replace(out=cand[:, h, :], in_to_replace=vs[:, h, 0:8],
                                                        in_values=cand[:, h, :], imm_value=-1e30), [r_cdh[h], r_vsh[h]], [r_cdh[h]])
                        for h in (h0, h0 + 1):
                            V(lambda e: e.max(out=vs[:, h, 8:16], in_=cand[:, h, :]), [r_cdh[h]], [r_vsh[h]])
                        yield
                    V(lambda e: e.tensor_single_scalar(out=ic[:, :, :], in_=vs[:, :, :].bitcast(U32), scalar=255,
                                                       op=ALU.bitwise_and), r_vsh, [r_ic])
                    V(lambda e: e.tensor_single_scalar(out=icab[:, 0, :, :], in_=ic[:, :, :], scalar=4,
                                                       op=ALU.logical_shift_right), [r_ic], [r_icab])
                    V(lambda e: e.tensor_single_scalar(out=icab[:, 1, :, :], in_=ic[:, :, :], scalar=15,
                                                       op=ALU.bitwise_and), [r_ic], [r_icab])
                    V(lambda e: e.tensor_copy(out=icf[:, :, :, :], in_=icab[:, :, :, :]), [r_icab], [r_icf])
                    yield
                    for hf in range(2):
                        V(lambda e: e.tensor_tensor(out=oh[:, :, :, :],
                                                    in0=icf[:, hf, :, :].unsqueeze(3).to_broadcast([128, 8, 16, 16]),
                                                    in1=iotb[:, :].unsqueeze(1).unsqueeze(1).to_broadcast([128, 8, 16, 16]),
                                                    op=ALU.is_equal), [r_icf, r_iotb], [r_oh])
                        yield
                        V(lambda e: e.tensor_tensor(out=oh[:, :, :, :], in0=oh[:, :, :, :],
                                                    in1=iif[:, hf, :, :].unsqueeze(2).to_broadcast([128, 8, 16, 16]),
                                                    op=ALU.mult), [r_oh, r_iif], [r_oh])
                        yield
                        V(lambda e: e.tensor_reduce(out=ef[:, hf, :, :], in_=oh[:, :, :, :], axis=AX.X, op=ALU.add),
                          [r_oh], [r_ef])
                        yield
                    V(lambda e: e.scalar_tensor_tensor(out=idxf[:, :], in0=ef[:, 0, :, :].rearrange("p h k -> p (h k)"),
                                                       scalar=128.0, in1=ef[:, 1, :, :].rearrange("p h k -> p (h k)"),
                                                       op0=ALU.mult, op1=ALU.add), [r_ef], [r_idxf])
                    V(lambda e: e.tensor_copy(out=idx[:, :], in_=idxf[:, :]), [r_idxf], [r_idx])
                    V(lambda e: e.tensor_tensor(out=gg[:, :, :], in0=vs[:, :, :],
                                                in1=vs[:, :, 0:1].to_broadcast([128, 8, 16]), op=ALU.subtract), r_vsh, [r_gg])
                    A(lambda e: e.activation(out=gg[:, :, :], in_=gg[:, :, :], func=AF.Exp), [r_gg], [r_gg])
                    yield
                    V(lambda e: e.tensor_reduce(out=sm[:, 0:8], in_=gg[:, :, :], axis=AX.X, op=ALU.add), [r_gg], [r_sm])
                    V(lambda e: e.reciprocal(out=sm[:, 8:16], in_=sm[:, 0:8]), [r_sm], [r_sm])
                    V(lambda e: e.tensor_tensor(out=gg[:, :, :], in0=gg[:, :, :],
                                                in1=sm[:, 8:16].unsqueeze(2).to_broadcast([128, 8, 16]), op=ALU.mult),
                      [r_gg, r_sm], [r_gg])
                    yield

                def stageB(t, agen, prev_tail):
                    slot = t % 2
                    x_t, r_x = xt[t % 3]
                    idx, r_idx = idxs[slot]
                    gg, r_gg = ggs[slot]
                    H2P = HP[slot]
                    NBT = 32
                    pbuf = {}
                    for bt in range(NBT + 1):
                        if bt < NBT:
                            b = gi_[0] % NGB
                            gi_[0] += 1
                            pbuf[bt] = b
                            g_t = gbuf[b][0]
                            for i in range(4):
                                j = bt * 4 + i
                                S.dma("gpsimd", lambda e: e.indirect_dma_start(
                                    out=g_t[:, i, :], out_offset=None, in_=uvd[:, :],
                                    in_offset=bass.IndirectOffsetOnAxis(ap=idx[:, j:j + 1], axis=0)),
                                    [r_idx], [r_gs[b][i]])
                            for i in range(4):
                                j = bt * 4 + i
                                jk, r_jk = junks[j % 2]
                                V(lambda e: e.scalar_tensor_tensor(
                                    out=jk[:, :], in0=g_t[:, i, 0:D], scalar=1.0, in1=ps2(H2P), op0=ALU.mult, op1=ALU.mult,
                                    accum_out=aa[:, j:j + 1]),
                                  [r_gs[b][i], ps_r[H2P[0]], ps_r[H2P[1]]], [r_jk, r_aaj[j]])
                            A(lambda e: e.activation(out=ww[:, bt * 4:(bt + 1) * 4], in_=aa[:, bt * 4:(bt + 1) * 4], func=AF.Gelu),
                              [r_aaj[bt * 4 + i] for i in range(4)], [r_wwb[bt]])
                        if bt == 0 and prev_tail is not None:
                            prev_tail()
                        if bt in (18, 22, 26, 30):
                            tbl_trickle(1)
                        if agen is not None:
                            npull = 1 if bt in (0, 2) else (0 if bt < 4 else (1 if bt < 16 else 2))
                            for _ in range(npull):
                                next(agen, None)
                        if bt >= 1:
                            pb_ = bt - 1
                            b = pbuf[pb_]
                            g_t = gbuf[b][0]
                            wd_t, r_wd = wds[pb_ % 2]
                            V(lambda e: e.tensor_tensor(out=ww[:, pb_ * 4:(pb_ + 1) * 4], in0=ww[:, pb_ * 4:(pb_ + 1) * 4],
                                                        in1=gg[:, :, :].rearrange("p h k -> p (h k)")[:, pb_ * 4:(pb_ + 1) * 4],
                                                        op=ALU.mult), [r_wwb[pb_], r_gg], [r_wwb[pb_]])
                            for i in range(4):
                                j = pb_ * 4 + i
                                A(lambda e: e.activation(out=wd_t[:, i, :], in_=ident[:, :], func=AF.Copy, scale=ww[:, j:j + 1]),
                                  [r_ident, r_wwb[pb_]], [r_wd])
                            for i in range(4):
                                j = pb_ * 4 + i
                                for nb in range(2):
                                    PE(lambda e: e.matmul(ps[ACC[nb]][:, :], lhsT=wd_t[:, i, :],
                                                          rhs=g_t[:, i, D + nb * 512:D + (nb + 1) * 512],
                                                          start=(j == 0), stop=(j == 127)),
                                       [r_wd, r_gs[b][i]], [ps_r[ACC[nb]]])
                    if agen is not None:
                        for _ in agen:
                            pass
                    def tail():
                        V(lambda e: e.tensor_tensor(out=fin[:, :], in0=ps2(ACC), in1=gt2b[:, :], op=ALU.mult),
                          [ps_r[ACC[0]], ps_r[ACC[1]], r_gt2b], [r_fin])
                        V(lambda e: e.tensor_tensor(out=x_t[:, :], in0=x_t[:, :], in1=fin[:, :], op=ALU.add), [r_x, r_fin], [r_x])
                        DMA(y_out[s, t * 128:(t + 1) * 128, :], x_t[:, :], r=[r_x], w=[xres[s][t]])
                    return tail

                for _ in stageA(0):
                    pass
                tail = None
                for t in range(NTOK_PEER):
                    agen = stageA(t + 1) if t + 1 < NTOK_PEER else None
                    tail = stageB(t, agen, tail)
                tail()
            S.barrier()

        if do_peer:
            issue_tables(0)
            if DEPTH > 1:
                issue_tables(1, deferred=True)
        for l in range(DEPTH):
            mod_phase(l)
            prep_weights(l)
            if do_mixer:
                filter_phase(l)
            if do_peer and l == 0:
                pass
            for s in range(NSEQ):
                if do_mixer:
                    with ExitStack() as sq:
                        hv, _ = sb(sq, "hv", [128, NT, 512], BF16)
                        ybt, _ = sb(sq, "ybt", [128, NT, 512], BF16)
                        r_hv = [Res(f"hv{t}") for t in range(NT)]
                        r_ybt = [Res(f"yb{t}") for t in range(NT)]
                        mixer_seq(l, s, hv, r_hv, ybt, r_ybt)
                if do_peer:
                    peer_seq(l, s, from_input=(l == 0 and not do_mixer))
        S.barrier(include_tbl=True)
        print("bass program: ops", S.nops, "waits", S.nwaits, flush=True)
    return nc


N_CORES = 8
_NC_CACHE = {}


def kernel(**inputs):
    f32 = np.float32
    x = np.concatenate([np.asarray(inputs["x_prompt"], f32), np.asarray(inputs["x_sample"], f32)], axis=0)
    c = np.concatenate([np.asarray(inputs["c_prompt"], f32), np.asarray(inputs["c_sample"], f32)], axis=0)
    nseq = x.shape[0] // N_CORES
    cst = _consts()
    shared = {}
    for k in ("w_mod", "b_mod", "g_norm1", "g_norm2", "w_in", "conv_w", "conv_b", "f_w1", "f_b1", "f_freq", "f_w2",
              "f_b2", "f_w3", "f_bias", "q_gain", "k_gain", "sink", "w_pa", "w_pb", "w_out", "peer_wq"):
        shared[k] = np.ascontiguousarray(np.asarray(inputs[k], f32))
    shared["peer_k1T"] = np.ascontiguousarray(np.asarray(inputs["peer_k1"], f32).transpose(0, 2, 1))
    shared["peer_k2T"] = np.ascontiguousarray(np.asarray(inputs["peer_k2"], f32).transpose(0, 2, 1))
    shared["peer_u"] = np.ascontiguousarray(np.asarray(inputs["peer_u"], f32).reshape(2 * 16384, D))
    shared["peer_v"] = np.ascontiguousarray(np.asarray(inputs["peer_v"], f32).reshape(2 * 16384, D))
    for k in ("fwdF", "invG", "zfT", "negt", "absdelta", "cs", "masks", "ident", "iota16", "iota256"):
        shared[k] = cst[k]
    in_maps = []
    for i in range(N_CORES):
        m = dict(shared)
        m["x"] = np.ascontiguousarray(x[i * nseq:(i + 1) * nseq])
        ci = c[i * nseq:(i + 1) * nseq]
        m["cT"] = np.ascontiguousarray(ci.T.reshape(8, 128, nseq).transpose(1, 0, 2))
        in_maps.append(m)
    if "nc" not in _NC_CACHE:
        _NC_CACHE["nc"] = build(NSEQ=nseq, DEPTH=2)
    res = run_bass_kernel_spmd(_NC_CACHE["nc"], in_maps, core_ids=list(range(N_CORES)))
    y = np.concatenate([np.asarray(r["y"], f32) for r in res.results], axis=0)
    nb = inputs["x_prompt"].shape[0]
    return (np.ascontiguousarray(y[:nb]), np.ascontiguousarray(y[nb:]))
```

```python
import math
from contextlib import ExitStack

import numpy as np
import ml_dtypes
import concourse.bass as bass
import concourse.mybir as mybir
from concourse.bass_utils import run_bass_kernel_spmd

F32 = mybir.dt.float32
BF16 = mybir.dt.bfloat16
U32 = mybir.dt.uint32
I32 = mybir.dt.int32
AF = mybir.ActivationFunctionType
ALU = mybir.AluOpType
AX = mybir.AxisListType

L = 2048
D = 1024
NT = 16
EPS = 1e-6
NEG = -30000.0
MAGIC = 12582912.0
TWO_PI = 2.0 * math.pi


class Res:
    __slots__ = ("name", "w", "r")

    def __init__(self, name=""):
        self.name = name
        self.w = None
        self.r = {}


class Sched:
    ENGS = ("sync", "scalar", "vector", "gpsimd", "tensor")
    RING = 16

    def __init__(self, nc, es):
        self.nc = nc
        self.eng = {e: getattr(nc, e) for e in self.ENGS}
        self.sems = []
        self.semid = {}
        self.cnt = {}
        for e in self.ENGS:
            self.semid[e] = len(self.sems)
            self.sems.append(es.enter_context(nc.semaphore("c_" + e)))
            self.cnt[e] = 0
        self.ring = {}
        self.ring_cnt = {}
        for q in ("sync", "gpsimd", "tbl"):
            ids = []
            for i in range(self.RING):
                ids.append(len(self.sems))
                self.sems.append(es.enter_context(nc.semaphore(f"d_{q}{i}")))
            self.ring[q] = ids
            self.ring_cnt[q] = 0
        self.known = {e: {} for e in self.ENGS}
        self.nwaits = 0
        self.nops = 0

    def _waits(self, eng, reads, writes, extra=()):
        need = {}

        def add(ev):
            if ev is None:
                return
            s, v = ev
            if need.get(s, 0) < v:
                need[s] = v
        for r in reads:
            add(r.w)
        for w in writes:
            add(w.w)
            for s, v in w.r.items():
                add((s, v))
        for ev in extra:
            add(ev)
        kn = self.known[eng]
        e = self.eng[eng]
        own = self.semid[eng]
        for s, v in need.items():
            if eng == "tensor" and s == own:
                continue
            if kn.get(s, 0) >= v:
                continue
            e.wait_ge(self.sems[s], v)
            kn[s] = v
            self.nwaits += 1

    def _mark(self, ev, reads, writes):
        s, v = ev
        for w in writes:
            w.w = ev
            w.r = {}
        for r in reads:
            if r in writes:
                continue
            if r.r.get(s, 0) < v:
                r.r[s] = v

    def op(self, eng, fn, reads=(), writes=()):
        self._waits(eng, reads, writes)
        ins = fn(self.eng[eng])
        self.cnt[eng] += 1
        ev = (self.semid[eng], self.cnt[eng])
        ins.then_inc(self.sems[ev[0]], 1)
        self._mark(ev, reads, writes)
        self.nops += 1
        return ev

    def dma(self, q, fn, reads=(), writes=(), eng=None):
        eng = eng or q
        i = self.ring_cnt[q]
        self.ring_cnt[q] = i + 1
        slot = self.ring[q][i % self.RING]
        rnd = i // self.RING
        extra = []
        if rnd > 0:
            extra.append((slot, 16 * rnd))
        self._waits(eng, reads, writes, extra)
        ins = fn(self.eng[eng])
        ev = (slot, 16 * (rnd + 1))
        ins.then_inc(self.sems[slot], 16)
        self._mark(ev, reads, writes)
        self.nops += 1
        return ev

    def ring_events(self, q):
        evs = []
        n = self.ring_cnt[q]
        for k in range(self.RING):
            cntk = (n - k + self.RING - 1) // self.RING if n > k else 0
            if cntk > 0:
                evs.append((self.ring[q][k], 16 * cntk))
        return evs

    def wait_all(self, eng, q):
        self._waits(eng, (), (), self.ring_events(q))

    def barrier(self, include_tbl=False):
        evs = [(self.semid[e], self.cnt[e]) for e in self.ENGS if self.cnt[e] > 0]
        for q in self.ring:
            if q == "tbl" and not include_tbl:
                continue
            n = self.ring_cnt[q]
            for k in range(self.RING):
                cntk = (n - k + self.RING - 1) // self.RING if n > k else 0
                if cntk > 0:
                    evs.append((self.ring[q][k], 16 * cntk))
        for eng in self.ENGS:
            self._waits(eng, (), (), evs)


_CONST = {}


def _consts():
    if _CONST:
        return _CONST
    bf = ml_dtypes.bfloat16
    N2 = 2 * L
    n = np.arange(L, dtype=np.int64)
    fwd = np.zeros((16, 128, 16, 256), np.float32)
    nn = (np.arange(16)[None, :] * 128 + np.arange(128)[:, None])
    for fc in range(16):
        f = fc * 128 + np.arange(128)
        ang = 2.0 * np.pi * ((nn[:, :, None] * f[None, None, :]) % N2) / N2
        fwd[fc, :, :, 0:128] = np.cos(ang)
        fwd[fc, :, :, 128:256] = -np.sin(ang)
    fwd[0, :, :, 128] = np.where(nn % 2 == 0, 1.0, -1.0)
    inv = np.zeros((4, 128, 32, 512), np.float32)
    for tg in range(4):
        t = tg * 512 + np.arange(512)
        for fc in range(16):
            f = fc * 128 + np.arange(128)
            ang = 2.0 * np.pi * ((f[:, None] * t[None, :]) % N2) / N2
            gre = (2.0 / N2) * np.cos(ang)
            gim = -(2.0 / N2) * np.sin(ang)
            if fc == 0:
                gre[0, :] = 1.0 / N2
                gim[0, :] = np.where(t % 2 == 0, 1.0, -1.0) / N2
            inv[tg, :, fc, :] = gre
            inv[tg, :, 16 + fc, :] = gim
    tl = np.linspace(0.0, 1.0, L, dtype=np.float32)
    w = (np.float32(2.0 * math.pi) * np.arange(L, dtype=np.float32) / np.float32(L)).astype(np.float32)
    bands = np.linspace(1e-4, 15.0, 16, dtype=np.float32)
    bw = (bands[:, None] * w[None, :]).astype(np.float32).astype(np.float64)
    zfT = np.concatenate([tl[None, :].astype(np.float64), np.cos(bw), -np.sin(bw)], axis=0).astype(np.float32)
    negt = np.zeros((128, 16), np.float32)
    negt[:, :] = -tl[nn]
    max_decay = math.log(1e-2) / 0.3
    min_decay = math.log(1e-2) / 1.5
    absdelta = np.abs(np.linspace(min_decay, max_decay, 512, dtype=np.float32)).reshape(1, 512)
    invf = (10000.0 ** (-np.arange(0, 64, 2, dtype=np.float32) / np.float32(64))).astype(np.float32)
    ang = (np.arange(L, dtype=np.float32)[:, None] * invf[None, :]).astype(np.float32).astype(np.float64)
    cs = np.zeros((L, 128), np.float32)
    cs[:, 0:32] = np.cos(ang)
    cs[:, 32:64] = np.cos(ang)
    cs[:, 64:96] = np.sin(ang)
    cs[:, 96:128] = np.sin(ang)
    cs = np.ascontiguousarray(cs.reshape(16, 128, 128).transpose(1, 0, 2))
    j = np.arange(128)[:, None]
    q = np.arange(128)[None, :]
    mprev = np.where(j >= q, 0.0, NEG).astype(np.float32)
    mnext = np.where(j <= q, 0.0, NEG).astype(np.float32)
    masks = np.stack([np.tile(mprev, (1, 4)), np.tile(mnext, (1, 4))], axis=0)
    _CONST.update(
        fwdF=fwd.astype(bf), invG=inv.astype(bf), zfT=zfT, negt=negt, absdelta=absdelta,
        cs=cs, masks=masks.astype(bf), ident=np.eye(128, dtype=np.float32).astype(bf),
        iota16=np.tile(np.arange(16, dtype=np.float32)[None, :], (128, 1)),
        iota256=np.tile(np.arange(256, dtype=np.int32)[None, :], (128, 1)),
    )
    return _CONST


def build(NSEQ=3, DEPTH=2, do_mixer=True, do_peer=True, NTOK_PEER=NT):
    nc = bass.Bass("TRN2", target_bir_lowering=False)

    def din(name, shape, dt=F32):
        return nc.dram_tensor(name, list(shape), dt, kind="ExternalInput").ap()

    def dscr(name, shape, dt):
        return nc.dram_tensor(name, list(shape), dt, kind="Internal").ap()

    x_in = din("x", [NSEQ, L, D])
    cT_in = din("cT", [128, 8, NSEQ])
    w_mod = din("w_mod", [2, D, 6 * D])
    b_mod = din("b_mod", [2, 6 * D])
    g_norm1 = din("g_norm1", [2, D])
    g_norm2 = din("g_norm2", [2, D])
    w_in = din("w_in", [2, D, 4352])
    conv_w = din("conv_w", [2, 3, 1536])
    conv_b = din("conv_b", [2, 1536])
    f_w1 = din("f_w1", [2, 33, 64])
    f_b1 = din("f_b1", [2, 64])
    f_freq = din("f_freq", [2, 64])
    f_w2 = din("f_w2", [2, 64, 64])
    f_b2 = din("f_b2", [2, 64])
    f_w3 = din("f_w3", [2, 64, 2048])
    f_bias = din("f_bias", [2, 2, 512])
    q_gain = din("q_gain", [2, 64])
    k_gain = din("k_gain", [2, 64])
    sink = din("sink", [2, 8])
    w_pa = din("w_pa", [2, 512, D])
    w_pb = din("w_pb", [2, 512, D])
    w_out = din("w_out", [2, D, D])
    peer_wq = din("peer_wq", [2, D, 2048])
    peer_k1T = din("peer_k1T", [2, 128, 128])
    peer_k2T = din("peer_k2T", [2, 128, 128])
    peer_u = nc.dram_tensor("peer_u", [2 * 16384, D], F32, kind="ExternalInput")
    peer_v = nc.dram_tensor("peer_v", [2 * 16384, D], F32, kind="ExternalInput")
    c_fwdF = din("fwdF", [16, 128, 16, 256], BF16)
    c_invG = din("invG", [4, 128, 32, 512], BF16)
    c_zfT = din("zfT", [33, L])
    c_negt = din("negt", [128, 16])
    c_absd = din("absdelta", [1, 512])
    c_cs = din("cs", [128, 16, 128])
    c_masks = din("masks", [2, 128, 512], BF16)
    c_ident = din("ident", [128, 128], BF16)
    c_iota = din("iota16", [128, 16])
    c_iota256 = din("iota256", [128, 256], I32)
    y_out = nc.dram_tensor("y", [NSEQ, L, D], F32, kind="ExternalOutput").ap()

    modd = dscr("modd", [NSEQ, 6 * D], F32)
    whY = dscr("whY", [9, 128, 8, 512], BF16)
    wqa = dscr("wqa", [128, 8, 512], BF16)
    wkv = dscr("wkv", [128, 8, 256], BF16)
    wg = dscr("wg", [4, 128, 8, 512], BF16)
    wpa_s = dscr("wpa_s", [128, 4, 1024], BF16)
    wpb_s = dscr("wpb_s", [128, 4, 1024], BF16)
    wout_s = dscr("wout_s", [128, 8, 1024], BF16)
    pwq_s = dscr("pwq_s", [128, 8, 2048], BF16)
    x12 = dscr("x12", [2, L, 512], BF16)
    sigd = dscr("sigd", [2, L, 1024], BF16)
    hfd = dscr("hfd", [2, 16, 128, 2, 512], F32)
    uvds = [nc.dram_tensor(f"uvd{i}", [16384, 2048], BF16, kind="Internal") for i in range(2)]

    with ExitStack() as es:
        S = Sched(nc, es)
        es.enter_context(nc.allow_non_contiguous_dma(reason="small strided param loads"))

        def V(fn, r=(), w=()):
            return S.op("vector", fn, r, w)

        def A(fn, r=(), w=()):
            return S.op("scalar", fn, r, w)

        def G(fn, r=(), w=()):
            return S.op("gpsimd", fn, r, w)

        def PE(fn, r=(), w=()):
            return S.op("tensor", fn, r, w)

        def DMA(out, in_, r=(), w=(), q="sync"):
            return S.dma(q, lambda e: e.dma_start(out=out, in_=in_), r, w)

        uid = [0]

        def sb(stack, name, shape, dt):
            uid[0] += 1
            t = stack.enter_context(nc.sbuf_tensor(f"sb{uid[0]}_{name}", list(shape), dt))
            return t, Res(name)

        psbig = es.enter_context(nc.psum_tensor("psbig", [128, 4096], F32))
        ps = [psbig[:, i * 512:(i + 1) * 512] for i in range(8)]
        ps_r = [Res(f"ps{i}") for i in range(8)]

        def ps2(banks):
            return psbig[:, banks[0] * 512:(banks[1] + 1) * 512]
        ident, r_ident = sb(es, "ident", [128, 128], BF16)
        DMA(ident[:, :], c_ident, w=[r_ident])
        xres = [[Res(f"x{s}_{t}") for t in range(NT)] for s in range(NSEQ)]
        r_modd = Res("modd")
        r_hfd = Res("hfd")
        r_w = {k: Res(k) for k in ("whY", "wqa", "wkv", "wg", "wpa", "wpb", "wout", "pwq")}
        hnyq, r_hnyq = sb(es, "hnyq", [1, 2, 512], F32)

        def mod_phase(l):
            with ExitStack() as st:
                cT, r_cT = sb(st, "cT", [128, 8, NSEQ], F32)
                scT, r_scT = sb(st, "scT", [128, 8, NSEQ], F32)
                modr, r_modr = sb(st, "modr", [NSEQ, 6 * D], F32)
                bmb, r_bmb = sb(st, "bmb", [NSEQ, 6 * D], F32)
                g1b, r_g1b = sb(st, "g1b", [NSEQ, D], F32)
                g2b, r_g2b = sb(st, "g2b", [NSEQ, D], F32)
                wst = [sb(st, f"wst{i}", [128, 8, 512], F32) for i in range(2)]
                DMA(cT[:, :, :], cT_in, w=[r_cT])
                DMA(bmb[:, :], b_mod[l:l + 1, :].to_broadcast([NSEQ, 6 * D]), w=[r_bmb])
                DMA(g1b[:, :], g_norm1[l:l + 1, :].to_broadcast([NSEQ, D]), w=[r_g1b])
                DMA(g2b[:, :], g_norm2[l:l + 1, :].to_broadcast([NSEQ, D]), w=[r_g2b])
                A(lambda e: e.activation(out=scT[:, :, :], in_=cT[:, :, :], func=AF.Silu), [r_cT], [r_scT])
                wv = w_mod[l].rearrange("(k p) n -> p k n", p=128)
                for nb in range(12):
                    wt, r_wt = wst[nb % 2]
                    DMA(wt[:, :, :], wv[:, :, nb * 512:(nb + 1) * 512], w=[r_wt])
                    b = nb % 2
                    for k in range(8):
                        PE(lambda e: e.matmul(ps[b][0:NSEQ, :], lhsT=scT[:, k, :], rhs=wt[:, k, :],
                                              start=(k == 0), stop=(k == 7)),
                           [r_scT, r_wt], [ps_r[b]])
                    V(lambda e: e.tensor_tensor(out=modr[:, nb * 512:(nb + 1) * 512], in0=ps[b][0:NSEQ, :],
                                                in1=bmb[:, nb * 512:(nb + 1) * 512], op=ALU.add),
                      [ps_r[b], r_bmb], [r_modr])
                for (c0, gb_, rg) in ((1024, g1b, r_g1b), (4096, g2b, r_g2b)):
                    V(lambda e: e.scalar_tensor_tensor(out=modr[:, c0:c0 + 1024], in0=modr[:, c0:c0 + 1024],
                                                       scalar=1.0, in1=gb_[:, :], op0=ALU.add, op1=ALU.mult),
                      [r_modr, rg], [r_modr])
                DMA(modd, modr[:, :], r=[r_modr], w=[r_modd])
            S.barrier()

        def prep_weights(l):
            with ExitStack() as st:
                stg = [sb(st, f"pstg{i}", [128, 8, 512], F32) for i in range(2)]
                obf = [sb(st, f"pobf{i}", [128, 8, 512], BF16) for i in range(2)]
                cwb, r_cwb = sb(st, "cwb", [128, 3, 1536], F32)
                for j in range(3):
                    DMA(cwb[:, j, :], conv_w[l, j:j + 1, :].to_broadcast([128, 1536]), w=[r_cwb])
                cnt = [0]

                def one(src, dst, K, N, rdst, mul=None):
                    i = cnt[0] % 2
                    cnt[0] += 1
                    s_t, r_s = stg[i]
                    o_t, r_o = obf[i]
                    DMA(s_t[:, 0:K, 0:N], src, w=[r_s])
                    if mul is None:
                        if cnt[0] % 2 == 0:
                            V(lambda e: e.tensor_copy(out=o_t[:, 0:K, 0:N], in_=s_t[:, 0:K, 0:N]), [r_s], [r_o])
                        else:
                            A(lambda e: e.copy(out=o_t[:, 0:K, 0:N], in_=s_t[:, 0:K, 0:N]), [r_s], [r_o])
                    else:
                        for k in range(K):
                            V(lambda e: e.tensor_tensor(out=o_t[:, k, 0:N], in0=s_t[:, k, 0:N], in1=mul,
                                                        op=ALU.mult), [r_s, r_cwb], [r_o])
                    DMA(dst, o_t[:, 0:K, 0:N], r=[r_o], w=[rdst])

                wiv = w_in[l].rearrange("(k p) n -> p k n", p=128)
                for ob in range(3):
                    for j in range(3):
                        one(wiv[:, :, ob * 512:(ob + 1) * 512], whY[ob * 3 + j], 8, 512, r_w["whY"],
                            mul=cwb[:, j, ob * 512:(ob + 1) * 512])
                one(wiv[:, :, 1536:2048], wqa, 8, 512, r_w["wqa"])
                one(wiv[:, :, 2048:2304], wkv, 8, 256, r_w["wkv"])
                for gi in range(4):
                    one(wiv[:, :, 2304 + gi * 512:2304 + (gi + 1) * 512], wg[gi], 8, 512, r_w["wg"])
                pav = w_pa[l].rearrange("(k p) n -> p k n", p=128)
                pbv = w_pb[l].rearrange("(k p) n -> p k n", p=128)
                wov = w_out[l].rearrange("(k p) n -> p k n", p=128)
                pqv = peer_wq[l].rearrange("(k p) n -> p k n", p=128)
                for nb in range(2):
                    one(pav[:, :, nb * 512:(nb + 1) * 512], wpa_s[:, :, nb * 512:(nb + 1) * 512], 4, 512, r_w["wpa"])
                    one(pbv[:, :, nb * 512:(nb + 1) * 512], wpb_s[:, :, nb * 512:(nb + 1) * 512], 4, 512, r_w["wpb"])
                    one(wov[:, :, nb * 512:(nb + 1) * 512], wout_s[:, :, nb * 512:(nb + 1) * 512], 8, 512, r_w["wout"])
                for nb in range(4):
                    one(pqv[:, :, nb * 512:(nb + 1) * 512], pwq_s[:, :, nb * 512:(nb + 1) * 512], 8, 512, r_w["pwq"])
            S.barrier()

        def range_reduce_sin(st, dst, src_ps, bcol, fcol, r_cols, r_dst, psr, npart, tmp, r_tmp, tmp2, r_tmp2):
            V(lambda e: e.tensor_scalar(out=tmp[0:npart, :], in0=src_ps, scalar1=bcol, scalar2=fcol,
                                        op0=ALU.add, op1=ALU.mult), [psr, r_cols], [r_tmp])
            V(lambda e: e.tensor_scalar(out=tmp2[0:npart, :], in0=tmp[0:npart, :], scalar1=1.0 / TWO_PI,
                                        scalar2=MAGIC, op0=ALU.mult, op1=ALU.add), [r_tmp], [r_tmp2])
            V(lambda e: e.tensor_scalar(out=tmp2[0:npart, :], in0=tmp2[0:npart, :], scalar1=MAGIC,
                                        scalar2=-TWO_PI, op0=ALU.subtract, op1=ALU.mult), [r_tmp2], [r_tmp2])
            V(lambda e: e.tensor_tensor(out=tmp[0:npart, :], in0=tmp[0:npart, :], in1=tmp2[0:npart, :],
                                        op=ALU.add), [r_tmp, r_tmp2], [r_tmp])
            V(lambda e: e.tensor_scalar(out=tmp[0:npart, :], in0=tmp[0:npart, :], scalar1=3.1415925,
                                        scalar2=-3.1415925, op0=ALU.min, op1=ALU.max), [r_tmp], [r_tmp])
            A(lambda e: e.activation(out=dst, in_=tmp[0:npart, :], func=AF.Sin), [r_tmp], [r_dst])

        def filter_phase(l):
            with ExitStack() as st:
                zf, r_zf = sb(st, "zf", [33, L], F32)
                w1, r_w1 = sb(st, "fw1", [33, 64], F32)
                w2, r_w2 = sb(st, "fw2", [64, 64], F32)
                w3, r_w3 = sb(st, "fw3", [64, 2048], F32)
                cols, r_cols = sb(st, "fcols", [64, 4], F32)
                a1, r_a1 = sb(st, "fa1", [64, L], F32)
                a2, r_a2 = sb(st, "fa2", [64, L], F32)
                tmp, r_tmp = sb(st, "ftmp", [128, 512], F32)
                tmp2, r_tmp2 = sb(st, "ftmp2", [128, 512], F32)
                absd, r_absd = sb(st, "absd", [128, 512], F32)
                negt, r_negt = sb(st, "negt", [128, 16], F32)
                dec, r_dec = sb(st, "dec", [128, 512], F32)
                fa, r_fa = sb(st, "fa", [128, 16, 1024], BF16)
                fb, r_fb = sb(st, "fb", [128, 16, 1024], BF16)
                h0, r_h0 = sb(st, "h0", [128, 512], F32)
                h1, r_h1 = sb(st, "h1", [128, 512], F32)
                fbias, r_fbias = sb(st, "fbias", [128, 2, 512], F32)
                Fb = [sb(st, f"Fbf{i}", [128, 16, 256], BF16) for i in range(2)]
                ho = [sb(st, f"hout{i}", [128, 2, 512], F32) for i in range(2)]
                DMA(zf[:, :], c_zfT, w=[r_zf])
                DMA(w1[:, :], f_w1[l], w=[r_w1])
                DMA(w2[:, :], f_w2[l], w=[r_w2])
                DMA(w3[:, :], f_w3[l], w=[r_w3])
                DMA(cols[:, 0:1], f_b1[l:l + 1, :].rearrange("o n -> n o"), w=[r_cols])
                DMA(cols[:, 1:2], f_freq[l:l + 1, :].rearrange("o n -> n o"), w=[r_cols])
                DMA(cols[:, 2:3], f_b2[l:l + 1, :].rearrange("o n -> n o"), w=[r_cols])
                DMA(absd[:, :], c_absd.to_broadcast([128, 512]), w=[r_absd])
                DMA(negt[:, :], c_negt, w=[r_negt])
                for o in range(2):
                    DMA(fbias[:, o, :], f_bias[l, o:o + 1, :].to_broadcast([128, 512]), w=[r_fbias])
                for ch in range(4):
                    b = ch % 2
                    PE(lambda e: e.matmul(ps[b][0:64, :], lhsT=w1[:, :], rhs=zf[:, ch * 512:(ch + 1) * 512],
                                          start=True, stop=True), [r_w1, r_zf], [ps_r[b]])
                    range_reduce_sin(st, a1[:, ch * 512:(ch + 1) * 512], ps[b][0:64, :], cols[:, 0:1], cols[:, 1:2],
                                     r_cols, r_a1, ps_r[b], 64, tmp, r_tmp, tmp2, r_tmp2)
                for ch in range(4):
                    b = ch % 2
                    PE(lambda e: e.matmul(ps[b][0:64, :], lhsT=w2[:, :], rhs=a1[:, ch * 512:(ch + 1) * 512],
                                          start=True, stop=True), [r_w2, r_a1], [ps_r[b]])
                    range_reduce_sin(st, a2[:, ch * 512:(ch + 1) * 512], ps[b][0:64, :], cols[:, 2:3], cols[:, 1:2],
                                     r_cols, r_a2, ps_r[b], 64, tmp, r_tmp, tmp2, r_tmp2)
                for tc in range(16):
                    A(lambda e: e.activation(out=dec[:, :], in_=absd[:, :], func=AF.Exp, scale=negt[:, tc:tc + 1]),
                      [r_absd, r_negt], [r_dec])
                    for o in range(2):
                        for dr in range(2):
                            b = 2 + dr
                            cb = (o * 2 + dr) * 512
                            PE(lambda e: e.matmul(ps[b][:, :], lhsT=a2[:, tc * 128:(tc + 1) * 128],
                                                  rhs=w3[:, cb:cb + 512], start=True, stop=True),
                               [r_a2, r_w3], [ps_r[b]])
                            hh, r_hh = (h0, r_h0) if dr == 0 else (h1, r_h1)
                            V(lambda e: e.tensor_tensor(out=hh[:, :], in0=ps[b][:, :], in1=dec[:, :], op=ALU.mult),
                              [ps_r[b], r_dec], [r_hh])
                        if tc == 0:
                            V(lambda e: e.memset(h1[0:1, :], 0.0), [], [r_h1])
                        V(lambda e: e.tensor_tensor(out=fa[:, tc, o * 512:(o + 1) * 512], in0=h0[:, :], in1=h1[:, :],
                                                    op=ALU.add), [r_h0, r_h1], [r_fa])
                        V(lambda e: e.tensor_tensor(out=fb[:, tc, o * 512:(o + 1) * 512], in0=h0[:, :], in1=h1[:, :],
                                                    op=ALU.subtract), [r_h0, r_h1], [r_fb])
                for fc in range(16):
                    Ft, r_Ft = Fb[fc % 2]
                    DMA(Ft[:, :, :], c_fwdF[fc], w=[r_Ft])
                    hot, r_hot = ho[fc % 2]
                    for o in range(2):
                        bre, bim = 4 + 2 * (o % 2), 5 + 2 * (o % 2)
                        for tc in range(16):
                            PE(lambda e: e.matmul(ps[bre][:, :], lhsT=Ft[:, tc, 0:128],
                                                  rhs=fa[:, tc, o * 512:(o + 1) * 512], start=(tc == 0), stop=(tc == 15)),
                               [r_Ft, r_fa], [ps_r[bre]])
                        for tc in range(16):
                            PE(lambda e: e.matmul(ps[bim][:, :], lhsT=Ft[:, tc, 128:256],
                                                  rhs=fb[:, tc, o * 512:(o + 1) * 512], start=(tc == 0), stop=(tc == 15)),
                               [r_Ft, r_fb], [ps_r[bim]])
                        V(lambda e: e.tensor_tensor(out=hot[:, 0, :], in0=ps[bre][:, :], in1=fbias[:, o, :], op=ALU.add),
                          [ps_r[bre], r_fbias], [r_hot])
                        A(lambda e: e.copy(out=hot[:, 1, :], in_=ps[bim][:, :]), [ps_r[bim]], [r_hot])
                        DMA(hfd[o, fc], hot[:, :, :], r=[r_hot], w=[r_hfd])
                        if fc == 0:
                            for tc in range(16):
                                PE(lambda e: e.matmul(ps[0][0:1, :], lhsT=Ft[:, tc, 128:129],
                                                      rhs=fa[:, tc, o * 512:(o + 1) * 512], start=(tc == 0), stop=(tc == 15)),
                                   [r_Ft, r_fa], [ps_r[0]])
                            V(lambda e: e.tensor_tensor(out=hnyq[0:1, o, :], in0=ps[0][0:1, :], in1=fbias[0:1, o, :],
                                                        op=ALU.add), [ps_r[0], r_fbias], [r_hnyq])
            S.barrier()

        def qk_norm_rope(nh, src_ps, psr, gainb, r_gain, cst, r_cs, t, qf, r_qf, qsq, r_qsq, ss, r_ss, qr, r_qr):
            W = nh * 64
            A(lambda e: e.copy(out=qf[:, 0:W], in_=src_ps), [psr], [r_qf])
            yield
            V(lambda e: e.tensor_tensor(out=qsq[:, 0:W], in0=qf[:, 0:W], in1=qf[:, 0:W], op=ALU.mult), [r_qf], [r_qsq])
            yield
            V(lambda e: e.tensor_reduce(out=ss[:, 0:nh], in_=qsq[:, 0:W].rearrange("p (h d) -> p h d", h=nh),
                                        axis=AX.X, op=ALU.add), [r_qsq], [r_ss])
            yield
            A(lambda e: e.activation(out=ss[:, 8:8 + nh], in_=ss[:, 0:nh], func=AF.Sqrt, scale=1.0 / 64.0, bias=eps_t[:, 0:1]),
              [r_ss], [r_ss])
            yield
            V(lambda e: e.reciprocal(out=ss[:, 16:16 + nh], in_=ss[:, 8:8 + nh]), [r_ss], [r_ss])
            yield
            q3 = qf[:, 0:W].rearrange("p (h d) -> p h d", h=nh)
            V(lambda e: e.tensor_tensor(out=q3, in0=q3, in1=ss[:, 16:16 + nh].unsqueeze(2).to_broadcast([128, nh, 64]),
                                        op=ALU.mult), [r_qf, r_ss], [r_qf])
            yield
            V(lambda e: e.tensor_tensor(out=qf[:, 0:W], in0=qf[:, 0:W], in1=gainb[:, 0:W], op=ALU.mult),
              [r_qf, r_gain], [r_qf])
            yield
            cosb = cst[:, t, 0:64].unsqueeze(1).to_broadcast([128, nh, 64])
            sinb = cst[:, t, 64:128].unsqueeze(1).to_broadcast([128, nh, 64])
            s3 = qsq[:, 0:W].rearrange("p (h d) -> p h d", h=nh)
            V(lambda e: e.tensor_tensor(out=s3, in0=q3, in1=sinb, op=ALU.mult), [r_qf, r_cs], [r_qsq])
            yield
            V(lambda e: e.tensor_tensor(out=q3, in0=q3, in1=cosb, op=ALU.mult), [r_qf, r_cs], [r_qf])
            yield
            r3 = qr[:, 0:W].rearrange("p (h d) -> p h d", h=nh)
            V(lambda e: e.tensor_tensor(out=r3[:, :, 0:32], in0=q3[:, :, 0:32], in1=s3[:, :, 32:64], op=ALU.subtract),
              [r_qf, r_qsq], [r_qr])
            yield
            V(lambda e: e.tensor_tensor(out=r3[:, :, 32:64], in0=q3[:, :, 32:64], in1=s3[:, :, 0:32], op=ALU.add),
              [r_qf, r_qsq], [r_qr])
            yield

        eps_t, r_eps = sb(es, "eps_t", [128, 1], F32)
        V(lambda e: e.memset(eps_t[:, :], EPS), [], [r_eps])

        def psb(i):
            return ps[i][:, :].bitcast(BF16)

        def mixer_seq(l, s, hv, r_hv, ybt, r_ybt):
            src_x = x_in if l == 0 else y_out
            with ExitStack() as st:
                hT, r_hT = sb(st, "hT", [128, 8, L + 2], BF16)
                modT, r_modT = sb(st, "modT", [128, 2, 8], F32)
                DMA(modT[:, 0, :], modd[s, 0:1024].rearrange("(k p) -> p k", p=128), r=[r_modd], w=[r_modT])
                DMA(modT[:, 1, :], modd[s, 1024:2048].rearrange("(k p) -> p k", p=128), r=[r_modd], w=[r_modT])
                V(lambda e: e.memset(hT[:, :, 0:1], 0.0), [], [r_hT])
                V(lambda e: e.memset(hT[:, :, L + 1:L + 2], 0.0), [], [r_hT])
                xt = [sb(st, f"xt{i}", [128, D], F32) for i in range(2)]
                junk, r_junk = sb(st, "junk", [128, D], BF16)
                xn = [sb(st, f"xn{i}", [128, D], BF16) for i in range(2)]
                st8, r_st8 = sb(st, "st8", [128, 8], F32)
                evt = [sb(st, f"evt{i}", [128, 8, 128], F32) for i in range(2)]
                r_hTt = [Res(f"hT{t}") for t in range(NT)]
                for t in range(NT):
                    x_t, r_x = xt[t % 2]
                    xn_t, r_xn = xn[t % 2]
                    b = t % 2
                    DMA(x_t[:, :], src_x[s, t * 128:(t + 1) * 128, :], r=[xres[s][t]], w=[r_x])
                    A(lambda e: e.activation(out=junk[:, :], in_=x_t[:, :], func=AF.Square, accum_out=st8[:, 0:1]),
                      [r_x], [r_junk, r_st8])
                    A(lambda e: e.activation(out=st8[:, 1:2], in_=st8[:, 0:1], func=AF.Sqrt, scale=1.0 / D,
                                             bias=eps_t[:, 0:1]), [r_st8], [r_st8])
                    V(lambda e: e.reciprocal(out=st8[:, 2:3], in_=st8[:, 1:2]), [r_st8], [r_st8])
                    A(lambda e: e.activation(out=xn_t[:, :], in_=x_t[:, :], func=AF.Copy, scale=st8[:, 2:3]),
                      [r_x, r_st8], [r_xn])
                    for k in range(8):
                        PE(lambda e: e.transpose(out=psb(b)[:, k * 128:(k + 1) * 128], in_=xn_t[:, k * 128:(k + 1) * 128],
                                                 identity=ident[:, :]), [r_xn, r_ident], [ps_r[b]])
                    ev_t, r_ev = evt[t % 2]
                    V(lambda e: e.tensor_tensor(out=ev_t[:, :, :], in0=psb(b).rearrange("p (k q) -> p k q", k=8),
                                                in1=modT[:, 1, :].unsqueeze(2).to_broadcast([128, 8, 128]), op=ALU.mult),
                      [ps_r[b], r_modT], [r_ev])
                    G(lambda e: e.tensor_tensor(out=hT[:, :, 1 + t * 128:1 + (t + 1) * 128], in0=ev_t[:, :, :],
                                                in1=modT[:, 0, :].unsqueeze(2).to_broadcast([128, 8, 128]), op=ALU.add),
                      [r_ev, r_modT], [r_hTt[t], r_hT])
                wr = [sb(st, f"wr{i}", [128, 8, 512], BF16) for i in range(4)]
                cbb, r_cbb = sb(st, "cbb", [128, 1536], F32)
                DMA(cbb[:, :], conv_b[l:l + 1, :].to_broadcast([128, 1536]), w=[r_cbb])
                stg = [sb(st, f"stg{i}", [128, 512], BF16) for i in range(4)]
                wi = [0]

                def getw(src, K=8, N=512):
                    i = wi[0] % 4
                    wi[0] += 1
                    wt, r_wt = wr[i]
                    DMA(wt[:, 0:K, 0:N], src, r=[r_w["whY"], r_w["wqa"], r_w["wkv"], r_w["wg"]], w=[r_wt])
                    return wt, r_wt
                pbank = [2]

                def nextbank():
                    b = pbank[0]
                    pbank[0] = 2 + (pbank[0] - 2 + 1) % 4
                    return b
                si = [0]
                for ob in range(3):
                    wts = [getw(whY[ob * 3 + j]) for j in range(3)]
                    for t in range(NT):
                        b = nextbank()
                        for j in range(3):
                            wt, r_wt = wts[j]
                            for k in range(8):
                                PE(lambda e: e.matmul(ps[b][:, :], lhsT=hT[:, k, t * 128 + j:t * 128 + j + 128],
                                                      rhs=wt[:, k, :], start=(j == 0 and k == 0), stop=(j == 2 and k == 7)),
                                   [r_hT, r_wt], [ps_r[b]])
                        if ob == 0:
                            V(lambda e: e.tensor_tensor(out=hv[:, t, :], in0=ps[b][:, :], in1=cbb[:, 0:512], op=ALU.add),
                              [ps_r[b], r_cbb], [r_hv[t]])
                        else:
                            sg, r_sg = stg[si[0] % 4]
                            si[0] += 1
                            V(lambda e: e.tensor_tensor(out=sg[:, :], in0=ps[b][:, :], in1=cbb[:, ob * 512:(ob + 1) * 512],
                                                        op=ALU.add), [ps_r[b], r_cbb], [r_sg])
                            DMA(x12[ob - 1, t * 128:(t + 1) * 128, :], sg[:, :], r=[r_sg], w=[r_x12])
                for gi in range(4):
                    wt, r_wt = getw(wg[gi])
                    for t in range(NT):
                        b = nextbank()
                        for k in range(8):
                            PE(lambda e: e.matmul(ps[b][:, :], lhsT=hT[:, k, t * 128 + 1:t * 128 + 129], rhs=wt[:, k, :],
                                                  start=(k == 0), stop=(k == 7)), [r_hT, r_wt], [ps_r[b]])
                        sg, r_sg = stg[si[0] % 4]
                        si[0] += 1
                        A(lambda e: e.activation(out=sg[:, :], in_=ps[b][:, :], func=AF.Sigmoid), [ps_r[b]], [r_sg])
                        DMA(sigd[gi // 2, t * 128:(t + 1) * 128, (gi % 2) * 512:(gi % 2 + 1) * 512], sg[:, :],
                            r=[r_sg], w=[r_sigd])
                qT, r_qT = sb(st, "qT", [64, 8, L], BF16)
                kT, r_kT = sb(st, "kT", [64, 2, L], BF16)
                vt, r_vt = sb(st, "vt", [128, NT, 2, 65], BF16)
                cst, r_cs = sb(st, "cst", [128, NT, 128], F32)
                qgb, r_qgb = sb(st, "qgb", [128, 512], F32)
                kgb, r_kgb = sb(st, "kgb", [128, 128], F32)
                esk, r_esk = sb(st, "esk", [128, 8], F32)
                DMA(cst[:, :, :], c_cs, w=[r_cs])
                for h in range(8):
                    DMA(qgb[:, h * 64:(h + 1) * 64], q_gain[l:l + 1, :].to_broadcast([128, 64]), w=[r_qgb])
                for h in range(2):
                    DMA(kgb[:, h * 64:(h + 1) * 64], k_gain[l:l + 1, :].to_broadcast([128, 64]), w=[r_kgb])
                V(lambda e: e.tensor_scalar(out=qgb[:, :], in0=qgb[:, :], scalar1=0.125, scalar2=None, op0=ALU.mult),
                  [r_qgb], [r_qgb])
                DMA(esk[:, :], sink[l:l + 1, :].to_broadcast([128, 8]), w=[r_esk])
                A(lambda e: e.activation(out=esk[:, :], in_=esk[:, :], func=AF.Exp), [r_esk], [r_esk])
                V(lambda e: e.memset(vt[:, :, :, 64:65], 1.0), [], [r_vt])
                qbufs = [(sb(st, f"qf{i}", [128, 512], F32), sb(st, f"qsq{i}", [128, 512], F32),
                          sb(st, f"ss{i}", [128, 24], F32), sb(st, f"qr{i}", [128, 512], BF16)) for i in range(2)]
                wt_q, r_wt_q = getw(wqa)
                wt_k, r_wt_k = getw(wkv, 8, 256)

                def qk_front(kind, t):
                    b = nextbank()
                    if kind == 0:
                        for k in range(8):
                            PE(lambda e: e.matmul(ps[b][:, :], lhsT=hT[:, k, t * 128 + 1:t * 128 + 129], rhs=wt_q[:, k, :],
                                                  start=(k == 0), stop=(k == 7)), [r_hT, r_wt_q], [ps_r[b]])
                    else:
                        for k in range(8):
                            PE(lambda e: e.matmul(ps[b][:, 0:256], lhsT=hT[:, k, t * 128 + 1:t * 128 + 129], rhs=wt_k[:, k, 0:256],
                                                  start=(k == 0), stop=(k == 7)), [r_hT, r_wt_k], [ps_r[b]])
                    return b

                def qk_back(kind, t, b, ui):
                    (qf_, r_qf_), (qsq_, r_qsq_), (ss_, r_ss_), (qr_, r_qr_) = qbufs[ui % 2]
                    tb = ui % 2
                    if kind == 0:
                        yield from qk_norm_rope(8, ps[b][:, :], ps_r[b], qgb, r_qgb, cst, r_cs, t, qf_, r_qf_, qsq_, r_qsq_, ss_, r_ss_, qr_, r_qr_)
                        for h in range(8):
                            PE(lambda e: e.transpose(out=psb(tb)[0:64, h * 128:(h + 1) * 128], in_=qr_[:, h * 64:(h + 1) * 64],
                                                     identity=ident[:, :]), [r_qr_, r_ident], [ps_r[tb]])
                        A(lambda e: e.copy(out=qT[:, :, t * 128:(t + 1) * 128],
                                           in_=psb(tb)[0:64, :].rearrange("p (h q) -> p h q", h=8)), [ps_r[tb]], [r_qT])
                    else:
                        A(lambda e: e.copy(out=vt[:, t, :, 0:64], in_=ps[b][:, 128:256].rearrange("p (h d) -> p h d", h=2)),
                          [ps_r[b]], [r_vt])
                        yield from qk_norm_rope(2, ps[b][:, 0:128], ps_r[b], kgb, r_kgb, cst, r_cs, t, qf_, r_qf_, qsq_, r_qsq_, ss_, r_ss_, qr_, r_qr_)
                        for h in range(2):
                            PE(lambda e: e.transpose(out=psb(tb)[0:64, h * 128:(h + 1) * 128], in_=qr_[:, h * 64:(h + 1) * 64],
                                                     identity=ident[:, :]), [r_qr_, r_ident], [ps_r[tb]])
                        A(lambda e: e.copy(out=kT[:, :, t * 128:(t + 1) * 128],
                                           in_=psb(tb)[0:64, 0:256].rearrange("p (h q) -> p h q", h=2)), [ps_r[tb]], [r_kT])

                units = [(0, t) for t in range(NT)] + [(1, t) for t in range(NT)]
                pend = [qk_front(*units[0]), qk_front(*units[1])]
                for pi in range(0, len(units), 2):
                    nxt = [qk_front(*units[pi + 2 + k]) for k in range(2) if pi + 2 + k < len(units)]
                    alive = [qk_back(units[pi + k][0], units[pi + k][1], pend[k], pi + k) for k in range(2)]
                    while alive:
                        for g in list(alive):
                            try:
                                next(g)
                            except StopIteration:
                                alive.remove(g)
                    pend = nxt
                mk, r_mk = sb(st, "mk", [128, 2, 512], BF16)
                DMA(mk[:, 0, :], c_masks[0], w=[r_mk])
                DMA(mk[:, 1, :], c_masks[1], w=[r_mk])
                pT = [sb(st, f"pT{i}", [128, 3, 512], BF16) for i in range(2)]
                den, r_den = sb(st, "den", [128, 8], F32)
                dens = [(den, r_den), sb(st, "den2", [128, 8], F32)]

                def att_front(ai, n, kvh):
                    p_t, r_p = pT[ai % 2]
                    kbs = [kb for kb in (n - 1, n, n + 1) if 0 <= kb < NT]
                    for i, kb in enumerate(kbs):
                        b = nextbank()
                        PE(lambda e: e.matmul(ps[b][:, :], lhsT=kT[:, kvh, kb * 128:(kb + 1) * 128],
                                              rhs=qT[:, 4 * kvh:4 * kvh + 4, n * 128:(n + 1) * 128],
                                              start=True, stop=(kb == n)), [r_kT, r_qT], [ps_r[b]])
                        if kb != n:
                            mi = 0 if kb < n else 1
                            PE(lambda e: e.matmul(ps[b][:, :], lhsT=ident[:, :], rhs=mk[:, mi, :], start=False, stop=True),
                               [r_ident, r_mk], [ps_r[b]])
                        A(lambda e: e.activation(out=p_t[:, i, :], in_=ps[b][:, :], func=AF.Exp), [ps_r[b]], [r_p])

                def att_back(ai, n, kvh):
                    p_t, r_p = pT[ai % 2]
                    dn, r_dn = dens[ai % 2]
                    ob_ = 6 + (ai % 2)
                    kbs = [kb for kb in (n - 1, n, n + 1) if 0 <= kb < NT]
                    for h in range(4):
                        for i, kb in enumerate(kbs):
                            PE(lambda e: e.matmul(ps[ob_][:, h * 65:(h + 1) * 65], lhsT=p_t[:, i, h * 128:(h + 1) * 128],
                                                  rhs=vt[:, kb, kvh, :], start=(i == 0), stop=(i == len(kbs) - 1)),
                               [r_p, r_vt], [ps_r[ob_]])
                    o3 = ps[ob_][:, 0:260].rearrange("p (h d) -> p h d", h=4)
                    V(lambda e: e.tensor_tensor(out=dn[:, 0:4], in0=o3[:, :, 64], in1=esk[:, 4 * kvh:4 * kvh + 4],
                                                op=ALU.add), [ps_r[ob_], r_esk], [r_dn])
                    V(lambda e: e.reciprocal(out=dn[:, 4:8], in_=dn[:, 0:4]), [r_dn], [r_dn])
                    V(lambda e: e.tensor_tensor(
                        out=ybt[:, n, kvh * 256:(kvh + 1) * 256].rearrange("p (h d) -> p h d", h=4),
                        in0=o3[:, :, 0:64], in1=dn[:, 4:8].unsqueeze(2).to_broadcast([128, 4, 64]), op=ALU.mult),
                      [ps_r[ob_], r_dn], [r_ybt[n]])

                aunits = [(n, kvh) for n in range(NT) for kvh in range(2)]
                att_front(0, *aunits[0])
                for ai, (n, kvh) in enumerate(aunits):
                    if ai + 1 < len(aunits):
                        att_front(ai + 1, *aunits[ai + 1])
                    att_back(ai, n, kvh)
            S.barrier()
            with ExitStack() as st:
                Y, r_Y = sb(st, "Y", [128, 32, 512], BF16)
                Fb = [sb(st, f"Fb{i}", [128, 16, 256], BF16) for i in range(2)]
                Hb = [sb(st, f"Hb{i}", [128, 2, 512], F32) for i in range(2)]
                Gb = [sb(st, f"Gb{i}", [128, 16, 512], BF16) for i in range(4)]
                tm = [sb(st, f"tm{i}", [128, 512], F32) for i in range(4)]
                xg = [sb(st, f"xg{i}", [128, 512], BF16) for i in range(2)]
                r_Yc = [Res(f"Y{c}") for c in range(32)]
                gi_ = [0]
                for o in range(2):
                    for fc in range(16):
                        Ft, r_Ft = Fb[fc % 2]
                        Ht, r_Ht = Hb[fc % 2]
                        DMA(Ft[:, :, :], c_fwdF[fc], w=[r_Ft])
                        DMA(Ht[:, :, :], hfd[o, fc], r=[r_hfd], w=[r_Ht])
                        bre, bim = 2 * (fc % 2), 2 * (fc % 2) + 1
                        for tc in range(16):
                            PE(lambda e: e.matmul(ps[bre][:, :], lhsT=Ft[:, tc, 0:128], rhs=hv[:, tc, :],
                                                  start=(tc == 0), stop=(tc == 15)), [r_Ft, r_hv[tc]], [ps_r[bre]])
                        for tc in range(16):
                            PE(lambda e: e.matmul(ps[bim][:, :], lhsT=Ft[:, tc, 128:256], rhs=hv[:, tc, :],
                                                  start=(tc == 0), stop=(tc == 15)), [r_Ft, r_hv[tc]], [ps_r[bim]])
                        (t1, r1), (t2, r2), (t3, r3), (t4, r4) = tm
                        V(lambda e: e.tensor_tensor(out=t1[:, :], in0=ps[bre][:, :], in1=Ht[:, 0, :], op=ALU.mult),
                          [ps_r[bre], r_Ht], [r1])
                        V(lambda e: e.tensor_tensor(out=t2[:, :], in0=ps[bim][:, :], in1=Ht[:, 1, :], op=ALU.mult),
                          [ps_r[bim], r_Ht], [r2])
                        V(lambda e: e.tensor_tensor(out=t3[:, :], in0=ps[bre][:, :], in1=Ht[:, 1, :], op=ALU.mult),
                          [ps_r[bre], r_Ht], [r3])
                        V(lambda e: e.tensor_tensor(out=t4[:, :], in0=ps[bim][:, :], in1=Ht[:, 0, :], op=ALU.mult),
                          [ps_r[bim], r_Ht], [r4])
                        G(lambda e: e.tensor_tensor(out=Y[:, fc, :], in0=t1[:, :], in1=t2[:, :], op=ALU.subtract),
                          [r1, r2], [r_Yc[fc]])
                        G(lambda e: e.tensor_tensor(out=Y[:, 16 + fc, :], in0=t3[:, :], in1=t4[:, :], op=ALU.add),
                          [r3, r4], [r_Yc[16 + fc]])
                        if fc == 0:
                            V(lambda e: e.tensor_copy(out=Y[0:1, 0, :], in_=t1[0:1, :]), [r1], [r_Yc[0]])
                            V(lambda e: e.tensor_tensor(out=Y[0:1, 16, :], in0=ps[bim][0:1, :], in1=hnyq[0:1, o, :],
                                                        op=ALU.mult), [ps_r[bim], r_hnyq], [r_Yc[16]])
                    for tg in range(4):
                        gts = []
                        for half in range(2):
                            g_t, r_g = Gb[gi_[0] % 4]
                            gi_[0] += 1
                            DMA(g_t[:, :, :], c_invG[tg, :, half * 16:(half + 1) * 16, :], w=[r_g])
                            gts.append((g_t, r_g))
                        for ti in range(4):
                            t = tg * 4 + ti
                            b = 4 + (t % 4)
                            x_t, r_xg = xg[t % 2]
                            DMA(x_t[:, :], x12[o, t * 128:(t + 1) * 128, :], r=[r_x12], w=[r_xg])
                            for c in range(32):
                                g_t, r_g = gts[c // 16]
                                PE(lambda e: e.matmul(ps[b][:, :], lhsT=g_t[:, c % 16, ti * 128:(ti + 1) * 128], rhs=Y[:, c, :],
                                                      start=(c == 0), stop=(c == 31)), [r_g, r_Yc[c]], [ps_r[b]])
                            V(lambda e: e.tensor_tensor(out=hv[:, t, :], in0=ps[b][:, :], in1=x_t[:, :], op=ALU.mult),
                              [ps_r[b], r_xg], [r_hv[t]])
            S.barrier()
            with ExitStack() as st:
                wpa_t, r_wpa = sb(st, "wpa_t", [128, 4, 1024], BF16)
                wpb_t, r_wpb = sb(st, "wpb_t", [128, 4, 1024], BF16)
                wo_t, r_wo = sb(st, "wo_t", [128, 8, 1024], BF16)
                gtb, r_gtb = sb(st, "gtb", [128, D], F32)
                DMA(wpa_t[:, :, :], wpa_s, r=[r_w["wpa"]], w=[r_wpa])
                DMA(wpb_t[:, :, :], wpb_s, r=[r_w["wpb"]], w=[r_wpb])
                DMA(wo_t[:, :, :], wout_s, r=[r_w["wout"]], w=[r_wo])
                DMA(gtb[:, :], modd[s:s + 1, 2048:3072].to_broadcast([128, D]), r=[r_modd], w=[r_gtb])
                abT = [sb(st, f"abT{i}", [128, 8, 128], BF16) for i in range(2)]
                mT = [sb(st, f"mT{i}", [128, 8, 128], BF16) for i in range(2)]
                sg = [sb(st, f"sg{i}", [128, 2, D], BF16) for i in range(2)]
                m1, r_m1 = sb(st, "m1", [128, D], F32)
                m2, r_m2 = sb(st, "m2", [128, D], F32)
                mg, r_mg = sb(st, "mg", [128, D], BF16)
                xt = [sb(st, f"xt5{i}", [128, D], F32) for i in range(2)]
                mgs = [(mg, r_mg), sb(st, "mg2", [128, D], BF16)]
                o1, r_o1 = sb(st, "o1", [128, D], F32)

                def stX(t):
                    ab, r_ab = abT[t % 2]
                    s_t, r_s = sg[t % 2]
                    x_t, r_x = xt[t % 2]
                    mg_t, r_mgt = mgs[t % 2]
                    DMA(s_t[:, 0, :], sigd[0, t * 128:(t + 1) * 128, :], r=[r_sigd], w=[r_s])
                    DMA(s_t[:, 1, :], sigd[1, t * 128:(t + 1) * 128, :], r=[r_sigd], w=[r_s])
                    DMA(x_t[:, :], src_x[s, t * 128:(t + 1) * 128, :], r=[xres[s][t]], w=[r_x])
                    for k in range(4):
                        PE(lambda e: e.transpose(out=psb(0)[:, k * 128:(k + 1) * 128], in_=hv[:, t, k * 128:(k + 1) * 128],
                                                 identity=ident[:, :]), [r_hv[t], r_ident], [ps_r[0]])
                    for k in range(4):
                        PE(lambda e: e.transpose(out=psb(0)[:, (4 + k) * 128:(5 + k) * 128],
                                                 in_=ybt[:, t, k * 128:(k + 1) * 128], identity=ident[:, :]),
                           [r_ybt[t], r_ident], [ps_r[0]])
                    A(lambda e: e.copy(out=ab[:, :, :], in_=psb(0).rearrange("p (k q) -> p k q", k=8)), [ps_r[0]], [r_ab])
                    for nb in range(2):
                        for k in range(4):
                            PE(lambda e: e.matmul(ps[1 + nb][:, :], lhsT=ab[:, k, :], rhs=wpa_t[:, k, nb * 512:(nb + 1) * 512],
                                                  start=(k == 0), stop=(k == 3)), [r_ab, r_wpa], [ps_r[1 + nb]])
                        for k in range(4):
                            PE(lambda e: e.matmul(ps[3 + nb][:, :], lhsT=ab[:, 4 + k, :], rhs=wpb_t[:, k, nb * 512:(nb + 1) * 512],
                                                  start=(k == 0), stop=(k == 3)), [r_ab, r_wpb], [ps_r[3 + nb]])
                    for nb in range(2):
                        V(lambda e: e.tensor_tensor(out=m1[:, nb * 512:(nb + 1) * 512], in0=ps[1 + nb][:, :],
                                                    in1=s_t[:, 0, nb * 512:(nb + 1) * 512], op=ALU.mult),
                          [ps_r[1 + nb], r_s], [r_m1])
                        V(lambda e: e.tensor_tensor(out=m2[:, nb * 512:(nb + 1) * 512], in0=ps[3 + nb][:, :],
                                                    in1=s_t[:, 1, nb * 512:(nb + 1) * 512], op=ALU.mult),
                          [ps_r[3 + nb], r_s], [r_m2])
                    G(lambda e: e.tensor_tensor(out=mg_t[:, :], in0=m1[:, :], in1=m2[:, :], op=ALU.add), [r_m1, r_m2], [r_mgt])

                def stY(t):
                    mt, r_mt = mT[t % 2]
                    x_t, r_x = xt[t % 2]
                    mg_t, r_mgt = mgs[t % 2]
                    for k in range(8):
                        PE(lambda e: e.transpose(out=psb(5)[:, k * 128:(k + 1) * 128], in_=mg_t[:, k * 128:(k + 1) * 128],
                                                 identity=ident[:, :]), [r_mgt, r_ident], [ps_r[5]])
                    A(lambda e: e.copy(out=mt[:, :, :], in_=psb(5).rearrange("p (k q) -> p k q", k=8)), [ps_r[5]], [r_mt])
                    for nb in range(2):
                        for k in range(8):
                            PE(lambda e: e.matmul(ps[6 + nb][:, :], lhsT=mt[:, k, :], rhs=wo_t[:, k, nb * 512:(nb + 1) * 512],
                                                  start=(k == 0), stop=(k == 7)), [r_mt, r_wo], [ps_r[6 + nb]])
                        V(lambda e: e.tensor_tensor(out=o1[:, nb * 512:(nb + 1) * 512], in0=ps[6 + nb][:, :],
                                                    in1=gtb[:, nb * 512:(nb + 1) * 512], op=ALU.mult),
                          [ps_r[6 + nb], r_gtb], [r_o1])
                    G(lambda e: e.tensor_tensor(out=x_t[:, :], in0=x_t[:, :], in1=o1[:, :], op=ALU.add), [r_x, r_o1], [r_x])
                    DMA(y_out[s, t * 128:(t + 1) * 128, :], x_t[:, :], r=[r_x], w=[xres[s][t]])

                stX(0)
                for t in range(NT):
                    if t + 1 < NT:
                        stX(t + 1)
                    stY(t)
            S.barrier()

        r_x12 = Res("x12")
        r_sigd = Res("sigd")
        r_uvd = [Res("uvd0"), Res("uvd1")]

        tbl_pending = []

        def issue_tables(l, deferred=False):
            CH = 256 if deferred else 512
            for which, tab in ((0, peer_u), (1, peer_v)):
                tv = tab.ap()[l * 16384:(l + 1) * 16384, :]
                for ch in range(16384 // CH):
                    def go(which=which, tv=tv, ch=ch):
                        S.dma("tbl", lambda e: e.dma_start(out=uvds[l].ap()[ch * CH:(ch + 1) * CH, which * D:(which + 1) * D],
                                                           in_=tv[ch * CH:(ch + 1) * CH, :]), [], [Res()], eng="gpsimd")
                    if deferred:
                        tbl_pending.append(go)
                    else:
                        go()

        def tbl_trickle(n=1):
            for _ in range(n):
                if tbl_pending:
                    tbl_pending.pop(0)()

        def peer_seq(l, s, from_input=False):
            src_x = x_in if from_input else y_out
            if l > 0:
                tbl_trickle(len(tbl_pending))
            S.wait_all("gpsimd", "tbl")
            uvd = uvds[l]
            with ExitStack() as st:
                wq_t, r_wq = sb(st, "wq_t", [128, 8, 2048], BF16)
                DMA(wq_t[:, :, :], pwq_s, r=[r_w["pwq"]], w=[r_wq])
                kst, r_kst = sb(st, "kst", [128, 2, 128], F32)
                k12, r_k12 = sb(st, "k12", [128, 2, 128], BF16)
                DMA(kst[:, 0, :], peer_k1T[l], w=[r_kst])
                DMA(kst[:, 1, :], peer_k2T[l], w=[r_kst])
                V(lambda e: e.tensor_copy(out=k12[:, :, :], in_=kst[:, :, :]), [r_kst], [r_k12])
                a2b, r_a2b = sb(st, "a2b", [128, D], F32)
                sh2b, r_sh2b = sb(st, "sh2b", [128, D], F32)
                gt2b, r_gt2b = sb(st, "gt2b", [128, D], F32)
                DMA(sh2b[:, :], modd[s:s + 1, 3072:4096].to_broadcast([128, D]), r=[r_modd], w=[r_sh2b])
                DMA(a2b[:, :], modd[s:s + 1, 4096:5120].to_broadcast([128, D]), r=[r_modd], w=[r_a2b])
                DMA(gt2b[:, :], modd[s:s + 1, 5120:6144].to_broadcast([128, D]), r=[r_modd], w=[r_gt2b])
                iot, r_iot = sb(st, "iot", [128, 16], F32)
                DMA(iot[:, :], c_iota, w=[r_iot])
                ioti, r_ioti = sb(st, "ioti", [128, 256], I32)
                DMA(ioti[:, :], c_iota256, w=[r_ioti])
                xt = [sb(st, f"xp{i}", [128, D], F32) for i in range(3)]
                h2, r_h2 = sb(st, "h2", [128, D], F32)
                fin, r_fin = h2, r_h2
                h2b, r_h2b = sb(st, "h2b", [128, D], BF16)
                junk, r_junk = sb(st, "junkp", [128, D], BF16)
                junks = [(junk, r_junk), sb(st, "junk2", [128, D], BF16)]
                junkA, r_junkA = h2b, r_h2b
                st8, r_st8 = sb(st, "st8p", [128, 8], F32)
                h2T, r_h2T = sb(st, "h2T", [128, 8, 128], BF16)
                qT, r_qT = sb(st, "qTp", [128, 16, 128], BF16)
                sc, r_sc = sb(st, "sc", [128, 2, 8, 128], F32)
                r_sch = [[Res(f"sc{a}{b}") for b in range(8)] for a in range(2)]
                vv, r_vv = sb(st, "vv", [128, 2, 8, 16], F32)
                r_vvh = [[Res(f"vv{a}{b}") for b in range(8)] for a in range(2)]
                r_vvall = r_vvh[0] + r_vvh[1]
                r_vsh = [Res(f"vs{b}") for b in range(8)]
                ii, r_ii = sb(st, "ii", [128, 2, 8, 16], U32)
                iif, r_iif = sb(st, "iif", [128, 2, 8, 16], BF16)
                cand, r_cand = sb(st, "cand", [128, 8, 256], F32)
                r_cdh = [Res(f"cd{b}") for b in range(8)]
                vs, r_vs = sb(st, "vs", [128, 8, 16], F32)
                ic, r_ic = sb(st, "ic", [128, 8, 16], U32)
                icab, r_icab = sb(st, "icab", [128, 2, 8, 16], U32)
                icf, r_icf = sb(st, "icf", [128, 2, 8, 16], BF16)
                oh, r_oh = sb(st, "oh", [128, 8, 16, 16], BF16)
                iotb, r_iotb = sb(st, "iotb", [128, 16], BF16)
                V(lambda e: e.tensor_copy(out=iotb[:, :], in_=iot[:, :]), [r_iot], [r_iotb])
                ef, r_ef = sb(st, "ef", [128, 2, 8, 16], F32)
                idxf, r_idxf = sb(st, "idxf", [128, 128], F32)
                idxs = [sb(st, f"idx{i}", [128, 128], I32) for i in range(2)]
                ggs = [sb(st, f"gg{i}", [128, 8, 16], F32) for i in range(2)]
                sm, r_sm = sb(st, "sm", [128, 16], F32)
                aa, _ = sb(st, "aa", [128, 128], F32)
                ww, _ = sb(st, "ww", [128, 128], F32)
                NGB = 6
                gbuf = [sb(st, f"gbuf{i}", [128, 4, 2 * D], BF16) for i in range(NGB)]
                r_gs = [[Res(f"gs{b}_{i}") for i in range(4)] for b in range(NGB)]
                wds = [sb(st, f"wd{i}", [128, 4, 128], BF16) for i in range(2)]
                r_aaj = [Res(f"aa{j}") for j in range(128)]
                r_wwb = [Res(f"ww{j}") for j in range(32)]
                gi_ = [0]
                HP = ((4, 5), (6, 7))
                ACC = (2, 3)
                BT, BS = 0, 1

                def stageA(t):
                    slot = t % 2
                    x_t, r_x = xt[t % 3]
                    idx, r_idx = idxs[slot]
                    gg, r_gg = ggs[slot]
                    H2P = HP[slot]
                    DMA(x_t[:, :], src_x[s, t * 128:(t + 1) * 128, :], r=[xres[s][t]], w=[r_x])
                    yield
                    A(lambda e: e.activation(out=junkA[:, :], in_=x_t[:, :], func=AF.Square, accum_out=st8[:, 0:1]),
                      [r_x], [r_junkA, r_st8])
                    A(lambda e: e.activation(out=st8[:, 1:2], in_=st8[:, 0:1], func=AF.Sqrt, scale=1.0 / D,
                                             bias=eps_t[:, 0:1]), [r_st8], [r_st8])
                    yield
                    V(lambda e: e.reciprocal(out=st8[:, 2:3], in_=st8[:, 1:2]), [r_st8], [r_st8])
                    V(lambda e: e.scalar_tensor_tensor(out=h2[:, :], in0=x_t[:, :], scalar=st8[:, 2:3], in1=a2b[:, :],
                                                       op0=ALU.mult, op1=ALU.mult), [r_x, r_st8, r_a2b], [r_h2])
                    V(lambda e: e.tensor_tensor(out=h2[:, :], in0=h2[:, :], in1=sh2b[:, :], op=ALU.add), [r_h2, r_sh2b], [r_h2])
                    A(lambda e: e.copy(out=h2b[:, :], in_=h2[:, :]), [r_h2], [r_h2b])
                    yield
                    for nb in range(2):
                        V(lambda e: e.tensor_copy(out=ps[H2P[nb]][:, :], in_=h2[:, nb * 512:(nb + 1) * 512]),
                          [r_h2], [ps_r[H2P[nb]]])
                    for k in range(8):
                        PE(lambda e: e.transpose(out=psb(BT)[:, k * 128:(k + 1) * 128], in_=h2b[:, k * 128:(k + 1) * 128],
                                                 identity=ident[:, :]), [r_h2b, r_ident], [ps_r[BT]])
                    yield
                    A(lambda e: e.copy(out=h2T[:, :, :], in_=psb(BT).rearrange("p (k q) -> p k q", k=8)), [ps_r[BT]], [r_h2T])
                    yield
                    for rnd in range(4):
                        for c4 in range(4):
                            cb = rnd * 4 + c4
                            for k in range(8):
                                PE(lambda e: e.matmul(ps[BT][:, c4 * 128:(c4 + 1) * 128], lhsT=wq_t[:, k, cb * 128:(cb + 1) * 128],
                                                      rhs=h2T[:, k, :], start=(k == 0), stop=(k == 7)),
                                   [r_wq, r_h2T], [ps_r[BT]])
                        yield
                        A(lambda e: e.copy(out=qT[:, rnd * 4:(rnd + 1) * 4, :],
                                           in_=ps[BT][:, :].rearrange("p (c q) -> p c q", c=4)), [ps_r[BT]], [r_qT])
                    yield
                    for hf in range(2):
                        for rnd in range(2):
                            for h4 in range(4):
                                h = rnd * 4 + h4
                                PE(lambda e: e.matmul(ps[BS][:, h4 * 128:(h4 + 1) * 128], lhsT=qT[:, 2 * h + hf, :],
                                                      rhs=k12[:, hf, :], start=True, stop=True), [r_qT, r_k12], [ps_r[BS]])
                            yield
                            A(lambda e: e.copy(out=sc[:, hf, rnd * 4:(rnd + 1) * 4, :],
                                               in_=ps[BS][:, :].rearrange("p (h n) -> p h n", h=4)), [ps_r[BS]],
                              [r_sch[hf][rnd * 4 + i] for i in range(4)])
                    yield
                    for hf in range(2):
                        sci = sc[:, hf, :, :].bitcast(U32)
                        V(lambda e: e.tensor_scalar(out=sci, in0=sci, scalar1=7, scalar2=7, op0=ALU.logical_shift_right,
                                                    op1=ALU.logical_shift_left), r_sch[hf], r_sch[hf])
                        V(lambda e: e.tensor_tensor(out=sci, in0=sci,
                                                    in1=ioti[:, 0:128].bitcast(U32).unsqueeze(1).to_broadcast([128, 8, 128]),
                                                    op=ALU.bitwise_or), r_sch[hf] + [r_ioti], r_sch[hf])
                        yield
                    for hf in range(2):
                        for h0 in range(0, 8, 2):
                            for h in (h0, h0 + 1):
                                V(lambda e: e.max(out=vv[:, hf, h, 0:8], in_=sc[:, hf, h, :]), [r_sch[hf][h]], [r_vvh[hf][h]])
                            for h in (h0, h0 + 1):
                                V(lambda e: e.match_replace(out=sc[:, hf, h, :], in_to_replace=vv[:, hf, h, 0:8],
                                                            in_values=sc[:, hf, h, :], imm_value=-1e30),
                                  [r_sch[hf][h], r_vvh[hf][h]], [r_sch[hf][h]])
                            for h in (h0, h0 + 1):
                                V(lambda e: e.max(out=vv[:, hf, h, 8:16], in_=sc[:, hf, h, :]), [r_sch[hf][h]], [r_vvh[hf][h]])
                            yield
                    V(lambda e: e.tensor_single_scalar(out=ii[:, :, :, :], in_=vv[:, :, :, :].bitcast(U32), scalar=127,
                                                       op=ALU.bitwise_and), r_vvall, [r_ii])
                    V(lambda e: e.tensor_copy(out=iif[:, :, :, :], in_=ii[:, :, :, :]), [r_ii], [r_iif])
                    c4v = cand[:, :, :].rearrange("p h (a b) -> p h a b", a=16)
                    V(lambda e: e.tensor_tensor(out=c4v, in0=vv[:, 0, :, :].unsqueeze(3).to_broadcast([128, 8, 16, 16]),
                                                in1=vv[:, 1, :, :].unsqueeze(2).to_broadcast([128, 8, 16, 16]), op=ALU.add),
                      r_vvall, r_cdh)
                    yield
                    cdi = cand[:, :, :].bitcast(U32)
                    V(lambda e: e.tensor_scalar(out=cdi, in0=cdi, scalar1=8, scalar2=8, op0=ALU.logical_shift_right,
                                                op1=ALU.logical_shift_left), r_cdh, r_cdh)
                    V(lambda e: e.tensor_tensor(out=cdi, in0=cdi,
                                                in1=ioti[:, :].bitcast(U32).unsqueeze(1).to_broadcast([128, 8, 256]),
                                                op=ALU.bitwise_or), r_cdh + [r_ioti], r_cdh)
                    yield
                    for h0 in range(0, 8, 2):
                        for h in (h0, h0 + 1):
                            V(lambda e: e.max(out=vs[:, h, 0:8], in_=cand[:, h, :]), [r_cdh[h]], [r_vsh[h]])
                        for h in (h0, h0 + 1):
                            V(lambda e: e.match_replace(out=cand[:, h, :], in_to_replace=vs[:, h, 0:8],
                                                        in_values=cand[:, h, :], imm_value=-1e30), [r_cdh[h], r_vsh[h]], [r_cdh[h]])
                        for h in (h0, h0 + 1):
                            V(lambda e: e.max(out=vs[:, h, 8:16], in_=cand[:, h, :]), [r_cdh[h]], [r_vsh[h]])
                        yield
                    V(lambda e: e.tensor_single_scalar(out=ic[:, :, :], in_=vs[:, :, :].bitcast(U32), scalar=255,
                                                       op=ALU.bitwise_and), r_vsh, [r_ic])
                    V(lambda e: e.tensor_single_scalar(out=icab[:, 0, :, :], in_=ic[:, :, :], scalar=4,
                                                       op=ALU.logical_shift_right), [r_ic], [r_icab])
                    V(lambda e: e.tensor_single_scalar(out=icab[:, 1, :, :], in_=ic[:, :, :], scalar=15,
                                                       op=ALU.bitwise_and), [r_ic], [r_icab])
                    V(lambda e: e.tensor_copy(out=icf[:, :, :, :], in_=icab[:, :, :, :]), [r_icab], [r_icf])
                    yield
                    for hf in range(2):
                        V(lambda e: e.tensor_tensor(out=oh[:, :, :, :],
                                                    in0=icf[:, hf, :, :].unsqueeze(3).to_broadcast([128, 8, 16, 16]),
                                                    in1=iotb[:, :].unsqueeze(1).unsqueeze(1).to_broadcast([128, 8, 16, 16]),
                                                    op=ALU.is_equal), [r_icf, r_iotb], [r_oh])
                        yield
                        V(lambda e: e.tensor_tensor(out=oh[:, :, :, :], in0=oh[:, :, :, :],
                                                    in1=iif[:, hf, :, :].unsqueeze(2).to_broadcast([128, 8, 16, 16]),
                                                    op=ALU.mult), [r_oh, r_iif], [r_oh])
                        yield
                        V(lambda e: e.tensor_reduce(out=ef[:, hf, :, :], in_=oh[:, :, :, :], axis=AX.X, op=ALU.add),
                          [r_oh], [r_ef])
                        yield
                    V(lambda e: e.scalar_tensor_tensor(out=idxf[:, :], in0=ef[:, 0, :, :].rearrange("p h k -> p (h k)"),
                                                       scalar=128.0, in1=ef[:, 1, :, :].rearrange("p h k -> p (h k)"),
                                                       op0=ALU.mult, op1=ALU.add), [r_ef], [r_idxf])
                    V(lambda e: e.tensor_copy(out=idx[:, :], in_=idxf[:, :]), [r_idxf], [r_idx])
                    V(lambda e: e.tensor_tensor(out=gg[:, :, :], in0=vs[:, :, :],
                                                in1=vs[:, :, 0:1].to_broadcast([128, 8, 16]), op=ALU.subtract), r_vsh, [r_gg])
                    A(lambda e: e.activation(out=gg[:, :, :], in_=gg[:, :, :], func=AF.Exp), [r_gg], [r_gg])
                    yield
                    V(lambda e: e.tensor_reduce(out=sm[:, 0:8], in_=gg[:, :, :], axis=AX.X, op=ALU.add), [r_gg], [r_sm])
                    V(lambda e: e.reciprocal(out=sm[:, 8:16], in_=sm[:, 0:8]), [r_sm], [r_sm])
                    V(lambda e: e.tensor_tensor(out=gg[:, :, :], in0=gg[:, :, :],
                                                in1=sm[:, 8:16].unsqueeze(2).to_broadcast([128, 8, 16]), op=ALU.mult),
                      [r_gg, r_sm], [r_gg])
                    yield

                def stageB(t, agen, prev_tail):
                    slot = t % 2
                    x_t, r_x = xt[t % 3]
                    idx, r_idx = idxs[slot]
                    gg, r_gg = ggs[slot]
                    H2P = HP[slot]
                    NBT = 32
                    pbuf = {}
                    for bt in range(NBT + 1):
                        if bt < NBT:
                            b = gi_[0] % NGB
                            gi_[0] += 1
                            pbuf[bt] = b
                            g_t = gbuf[b][0]
                            for i in range(4):
                                j = bt * 4 + i
                                S.dma("gpsimd", lambda e: e.indirect_dma_start(
                                    out=g_t[:, i, :], out_offset=None, in_=uvd[:, :],
                                    in_offset=bass.IndirectOffsetOnAxis(ap=idx[:, j:j + 1], axis=0)),
                                    [r_idx], [r_gs[b][i]])
                            for i in range(4):
                                j = bt * 4 + i
                                jk, r_jk = junks[j % 2]
                                V(lambda e: e.scalar_tensor_tensor(
                                    out=jk[:, :], in0=g_t[:, i, 0:D], scalar=1.0, in1=ps2(H2P), op0=ALU.mult, op1=ALU.mult,
                                    accum_out=aa[:, j:j + 1]),
                                  [r_gs[b][i], ps_r[H2P[0]], ps_r[H2P[1]]], [r_jk, r_aaj[j]])
                            A(lambda e: e.activation(out=ww[:, bt * 4:(bt + 1) * 4], in_=aa[:, bt * 4:(bt + 1) * 4], func=AF.Gelu),
                              [r_aaj[bt * 4 + i] for i in range(4)], [r_wwb[bt]])
                        if bt == 0 and prev_tail is not None:
                            prev_tail()
                        if bt in (18, 22, 26, 30):
                            tbl_trickle(1)
                        if agen is not None:
                            npull = 1 if bt in (0, 2) else (0 if bt < 4 else (1 if bt < 16 else 2))
                            for _ in range(npull):
                                next(agen, None)
                        if bt >= 1:
                            pb_ = bt - 1
                            b = pbuf[pb_]
                            g_t = gbuf[b][0]
                            wd_t, r_wd = wds[pb_ % 2]
                            V(lambda e: e.tensor_tensor(out=ww[:, pb_ * 4:(pb_ + 1) * 4], in0=ww[:, pb_ * 4:(pb_ + 1) * 4],
                                                        in1=gg[:, :, :].rearrange("p h k -> p (h k)")[:, pb_ * 4:(pb_ + 1) * 4],
                                                        op=ALU.mult), [r_wwb[pb_], r_gg], [r_wwb[pb_]])
                            for i in range(4):
                                j = pb_ * 4 + i
                                A(lambda e: e.activation(out=wd_t[:, i, :], in_=ident[:, :], func=AF.Copy, scale=ww[:, j:j + 1]),
                                  [r_ident, r_wwb[pb_]], [r_wd])
                            for i in range(4):
                                j = pb_ * 4 + i
                                for nb in range(2):
                                    PE(lambda e: e.matmul(ps[ACC[nb]][:, :], lhsT=wd_t[:, i, :],
                                                          rhs=g_t[:, i, D + nb * 512:D + (nb + 1) * 512],
                                                          start=(j == 0), stop=(j == 127)),
                                       [r_wd, r_gs[b][i]], [ps_r[ACC[nb]]])
                    if agen is not None:
                        for _ in agen:
                            pass
                    def tail():
                        V(lambda e: e.tensor_tensor(out=fin[:, :], in0=ps2(ACC), in1=gt2b[:, :], op=ALU.mult),
                          [ps_r[ACC[0]], ps_r[ACC[1]], r_gt2b], [r_fin])
                        V(lambda e: e.tensor_tensor(out=x_t[:, :], in0=x_t[:, :], in1=fin[:, :], op=ALU.add), [r_x, r_fin], [r_x])
                        DMA(y_out[s, t * 128:(t + 1) * 128, :], x_t[:, :], r=[r_x], w=[xres[s][t]])
                    return tail

                for _ in stageA(0):
                    pass
                tail = None
                for t in range(NTOK_PEER):
                    agen = stageA(t + 1) if t + 1 < NTOK_PEER else None
                    tail = stageB(t, agen, tail)
                tail()
            S.barrier()

        if do_peer:
            issue_tables(0)
            if DEPTH > 1:
                issue_tables(1, deferred=True)
        for l in range(DEPTH):
            mod_phase(l)
            prep_weights(l)
            if do_mixer:
                filter_phase(l)
            if do_peer and l == 0:
                pass
            for s in range(NSEQ):
                if do_mixer:
                    with ExitStack() as sq:
                        hv, _ = sb(sq, "hv", [128, NT, 512], BF16)
                        ybt, _ = sb(sq, "ybt", [128, NT, 512], BF16)
                        r_hv = [Res(f"hv{t}") for t in range(NT)]
                        r_ybt = [Res(f"yb{t}") for t in range(NT)]
                        mixer_seq(l, s, hv, r_hv, ybt, r_ybt)
                if do_peer:
                    peer_seq(l, s, from_input=(l == 0 and not do_mixer))
        S.barrier(include_tbl=True)
        print("bass program: ops", S.nops, "waits", S.nwaits, flush=True)
    return nc


N_CORES = 8
_NC_CACHE = {}


def kernel(**inputs):
    f32 = np.float32
    x = np.concatenate([np.asarray(inputs["x_prompt"], f32), np.asarray(inputs["x_sample"], f32)], axis=0)
    c = np.concatenate([np.asarray(inputs["c_prompt"], f32), np.asarray(inputs["c_sample"], f32)], axis=0)
    nseq = x.shape[0] // N_CORES
    cst = _consts()
    shared = {}
    for k in ("w_mod", "b_mod", "g_norm1", "g_norm2", "w_in", "conv_w", "conv_b", "f_w1", "f_b1", "f_freq", "f_w2",
              "f_b2", "f_w3", "f_bias", "q_gain", "k_gain", "sink", "w_pa", "w_pb", "w_out", "peer_wq"):
        shared[k] = np.ascontiguousarray(np.asarray(inputs[k], f32))
    shared["peer_k1T"] = np.ascontiguousarray(np.asarray(inputs["peer_k1"], f32).transpose(0, 2, 1))
    shared["peer_k2T"] = np.ascontiguousarray(np.asarray(inputs["peer_k2"], f32).transpose(0, 2, 1))
    shared["peer_u"] = np.ascontiguousarray(np.asarray(inputs["peer_u"], f32).reshape(2 * 16384, D))
    shared["peer_v"] = np.ascontiguousarray(np.asarray(inputs["peer_v"], f32).reshape(2 * 16384, D))
    for k in ("fwdF", "invG", "zfT", "negt", "absdelta", "cs", "masks", "ident", "iota16", "iota256"):
        shared[k] = cst[k]
    in_maps = []
    for i in range(N_CORES):
        m = dict(shared)
        m["x"] = np.ascontiguousarray(x[i * nseq:(i + 1) * nseq])
        ci = c[i * nseq:(i + 1) * nseq]
        m["cT"] = np.ascontiguousarray(ci.T.reshape(8, 128, nseq).transpose(1, 0, 2))
        in_maps.append(m)
    if "nc" not in _NC_CACHE:
        _NC_CACHE["nc"] = build(NSEQ=nseq, DEPTH=2)
    res = run_bass_kernel_spmd(_NC_CACHE["nc"], in_maps, core_ids=list(range(N_CORES)))
    y = np.concatenate([np.asarray(r["y"], f32) for r in res.results], axis=0)
    nb = inputs["x_prompt"].shape[0]
    return (np.ascontiguousarray(y[:nb]), np.ascontiguousarray(y[nb:]))
```
